# Optimizing a Trainium2 kernel written in Bass

```python
import jax
import jax.numpy as jnp
from jax import lax
import numpy as np

D_MODEL = 1024
BATCH = 8
SEQ = 4096
DEPTH = 2

DN_HEADS = 4
DN_HEAD_DIM = 128
DN_WIDTH = DN_HEADS * DN_HEAD_DIM
DN_CONV = 4
DN_CHUNK = 64
MB_HEADS = 8
MB_HEAD_DIM = 64
MB_WIDTH = MB_HEADS * MB_HEAD_DIM
MB_BLOCK = 256
MB_TOPK = 3
MB_QCHUNK = 32
N_BRANCH = 2
IN_COLS = 4 * DN_WIDTH + 2 * DN_HEADS + 4 * MB_WIDTH + N_BRANCH * D_MODEL
NORM_EPS = 1e-6

kernel_name = 'hybrid_deltanet_moba_gated_block'


def rms_norm(x, gain):
    xf = x.astype(jnp.float32)
    y = xf * lax.rsqrt(jnp.mean(xf * xf, axis=-1, keepdims=True) + NORM_EPS)
    return (y * gain.astype(jnp.float32)).astype(x.dtype)


def l2_normalize(x):
    xf = x.astype(jnp.float32)
    return (xf * lax.rsqrt(jnp.sum(xf * xf, axis=-1, keepdims=True) + NORM_EPS)).astype(x.dtype)


def causal_depthwise_conv(x, w):
    k_width, channels = w.shape
    return lax.conv_general_dilated(
        x, w[:, None, :].astype(x.dtype), window_strides=(1,),
        padding=((k_width - 1, 0),), dimension_numbers=('NWC', 'WIO', 'NWC'),
        feature_group_count=channels)


def to_heads(t, n_heads):
    b, s, _ = t.shape
    return t.reshape(b, s, n_heads, -1).transpose(0, 2, 1, 3)


def gated_delta_rule_chunked(q, k, v, g, beta):
    out_dtype = v.dtype
    f32 = jnp.float32
    b, h, s, dk = k.shape
    dv = v.shape[-1]
    c = DN_CHUNK
    n = s // c
    beta = beta.astype(f32)[..., None]
    qf = (q.astype(f32) * (dk ** -0.5)).reshape(b, h, n, c, dk)
    kf = k.astype(f32)
    vf = v.astype(f32)
    k_beta = (kf * beta).reshape(b, h, n, c, dk)
    v_beta = (vf * beta).reshape(b, h, n, c, dv)
    kf = kf.reshape(b, h, n, c, dk)
    g = jnp.cumsum(g.astype(f32).reshape(b, h, n, c), axis=-1)
    causal = jnp.tril(jnp.ones((c, c), dtype=bool))
    strict = jnp.tril(jnp.ones((c, c), dtype=bool), k=-1)
    decay = jnp.exp(jnp.where(causal, g[..., :, None] - g[..., None, :], -jnp.inf))
    a = jnp.where(strict, jnp.einsum('bhnid,bhnjd->bhnij', k_beta, kf) * decay, 0.0)
    eye = jnp.eye(c, dtype=f32)
    t_inv = lax.linalg.triangular_solve(eye + a, jnp.broadcast_to(eye, a.shape),
                                        left_side=True, lower=True)
    u = t_inv @ v_beta
    w = t_inv @ (k_beta * jnp.exp(g)[..., None])
    qk = jnp.einsum('bhnid,bhnjd->bhnij', qf, kf) * decay

    def chunk_step(state, xs):
        q_i, k_i, u_i, w_i, qk_i, g_i = xs
        v_new = u_i - w_i @ state
        o_i = (q_i * jnp.exp(g_i)[..., None]) @ state + qk_i @ v_new
        g_last = g_i[..., -1:]
        state = state * jnp.exp(g_last)[..., None] + jnp.einsum(
            'bhcd,bhce->bhde', k_i * jnp.exp(g_last - g_i)[..., None], v_new)
        return state, o_i

    xs = tuple(jnp.moveaxis(t_, 2, 0) for t_ in (qf, kf, u, w, qk, g))
    state0 = jnp.zeros((b, h, dk, dv), f32)
    _, o = lax.scan(chunk_step, state0, xs)
    return jnp.moveaxis(o, 0, 2).reshape(b, h, s, dv).astype(out_dtype)


def moba_attention(q, k, v):
    b, h, s, dh = q.shape
    nb = s // MB_BLOCK
    n_sel = min(MB_TOPK, nb)
    scale = dh ** -0.5
    kb = k.reshape(b, h, nb, MB_BLOCK, dh)
    vb = v.reshape(b, h, nb, MB_BLOCK, dh)
    k_mean = jnp.mean(kb.astype(jnp.float32), axis=3).astype(k.dtype)
    bi = jnp.arange(b)[:, None, None, None]
    hi = jnp.arange(h)[None, :, None, None]
    blk_ids = jnp.arange(nb)

    def attend_chunk(ci):
        start = ci * MB_QCHUNK
        own = start // MB_BLOCK
        q_c = lax.dynamic_slice_in_dim(q, start, MB_QCHUNK, axis=2)
        q_pos = start + jnp.arange(MB_QCHUNK)
        gate = jnp.einsum('bhqd,bhnd->bhqn', q_c, k_mean).astype(jnp.float32)
        gate = jnp.where(blk_ids < own, gate, -jnp.inf)
        _, sel = lax.top_k(gate, n_sel)
        sel_ok = sel < own
        k_sel = kb[bi, hi, sel]
        v_sel = vb[bi, hi, sel]
        s_sel = jnp.einsum('bhqd,bhqjtd->bhqjt', q_c, k_sel).astype(jnp.float32) * scale
        s_sel = jnp.where(sel_ok[..., None], s_sel, -jnp.inf).reshape(b, h, MB_QCHUNK, n_sel * MB_BLOCK)
        k_own = lax.dynamic_slice_in_dim(k, own * MB_BLOCK, MB_BLOCK, axis=2)
        v_own = lax.dynamic_slice_in_dim(v, own * MB_BLOCK, MB_BLOCK, axis=2)
        k_pos = own * MB_BLOCK + jnp.arange(MB_BLOCK)
        s_own = jnp.einsum('bhqd,bhtd->bhqt', q_c, k_own).astype(jnp.float32) * scale
        s_own = jnp.where(k_pos[None, :] <= q_pos[:, None], s_own, -jnp.inf)
        p = jax.nn.softmax(jnp.concatenate([s_own, s_sel], axis=-1), axis=-1).astype(v.dtype)
        p_own = p[..., :MB_BLOCK]
        p_sel = p[..., MB_BLOCK:].reshape(b, h, MB_QCHUNK, n_sel, MB_BLOCK)
        return (jnp.einsum('bhqt,bhtd->bhqd', p_own, v_own)
                + jnp.einsum('bhqjt,bhqjtd->bhqd', p_sel, v_sel))

    out = lax.map(attend_chunk, jnp.arange(s // MB_QCHUNK))
    return jnp.moveaxis(out, 0, 2).reshape(b, h, s, dh)


def hybrid_layer(x, c, w_ada, b_ada, g_pre, g_post, w_in, conv_w, a_log, dt_bias,
                 dn_norm_g, w_proj_dn, w_proj_mb, w_out):
    b, s, _ = x.shape
    f32 = jnp.float32
    shift, scale, gate = jnp.split(jax.nn.silu(c) @ w_ada + b_ada, 3, axis=-1)
    h = rms_norm(x, g_pre) * (1 + scale[:, None, :]) + shift[:, None, :]
    sizes = (3 * DN_WIDTH, DN_WIDTH, DN_HEADS, DN_HEADS, 3 * MB_WIDTH, MB_WIDTH, N_BRANCH * D_MODEL)
    cuts = np.cumsum(sizes)[:-1].tolist()
    qkv_dn, z_dn, beta_logit, a_logit, qkv_mb, z_mb, merge_logit = jnp.split(h @ w_in, cuts, axis=-1)

    qkv_dn = jax.nn.silu(causal_depthwise_conv(qkv_dn, conv_w))
    q_dn, k_dn, v_dn = (to_heads(t_, DN_HEADS) for t_ in jnp.split(qkv_dn, 3, axis=-1))
    beta = jax.nn.sigmoid(beta_logit).transpose(0, 2, 1)
    g = (-jnp.exp(a_log.astype(f32))
         * jax.nn.softplus(a_logit.astype(f32) + dt_bias.astype(f32))).transpose(0, 2, 1)
    o_dn = gated_delta_rule_chunked(l2_normalize(q_dn), l2_normalize(k_dn), v_dn, g, beta)
    o_dn = rms_norm(o_dn.transpose(0, 2, 1, 3), dn_norm_g) * jax.nn.silu(
        z_dn.reshape(b, s, DN_HEADS, DN_HEAD_DIM))
    y_dn = o_dn.reshape(b, s, DN_WIDTH) @ w_proj_dn

    pad = (-s) % MB_BLOCK
    q_mb, k_mb, v_mb = (jnp.pad(to_heads(t_, MB_HEADS), ((0, 0), (0, 0), (0, pad), (0, 0)))
                        for t_ in jnp.split(qkv_mb, 3, axis=-1))
    o_mb = moba_attention(q_mb, k_mb, v_mb)[:, :, :s]
    o_mb = o_mb.transpose(0, 2, 1, 3).reshape(b, s, MB_WIDTH) * jax.nn.silu(z_mb)
    y_mb = o_mb @ w_proj_mb

    gate_dn, gate_mb = jnp.split(jax.nn.sigmoid(merge_logit), N_BRANCH, axis=-1)
    mixed = (gate_dn * y_dn + gate_mb * y_mb) @ w_out
    return x + gate[:, None, :] * rms_norm(mixed, g_post)


def setup_inputs(seed: int = 0) -> dict:
    key = jax.random.key(seed)
    ks = jax.random.split(key, 16)
    nrm = jax.random.normal
    d = D_MODEL
    dt = jnp.exp(jax.random.uniform(ks[9], (DEPTH, DN_HEADS), minval=np.log(1e-3), maxval=np.log(1e-1)))
    return {
        'x': nrm(ks[0], (BATCH, SEQ, d), jnp.float32),
        'c': nrm(ks[1], (BATCH, d), jnp.float32),
        'w_ada': nrm(ks[2], (DEPTH, d, 3 * d), jnp.float32) * (0.5 * d ** -0.5),
        'b_ada': nrm(ks[3], (DEPTH, 3 * d), jnp.float32) * 0.1,
        'g_pre': 1.0 + 0.02 * nrm(ks[4], (DEPTH, d), jnp.float32),
        'g_post': 1.0 + 0.02 * nrm(ks[5], (DEPTH, d), jnp.float32),
        'w_in': nrm(ks[6], (DEPTH, d, IN_COLS), jnp.float32) * d ** -0.5,
        'conv_w': nrm(ks[7], (DEPTH, DN_CONV, 3 * DN_WIDTH), jnp.float32) * DN_CONV ** -0.5,
        'a_log': jnp.log(jax.random.uniform(ks[8], (DEPTH, DN_HEADS), minval=1.0, maxval=16.0)),
        'dt_bias': dt + jnp.log(-jnp.expm1(-dt)),
        'dn_norm_g': 1.0 + 0.02 * nrm(ks[10], (DEPTH, DN_HEAD_DIM), jnp.float32),
        'w_proj_dn': nrm(ks[11], (DEPTH, DN_WIDTH, d), jnp.float32) * DN_WIDTH ** -0.5,
        'w_proj_mb': nrm(ks[12], (DEPTH, MB_WIDTH, d), jnp.float32) * MB_WIDTH ** -0.5,
        'w_out': nrm(ks[13], (DEPTH, d, d), jnp.float32) * d ** -0.5,
    }


def reference(x, c, w_ada, b_ada, g_pre, g_post, w_in, conv_w, a_log, dt_bias,
              dn_norm_g, w_proj_dn, w_proj_mb, w_out):
    for l in range(DEPTH):
        x = hybrid_layer(x, c, w_ada[l], b_ada[l], g_pre[l], g_post[l], w_in[l], conv_w[l],
                         a_log[l], dt_bias[l], dn_norm_g[l], w_proj_dn[l], w_proj_mb[l], w_out[l])
    return x
```

```python
from contextlib import ExitStack

import numpy as np
import concourse.bass as bass
import concourse.mybir as mybir
from concourse.bass_utils import run_bass_kernel_spmd

F32 = mybir.dt.float32
BF16 = mybir.dt.bfloat16
F32R = mybir.dt.float32r


def R(ap):
    return ap.bitcast(F32R)
AF = mybir.ActivationFunctionType
ALU = mybir.AluOpType
AX = mybir.AxisListType

D_MODEL = 1024
NEG = -30000.0
EPS = 1e-6


class SV:
    def __init__(self, tile, j, n=1):
        self.tile, self.sub = tile, j
        self.ap = tile[:, j * 128:(j + n) * 128]

    def __getitem__(self, idx):
        return self.ap[idx]


class KB:
    NRING = {"sp": 6, "pool": 4}

    def __init__(self, nc, stack):
        self.nc = nc
        self.eng = {"pe": nc.tensor, "act": nc.scalar, "dve": nc.vector,
                    "pool": nc.gpsimd, "sp": nc.sync}
        self.sem = {}
        for e in ("pe", "act", "dve", "pool"):
            self.sem[e] = stack.enter_context(nc.semaphore("s_" + e))
        self.ring = {q: [stack.enter_context(nc.semaphore("s_dma_%s%d" % (q, i))) for i in range(n)]
                     for q, n in self.NRING.items()}
        self.ndmaq = {q: 0 for q in self.NRING}
        self.count = {e: 0 for e in ("pe", "act", "dve", "pool")}
        self.waited = {}
        self.track = {}
        self.ninstr = 0
        self.streams = {e: [] for e in self.eng}
        self.psum_ids = set()

    def _semof(self, dep):
        if dep[0] == "e":
            return self.sem[dep[1]], ("e", dep[1])
        return self.ring[dep[1][0]][dep[1][1]], ("d", dep[1])

    def _wait(self, e, dep):
        sem, sk = self._semof(dep)
        val = dep[2]
        k = (e, sk)
        if self.waited.get(k, 0) >= val:
            return
        self.eng[e].wait_ge(sem, val)
        self.streams[e].append(("wait", sk, val))
        self.ninstr += 1
        self.waited[k] = val

    @staticmethod
    def _keys(items):
        out = []
        for it in items:
            if isinstance(it, SV):
                out.append((id(it.tile), it.sub))
            elif isinstance(it, tuple):
                out.append((id(it[0]), it[1]))
            else:
                out.append((id(it), None))
        return out

    def _conflicts(self, key):
        tid, sub = key
        d = self.track.get(tid)
        if d is None:
            return []
        if sub is None:
            return list(d.values())
        res = []
        if sub in d:
            res.append(d[sub])
        if None in d:
            res.append(d[None])
        return res

    def _entry(self, key):
        tid, sub = key
        d = self.track.setdefault(tid, {})
        if sub is None:
            ent = {"w": [], "r": []}
            for v in d.values():
                ent["w"] += v["w"]
                ent["r"] += v["r"]
            d.clear()
            d[None] = ent
            return ent
        if sub not in d:
            ent = {"w": [], "r": []}
            if None in d:
                ent["w"] = list(d[None]["w"])
                ent["r"] = list(d[None]["r"])
            d[sub] = ent
        return d[sub]

    def _deps(self, e, reads, writes):
        deps = []
        for k in reads:
            for ent in self._conflicts(k):
                for w in ent["w"]:
                    deps.append((w, "raw"))
        for k in writes:
            for ent in self._conflicts(k):
                for w in ent["w"]:
                    deps.append((w, "waw"))
                for r in ent["r"]:
                    deps.append((r, "war"))
        out = []
        for dep, kind in deps:
            if dep[0] == "e" and dep[1] == e:
                if e == "pe":
                    continue
            out.append(dep)
        return out

    @staticmethod
    def _prune(lst):
        best = {}
        for d in lst:
            kk = (d[0], d[1])
            if kk not in best or best[kk][2] < d[2]:
                best[kk] = d
        return list(best.values())

    def _record(self, me, reads, writes):
        for k in reads:
            ent = self._entry(k)
            ent["r"].append(me)
            if len(ent["r"]) > 16:
                ent["r"] = self._prune(ent["r"])
        for k in writes:
            ent = self._entry(k)
            ent["w"] = [me]
            ent["r"] = []

    def _rw(self, r, w):
        reads, writes = [], []
        for k in self._keys(r):
            if k[0] in self.psum_ids:
                writes.append((k[0], None))
            else:
                reads.append(k)
        for k in self._keys(w):
            writes.append((k[0], None) if k[0] in self.psum_ids else k)
        return reads, writes

    def op(self, e, fn, r=(), w=(), inc=True):
        reads, writes = self._rw(r, w)
        for dep in self._deps(e, reads, writes):
            self._wait(e, dep)
        ins = fn(self.eng[e])
        self.ninstr += 1
        if inc:
            self.count[e] += 1
            ins.then_inc(self.sem[e], 1)
            self.streams[e].append(("inc", ("e", e), 1))
            me = ("e", e, self.count[e])
        else:
            me = ("e", e, self.count[e] + 1)
        self._record(me, reads, writes)
        return ins

    def dma(self, out, in_, r=(), w=(), q="sp", **kw):
        e = q
        reads = self._keys(r)
        writes = self._keys(w)
        k = self.ndmaq[q]
        nr = self.NRING[q]
        slot = k % nr
        gen = k // nr
        for dep in self._deps(e, reads, writes):
            self._wait(e, dep)
        if gen > 0:
            self._wait(e, ("d", (q, slot), 16 * gen))
        ins = self.eng[e].dma_start(out=out, in_=in_, **kw)
        ins.then_inc(self.ring[q][slot], 16)
        self.streams[e].append(("inc", ("d", (q, slot)), 16))
        self.ndmaq[q] += 1
        self.ninstr += 1
        me = ("d", (q, slot), 16 * (gen + 1))
        self._record(me, reads, writes)
        return ins

    def _alldma(self):
        out = []
        for q, nr in self.NRING.items():
            n = self.ndmaq[q]
            for slot in range(nr):
                cnt = (n - 1 - slot) // nr + 1 if n > slot else 0
                if cnt > 0:
                    out.append(("d", (q, slot), 16 * cnt))
        return out

    def barrier(self):
        for e in ("pe", "act", "dve", "pool", "sp"):
            for o in ("pe", "act", "dve", "pool"):
                if o != e and self.count[o] > 0:
                    self._wait(e, ("e", o, self.count[o]))
            for dep in self._alldma():
                self._wait(e, dep)
        self.track = {}

    def finish(self):
        for dep in self._alldma():
            self._wait("sp", dep)

    def simulate(self):
        sems = {}
        pc = {e: 0 for e in self.streams}
        progress = True
        while progress:
            progress = False
            for e, st in self.streams.items():
                while pc[e] < len(st):
                    kind, sk, val = st[pc[e]]
                    if kind == "wait":
                        if sems.get(sk, 0) < val:
                            break
                    else:
                        sems[sk] = sems.get(sk, 0) + val
                    pc[e] += 1
                    progress = True
        stuck = {e: (pc[e], len(st), st[pc[e]], sems.get(st[pc[e]][1], 0)) for e, st in self.streams.items()
                 if pc[e] < len(st)}
        return stuck


C_IDENT, C_U, C_MBD, C_MOFF, C_TRI, C_ONES, C_PB = 0, 128, 256, 384, 512, 640, 768
C_PB4 = 768
NCONST = 768 + 512


def make_consts():
    c = np.zeros((128, NCONST), np.float32)
    i = np.arange(128)
    c[:, C_IDENT:C_IDENT + 128] = np.eye(128)
    c[:, C_U:C_U + 128] = (i[:, None] <= i[None, :])
    c[:, C_MBD:C_MBD + 128] = (i[:, None] < i[None, :]) & ((i[:, None] // 64) == (i[None, :] // 64))
    c[:, C_MOFF:C_MOFF + 128] = (i[:, None] < 64) & (i[None, :] >= 64)
    c[:, C_TRI:C_TRI + 128] = np.where(i[:, None] <= i[None, :], 0.0, NEG)
    c[:, C_ONES:C_ONES + 128] = 1.0
    pb = np.zeros((16, 16), np.float32)
    for own in range(16):
        pb[own, own:] = -1e30
    for g in range(8):
        for tt in range(4):
            own = (4 * g + tt) // 2
            c[:, C_PB4 + g * 64 + tt * 16:C_PB4 + g * 64 + (tt + 1) * 16] = pb[own][None, :]
    return c


def build(S=4096, DEPTH=2, LS=2, dbg=None, phases=("p1", "dn", "mb", "fin"), stop=0, JUNK=0):
    dbg = dbg or set()
    NT = S // 128
    NG = S // 512
    NB = S // 256
    nc = bass.Bass("TRN2", target_bir_lowering=False)

    def din(name, shape, dt=F32):
        return nc.dram_tensor(name, shape, dt, kind="ExternalInput").ap()

    x_d = din("x", [S, 1024])
    cT_d = din("cT", [128, 8])
    wada_d = din("wada", [DEPTH, 128, 8, 3072])
    bada_d = din("bada", [DEPTH, 1, 3072])
    gpre_d = din("gpre", [DEPTH, 1, 1024])
    gpost_d = din("gpost", [DEPTH, 1, 1024])
    wdn_d = din("wdn", [DEPTH, 128, 8, 2048])
    wba_d = din("wba", [DEPTH, 128, 8, 8])
    wmb_d = din("wmb", [DEPTH, 128, 8, 2048])
    wmg_d = din("wmg", [DEPTH, 128, 8, 2048])
    convw_d = din("convw", [DEPTH, 128, 48])
    alog_d = din("alog", [DEPTH, 128, 4])
    dtb_d = din("dtb", [DEPTH, 128, 4])
    dng_d = din("dng", [DEPTH, 128, 1])
    wpdn_d = din("wpdn", [DEPTH, 128, 4, 1024])
    wpmb_d = din("wpmb", [DEPTH, 64, 8, 1024])
    wout_d = din("wout", [DEPTH, 128, 8, 1024])
    consts_d = din("consts", [128, NCONST])
    y_d = nc.dram_tensor("y", [S, 1024], F32, kind="ExternalOutput").ap()
    xmid_d = nc.dram_tensor("xmid", [S, 1024], F32, kind="Internal").ap()
    ogdn_d = nc.dram_tensor("ogdn", [4, 128, S], BF16, kind="Internal").ap()
    ogmb_d = nc.dram_tensor("ogmb", [8, 64, S], BF16, kind="Internal").ap()
    dbg_d = {}

    class _K:
        pass
    kx, kmid, ky, kogdn, kogmb = _K(), _K(), _K(), _K(), _K()

    def dbg_out(name, shape):
        dbg_d[name] = nc.dram_tensor("dbg_" + name, shape, F32, kind="ExternalOutput").ap()
        return dbg_d[name]

    with ExitStack() as gst:
        kb = KB(nc, gst)
        op, dma = kb.op, kb.dma
        gst.enter_context(nc.allow_low_precision("float32r (1-pass PE) operands for non-critical fp32 matmuls"))

        def mm(out, lhsT, rhs, start=True, stop=True, r=(), w=(), inc=True):
            return op("pe", lambda e: e.matmul(out, lhsT=lhsT, rhs=rhs, start=start, stop=stop),
                      r=r, w=w, inc=inc)

        def tr(out, in_, ident, r=(), w=()):
            return op("pe", lambda e: e.transpose(out=out, in_=in_, identity=ident), r=r, w=w)

        uid = [0]

        def T(st, name, shape, dt=F32):
            uid[0] += 1
            return st.enter_context(nc.sbuf_tensor("sb%d_%s" % (uid[0], name), shape, dt))

        class _Stop(Exception):
            pass

        def ck(level):
            if stop == level:
                raise _Stop()

        curgen = [None]

        def phase(name):
            if name in phases:
                st_ = ExitStack()
                curgen[0] = st_
                yield st_
                st_.close()

        PS = [gst.enter_context(nc.psum_tensor("ps%d" % i, [128, 512], F32)) for i in range(8)]
        kb.psum_ids = {id(p) for p in PS}
        C = T(gst, "consts", [128, NCONST])
        dma(C[:], consts_d[:, :], w=[C])
        ident = C[:, C_IDENT:C_IDENT + 128]
        U = C[:, C_U:C_U + 128]
        Mbd = C[:, C_MBD:C_MBD + 128]
        Moff = C[:, C_MOFF:C_MOFF + 128]
        ones = C[:, C_ONES:C_ONES + 128]
        identb = T(gst, "identb", [128, 128], BF16)
        trib = T(gst, "trib", [128, 128], BF16)
        epsT = T(gst, "epsT", [128, 1])
        op("dve", lambda e: e.tensor_copy(out=identb[:], in_=ident), r=[C], w=[identb])
        op("dve", lambda e: e.tensor_copy(out=trib[:], in_=C[:, C_TRI:C_TRI + 128]), r=[C], w=[trib])
        op("dve", lambda e: e.memset(epsT[:], EPS), w=[epsT])
        onesr = T(gst, "onesr", [128, 128])
        op("dve", lambda e: e.tensor_copy(out=R(onesr[:]), in_=ones), r=[C], w=[onesr])
        AB = [T(gst, "AB%d" % l, [128, 16]) for l in range(DEPTH)]
        gp_d = nc.dram_tensor("gp_scr", [DEPTH, 128, 1024], F32, kind="Internal").ap()
        kgp = _K()

        with ExitStack() as st:
            cT = T(st, "cT", [128, 8])
            sc = T(st, "sc", [128, 8])
            dma(cT[:], cT_d[:, :], w=[cT])
            op("act", lambda e: e.activation(out=sc[:], in_=cT[:], func=AF.Silu), r=[cT], w=[sc])
            wa = [T(st, "wa%d" % i, [128, 8, 512]) for i in range(2)]
            row = T(st, "row", [1, 3072])
            bada = T(st, "bada", [1, 3072])
            gpr = T(st, "gpr", [1, 1024])
            gpo = T(st, "gpo", [1, 1024])
            arow = T(st, "arow", [1, 1024])
            gprow = T(st, "gprow", [1, 1024])
            gptmp = T(st, "gptmp", [128, 1024])
            nwa = 0
            for l in range(DEPTH):
                dma(bada[:], bada_d[l], w=[bada])
                dma(gpr[:], gpre_d[l], w=[gpr])
                dma(gpo[:], gpost_d[l], w=[gpo])
                for cg in range(6):
                    wt = wa[nwa % 2]
                    nwa += 1
                    dma(wt[:], wada_d[l, :, :, cg * 512:(cg + 1) * 512], w=[wt])
                    pb = PS[cg % 2]
                    for c in range(8):
                        mm(pb[0:1, :], sc[:, c:c + 1], wt[:, c, :], start=(c == 0), stop=(c == 7),
                           r=[sc, wt], w=[pb], inc=(c == 7))
                    op("dve", lambda e: e.tensor_tensor(out=row[0:1, cg * 512:(cg + 1) * 512], in0=pb[0:1, :],
                                                        in1=bada[0:1, cg * 512:(cg + 1) * 512], op=ALU.add),
                       r=[pb, bada], w=[(row, cg)])
                op("dve", lambda e: e.scalar_tensor_tensor(out=arow[:], in0=row[0:1, 1024:2048], scalar=1.0,
                                                           in1=gpr[:], op0=ALU.add, op1=ALU.mult),
                   r=[row, gpr], w=[arow])
                op("dve", lambda e: e.tensor_tensor(out=gprow[:], in0=row[0:1, 2048:3072], in1=gpo[:], op=ALU.mult),
                   r=[row, gpo], w=[gprow])
                pc = PS[2]
                for c in range(8):
                    mm(pc[:, c:c + 1], arow[0:1, c * 128:(c + 1) * 128], ones[0:1, 0:1], r=[arow, C], w=[pc], inc=False)
                for c in range(8):
                    mm(pc[:, 8 + c:9 + c], row[0:1, c * 128:(c + 1) * 128], ones[0:1, 0:1], r=[row, C], w=[pc],
                       inc=(c == 7))
                op("dve", lambda e: e.tensor_copy(out=AB[l][:], in_=pc[:, 0:16]), r=[pc], w=[AB[l]])
                for hf in range(2):
                    pg = PS[3 + hf]
                    mm(pg[:, :], ones[0:1, 0:128], gprow[0:1, hf * 512:(hf + 1) * 512], r=[gprow, C], w=[pg])
                    op("act", lambda e: e.activation(out=gptmp[:, hf * 512:(hf + 1) * 512], in_=pg[:, :], func=AF.Copy),
                       r=[pg], w=[(gptmp, hf)])
                dma(gp_d[l], gptmp[:], r=[gptmp], w=[(kgp, l)])
            kb.barrier()

        hT = T(gst, "hT", [128, 8, S], BF16)

        for l in range(DEPTH):
          try:
            xin_d = x_d if l == 0 else xmid_d
            xout_d = y_d if l == DEPTH - 1 else xmid_d
            xin_k = kx if l == 0 else kmid
            xout_k = ky if l == DEPTH - 1 else kmid

            for st in phase("p1"):
                xt = [T(st, "xt%d" % i, [128, 1024]) for i in range(4)]
                xn = [T(st, "xn%d" % i, [128, 1024]) for i in range(3)]
                junk = T(st, "junk", [128, 1024], BF16)
                ss = T(st, "ss", [128, NT])
                rstd = T(st, "rstd", [128, NT])
                for t in range(NT):
                    xtt, xnt = xt[t % 4], xn[t % 3]
                    dma(xtt[:], xin_d[t * 128:(t + 1) * 128, :], r=[(xin_k, t)], w=[xtt])
                    op("act", lambda e: e.activation(out=junk[:], in_=xtt[:], func=AF.Square,
                                                     accum_out=ss[:, t:t + 1]), r=[xtt], w=[junk, (ss, t)])
                    op("act", lambda e: e.activation(out=rstd[:, t:t + 1], in_=ss[:, t:t + 1], func=AF.Sqrt,
                                                     scale=1.0 / D_MODEL, bias=epsT[:]), r=[(ss, t), epsT],
                       w=[(rstd, t)])
                    op("dve", lambda e: e.reciprocal(out=rstd[:, t:t + 1], in_=rstd[:, t:t + 1]), r=[(rstd, t)],
                       w=[(rstd, t)])
                    op("dve", lambda e: e.tensor_scalar(out=xnt[:], in0=xtt[:], scalar1=rstd[:, t:t + 1],
                                                        scalar2=None, op0=ALU.mult), r=[xtt, (rstd, t)], w=[xnt])
                    for hf in range(2):
                        pb = PS[(t % 2) * 2 + hf]
                        for cc in range(4):
                            c = hf * 4 + cc
                            tr(pb[:, cc * 128:(cc + 1) * 128], xnt[:, c * 128:(c + 1) * 128], ident, r=[xnt, C],
                               w=[pb])
                        for cc in range(4):
                            c = hf * 4 + cc
                            eng = "act" if hf == 0 else "dve"
                            if eng == "act":
                                op("act", lambda e: e.activation(out=hT[:, c, t * 128:(t + 1) * 128],
                                                                 in_=pb[:, cc * 128:(cc + 1) * 128], func=AF.Identity,
                                                                 scale=AB[l][:, c:c + 1], bias=AB[l][:, 8 + c:9 + c]),
                                   r=[pb, AB[l]], w=[(hT, t // 4)])
                            else:
                                op("dve", lambda e: e.tensor_scalar(out=hT[:, c, t * 128:(t + 1) * 128],
                                                                    in0=pb[:, cc * 128:(cc + 1) * 128],
                                                                    scalar1=AB[l][:, c:c + 1],
                                                                    scalar2=AB[l][:, 8 + c:9 + c],
                                                                    op0=ALU.mult, op1=ALU.add),
                                   r=[pb, AB[l]], w=[(hT, t // 4)])
                kb.barrier()
            if "hT" in dbg and l == 0:
                with ExitStack() as st:
                    d = dbg_out("hT", [128, 8, S])
                    tmp = T(st, "dbgtmp", [128, 8, S])
                    op("dve", lambda e: e.tensor_copy(out=tmp[:], in_=hT[:]), r=[hT], w=[tmp])
                    dma(d[:, :, :], tmp[:], r=[tmp])
                    kb.barrier()

            for st in phase("dn"):
                Wdn = T(st, "Wdn", [128, 8, 2048], BF16)
                Wba = T(st, "Wba", [128, 8, 8], BF16)
                for c in range(8):
                    dma(Wdn[:, c, :], wdn_d[l, :, c, :], w=[(Wdn, c)], q="pool")
                dma(Wba[:], wba_d[l], w=[Wba], q="pool")
                convw = T(st, "convw", [128, 48])
                alog = T(st, "alog", [128, 4])
                dtb = T(st, "dtb", [128, 4])
                dng = T(st, "dng", [128, 1])
                dma(convw[:], convw_d[l], w=[convw])
                dma(alog[:], alog_d[l], w=[alog])
                dma(dtb[:], dtb_d[l], w=[dtb])
                dma(dng[:], dng_d[l], w=[dng])
                st2 = ExitStack()
                BETA = T(st, "BETA", [128, NT, 4])
                NBETA = T(st, "NBETA", [128, NT, 4])
                GRAW = T(st, "GRAW", [128, NT, 4])
                GC = T(st, "GC", [128, NT, 4])
                NEXPG = T(st, "NEXPG", [128, NT, 4])
                negA = T(st, "negA", [128, 4])
                BG = T(st2, "BG", [128, NT, 8])
                AA = T(st2, "AA", [128, NT, 4])
                AX_ = T(st2, "AXs", [128, NT, 4])
                pbg = PS[0]
                for t in range(NT):
                    for c in range(8):
                        mm(pbg[:, t * 8:(t + 1) * 8], hT[:, c, t * 128:(t + 1) * 128], Wba[:, c, :],
                           start=(c == 0), stop=(c == 7), r=[(hT, t // 4), Wba], w=[pbg], inc=(c == 7))
                op("dve", lambda e: e.tensor_copy(out=BG[:].rearrange("p t e -> p (t e)"), in_=pbg[:, 0:NT * 8]),
                   r=[pbg], w=[BG])
                op("act", lambda e: e.activation(out=BETA[:], in_=BG[:, :, 0:4], func=AF.Sigmoid), r=[BG], w=[BETA])
                op("dve", lambda e: e.tensor_scalar(out=NBETA[:], in0=BETA[:], scalar1=-1.0, scalar2=None,
                                                    op0=ALU.mult), r=[BETA], w=[NBETA])
                for h in range(4):
                    op("dve", lambda e: e.tensor_scalar(out=AA[:, :, h], in0=BG[:, :, 4 + h], scalar1=dtb[:, h:h + 1],
                                                        scalar2=None, op0=ALU.add), r=[BG, dtb], w=[AA])
                op("act", lambda e: e.activation(out=AX_[:], in_=AA[:], func=AF.Abs), r=[AA], w=[AX_])
                op("act", lambda e: e.activation(out=AX_[:], in_=AX_[:], func=AF.Exp, scale=-1.0), r=[AX_], w=[AX_])
                op("dve", lambda e: e.tensor_scalar(out=AX_[:], in0=AX_[:], scalar1=1.0, scalar2=None, op0=ALU.add),
                   r=[AX_], w=[AX_])
                op("act", lambda e: e.activation(out=AX_[:], in_=AX_[:], func=AF.Ln), r=[AX_], w=[AX_])
                op("dve", lambda e: e.scalar_tensor_tensor(out=AA[:], in0=AA[:], scalar=0.0, in1=AX_[:],
                                                           op0=ALU.max, op1=ALU.add), r=[AA, AX_], w=[AA])
                op("act", lambda e: e.activation(out=negA[:], in_=alog[:], func=AF.Exp), r=[alog], w=[negA])
                op("dve", lambda e: e.tensor_scalar(out=negA[:], in0=negA[:], scalar1=-1.0, scalar2=None,
                                                    op0=ALU.mult), r=[negA], w=[negA])
                for h in range(4):
                    op("dve", lambda e: e.tensor_scalar(out=GRAW[:, :, h], in0=AA[:, :, h], scalar1=negA[:, h:h + 1],
                                                        scalar2=None, op0=ALU.mult), r=[AA, negA], w=[GRAW])
                pgc = PS[1]
                mm(pgc[:, 0:NT * 4], U, GRAW[:].rearrange("p t e -> p (t e)"), r=[C, GRAW], w=[pgc])
                op("dve", lambda e: e.tensor_copy(out=GC[:].rearrange("p t e -> p (t e)"), in_=pgc[:, 0:NT * 4]),
                   r=[pgc], w=[GC])
                op("act", lambda e: e.activation(out=NEXPG[:], in_=GC[:], func=AF.Exp), r=[GC], w=[NEXPG])
                op("dve", lambda e: e.tensor_scalar(out=NEXPG[:], in0=NEXPG[:], scalar1=-1.0, scalar2=None,
                                                    op0=ALU.mult), r=[NEXPG], w=[NEXPG])
                if "graw" in dbg and l == 0:
                    d = dbg_out("graw", [128, NT, 4])
                    dma(d[:, :, :], GRAW[:], r=[GRAW])
                    d = dbg_out("beta", [128, NT, 4])
                    dma(d[:, :, :], BETA[:], r=[BETA])

                kb.barrier()
                st2.close()
                ck(1)
                pre = [T(st, "pre%d" % i, [128, 515]) for i in range(3)]
                halo = T(st, "halo", [128, 12, 3])
                op("dve", lambda e: e.memset(halo[:], 0.0), w=[halo])
                qkv = [[T(st, "qkv%d_%d" % (i, j), [128, 512]) for j in range(3)] for i in range(2)]
                zs = [T(st, "zs%d" % i, [128, 512], BF16) for i in range(4)]
                cv = [T(st, "cv0", [128, 512])] * 2
                sq = [T(st, "sq0", [128, 512])] * 2
                oTg = [T(st, "oTg%d" % i, [128, 512]) for i in range(2)]
                ogb = [T(st, "ogb0", [128, 512], BF16)] * 2
                Sst = [[T(st, "S%d_%d" % (h, i), [128, 128]) for i in range(2)] for h in range(4)]
                for h in range(4):
                    op("dve", lambda e: e.tensor_scalar(out=R(Sst[h][0][:]), in0=ident, scalar1=0.0, scalar2=None,
                                                        op0=ALU.mult), r=[C], w=[Sst[h][0]])
                spar = [0, 0, 0, 0]
                NCH = 8
                ALIAS = {"Dm": 0, "tq": 0, "DecT": 1, "Xo": 1, "Pe": 2, "ktok": 3, "Xe": 3, "NoffT": 3, "Po": 4,
                         "B": 5, "BT": 6, "WbdT": 6, "Noff": 7, "Z1": 7, "ExpG": 8, "XTo": 8, "Lm": 9, "XTe": 9}
                scr = []
                for i in range(NCH):
                    wide = T(st, "s%d" % i, [128, 9 * 128])
                    plain = T(st, "s%da" % i, [128, 128])
                    d_ = {n: (SV(wide, j - 1) if j > 0 else plain) for n, j in ALIAS.items()}
                    d_["XPo"] = SV(wide, 0, 2)
                    d_["XPe"] = SV(wide, 2, 2)
                    scr.append(d_)
                OUTN = ["W", "QKdT", "kdec", "qg", "vtok", "kTc"]
                outs = [{n: T(st, "o%d_%s" % (i, n), [128, 128]) for n in OUTN} for i in range(NCH)]
                EGL = T(st, "EGL", [128, NCH])
                Yr = [T(st, "Yr%d" % i, [128, 128]) for i in range(2)]
                Vn = [T(st, "Vn%d" % i, [128, 128]) for i in range(2)]

                def prep8(chains, pump=lambda: None):
                    def each(fn):
                        for ch in chains:
                            fn(ch, ch["s"], ch["o"], PS[ch["i"]])

                    def f(ch, s, o, pb):
                        h, n, sl = ch["h"], ch["n"], ch["sl"]
                        kT, vT = ch["kT"], ch["vT"]
                        tr(pb[:, 0:128], kT[:, sl], ident, r=[kT, C], w=[pb])
                        tr(pb[:, 128:256], vT[:, sl], ident, r=[vT, C], w=[pb])
                        mm(pb[:, 256:384], GRAW[:, n, h:h + 1].to_broadcast([128, 128]), U, r=[GRAW, C], w=[pb])
                        op("act", lambda e: e.activation(out=R(s["ktok"][:]), in_=pb[:, 0:128], func=AF.Copy),
                           r=[pb], w=[s["ktok"]])
                        op("dve", lambda e: e.tensor_copy(out=o["vtok"][:], in_=pb[:, 128:256]), r=[pb],
                           w=[o["vtok"]])
                        op("dve", lambda e: e.tensor_scalar(out=s["Dm"][:], in0=pb[:, 256:384],
                                                            scalar1=GC[:, n, h:h + 1], scalar2=0.0,
                                                            op0=ALU.subtract, op1=ALU.min),
                           r=[pb, GC], w=[s["Dm"]])
                        op("act", lambda e: e.activation(out=R(s["ExpG"][:]), in_=pb[:, 256:384], func=AF.Exp),
                           r=[pb], w=[s["ExpG"]])
                    each(f)

                    def f(ch, s, o, pb):
                        op("act", lambda e: e.activation(out=R(s["DecT"][:]), in_=s["Dm"][:], func=AF.Exp),
                           r=[s["Dm"]], w=[s["DecT"]])
                    each(f)

                    def f(ch, s, o, pb):
                        h, n, sl = ch["h"], ch["n"], ch["sl"]
                        kT, qT = ch["kT"], ch["qT"]
                        mm(pb[:, 0:128], R(kT[:, sl]), R(kT[:, sl]), r=[kT], w=[pb], inc=False)
                        mm(pb[:, 128:256], R(kT[:, sl]), R(qT[:, sl]), r=[kT, qT], w=[pb])
                        op("dve", lambda e: e.tensor_tensor(out=R(s["Lm"][:]), in0=pb[:, 0:128], in1=s["DecT"][:],
                                                            op=ALU.mult), r=[pb, s["DecT"]], w=[s["Lm"]])
                        op("dve", lambda e: e.tensor_tensor(out=s["tq"][:], in0=pb[:, 128:256], in1=s["DecT"][:],
                                                            op=ALU.mult), r=[pb, s["DecT"]], w=[s["tq"]])
                        op("dve", lambda e: e.scalar_tensor_tensor(out=R(s["B"][:]), in0=s["Lm"][:],
                                                                   scalar=NBETA[:, n, h:h + 1], in1=Mbd,
                                                                   op0=ALU.mult, op1=ALU.mult),
                           r=[s["Lm"], NBETA, C], w=[s["B"]])
                        op("pool", lambda e: e.tensor_tensor(out=R(s["Noff"][:]), in0=s["Lm"][:], in1=Moff,
                                                             op=ALU.mult), r=[s["Lm"], C], w=[s["Noff"]])
                        op("pool", lambda e: e.tensor_scalar(out=R(s["Noff"][:]), in0=s["Noff"][:],
                                                             scalar1=BETA[:, n, h:h + 1], scalar2=None,
                                                             op0=ALU.mult), r=[s["Noff"], BETA], w=[s["Noff"]])
                        op("pool", lambda e: e.tensor_tensor(out=R(o["QKdT"][:]), in0=s["tq"][:], in1=U,
                                                             op=ALU.mult), r=[s["tq"], C], w=[o["QKdT"]])
                        op("pool", lambda e: e.tensor_scalar(out=R(o["kdec"][:]), in0=s["ktok"][:],
                                                             scalar1=s["DecT"][:, 127:128], scalar2=None,
                                                             op0=ALU.mult), r=[s["ktok"], s["DecT"]],
                           w=[o["kdec"]])
                        op("dve", lambda e: e.tensor_tensor(out=R(o["qg"][:]), in0=qT[:, sl], in1=s["ExpG"][:],
                                                            op=ALU.mult), r=[qT, s["ExpG"]], w=[o["qg"]])
                        op("dve", lambda e: e.tensor_copy(out=R(o["kTc"][:]), in_=kT[:, sl]), r=[kT], w=[o["kTc"]])
                        op("pool", lambda e: e.tensor_copy(out=EGL[:, ch["i"]:ch["i"] + 1],
                                                           in_=s["ExpG"][:, 127:128]),
                           r=[s["ExpG"]], w=[(EGL, ch["i"])])
                    each(f)

                    def f(ch, s, o, pb):
                        tr(pb[:, 256:384], s["B"][:], ident, r=[s["B"], C], w=[pb])
                        op("act", lambda e: e.activation(out=R(s["BT"][:]), in_=pb[:, 256:384], func=AF.Copy),
                           r=[pb], w=[s["BT"]])
                        op("dve", lambda e: e.tensor_tensor(out=R(s["Pe"][:]), in0=s["B"][:], in1=ident,
                                                            op=ALU.add), r=[s["B"], C], w=[s["Pe"]])
                    each(f)

                    def f(ch, s, o, pb):
                        mm(pb[:, 0:128], R(s["BT"][:]), R(s["B"][:]), r=[s["BT"], s["B"]], w=[pb])
                        op("act", lambda e: e.activation(out=R(s["Xo"][:]), in_=pb[:, 0:128], func=AF.Copy),
                           r=[pb], w=[s["Xo"]])
                    each(f)
                    pump()
                    for j in range(1, 6):
                        odd = (j % 2 == 1)
                        Xc, Pc, XTc, XPc = ("Xo", "Pe", "XTo", "XPo") if odd else ("Xe", "Po", "XTe", "XPe")
                        Xn, Pn = ("Xe", "Po") if odd else ("Xo", "Pe")

                        def f(ch, s, o, pb):
                            tr(pb[:, 256:384], s[Xc][:], ident, r=[s[Xc], C], w=[pb])
                            op("act", lambda e: e.activation(out=R(s[XTc][:]), in_=pb[:, 256:384], func=AF.Copy),
                               r=[pb], w=[s[XTc]])
                        each(f)
                        pump()

                        def f(ch, s, o, pb):
                            if j < 5:
                                mm(pb[:, 0:256], R(s[XTc][:]), R(s[XPc][:]), r=[s[XTc], s[Xc], s[Pc]], w=[pb])
                                op("act", lambda e: e.activation(out=R(s[Xn][:]), in_=pb[:, 0:128], func=AF.Copy),
                                   r=[pb], w=[s[Xn]])
                                op("dve", lambda e: e.tensor_tensor(out=R(s[Pn][:]), in0=pb[:, 128:256],
                                                                    in1=s[Pc][:], op=ALU.add),
                                   r=[pb, s[Pc]], w=[s[Pn]])
                            else:
                                mm(pb[:, 128:256], R(s[XTc][:]), R(s[Pc][:]), r=[s[XTc], s[Pc]], w=[pb])
                                op("dve", lambda e: e.tensor_tensor(out=R(s[Pn][:]), in0=pb[:, 128:256],
                                                                    in1=s[Pc][:], op=ALU.add),
                                   r=[pb, s[Pc]], w=[s[Pn]])
                        each(f)
                        pump()
                    Pf = "Po"

                    def f(ch, s, o, pb):
                        tr(pb[:, 0:128], s[Pf][:], ident, r=[s[Pf], C], w=[pb])
                        tr(pb[:, 128:256], s["Noff"][:], ident, r=[s["Noff"], C], w=[pb])
                        op("act", lambda e: e.activation(out=R(s["WbdT"][:]), in_=pb[:, 0:128], func=AF.Copy),
                           r=[pb], w=[s["WbdT"]])
                        op("dve", lambda e: e.tensor_copy(out=R(s["NoffT"][:]), in_=pb[:, 128:256]), r=[pb],
                           w=[s["NoffT"]])
                    each(f)
                    pump()

                    def f(ch, s, o, pb):
                        mm(pb[:, 0:128], R(s["NoffT"][:]), R(s[Pf][:]), r=[s["NoffT"], s[Pf]], w=[pb])
                        op("act", lambda e: e.activation(out=R(s["Z1"][:]), in_=pb[:, 0:128], func=AF.Copy),
                           r=[pb], w=[s["Z1"]])
                    each(f)
                    pump()

                    def f(ch, s, o, pb):
                        mm(pb[:, 128:256], R(s["WbdT"][:]), R(s["Z1"][:]), r=[s["WbdT"], s["Z1"]], w=[pb])
                        op("dve", lambda e: e.tensor_tensor(out=R(o["W"][:]), in0=s[Pf][:], in1=pb[:, 128:256],
                                                            op=ALU.subtract), r=[s[Pf], pb], w=[o["W"]])
                    each(f)
                    pump()

                nrec = [0]

                def recur_pair(chs, g):
                    st_ = []
                    for ch in chs:
                        h = ch["h"]
                        So = Sst[h][spar[h]]
                        Sn = Sst[h][1 - spar[h]]
                        spar[h] = 1 - spar[h]
                        bb = 4 * ch["hi"]
                        st_.append((ch, So, Sn, PS[bb], PS[bb + 1], PS[bb + 2], PS[bb + 3],
                                    Yr[nrec[0] % 2], Vn[nrec[0] % 2]))
                        nrec[0] += 1
                    for ch, So, Sn, pa, pb2, pc, pd, Y, vnew in st_:
                        h, n, o = ch["h"], ch["n"], ch["o"]
                        mm(pa[:, 0:128], R(o["kTc"][:]), R(So[:]), r=[o["kTc"], So], w=[pa])
                        op("dve", lambda e: e.scalar_tensor_tensor(out=R(Y[:]), in0=pa[:, 0:128],
                                                                   scalar=NEXPG[:, n, h:h + 1], in1=o["vtok"][:],
                                                                   op0=ALU.mult, op1=ALU.add),
                           r=[pa, NEXPG, o["vtok"]], w=[Y])
                    for ch, So, Sn, pa, pb2, pc, pd, Y, vnew in st_:
                        h, n, o = ch["h"], ch["n"], ch["o"]
                        mm(pb2[:, 0:128], R(o["W"][:]), R(Y[:]), r=[o["W"], Y], w=[pb2])
                        op("act", lambda e: e.activation(out=R(vnew[:]), in_=pb2[:, 0:128], func=AF.Identity,
                                                         scale=BETA[:, n, h:h + 1]), r=[pb2, BETA], w=[vnew])
                    for ch, So, Sn, pa, pb2, pc, pd, Y, vnew in st_:
                        o = ch["o"]
                        mm(pd[:, 0:128], R(o["kdec"][:]), R(vnew[:]), r=[o["kdec"], vnew], w=[pd])
                        op("dve", lambda e: e.scalar_tensor_tensor(out=R(Sn[:]), in0=So[:],
                                                                   scalar=EGL[:, ch["i"]:ch["i"] + 1],
                                                                   in1=pd[:, 0:128], op0=ALU.mult, op1=ALU.add),
                           r=[So, (EGL, ch["i"]), pd], w=[Sn])
                    for ch, So, Sn, pa, pb2, pc, pd, Y, vnew in st_:
                        o, sl = ch["o"], ch["sl"]
                        mm(pc[:, 0:128], R(So[:]), R(o["qg"][:]), start=True, stop=False, r=[So, o["qg"]], w=[pc],
                           inc=False)
                        mm(pc[:, 0:128], R(vnew[:]), R(o["QKdT"][:]), start=False, stop=True, r=[vnew, o["QKdT"]],
                           w=[pc])
                        ot = oTg[ch["hi"]]
                        op("act", lambda e: e.activation(out=ot[:, sl], in_=pc[:, 0:128], func=AF.Copy), r=[pc],
                           w=[(ot, ch["cc"])])

                def stageA(g, hp, par, res):
                    gs = slice(g * 512, (g + 1) * 512)
                    chains = []
                    for hi in range(2):
                        h = 2 * hp + hi
                        qk = qkv[hi]
                        zt = zs[2 * par + hi]
                        for ty in range(4):
                            pb = PS[4 * hi + ty]
                            col = ty * 512 + h * 128
                            for c in range(8):
                                mm(pb[:, :], Wdn[:, c, col:col + 128], hT[:, c, gs], start=(c == 0),
                                   stop=(c == 7), r=[(Wdn, c), (hT, g)], w=[pb], inc=(c == 7))
                            if ty < 3:
                                ch_ = ty * 4 + h
                                op("dve", lambda e: e.tensor_copy(out=pre[ty][:, 0:3], in_=halo[:, ch_, :]),
                                   r=[(halo, ch_)], w=[(pre[ty], 0)])
                                op("act", lambda e: e.activation(out=pre[ty][:, 3:515], in_=pb[:, :],
                                                                 func=AF.Copy), r=[pb], w=[(pre[ty], 1)])
                                op("dve", lambda e: e.tensor_copy(out=halo[:, ch_, :], in_=pre[ty][:, 512:515]),
                                   r=[(pre[ty], 1)], w=[(halo, ch_)])
                                yield
                                cvt = cv[ty % 2]
                                wk = lambda k: convw[:, ch_ * 4 + k:ch_ * 4 + k + 1]
                                op("act", lambda e: e.activation(out=cvt[:], in_=pre[ty][:, 0:512],
                                                                 func=AF.Identity, scale=wk(0)),
                                   r=[pre[ty], convw], w=[cvt])
                                for k in range(1, 4):
                                    op("dve", lambda e: e.scalar_tensor_tensor(out=cvt[:],
                                                                               in0=pre[ty][:, k:k + 512],
                                                                               scalar=wk(k), in1=cvt[:],
                                                                               op0=ALU.mult, op1=ALU.add),
                                       r=[pre[ty], convw, cvt], w=[cvt])
                                    yield
                                op("act", lambda e: e.activation(out=(R(qk[ty][:]) if ty < 2 else qk[ty][:]),
                                                                 in_=cvt[:], func=AF.Silu),
                                   r=[cvt], w=[qk[ty]])
                            else:
                                op("act", lambda e: e.activation(out=zt[:], in_=pb[:, :], func=AF.Silu),
                                   r=[pb], w=[zt])
                            yield
                        for ty in range(2):
                            sqt = sq[ty]
                            pb = PS[4 * hi + ty]
                            op("act", lambda e: e.activation(out=R(sqt[:]), in_=qk[ty][:], func=AF.Square),
                               r=[qk[ty]], w=[sqt])
                            mm(pb[:, :], R(onesr[:]), R(sqt[:]), r=[onesr, sqt], w=[pb])
                            rt_ = cv[0]
                            op("act", lambda e: e.activation(out=rt_[:], in_=pb[:, :], func=AF.Ln,
                                                             bias=epsT[:]), r=[pb, epsT], w=[rt_])
                            op("act", lambda e: e.activation(out=rt_[:], in_=rt_[:], func=AF.Exp, scale=-0.5),
                               r=[rt_], w=[rt_])
                            sc_ = (128.0 ** -0.5) if ty == 0 else 1.0
                            op("dve", lambda e: e.scalar_tensor_tensor(out=R(qk[ty][:]), in0=qk[ty][:],
                                                                       scalar=sc_, in1=rt_[:], op0=ALU.mult,
                                                                       op1=ALU.mult),
                               r=[qk[ty], rt_], w=[qk[ty]])
                            yield
                        if dbg and l == 0 and g == 0 and h == 0:
                            for nm, tt in (("q", qk[0]), ("k", qk[1]), ("v", qk[2])):
                                if nm in dbg:
                                    d = dbg_out(nm, [128, 512])
                                    dma(d[:, :], tt[:], r=[tt])
                        for cc in range(4):
                            i = hi * 4 + cc
                            chains.append({"h": h, "hi": hi, "cc": cc, "n": g * 4 + cc, "i": i,
                                           "sl": slice(cc * 128, (cc + 1) * 128), "qT": qk[0], "kT": qk[1],
                                           "vT": qk[2], "s": scr[i], "o": outs[i]})
                    res["chains"] = chains

                def drain(gen):
                    if gen is not None:
                        for _ in gen:
                            pass

                pairs = [(g, hp) for g in range(NG) for hp in range(2)]
                resA = [dict() for _ in pairs]
                gens = [stageA(g, hp, pi % 2, resA[pi]) for pi, (g, hp) in enumerate(pairs)]
                drain(gens[0])
                for pi, (g, hp) in enumerate(pairs):
                    gs = slice(g * 512, (g + 1) * 512)
                    chains = resA[pi]["chains"]
                    nxt = gens[pi + 1] if pi + 1 < len(pairs) else None

                    def pump(n=5):
                        if nxt is not None:
                            for _ in range(n):
                                if next(nxt, "done") == "done":
                                    break
                    prep8(chains, pump)
                    drain(nxt)
                    for cc in range(4):
                        recur_pair([ch for ch in chains if ch["cc"] == cc], g)
                    for hi in range(2):
                        h = 2 * hp + hi
                        zt = zs[2 * (pi % 2) + hi]
                        ot = oTg[hi]
                        og = ogb[hi]
                        sqt = sq[hi]
                        pb = PS[4 * hi]
                        op("act", lambda e: e.activation(out=R(sqt[:]), in_=ot[:], func=AF.Square), r=[ot],
                           w=[sqt])
                        mm(pb[:, :], R(onesr[:]), R(sqt[:]), r=[onesr, sqt], w=[pb])
                        sqt = cv[0]
                        op("act", lambda e: e.activation(out=sqt[:], in_=pb[:, :], func=AF.Ln,
                                                         scale=1.0 / 128, bias=epsT[:]), r=[pb, epsT], w=[sqt])
                        op("act", lambda e: e.activation(out=sqt[:], in_=sqt[:], func=AF.Exp, scale=-0.5),
                           r=[sqt], w=[sqt])
                        if "odn" in dbg and l == 0 and h == 0 and g == 0:
                            d = dbg_out("odn", [128, 512])
                            dma(d[:, :], ot[:], r=[ot])
                        op("dve", lambda e: e.scalar_tensor_tensor(out=sqt[:], in0=ot[:], scalar=dng[:, 0:1],
                                                                   in1=sqt[:], op0=ALU.mult, op1=ALU.mult),
                           r=[ot, dng, sqt], w=[sqt])
                        op("dve", lambda e: e.tensor_tensor(out=og[:], in0=sqt[:], in1=zt[:], op=ALU.mult),
                           r=[sqt, zt], w=[og])
                        dma(ogdn_d[h, :, gs], og[:], r=[og], w=[(kogdn, g)])
                kb.barrier()

            for st in phase("mb"):
                Wmb = T(st, "Wmb", [128, 8, 2048], BF16)
                for c in range(8):
                    dma(Wmb[:, c, :], wmb_d[l, :, c, :], w=[(Wmb, c)], q="pool")
                Vt = T(st, "Vt", [128, NT, 8, 65], BF16)
                op("dve", lambda e: e.memset(Vt[:, :, :, 64:65], 1.0), w=[Vt])
                for t in range(NT):
                    pb = PS[t % 2]
                    for c in range(8):
                        mm(pb[:, :], hT[:, c, t * 128:(t + 1) * 128], Wmb[:, c, 1024:1536], start=(c == 0),
                           stop=(c == 7), r=[(hT, t // 4), (Wmb, c)], w=[pb], inc=(c == 7))
                    op("act" if t % 2 else "dve",
                       lambda e: (e.activation(out=Vt[:, t, :, 0:64], in_=pb[:, :].rearrange("p (h d) -> p h d", h=8),
                                               func=AF.Copy) if t % 2 else
                                  e.tensor_copy(out=Vt[:, t, :, 0:64],
                                                in_=pb[:, :].rearrange("p (h d) -> p h d", h=8))),
                       r=[pb], w=[(Vt, t)])
                KaT = T(st, "KaT", [128, S], BF16)
                QaT = T(st, "QaT", [128, S], BF16)
                zsm = T(st, "zsm", [64, S], BF16)
                ogm = T(st, "ogm", [64, S], BF16)
                kf = [T(st, "kf%d" % i, [128, 512]) for i in range(2)]
                qf = [T(st, "qf%d" % i, [128, 512]) for i in range(2)]
                sqm = [T(st, "sqm%d" % i, [128, 512]) for i in range(2)]
                kmT = T(st, "kmT", [128, 16])
                km2 = T(st, "km2", [64, NG + 1])
                gm4 = T(st, "gm4", [128, 4, 16])
                top84 = T(st, "top84", [128, 4, 8])
                mbt4 = T(st, "mbt4", [128, 4, 16])
                rden = T(st, "rden", [128, 256])
                t1 = [T(st, "t1_%d" % i, [64, 256]) for i in range(2)]
                PT = [T(st, "PT%d" % i, [128, 512], BF16) for i in range(3)]
                nS = [0]
                op("dve", lambda e: e.memset(KaT[0:64, :], 0.0), w=[KaT])
                op("dve", lambda e: e.memset(QaT[0:64, :], 0.0), w=[QaT])
                op("dve", lambda e: e.memset(KaT[32:33, :], 1.0), w=[KaT])
                for n in range(NB):
                    op("dve", lambda e: e.tensor_copy(out=KaT[0:16, n * 256:(n + 1) * 256],
                                                      in_=ident[0:16, n:n + 1].to_broadcast([16, 256])),
                       r=[C], w=[KaT])
                npt = 0
                for h in range(8):
                    op("dve", lambda e: e.memset(kmT[:], 0.0), w=[kmT])
                    for g in range(NG):
                        gs = slice(g * 512, (g + 1) * 512)
                        pb = PS[g % 2]
                        col = 512 + h * 64
                        for c in range(8):
                            mm(pb[64:128, :], Wmb[:, c, col:col + 64], hT[:, c, gs], start=(c == 0), stop=(c == 7),
                               r=[(Wmb, c), (hT, g)], w=[pb], inc=(c == 7))
                        kft = kf[g % 2]
                        op("act", lambda e: e.activation(out=kft[64:128, :], in_=pb[64:128, :], func=AF.Copy),
                           r=[pb], w=[kft])
                        op("dve", lambda e: e.tensor_copy(out=KaT[64:128, gs], in_=pb[64:128, :]), r=[pb],
                           w=[(KaT, g)])
                        op("dve", lambda e: e.tensor_reduce(out=kmT[64:128, 2 * g:2 * g + 2],
                                                            in_=kft[64:128, :].rearrange("p (b t) -> p b t", b=2),
                                                            axis=AX.X, op=ALU.add), r=[kft], w=[kmT])
                        sqt = sqm[g % 2]
                        op("act", lambda e: e.activation(out=sqt[64:128, :], in_=kft[64:128, :], func=AF.Square),
                           r=[kft], w=[sqt])
                        pr = PS[2 + g % 2]
                        mm(pr[32:33, :], ones[64:128, 0:1], sqt[64:128, :], r=[C, sqt], w=[pr])
                        op("dve", lambda e: e.tensor_reduce(out=km2[32:33, g:g + 1], in_=pr[32:33, :], axis=AX.X,
                                                            op=ALU.max), r=[pr], w=[km2])
                    op("dve", lambda e: e.tensor_scalar(out=kmT[64:128, :], in0=kmT[64:128, :], scalar1=1.0 / 256,
                                                        scalar2=None, op0=ALU.mult), r=[kmT], w=[kmT])
                    op("dve", lambda e: e.tensor_reduce(out=km2[32:33, NG:NG + 1], in_=km2[32:33, 0:NG], axis=AX.X,
                                                        op=ALU.max), r=[km2], w=[km2])
                    for g in range(NG):
                        gs = slice(g * 512, (g + 1) * 512)
                        pb = PS[g % 2]
                        col = 1536 + h * 64
                        for c in range(8):
                            mm(pb[0:64, :], Wmb[:, c, col:col + 64], hT[:, c, gs], start=(c == 0), stop=(c == 7),
                               r=[(Wmb, c), (hT, g)], w=[pb], inc=(c == 7))
                        op("act", lambda e: e.activation(out=zsm[:, gs], in_=pb[0:64, :], func=AF.Silu), r=[pb],
                           w=[(zsm, g)])
                    for g in range(NG):
                        gs = slice(g * 512, (g + 1) * 512)
                        pb = PS[g % 2]
                        col = h * 64
                        for c in range(8):
                            mm(pb[64:128, :], Wmb[:, c, col:col + 64], hT[:, c, gs], start=(c == 0), stop=(c == 7),
                               r=[(Wmb, c), (hT, g)], w=[pb], inc=(c == 7))
                        qft = qf[g % 2]
                        op("act", lambda e: e.activation(out=qft[64:128, :], in_=pb[64:128, :], func=AF.Identity,
                                                         scale=0.125), r=[pb], w=[qft])
                        op("dve", lambda e: e.tensor_scalar(out=QaT[64:128, gs], in0=pb[64:128, :], scalar1=0.125,
                                                            scalar2=None, op0=ALU.mult), r=[pb], w=[(QaT, g)])
                        sqt = sqm[g % 2]
                        op("act", lambda e: e.activation(out=sqt[64:128, :], in_=qft[64:128, :], func=AF.Square),
                           r=[qft], w=[sqt])
                        pr = PS[2 + g % 2]
                        mm(pr[32:33, :], ones[64:128, 0:1], sqt[64:128, :], r=[C, sqt], w=[pr])
                        op("act", lambda e: e.activation(out=sqt[32:33, :], in_=pr[32:33, :], func=AF.Sqrt,
                                                         scale=km2[32:33, NG:NG + 1]), r=[pr, km2], w=[sqt])
                        op("dve", lambda e: e.tensor_scalar(out=QaT[32:33, gs], in0=sqt[32:33, :], scalar1=-1.0,
                                                            scalar2=None, op0=ALU.mult), r=[sqt], w=[(QaT, g)])
                        pgt = PS[4]
                        for tt in range(4):
                            mm(pgt[:, tt * 16:(tt + 1) * 16], qft[64:128, tt * 128:(tt + 1) * 128], kmT[64:128, :],
                               r=[qft, kmT], w=[pgt], inc=(tt == 3))
                        op("dve", lambda e: e.tensor_tensor(out=gm4[:].rearrange("p a b -> p (a b)"), in0=pgt[:, 0:64],
                                                            in1=C[:, C_PB4 + g * 64:C_PB4 + (g + 1) * 64], op=ALU.add),
                           r=[pgt, C], w=[gm4])
                        for tt in range(4):
                            op("dve", lambda e: e.max(out=top84[:, tt, :], in_=gm4[:, tt, :]), r=[gm4],
                               w=[(top84, tt)])
                        for tt in range(4):
                            op("dve", lambda e: e.tensor_scalar(out=mbt4[:, tt, :], in0=gm4[:, tt, :],
                                                                scalar1=top84[:, tt, 2:3], scalar2=NEG,
                                                                op0=ALU.is_lt, op1=ALU.mult),
                               r=[gm4, (top84, tt)], w=[(mbt4, tt)])
                        for t2 in range(2):
                            own = 2 * g + t2
                            op("dve", lambda e: e.memset(mbt4[:, 2 * t2:2 * t2 + 2, own:own + 1], 0.0),
                               r=[(mbt4, 2 * t2), (mbt4, 2 * t2 + 1)], w=[(mbt4, 2 * t2), (mbt4, 2 * t2 + 1)])
                        pt_ = PS[3]
                        for tt in range(4):
                            tr(pt_[0:16, tt * 128:(tt + 1) * 128], mbt4[:, tt, :], ident, r=[(mbt4, tt), C], w=[pt_])
                        op("act", lambda e: e.activation(out=QaT[0:16, gs], in_=pt_[0:16, 0:512], func=AF.Copy),
                           r=[pt_], w=[(QaT, g)])
                    for qc in range(NB):
                        q0 = qc * 256
                        pO = PS[6 + qc % 2]
                        qk_ = (QaT, qc // 2)

                        def emitS(p):
                            pS = PS[nS[0] % 3]
                            nS[0] += 1
                            if p < qc:
                                for j in range(2):
                                    kt = 2 * p + j
                                    mm(pS[:, j * 256:(j + 1) * 256], KaT[:, kt * 128:(kt + 1) * 128],
                                       QaT[:, q0:q0 + 256], r=[(KaT, kt // 4), qk_], w=[pS], inc=(j == 1))
                            else:
                                kt = 2 * qc
                                mm(pS[:, 0:256], KaT[:, kt * 128:(kt + 1) * 128], QaT[:, q0:q0 + 256], start=True,
                                   stop=False, r=[(KaT, kt // 4), qk_], w=[pS], inc=False)
                                mm(pS[:, 0:128], identb[:], trib[:], start=False, stop=True, r=[identb, trib],
                                   w=[pS], inc=False)
                                kt = 2 * qc + 1
                                mm(pS[:, 256:384], KaT[:, kt * 128:(kt + 1) * 128], QaT[:, q0 + 128:q0 + 256],
                                   start=True, stop=False, r=[(KaT, kt // 4), qk_], w=[pS], inc=False)
                                mm(pS[:, 256:384], identb[:], trib[:], start=False, stop=True, r=[identb, trib],
                                   w=[pS])
                            return pS

                        pend = emitS(0)
                        for p in range(qc + 1):
                            pS = pend
                            if p + 1 <= qc:
                                pend = emitS(p + 1)
                            ptile = PT[npt % 3]
                            npt += 1
                            wdt = 512 if p < qc else 384
                            op("act", lambda e: e.activation(out=ptile[:, 0:wdt], in_=pS[:, 0:wdt], func=AF.Exp),
                               r=[pS], w=[ptile])
                            if p < qc:
                                for j in range(2):
                                    kt = 2 * p + j
                                    mm(pO[0:65, 0:256], Vt[:, kt, h, :], ptile[:, j * 256:(j + 1) * 256],
                                       start=(kt == 0), stop=False, r=[(Vt, kt), ptile], w=[pO], inc=False)
                            else:
                                kt = 2 * qc
                                mm(pO[0:65, 0:256], Vt[:, kt, h, :], ptile[:, 0:256], start=(kt == 0), stop=False,
                                   r=[(Vt, kt), ptile], w=[pO], inc=False)
                                mm(pO[0:65, 128:256], Vt[:, kt + 1, h, :], ptile[:, 256:384], start=False, stop=True,
                                   r=[(Vt, kt + 1), ptile], w=[pO])
                            for _ in range(JUNK):
                                kb.eng["pe"].matmul(PS[5][:, :], lhsT=identb[:], rhs=KaT[:, 0:512], start=True, stop=True)
                        op("dve", lambda e: e.reciprocal(out=R(rden[64:65, :]), in_=pO[64:65, 0:256]), r=[pO],
                           w=[rden])
                        pB = PS[3]
                        mm(pB[0:64, 0:256], R(onesr[64:65, 0:64]), R(rden[64:65, :]), r=[onesr, rden], w=[pB])
                        tt1 = t1[qc % 2]
                        op("dve", lambda e: e.tensor_tensor(out=tt1[:], in0=pO[0:64, 0:256],
                                                            in1=zsm[:, q0:q0 + 256], op=ALU.mult),
                           r=[pO, (zsm, qc // 2)], w=[tt1])
                        op("dve", lambda e: e.tensor_tensor(out=ogm[:, q0:q0 + 256], in0=tt1[:], in1=pB[0:64, 0:256],
                                                            op=ALU.mult), r=[tt1, pB], w=[(ogm, qc)])
                    if "omb" in dbg and l == 0 and h == 0:
                        d = dbg_out("omb", [64, S])
                        tmp = T(st, "dbgomb", [64, S])
                        op("dve", lambda e: e.tensor_copy(out=tmp[:], in_=ogm[:]), r=[ogm], w=[tmp])
                        dma(d[:, :], tmp[:], r=[tmp])
                    dma(ogmb_d[h, :, :], ogm[:], r=[ogm], w=[kogmb])
                kb.barrier()

            for st in phase("fin"):
                Wpdn = T(st, "Wpdn", [128, 4, 1024], BF16)
                Wpmb = T(st, "Wpmb", [64, 8, 1024], BF16)
                Wout = T(st, "Wout", [128, 8, 1024], BF16)
                Wmg = T(st, "Wmg", [128, 8, 2048], BF16)
                dma(Wpdn[:], wpdn_d[l], w=[Wpdn], q="pool")
                dma(Wpmb[:], wpmb_d[l], w=[Wpmb], q="pool")
                for c in range(8):
                    dma(Wout[:, c, :], wout_d[l, :, c, :], w=[(Wout, c)], q="pool")
                for c in range(8):
                    dma(Wmg[:, c, :], wmg_d[l, :, c, :], w=[(Wmg, c)], q="pool")
                OGD = [T(st, "OGD%d" % i, [128, 4, 512], BF16) for i in range(2)]
                OGM = [T(st, "OGM0", [64, 8, 512], BF16)] * 2
                mixT = T(st, "mixT", [128, 8, 512], BF16)
                gd = [T(st, "gd%d" % i, [128, 512]) for i in range(2)]
                gmm = [T(st, "gmm%d" % i, [128, 512]) for i in range(2)]
                u1 = [T(st, "u1_%d" % i, [128, 512]) for i in range(2)]
                u2 = [T(st, "u2_%d" % i, [128, 512]) for i in range(2)]
                xr = [T(st, "xr%d" % i, [128, 1024]) for i in range(2)]
                res = [T(st, "res%d" % i, [128, 512]) for i in range(2)]
                junk2 = T(st, "junk2", [128, 512], BF16)
                GPl = T(st, "GPl", [128, 1024])
                dma(GPl[:], gp_d[l], r=[(kgp, l)], w=[GPl])
                ss2 = T(st, "ss2", [128, NT, 2])
                rs2 = T(st, "rs2", [128, NT])
                for g in range(NG):
                    gs = slice(g * 512, (g + 1) * 512)
                    ogd, ogmm = OGD[g % 2], OGM[g % 2]
                    dma(ogd[:], ogdn_d[:, :, gs].rearrange("h p s -> p h s"), r=[(kogdn, g)], w=[ogd])
                    dma(ogmm[:], ogmb_d[:, :, gs].rearrange("h p s -> p h s"), r=[kogmb], w=[ogmm])
                    for d_ in range(8):
                        ds_ = slice(d_ * 128, (d_ + 1) * 128)
                        pa, pb, pc, pd = PS[0 + 4 * (d_ % 2)], PS[1 + 4 * (d_ % 2)], PS[2 + 4 * (d_ % 2)], PS[3 + 4 * (d_ % 2)]
                        for h in range(4):
                            mm(pa[:, :], Wpdn[:, h, ds_], ogd[:, h, :], start=(h == 0), stop=(h == 3),
                               r=[Wpdn, ogd], w=[pa], inc=(h == 3))
                        for h in range(8):
                            mm(pb[:, :], Wpmb[:, h, ds_], ogmm[:, h, :], start=(h == 0), stop=(h == 7),
                               r=[Wpmb, ogmm], w=[pb], inc=(h == 7))
                        for c in range(8):
                            mm(pc[:, :], Wmg[:, c, ds_], hT[:, c, gs], start=(c == 0), stop=(c == 7),
                               r=[(Wmg, c), (hT, g)], w=[pc], inc=(c == 7))
                        for c in range(8):
                            mm(pd[:, :], Wmg[:, c, 1024 + d_ * 128:1024 + (d_ + 1) * 128], hT[:, c, gs],
                               start=(c == 0), stop=(c == 7), r=[(Wmg, c), (hT, g)], w=[pd], inc=(c == 7))
                        gdt, gmt, u1t, u2t = gd[d_ % 2], gmm[d_ % 2], u1[d_ % 2], u2[d_ % 2]
                        op("act", lambda e: e.activation(out=gdt[:], in_=pc[:, :], func=AF.Sigmoid), r=[pc], w=[gdt])
                        op("act", lambda e: e.activation(out=gmt[:], in_=pd[:, :], func=AF.Sigmoid), r=[pd], w=[gmt])
                        op("dve", lambda e: e.tensor_tensor(out=u1t[:], in0=pa[:, :], in1=gdt[:], op=ALU.mult),
                           r=[pa, gdt], w=[u1t])
                        op("dve", lambda e: e.tensor_tensor(out=u2t[:], in0=pb[:, :], in1=gmt[:], op=ALU.mult),
                           r=[pb, gmt], w=[u2t])
                        op("dve", lambda e: e.tensor_tensor(out=mixT[:, d_, :], in0=u1t[:], in1=u2t[:], op=ALU.add),
                           r=[u1t, u2t], w=[(mixT, d_)])
                    for tt in range(4):
                        t = g * 4 + tt
                        xrt = xr[t % 2]
                        dma(xrt[:], xin_d[t * 128:(t + 1) * 128, :], r=[(xin_k, t)], w=[xrt])
                        for hf in range(2):
                            pb = PS[(t % 2) * 2 + hf]
                            for d_ in range(8):
                                mm(pb[:, :], mixT[:, d_, tt * 128:(tt + 1) * 128], Wout[:, d_, hf * 512:(hf + 1) * 512],
                                   start=(d_ == 0), stop=(d_ == 7), r=[(mixT, d_), (Wout, d_)], w=[pb],
                                   inc=(d_ == 7))
                            op("act", lambda e: e.activation(out=junk2[:], in_=pb[:, :], func=AF.Square,
                                                             accum_out=ss2[:, t, hf:hf + 1]), r=[pb],
                               w=[junk2, (ss2, t)])
                        op("dve", lambda e: e.tensor_tensor(out=rs2[:, t:t + 1], in0=ss2[:, t, 0:1],
                                                            in1=ss2[:, t, 1:2], op=ALU.add), r=[(ss2, t)],
                           w=[(rs2, t)])
                        op("act", lambda e: e.activation(out=rs2[:, t:t + 1], in_=rs2[:, t:t + 1], func=AF.Sqrt,
                                                         scale=1.0 / D_MODEL, bias=epsT[:]), r=[(rs2, t), epsT],
                           w=[(rs2, t)])
                        op("dve", lambda e: e.reciprocal(out=rs2[:, t:t + 1], in_=rs2[:, t:t + 1]), r=[(rs2, t)],
                           w=[(rs2, t)])
                        for hf in range(2):
                            pb = PS[(t % 2) * 2 + hf]
                            hs = slice(hf * 512, (hf + 1) * 512)
                            rt = res[hf]
                            op("dve", lambda e: e.scalar_tensor_tensor(out=rt[:], in0=pb[:, :],
                                                                       scalar=rs2[:, t:t + 1], in1=GPl[:, hs],
                                                                       op0=ALU.mult, op1=ALU.mult),
                               r=[pb, (rs2, t), GPl], w=[rt])
                            op("dve", lambda e: e.tensor_tensor(out=xrt[:, hs], in0=rt[:], in1=xrt[:, hs],
                                                                op=ALU.add), r=[rt, (xrt, hf)], w=[(xrt, hf)])
                        dma(xout_d[t * 128:(t + 1) * 128, :], xrt[:], r=[xrt], w=[(xout_k, t)])
                kb.barrier()
          except _Stop:
            kb.barrier()
            curgen[0].close()
            break
        kb.finish()
        stuck = kb.simulate()
        print("deadlock check:", stuck if stuck else "ok")
        print("instructions:", kb.ninstr, "counts", kb.count, "dmas", kb.ndmaq)
    return nc, dbg_d


def _pc(w):
    sh = w.shape
    w = w.reshape(sh[:-2] + (8, 128, sh[-1]))
    return np.ascontiguousarray(np.swapaxes(w, -3, -2))


def host_layout(inp):
    f = lambda a: np.ascontiguousarray(np.asarray(a, dtype=np.float32))
    w_in = f(inp["w_in"])
    depth = w_in.shape[0]
    shared = {
        "wada": _pc(f(inp["w_ada"])),
        "bada": f(inp["b_ada"]).reshape(depth, 1, 3072),
        "gpre": f(inp["g_pre"]).reshape(depth, 1, 1024),
        "gpost": f(inp["g_post"]).reshape(depth, 1, 1024),
        "wdn": _pc(w_in[:, :, 0:2048]),
        "wba": _pc(w_in[:, :, 2048:2056]),
        "wmb": _pc(w_in[:, :, 2056:4104]),
        "wmg": _pc(w_in[:, :, 4104:6152]),
        "convw": np.ascontiguousarray(
            f(inp["conv_w"]).transpose(0, 2, 1).reshape(depth, 12, 128, 4).transpose(0, 2, 1, 3)
        ).reshape(depth, 128, 48),
        "alog": np.ascontiguousarray(np.broadcast_to(f(inp["a_log"])[:, None, :], (depth, 128, 4))),
        "dtb": np.ascontiguousarray(np.broadcast_to(f(inp["dt_bias"])[:, None, :], (depth, 128, 4))),
        "dng": f(inp["dn_norm_g"]).reshape(depth, 128, 1),
        "wpdn": np.ascontiguousarray(f(inp["w_proj_dn"]).reshape(depth, 4, 128, 1024).transpose(0, 2, 1, 3)),
        "wpmb": np.ascontiguousarray(f(inp["w_proj_mb"]).reshape(depth, 8, 64, 1024).transpose(0, 2, 1, 3)),
        "wout": _pc(f(inp["w_out"])),
        "consts": make_consts(),
    }
    x = f(inp["x"])
    c = f(inp["c"])
    maps = []
    for b in range(x.shape[0]):
        m = dict(shared)
        m["x"] = x[b]
        m["cT"] = np.ascontiguousarray(c[b].reshape(8, 128).T)
        maps.append(m)
    return maps


_CACHE = {}


def kernel(**inputs):
    x = np.asarray(inputs["x"])
    B, S, _ = x.shape
    depth = np.asarray(inputs["w_in"]).shape[0]
    key = (S, depth)
    if key not in _CACHE:
        _CACHE[key] = build(S=S, DEPTH=depth)[0]
    nc = _CACHE[key]
    maps = host_layout(inputs)
    res = run_bass_kernel_spmd(nc, maps, core_ids=list(range(B)))
    return np.stack([np.asarray(r["y"], dtype=np.float32) for r in res.results], axis=0)
```

```python
from contextlib import ExitStack

import numpy as np
import concourse.bass as bass
import concourse.mybir as mybir
from concourse.bass_utils import run_bass_kernel_spmd

F32 = mybir.dt.float32
BF16 = mybir.dt.bfloat16
F32R = mybir.dt.float32r


def R(ap):
    return ap.bitcast(F32R)
AF = mybir.ActivationFunctionType
ALU = mybir.AluOpType
AX = mybir.AxisListType

D_MODEL = 1024
NEG = -30000.0
EPS = 1e-6


class SV:
    def __init__(self, tile, j, n=1):
        self.tile, self.sub = tile, j
        self.ap = tile[:, j * 128:(j + n) * 128]

    def __getitem__(self, idx):
        return self.ap[idx]


class KB:
    NRING = {"sp": 6, "pool": 4}

    def __init__(self, nc, stack):
        self.nc = nc
        self.eng = {"pe": nc.tensor, "act": nc.scalar, "dve": nc.vector,
                    "pool": nc.gpsimd, "sp": nc.sync}
        self.sem = {}
        for e in ("pe", "act", "dve", "pool"):
            self.sem[e] = stack.enter_context(nc.semaphore("s_" + e))
        self.ring = {q: [stack.enter_context(nc.semaphore("s_dma_%s%d" % (q, i))) for i in range(n)]
                     for q, n in self.NRING.items()}
        self.ndmaq = {q: 0 for q in self.NRING}
        self.count = {e: 0 for e in ("pe", "act", "dve", "pool")}
        self.waited = {}
        self.track = {}
        self.ninstr = 0
        self.streams = {e: [] for e in self.eng}
        self.psum_ids = set()

    def _semof(self, dep):
        if dep[0] == "e":
            return self.sem[dep[1]], ("e", dep[1])
        return self.ring[dep[1][0]][dep[1][1]], ("d", dep[1])

    def _wait(self, e, dep):
        sem, sk = self._semof(dep)
        val = dep[2]
        k = (e, sk)
        if self.waited.get(k, 0) >= val:
            return
        self.eng[e].wait_ge(sem, val)
        self.streams[e].append(("wait", sk, val))
        self.ninstr += 1
        self.waited[k] = val

    @staticmethod
    def _keys(items):
        out = []
        for it in items:
            if isinstance(it, SV):
                out.append((id(it.tile), it.sub))
            elif isinstance(it, tuple):
                out.append((id(it[0]), it[1]))
            else:
                out.append((id(it), None))
        return out

    def _conflicts(self, key):
        tid, sub = key
        d = self.track.get(tid)
        if d is None:
            return []
        if sub is None:
            return list(d.values())
        res = []
        if sub in d:
            res.append(d[sub])
        if None in d:
            res.append(d[None])
        return res

    def _entry(self, key):
        tid, sub = key
        d = self.track.setdefault(tid, {})
        if sub is None:
            ent = {"w": [], "r": []}
            for v in d.values():
                ent["w"] += v["w"]
                ent["r"] += v["r"]
            d.clear()
            d[None] = ent
            return ent
        if sub not in d:
            ent = {"w": [], "r": []}
            if None in d:
                ent["w"] = list(d[None]["w"])
                ent["r"] = list(d[None]["r"])
            d[sub] = ent
        return d[sub]

    def _deps(self, e, reads, writes):
        deps = []
        for k in reads:
            for ent in self._conflicts(k):
                for w in ent["w"]:
                    deps.append((w, "raw"))
        for k in writes:
            for ent in self._conflicts(k):
                for w in ent["w"]:
                    deps.append((w, "waw"))
                for r in ent["r"]:
                    deps.append((r, "war"))
        out = []
        for dep, kind in deps:
            if dep[0] == "e" and dep[1] == e:
                if e == "pe":
                    continue
            out.append(dep)
        return out

    @staticmethod
    def _prune(lst):
        best = {}
        for d in lst:
            kk = (d[0], d[1])
            if kk not in best or best[kk][2] < d[2]:
                best[kk] = d
        return list(best.values())

    def _record(self, me, reads, writes):
        for k in reads:
            ent = self._entry(k)
            ent["r"].append(me)
            if len(ent["r"]) > 16:
                ent["r"] = self._prune(ent["r"])
        for k in writes:
            ent = self._entry(k)
            ent["w"] = [me]
            ent["r"] = []

    def _rw(self, r, w):
        reads, writes = [], []
        for k in self._keys(r):
            if k[0] in self.psum_ids:
                writes.append((k[0], None))
            else:
                reads.append(k)
        for k in self._keys(w):
            writes.append((k[0], None) if k[0] in self.psum_ids else k)
        return reads, writes

    def op(self, e, fn, r=(), w=(), inc=True):
        reads, writes = self._rw(r, w)
        for dep in self._deps(e, reads, writes):
            self._wait(e, dep)
        ins = fn(self.eng[e])
        self.ninstr += 1
        if inc:
            self.count[e] += 1
            ins.then_inc(self.sem[e], 1)
            self.streams[e].append(("inc", ("e", e), 1))
            me = ("e", e, self.count[e])
        else:
            me = ("e", e, self.count[e] + 1)
        self._record(me, reads, writes)
        return ins

    def dma(self, out, in_, r=(), w=(), q="sp", **kw):
        e = q
        reads = self._keys(r)
        writes = self._keys(w)
        k = self.ndmaq[q]
        nr = self.NRING[q]
        slot = k % nr
        gen = k // nr
        for dep in self._deps(e, reads, writes):
            self._wait(e, dep)
        if gen > 0:
            self._wait(e, ("d", (q, slot), 16 * gen))
        ins = self.eng[e].dma_start(out=out, in_=in_, **kw)
        ins.then_inc(self.ring[q][slot], 16)
        self.streams[e].append(("inc", ("d", (q, slot)), 16))
        self.ndmaq[q] += 1
        self.ninstr += 1
        me = ("d", (q, slot), 16 * (gen + 1))
        self._record(me, reads, writes)
        return ins

    def _alldma(self):
        out = []
        for q, nr in self.NRING.items():
            n = self.ndmaq[q]
            for slot in range(nr):
                cnt = (n - 1 - slot) // nr + 1 if n > slot else 0
                if cnt > 0:
                    out.append(("d", (q, slot), 16 * cnt))
        return out

    def barrier(self):
        for e in ("pe", "act", "dve", "pool", "sp"):
            for o in ("pe", "act", "dve", "pool"):
                if o != e and self.count[o] > 0:
                    self._wait(e, ("e", o, self.count[o]))
            for dep in self._alldma():
                self._wait(e, dep)
        self.track = {}

    def finish(self):
        for dep in self._alldma():
            self._wait("sp", dep)

    def simulate(self):
        sems = {}
        pc = {e: 0 for e in self.streams}
        progress = True
        while progress:
            progress = False
            for e, st in self.streams.items():
                while pc[e] < len(st):
                    kind, sk, val = st[pc[e]]
                    if kind == "wait":
                        if sems.get(sk, 0) < val:
                            break
                    else:
                        sems[sk] = sems.get(sk, 0) + val
                    pc[e] += 1
                    progress = True
        stuck = {e: (pc[e], len(st), st[pc[e]], sems.get(st[pc[e]][1], 0)) for e, st in self.streams.items()
                 if pc[e] < len(st)}
        return stuck


C_IDENT, C_U, C_MBD, C_MOFF, C_TRI, C_ONES, C_PB = 0, 128, 256, 384, 512, 640, 768
C_PB4 = 768
NCONST = 768 + 512


def make_consts():
    c = np.zeros((128, NCONST), np.float32)
    i = np.arange(128)
    c[:, C_IDENT:C_IDENT + 128] = np.eye(128)
    c[:, C_U:C_U + 128] = (i[:, None] <= i[None, :])
    c[:, C_MBD:C_MBD + 128] = (i[:, None] < i[None, :]) & ((i[:, None] // 64) == (i[None, :] // 64))
    c[:, C_MOFF:C_MOFF + 128] = (i[:, None] < 64) & (i[None, :] >= 64)
    c[:, C_TRI:C_TRI + 128] = np.where(i[:, None] <= i[None, :], 0.0, NEG)
    c[:, C_ONES:C_ONES + 128] = 1.0
    pb = np.zeros((16, 16), np.float32)
    for own in range(16):
        pb[own, own:] = -1e30
    for g in range(8):
        for tt in range(4):
            own = (4 * g + tt) // 2
            c[:, C_PB4 + g * 64 + tt * 16:C_PB4 + g * 64 + (tt + 1) * 16] = pb[own][None, :]
    return c


def build(S=4096, DEPTH=2, LS=2, dbg=None, phases=("p1", "dn", "mb", "fin"), stop=0, JUNK=0):
    dbg = dbg or set()
    NT = S // 128
    NG = S // 512
    NB = S // 256
    nc = bass.Bass("TRN2", target_bir_lowering=False)

    def din(name, shape, dt=F32):
        return nc.dram_tensor(name, shape, dt, kind="ExternalInput").ap()

    x_d = din("x", [S, 1024])
    cT_d = din("cT", [128, 8])
    wada_d = din("wada", [DEPTH, 128, 8, 3072])
    bada_d = din("bada", [DEPTH, 1, 3072])
    gpre_d = din("gpre", [DEPTH, 1, 1024])
    gpost_d = din("gpost", [DEPTH, 1, 1024])
    wdn_d = din("wdn", [DEPTH, 128, 8, 2048])
    wba_d = din("wba", [DEPTH, 128, 8, 8])
    wmb_d = din("wmb", [DEPTH, 128, 8, 2048])
    wmg_d = din("wmg", [DEPTH, 128, 8, 2048])
    convw_d = din("convw", [DEPTH, 128, 48])
    alog_d = din("alog", [DEPTH, 128, 4])
    dtb_d = din("dtb", [DEPTH, 128, 4])
    dng_d = din("dng", [DEPTH, 128, 1])
    wpdn_d = din("wpdn", [DEPTH, 128, 4, 1024])
    wpmb_d = din("wpmb", [DEPTH, 64, 8, 1024])
    wout_d = din("wout", [DEPTH, 128, 8, 1024])
    consts_d = din("consts", [128, NCONST])
    y_d = nc.dram_tensor("y", [S, 1024], F32, kind="ExternalOutput").ap()
    xmid_d = nc.dram_tensor("xmid", [S, 1024], F32, kind="Internal").ap()
    ogdn_d = nc.dram_tensor("ogdn", [4, 128, S], BF16, kind="Internal").ap()
    ogmb_d = nc.dram_tensor("ogmb", [8, 64, S], BF16, kind="Internal").ap()
    dbg_d = {}

    class _K:
        pass
    kx, kmid, ky, kogdn, kogmb = _K(), _K(), _K(), _K(), _K()

    def dbg_out(name, shape):
        dbg_d[name] = nc.dram_tensor("dbg_" + name, shape, F32, kind="ExternalOutput").ap()
        return dbg_d[name]

    with ExitStack() as gst:
        kb = KB(nc, gst)
        op, dma = kb.op, kb.dma
        gst.enter_context(nc.allow_low_precision("float32r (1-pass PE) operands for non-critical fp32 matmuls"))

        def mm(out, lhsT, rhs, start=True, stop=True, r=(), w=(), inc=True):
            return op("pe", lambda e: e.matmul(out, lhsT=lhsT, rhs=rhs, start=start, stop=stop),
                      r=r, w=w, inc=inc)

        def tr(out, in_, ident, r=(), w=()):
            return op("pe", lambda e: e.transpose(out=out, in_=in_, identity=ident), r=r, w=w)

        uid = [0]

        def T(st, name, shape, dt=F32):
            uid[0] += 1
            return st.enter_context(nc.sbuf_tensor("sb%d_%s" % (uid[0], name), shape, dt))

        class _Stop(Exception):
            pass

        def ck(level):
            if stop == level:
                raise _Stop()

        curgen = [None]

        def phase(name):
            if name in phases:
                st_ = ExitStack()
                curgen[0] = st_
                yield st_
                st_.close()

        PS = [gst.enter_context(nc.psum_tensor("ps%d" % i, [128, 512], F32)) for i in range(8)]
        kb.psum_ids = {id(p) for p in PS}
        C = T(gst, "consts", [128, NCONST])
        dma(C[:], consts_d[:, :], w=[C])
        ident = C[:, C_IDENT:C_IDENT + 128]
        U = C[:, C_U:C_U + 128]
        Mbd = C[:, C_MBD:C_MBD + 128]
        Moff = C[:, C_MOFF:C_MOFF + 128]
        ones = C[:, C_ONES:C_ONES + 128]
        identb = T(gst, "identb", [128, 128], BF16)
        trib = T(gst, "trib", [128, 128], BF16)
        epsT = T(gst, "epsT", [128, 1])
        op("dve", lambda e: e.tensor_copy(out=identb[:], in_=ident), r=[C], w=[identb])
        op("dve", lambda e: e.tensor_copy(out=trib[:], in_=C[:, C_TRI:C_TRI + 128]), r=[C], w=[trib])
        op("dve", lambda e: e.memset(epsT[:], EPS), w=[epsT])
        onesr = T(gst, "onesr", [128, 128])
        op("dve", lambda e: e.tensor_copy(out=R(onesr[:]), in_=ones), r=[C], w=[onesr])
        AB = [T(gst, "AB%d" % l, [128, 16]) for l in range(DEPTH)]
        gp_d = nc.dram_tensor("gp_scr", [DEPTH, 128, 1024], F32, kind="Internal").ap()
        kgp = _K()

        with ExitStack() as st:
            cT = T(st, "cT", [128, 8])
            sc = T(st, "sc", [128, 8])
            dma(cT[:], cT_d[:, :], w=[cT])
            op("act", lambda e: e.activation(out=sc[:], in_=cT[:], func=AF.Silu), r=[cT], w=[sc])
            wa = [T(st, "wa%d" % i, [128, 8, 512]) for i in range(2)]
            row = T(st, "row", [1, 3072])
            bada = T(st, "bada", [1, 3072])
            gpr = T(st, "gpr", [1, 1024])
            gpo = T(st, "gpo", [1, 1024])
            arow = T(st, "arow", [1, 1024])
            gprow = T(st, "gprow", [1, 1024])
            gptmp = T(st, "gptmp", [128, 1024])
            nwa = 0
            for l in range(DEPTH):
                dma(bada[:], bada_d[l], w=[bada])
                dma(gpr[:], gpre_d[l], w=[gpr])
                dma(gpo[:], gpost_d[l], w=[gpo])
                for cg in range(6):
                    wt = wa[nwa % 2]
                    nwa += 1
                    dma(wt[:], wada_d[l, :, :, cg * 512:(cg + 1) * 512], w=[wt])
                    pb = PS[cg % 2]
                    for c in range(8):
                        mm(pb[0:1, :], sc[:, c:c + 1], wt[:, c, :], start=(c == 0), stop=(c == 7),
                           r=[sc, wt], w=[pb], inc=(c == 7))
                    op("dve", lambda e: e.tensor_tensor(out=row[0:1, cg * 512:(cg + 1) * 512], in0=pb[0:1, :],
                                                        in1=bada[0:1, cg * 512:(cg + 1) * 512], op=ALU.add),
                       r=[pb, bada], w=[(row, cg)])
                op("dve", lambda e: e.scalar_tensor_tensor(out=arow[:], in0=row[0:1, 1024:2048], scalar=1.0,
                                                           in1=gpr[:], op0=ALU.add, op1=ALU.mult),
                   r=[row, gpr], w=[arow])
                op("dve", lambda e: e.tensor_tensor(out=gprow[:], in0=row[0:1, 2048:3072], in1=gpo[:], op=ALU.mult),
                   r=[row, gpo], w=[gprow])
                pc = PS[2]
                for c in range(8):
                    mm(pc[:, c:c + 1], arow[0:1, c * 128:(c + 1) * 128], ones[0:1, 0:1], r=[arow, C], w=[pc], inc=False)
                for c in range(8):
                    mm(pc[:, 8 + c:9 + c], row[0:1, c * 128:(c + 1) * 128], ones[0:1, 0:1], r=[row, C], w=[pc],
                       inc=(c == 7))
                op("dve", lambda e: e.tensor_copy(out=AB[l][:], in_=pc[:, 0:16]), r=[pc], w=[AB[l]])
                for hf in range(2):
                    pg = PS[3 + hf]
                    mm(pg[:, :], ones[0:1, 0:128], gprow[0:1, hf * 512:(hf + 1) * 512], r=[gprow, C], w=[pg])
                    op("act", lambda e: e.activation(out=gptmp[:, hf * 512:(hf + 1) * 512], in_=pg[:, :], func=AF.Copy),
                       r=[pg], w=[(gptmp, hf)])
                dma(gp_d[l], gptmp[:], r=[gptmp], w=[(kgp, l)])
            kb.barrier()

        hT = T(gst, "hT", [128, 8, S], BF16)

        for l in range(DEPTH):
          try:
            xin_d = x_d if l == 0 else xmid_d
            xout_d = y_d if l == DEPTH - 1 else xmid_d
            xin_k = kx if l == 0 else kmid
            xout_k = ky if l == DEPTH - 1 else kmid

            for st in phase("p1"):
                xt = [T(st, "xt%d" % i, [128, 1024]) for i in range(4)]
                xn = [T(st, "xn%d" % i, [128, 1024]) for i in range(3)]
                junk = T(st, "junk", [128, 1024], BF16)
                ss = T(st, "ss", [128, NT])
                rstd = T(st, "rstd", [128, NT])
                for t in range(NT):
                    xtt, xnt = xt[t % 4], xn[t % 3]
                    dma(xtt[:], xin_d[t * 128:(t + 1) * 128, :], r=[(xin_k, t)], w=[xtt])
                    op("act", lambda e: e.activation(out=junk[:], in_=xtt[:], func=AF.Square,
                                                     accum_out=ss[:, t:t + 1]), r=[xtt], w=[junk, (ss, t)])
                    op("act", lambda e: e.activation(out=rstd[:, t:t + 1], in_=ss[:, t:t + 1], func=AF.Sqrt,
                                                     scale=1.0 / D_MODEL, bias=epsT[:]), r=[(ss, t), epsT],
                       w=[(rstd, t)])
                    op("dve", lambda e: e.reciprocal(out=rstd[:, t:t + 1], in_=rstd[:, t:t + 1]), r=[(rstd, t)],
                       w=[(rstd, t)])
                    op("dve", lambda e: e.tensor_scalar(out=xnt[:], in0=xtt[:], scalar1=rstd[:, t:t + 1],
                                                        scalar2=None, op0=ALU.mult), r=[xtt, (rstd, t)], w=[xnt])
                    for hf in range(2):
                        pb = PS[(t % 2) * 2 + hf]
                        for cc in range(4):
                            c = hf * 4 + cc
                            tr(pb[:, cc * 128:(cc + 1) * 128], xnt[:, c * 128:(c + 1) * 128], ident, r=[xnt, C],
                               w=[pb])
                        for cc in range(4):
                            c = hf * 4 + cc
                            eng = "act" if hf == 0 else "dve"
                            if eng == "act":
                                op("act", lambda e: e.activation(out=hT[:, c, t * 128:(t + 1) * 128],
                                                                 in_=pb[:, cc * 128:(cc + 1) * 128], func=AF.Identity,
                                                                 scale=AB[l][:, c:c + 1], bias=AB[l][:, 8 + c:9 + c]),
                                   r=[pb, AB[l]], w=[(hT, t // 4)])
                            else:
                                op("dve", lambda e: e.tensor_scalar(out=hT[:, c, t * 128:(t + 1) * 128],
                                                                    in0=pb[:, cc * 128:(cc + 1) * 128],
                                                                    scalar1=AB[l][:, c:c + 1],
                                                                    scalar2=AB[l][:, 8 + c:9 + c],
                                                                    op0=ALU.mult, op1=ALU.add),
                                   r=[pb, AB[l]], w=[(hT, t // 4)])
                kb.barrier()
            if "hT" in dbg and l == 0:
                with ExitStack() as st:
                    d = dbg_out("hT", [128, 8, S])
                    tmp = T(st, "dbgtmp", [128, 8, S])
                    op("dve", lambda e: e.tensor_copy(out=tmp[:], in_=hT[:]), r=[hT], w=[tmp])
                    dma(d[:, :, :], tmp[:], r=[tmp])
                    kb.barrier()

            for st in phase("dn"):
                Wdn = T(st, "Wdn", [128, 8, 2048], BF16)
                Wba = T(st, "Wba", [128, 8, 8], BF16)
                for c in range(8):
                    dma(Wdn[:, c, :], wdn_d[l, :, c, :], w=[(Wdn, c)], q="pool")
                dma(Wba[:], wba_d[l], w=[Wba], q="pool")
                convw = T(st, "convw", [128, 48])
                alog = T(st, "alog", [128, 4])
                dtb = T(st, "dtb", [128, 4])
                dng = T(st, "dng", [128, 1])
                dma(convw[:], convw_d[l], w=[convw])
                dma(alog[:], alog_d[l], w=[alog])
                dma(dtb[:], dtb_d[l], w=[dtb])
                dma(dng[:], dng_d[l], w=[dng])
                st2 = ExitStack()
                BETA = T(st, "BETA", [128, NT, 4])
                NBETA = T(st, "NBETA", [128, NT, 4])
                GRAW = T(st, "GRAW", [128, NT, 4])
                GC = T(st, "GC", [128, NT, 4])
                NEXPG = T(st, "NEXPG", [128, NT, 4])
                negA = T(st, "negA", [128, 4])
                BG = T(st2, "BG", [128, NT, 8])
                AA = T(st2, "AA", [128, NT, 4])
                AX_ = T(st2, "AXs", [128, NT, 4])
                pbg = PS[0]
                for t in range(NT):
                    for c in range(8):
                        mm(pbg[:, t * 8:(t + 1) * 8], hT[:, c, t * 128:(t + 1) * 128], Wba[:, c, :],
                           start=(c == 0), stop=(c == 7), r=[(hT, t // 4), Wba], w=[pbg], inc=(c == 7))
                op("dve", lambda e: e.tensor_copy(out=BG[:].rearrange("p t e -> p (t e)"), in_=pbg[:, 0:NT * 8]),
                   r=[pbg], w=[BG])
                op("act", lambda e: e.activation(out=BETA[:], in_=BG[:, :, 0:4], func=AF.Sigmoid), r=[BG], w=[BETA])
                op("dve", lambda e: e.tensor_scalar(out=NBETA[:], in0=BETA[:], scalar1=-1.0, scalar2=None,
                                                    op0=ALU.mult), r=[BETA], w=[NBETA])
                for h in range(4):
                    op("dve", lambda e: e.tensor_scalar(out=AA[:, :, h], in0=BG[:, :, 4 + h], scalar1=dtb[:, h:h + 1],
                                                        scalar2=None, op0=ALU.add), r=[BG, dtb], w=[AA])
                op("act", lambda e: e.activation(out=AX_[:], in_=AA[:], func=AF.Abs), r=[AA], w=[AX_])
                op("act", lambda e: e.activation(out=AX_[:], in_=AX_[:], func=AF.Exp, scale=-1.0), r=[AX_], w=[AX_])
                op("dve", lambda e: e.tensor_scalar(out=AX_[:], in0=AX_[:], scalar1=1.0, scalar2=None, op0=ALU.add),
                   r=[AX_], w=[AX_])
                op("act", lambda e: e.activation(out=AX_[:], in_=AX_[:], func=AF.Ln), r=[AX_], w=[AX_])
                op("dve", lambda e: e.scalar_tensor_tensor(out=AA[:], in0=AA[:], scalar=0.0, in1=AX_[:],
                                                           op0=ALU.max, op1=ALU.add), r=[AA, AX_], w=[AA])
                op("act", lambda e: e.activation(out=negA[:], in_=alog[:], func=AF.Exp), r=[alog], w=[negA])
                op("dve", lambda e: e.tensor_scalar(out=negA[:], in0=negA[:], scalar1=-1.0, scalar2=None,
                                                    op0=ALU.mult), r=[negA], w=[negA])
                for h in range(4):
                    op("dve", lambda e: e.tensor_scalar(out=GRAW[:, :, h], in0=AA[:, :, h], scalar1=negA[:, h:h + 1],
                                                        scalar2=None, op0=ALU.mult), r=[AA, negA], w=[GRAW])
                pgc = PS[1]
                mm(pgc[:, 0:NT * 4], U, GRAW[:].rearrange("p t e -> p (t e)"), r=[C, GRAW], w=[pgc])
                op("dve", lambda e: e.tensor_copy(out=GC[:].rearrange("p t e -> p (t e)"), in_=pgc[:, 0:NT * 4]),
                   r=[pgc], w=[GC])
                op("act", lambda e: e.activation(out=NEXPG[:], in_=GC[:], func=AF.Exp), r=[GC], w=[NEXPG])
                op("dve", lambda e: e.tensor_scalar(out=NEXPG[:], in0=NEXPG[:], scalar1=-1.0, scalar2=None,
                                                    op0=ALU.mult), r=[NEXPG], w=[NEXPG])
                if "graw" in dbg and l == 0:
                    d = dbg_out("graw", [128, NT, 4])
                    dma(d[:, :, :], GRAW[:], r=[GRAW])
                    d = dbg_out("beta", [128, NT, 4])
                    dma(d[:, :, :], BETA[:], r=[BETA])

                kb.barrier()
                st2.close()
                ck(1)
                pre = [T(st, "pre%d" % i, [128, 515]) for i in range(3)]
                halo = T(st, "halo", [128, 12, 3])
                op("dve", lambda e: e.memset(halo[:], 0.0), w=[halo])
                qkv = [[T(st, "qkv%d_%d" % (i, j), [128, 512]) for j in range(3)] for i in range(2)]
                zs = [T(st, "zs%d" % i, [128, 512], BF16) for i in range(4)]
                cv = [T(st, "cv0", [128, 512])] * 2
                sq = [T(st, "sq0", [128, 512])] * 2
                oTg = [T(st, "oTg%d" % i, [128, 512]) for i in range(2)]
                ogb = [T(st, "ogb0", [128, 512], BF16)] * 2
                Sst = [[T(st, "S%d_%d" % (h, i), [128, 128]) for i in range(2)] for h in range(4)]
                for h in range(4):
                    op("dve", lambda e: e.tensor_scalar(out=R(Sst[h][0][:]), in0=ident, scalar1=0.0, scalar2=None,
                                                        op0=ALU.mult), r=[C], w=[Sst[h][0]])
                spar = [0, 0, 0, 0]
                NCH = 8
                ALIAS = {"Dm": 0, "tq": 0, "DecT": 1, "Xo": 1, "Pe": 2, "ktok": 3, "Xe": 3, "NoffT": 3, "Po": 4,
                         "B": 5, "BT": 6, "WbdT": 6, "Noff": 7, "Z1": 7, "ExpG": 8, "XTo": 8, "Lm": 9, "XTe": 9}
                scr = []
                for i in range(NCH):
                    wide = T(st, "s%d" % i, [128, 9 * 128])
                    plain = T(st, "s%da" % i, [128, 128])
                    d_ = {n: (SV(wide, j - 1) if j > 0 else plain) for n, j in ALIAS.items()}
                    d_["XPo"] = SV(wide, 0, 2)
                    d_["XPe"] = SV(wide, 2, 2)
                    scr.append(d_)
                OUTN = ["W", "QKdT", "kdec", "qg", "vtok", "kTc"]
                outs = [{n: T(st, "o%d_%s" % (i, n), [128, 128]) for n in OUTN} for i in range(NCH)]
                EGL = T(st, "EGL", [128, NCH])
                Yr = [T(st, "Yr%d" % i, [128, 128]) for i in range(2)]
                Vn = [T(st, "Vn%d" % i, [128, 128]) for i in range(2)]

                def prep8(chains, pump=lambda: None):
                    def each(fn):
                        for ch in chains:
                            fn(ch, ch["s"], ch["o"], PS[ch["i"]])

                    def f(ch, s, o, pb):
                        h, n, sl = ch["h"], ch["n"], ch["sl"]
                        kT, vT = ch["kT"], ch["vT"]
                        tr(pb[:, 0:128], kT[:, sl], ident, r=[kT, C], w=[pb])
                        tr(pb[:, 128:256], vT[:, sl], ident, r=[vT, C], w=[pb])
                        mm(pb[:, 256:384], GRAW[:, n, h:h + 1].to_broadcast([128, 128]), U, r=[GRAW, C], w=[pb])
                        op("act", lambda e: e.activation(out=R(s["ktok"][:]), in_=pb[:, 0:128], func=AF.Copy),
                           r=[pb], w=[s["ktok"]])
                        op("dve", lambda e: e.tensor_copy(out=o["vtok"][:], in_=pb[:, 128:256]), r=[pb],
                           w=[o["vtok"]])
                        op("dve", lambda e: e.tensor_scalar(out=s["Dm"][:], in0=pb[:, 256:384],
                                                            scalar1=GC[:, n, h:h + 1], scalar2=0.0,
                                                            op0=ALU.subtract, op1=ALU.min),
                           r=[pb, GC], w=[s["Dm"]])
                        op("act", lambda e: e.activation(out=R(s["ExpG"][:]), in_=pb[:, 256:384], func=AF.Exp),
                           r=[pb], w=[s["ExpG"]])
                    each(f)

                    def f(ch, s, o, pb):
                        op("act", lambda e: e.activation(out=R(s["DecT"][:]), in_=s["Dm"][:], func=AF.Exp),
                           r=[s["Dm"]], w=[s["DecT"]])
                    each(f)

                    def f(ch, s, o, pb):
                        h, n, sl = ch["h"], ch["n"], ch["sl"]
                        kT, qT = ch["kT"], ch["qT"]
                        mm(pb[:, 0:128], R(kT[:, sl]), R(kT[:, sl]), r=[kT], w=[pb], inc=False)
                        mm(pb[:, 128:256], R(kT[:, sl]), R(qT[:, sl]), r=[kT, qT], w=[pb])
                        op("dve", lambda e: e.tensor_tensor(out=R(s["Lm"][:]), in0=pb[:, 0:128], in1=s["DecT"][:],
                                                            op=ALU.mult), r=[pb, s["DecT"]], w=[s["Lm"]])
                        op("dve", lambda e: e.tensor_tensor(out=s["tq"][:], in0=pb[:, 128:256], in1=s["DecT"][:],
                                                            op=ALU.mult), r=[pb, s["DecT"]], w=[s["tq"]])
                        op("dve", lambda e: e.scalar_tensor_tensor(out=R(s["B"][:]), in0=s["Lm"][:],
                                                                   scalar=NBETA[:, n, h:h + 1], in1=Mbd,
                                                                   op0=ALU.mult, op1=ALU.mult),
                           r=[s["Lm"], NBETA, C], w=[s["B"]])
                        op("dve", lambda e: e.scalar_tensor_tensor(out=R(s["Noff"][:]), in0=s["Lm"][:],
                                                                   scalar=BETA[:, n, h:h + 1], in1=Moff,
                                                                   op0=ALU.mult, op1=ALU.mult),
                           r=[s["Lm"], BETA, C], w=[s["Noff"]])
                        op("pool", lambda e: e.tensor_tensor(out=R(o["QKdT"][:]), in0=s["tq"][:], in1=U,
                                                             op=ALU.mult), r=[s["tq"], C], w=[o["QKdT"]])
                        op("act", lambda e: e.activation(out=R(o["kdec"][:]), in_=s["ktok"][:], func=AF.Identity,
                                                         scale=s["DecT"][:, 127:128]), r=[s["ktok"], s["DecT"]],
                           w=[o["kdec"]])
                        op("dve", lambda e: e.tensor_tensor(out=R(o["qg"][:]), in0=qT[:, sl], in1=s["ExpG"][:],
                                                            op=ALU.mult), r=[qT, s["ExpG"]], w=[o["qg"]])
                        op("dve", lambda e: e.tensor_copy(out=R(o["kTc"][:]), in_=kT[:, sl]), r=[kT], w=[o["kTc"]])
                        op("dve", lambda e: e.tensor_copy(out=EGL[:, ch["i"]:ch["i"] + 1],
                                                          in_=s["ExpG"][:, 127:128]),
                           r=[s["ExpG"]], w=[(EGL, ch["i"])])
                    each(f)

                    def f(ch, s, o, pb):
                        tr(pb[:, 256:384], s["B"][:], ident, r=[s["B"], C], w=[pb])
                        op("act", lambda e: e.activation(out=R(s["BT"][:]), in_=pb[:, 256:384], func=AF.Copy),
                           r=[pb], w=[s["BT"]])
                        op("dve", lambda e: e.tensor_tensor(out=R(s["Pe"][:]), in0=s["B"][:], in1=ident,
                                                            op=ALU.add), r=[s["B"], C], w=[s["Pe"]])
                    each(f)

                    def f(ch, s, o, pb):
                        mm(pb[:, 0:128], R(s["BT"][:]), R(s["B"][:]), r=[s["BT"], s["B"]], w=[pb])
                        op("act", lambda e: e.activation(out=R(s["Xo"][:]), in_=pb[:, 0:128], func=AF.Copy),
                           r=[pb], w=[s["Xo"]])
                    each(f)
                    pump()
                    for j in range(1, 6):
                        odd = (j % 2 == 1)
                        Xc, Pc, XTc, XPc = ("Xo", "Pe", "XTo", "XPo") if odd else ("Xe", "Po", "XTe", "XPe")
                        Xn, Pn = ("Xe", "Po") if odd else ("Xo", "Pe")

                        def f(ch, s, o, pb):
                            tr(pb[:, 256:384], s[Xc][:], ident, r=[s[Xc], C], w=[pb])
                            op("act", lambda e: e.activation(out=R(s[XTc][:]), in_=pb[:, 256:384], func=AF.Copy),
                               r=[pb], w=[s[XTc]])
                        each(f)
                        pump()

                        def f(ch, s, o, pb):
                            if j < 5:
                                mm(pb[:, 0:256], R(s[XTc][:]), R(s[XPc][:]), r=[s[XTc], s[Xc], s[Pc]], w=[pb])
                                op("act", lambda e: e.activation(out=R(s[Xn][:]), in_=pb[:, 0:128], func=AF.Copy),
                                   r=[pb], w=[s[Xn]])
                                op("dve", lambda e: e.tensor_tensor(out=R(s[Pn][:]), in0=pb[:, 128:256],
                                                                    in1=s[Pc][:], op=ALU.add),
                                   r=[pb, s[Pc]], w=[s[Pn]])
                            else:
                                mm(pb[:, 128:256], R(s[XTc][:]), R(s[Pc][:]), r=[s[XTc], s[Pc]], w=[pb])
                                op("dve", lambda e: e.tensor_tensor(out=R(s[Pn][:]), in0=pb[:, 128:256],
                                                                    in1=s[Pc][:], op=ALU.add),
                                   r=[pb, s[Pc]], w=[s[Pn]])
                        each(f)
                        pump()
                    Pf = "Po"

                    def f(ch, s, o, pb):
                        tr(pb[:, 0:128], s[Pf][:], ident, r=[s[Pf], C], w=[pb])
                        tr(pb[:, 128:256], s["Noff"][:], ident, r=[s["Noff"], C], w=[pb])
                        op("act", lambda e: e.activation(out=R(s["WbdT"][:]), in_=pb[:, 0:128], func=AF.Copy),
                           r=[pb], w=[s["WbdT"]])
                        op("dve", lambda e: e.tensor_copy(out=R(s["NoffT"][:]), in_=pb[:, 128:256]), r=[pb],
                           w=[s["NoffT"]])
                    each(f)
                    pump()

                    def f(ch, s, o, pb):
                        mm(pb[:, 0:128], R(s["NoffT"][:]), R(s[Pf][:]), r=[s["NoffT"], s[Pf]], w=[pb])
                        op("act", lambda e: e.activation(out=R(s["Z1"][:]), in_=pb[:, 0:128], func=AF.Copy),
                           r=[pb], w=[s["Z1"]])
                    each(f)
                    pump()

                    def f(ch, s, o, pb):
                        mm(pb[:, 128:256], R(s["WbdT"][:]), R(s["Z1"][:]), r=[s["WbdT"], s["Z1"]], w=[pb])
                        op("dve", lambda e: e.tensor_tensor(out=R(o["W"][:]), in0=s[Pf][:], in1=pb[:, 128:256],
                                                            op=ALU.subtract), r=[s[Pf], pb], w=[o["W"]])
                    each(f)
                    pump()

                nrec = [0]

                def recur_pair(chs, g):
                    st_ = []
                    for ch in chs:
                        h = ch["h"]
                        So = Sst[h][spar[h]]
                        Sn = Sst[h][1 - spar[h]]
                        spar[h] = 1 - spar[h]
                        bb = 4 * ch["hi"]
                        st_.append((ch, So, Sn, PS[bb], PS[bb + 1], PS[bb + 2], PS[bb + 3],
                                    Yr[nrec[0] % 2], Vn[nrec[0] % 2]))
                        nrec[0] += 1
                    for ch, So, Sn, pa, pb2, pc, pd, Y, vnew in st_:
                        h, n, o = ch["h"], ch["n"], ch["o"]
                        mm(pa[:, 0:128], R(o["kTc"][:]), R(So[:]), r=[o["kTc"], So], w=[pa])
                        op("dve", lambda e: e.scalar_tensor_tensor(out=R(Y[:]), in0=pa[:, 0:128],
                                                                   scalar=NEXPG[:, n, h:h + 1], in1=o["vtok"][:],
                                                                   op0=ALU.mult, op1=ALU.add),
                           r=[pa, NEXPG, o["vtok"]], w=[Y])
                    for ch, So, Sn, pa, pb2, pc, pd, Y, vnew in st_:
                        h, n, o = ch["h"], ch["n"], ch["o"]
                        mm(pb2[:, 0:128], R(o["W"][:]), R(Y[:]), r=[o["W"], Y], w=[pb2])
                        op("act", lambda e: e.activation(out=R(vnew[:]), in_=pb2[:, 0:128], func=AF.Identity,
                                                         scale=BETA[:, n, h:h + 1]), r=[pb2, BETA], w=[vnew])
                    for ch, So, Sn, pa, pb2, pc, pd, Y, vnew in st_:
                        o = ch["o"]
                        mm(pd[:, 0:128], R(o["kdec"][:]), R(vnew[:]), r=[o["kdec"], vnew], w=[pd])
                        op("dve", lambda e: e.scalar_tensor_tensor(out=R(Sn[:]), in0=So[:],
                                                                   scalar=EGL[:, ch["i"]:ch["i"] + 1],
                                                                   in1=pd[:, 0:128], op0=ALU.mult, op1=ALU.add),
                           r=[So, (EGL, ch["i"]), pd], w=[Sn])
                    for ch, So, Sn, pa, pb2, pc, pd, Y, vnew in st_:
                        o, sl = ch["o"], ch["sl"]
                        mm(pc[:, 0:128], R(So[:]), R(o["qg"][:]), start=True, stop=False, r=[So, o["qg"]], w=[pc],
                           inc=False)
                        mm(pc[:, 0:128], R(vnew[:]), R(o["QKdT"][:]), start=False, stop=True, r=[vnew, o["QKdT"]],
                           w=[pc])
                        ot = oTg[ch["hi"]]
                        op("act", lambda e: e.activation(out=ot[:, sl], in_=pc[:, 0:128], func=AF.Copy), r=[pc],
                           w=[(ot, ch["cc"])])

                def stageA(g, hp, par, res):
                    gs = slice(g * 512, (g + 1) * 512)
                    chains = []
                    for hi in range(2):
                        h = 2 * hp + hi
                        qk = qkv[hi]
                        zt = zs[2 * par + hi]
                        for ty in range(4):
                            pb = PS[4 * hi + ty]
                            col = ty * 512 + h * 128
                            for c in range(8):
                                mm(pb[:, :], Wdn[:, c, col:col + 128], hT[:, c, gs], start=(c == 0),
                                   stop=(c == 7), r=[(Wdn, c), (hT, g)], w=[pb], inc=(c == 7))
                            if ty < 3:
                                ch_ = ty * 4 + h
                                op("dve", lambda e: e.tensor_copy(out=pre[ty][:, 0:3], in_=halo[:, ch_, :]),
                                   r=[(halo, ch_)], w=[(pre[ty], 0)])
                                op("act", lambda e: e.activation(out=pre[ty][:, 3:515], in_=pb[:, :],
                                                                 func=AF.Copy), r=[pb], w=[(pre[ty], 1)])
                                op("dve", lambda e: e.tensor_copy(out=halo[:, ch_, :], in_=pre[ty][:, 512:515]),
                                   r=[(pre[ty], 1)], w=[(halo, ch_)])
                                yield
                                cvt = cv[ty % 2]
                                wk = lambda k: convw[:, ch_ * 4 + k:ch_ * 4 + k + 1]
                                op("act", lambda e: e.activation(out=cvt[:], in_=pre[ty][:, 0:512],
                                                                 func=AF.Identity, scale=wk(0)),
                                   r=[pre[ty], convw], w=[cvt])
                                for k in range(1, 4):
                                    op("dve", lambda e: e.scalar_tensor_tensor(out=cvt[:],
                                                                               in0=pre[ty][:, k:k + 512],
                                                                               scalar=wk(k), in1=cvt[:],
                                                                               op0=ALU.mult, op1=ALU.add),
                                       r=[pre[ty], convw, cvt], w=[cvt])
                                    yield
                                op("act", lambda e: e.activation(out=(R(qk[ty][:]) if ty < 2 else qk[ty][:]),
                                                                 in_=cvt[:], func=AF.Silu),
                                   r=[cvt], w=[qk[ty]])
                            else:
                                op("act", lambda e: e.activation(out=zt[:], in_=pb[:, :], func=AF.Silu),
                                   r=[pb], w=[zt])
                            yield
                        for ty in range(2):
                            sqt = sq[ty]
                            pb = PS[4 * hi + ty]
                            op("act", lambda e: e.activation(out=R(sqt[:]), in_=qk[ty][:], func=AF.Square),
                               r=[qk[ty]], w=[sqt])
                            mm(pb[:, :], R(onesr[:]), R(sqt[:]), r=[onesr, sqt], w=[pb])
                            rt_ = cv[0]
                            op("act", lambda e: e.activation(out=rt_[:], in_=pb[:, :], func=AF.Ln,
                                                             bias=epsT[:]), r=[pb, epsT], w=[rt_])
                            op("act", lambda e: e.activation(out=rt_[:], in_=rt_[:], func=AF.Exp, scale=-0.5),
                               r=[rt_], w=[rt_])
                            sc_ = (128.0 ** -0.5) if ty == 0 else 1.0
                            op("dve", lambda e: e.scalar_tensor_tensor(out=R(qk[ty][:]), in0=qk[ty][:],
                                                                       scalar=sc_, in1=rt_[:], op0=ALU.mult,
                                                                       op1=ALU.mult),
                               r=[qk[ty], rt_], w=[qk[ty]])
                            yield
                        if dbg and l == 0 and g == 0 and h == 0:
                            for nm, tt in (("q", qk[0]), ("k", qk[1]), ("v", qk[2])):
                                if nm in dbg:
                                    d = dbg_out(nm, [128, 512])
                                    dma(d[:, :], tt[:], r=[tt])
                        for cc in range(4):
                            i = hi * 4 + cc
                            chains.append({"h": h, "hi": hi, "cc": cc, "n": g * 4 + cc, "i": i,
                                           "sl": slice(cc * 128, (cc + 1) * 128), "qT": qk[0], "kT": qk[1],
                                           "vT": qk[2], "s": scr[i], "o": outs[i]})
                    res["chains"] = chains

                def drain(gen):
                    if gen is not None:
                        for _ in gen:
                            pass

                pairs = [(g, hp) for g in range(NG) for hp in range(2)]
                resA = [dict() for _ in pairs]
                gens = [stageA(g, hp, pi % 2, resA[pi]) for pi, (g, hp) in enumerate(pairs)]
                drain(gens[0])
                for pi, (g, hp) in enumerate(pairs):
                    gs = slice(g * 512, (g + 1) * 512)
                    chains = resA[pi]["chains"]
                    nxt = gens[pi + 1] if pi + 1 < len(pairs) else None

                    def pump(n=5):
                        if nxt is not None:
                            for _ in range(n):
                                if next(nxt, "done") == "done":
                                    break
                    prep8(chains, pump)
                    drain(nxt)
                    for cc in range(4):
                        recur_pair([ch for ch in chains if ch["cc"] == cc], g)
                    for hi in range(2):
                        h = 2 * hp + hi
                        zt = zs[2 * (pi % 2) + hi]
                        ot = oTg[hi]
                        og = ogb[hi]
                        sqt = sq[hi]
                        pb = PS[4 * hi]
                        op("act", lambda e: e.activation(out=R(sqt[:]), in_=ot[:], func=AF.Square), r=[ot],
                           w=[sqt])
                        mm(pb[:, :], R(onesr[:]), R(sqt[:]), r=[onesr, sqt], w=[pb])
                        sqt = cv[0]
                        op("act", lambda e: e.activation(out=sqt[:], in_=pb[:, :], func=AF.Ln,
                                                         scale=1.0 / 128, bias=epsT[:]), r=[pb, epsT], w=[sqt])
                        op("act", lambda e: e.activation(out=sqt[:], in_=sqt[:], func=AF.Exp, scale=-0.5),
                           r=[sqt], w=[sqt])
                        if "odn" in dbg and l == 0 and h == 0 and g == 0:
                            d = dbg_out("odn", [128, 512])
                            dma(d[:, :], ot[:], r=[ot])
                        op("dve", lambda e: e.scalar_tensor_tensor(out=sqt[:], in0=ot[:], scalar=dng[:, 0:1],
                                                                   in1=sqt[:], op0=ALU.mult, op1=ALU.mult),
                           r=[ot, dng, sqt], w=[sqt])
                        op("dve", lambda e: e.tensor_tensor(out=og[:], in0=sqt[:], in1=zt[:], op=ALU.mult),
                           r=[sqt, zt], w=[og])
                        dma(ogdn_d[h, :, gs], og[:], r=[og], w=[(kogdn, g)])
                kb.barrier()

            for st in phase("mb"):
                Wmb = T(st, "Wmb", [128, 8, 2048], BF16)
                for c in range(8):
                    dma(Wmb[:, c, :], wmb_d[l, :, c, :], w=[(Wmb, c)], q="pool")
                Vt = T(st, "Vt", [128, NT, 8, 65], BF16)
                op("dve", lambda e: e.memset(Vt[:, :, :, 64:65], 1.0), w=[Vt])
                for t in range(NT):
                    pb = PS[t % 2]
                    for c in range(8):
                        mm(pb[:, :], hT[:, c, t * 128:(t + 1) * 128], Wmb[:, c, 1024:1536], start=(c == 0),
                           stop=(c == 7), r=[(hT, t // 4), (Wmb, c)], w=[pb], inc=(c == 7))
                    op("act" if t % 2 else "dve",
                       lambda e: (e.activation(out=Vt[:, t, :, 0:64], in_=pb[:, :].rearrange("p (h d) -> p h d", h=8),
                                               func=AF.Copy) if t % 2 else
                                  e.tensor_copy(out=Vt[:, t, :, 0:64],
                                                in_=pb[:, :].rearrange("p (h d) -> p h d", h=8))),
                       r=[pb], w=[(Vt, t)])
                KaT = T(st, "KaT", [128, S], BF16)
                QaT = T(st, "QaT", [128, S], BF16)
                zsm = T(st, "zsm", [64, S], BF16)
                ogm = T(st, "ogm", [64, S], BF16)
                kf = [T(st, "kf%d" % i, [128, 512]) for i in range(2)]
                qf = [T(st, "qf%d" % i, [128, 512]) for i in range(2)]
                sqm = [T(st, "sqm%d" % i, [128, 512]) for i in range(2)]
                kmT = T(st, "kmT", [128, 16])
                km2 = T(st, "km2", [64, NG + 1])
                gm4 = T(st, "gm4", [128, 4, 16])
                top84 = T(st, "top84", [128, 4, 8])
                mbt4 = T(st, "mbt4", [128, 4, 16])
                rden = T(st, "rden", [128, 256])
                t1 = [T(st, "t1_%d" % i, [64, 256]) for i in range(2)]
                PT = [T(st, "PT%d" % i, [128, 512], BF16) for i in range(3)]
                nS = [0]
                op("dve", lambda e: e.memset(KaT[0:64, :], 0.0), w=[KaT])
                op("dve", lambda e: e.memset(QaT[0:64, :], 0.0), w=[QaT])
                op("dve", lambda e: e.memset(KaT[32:33, :], 1.0), w=[KaT])
                for n in range(NB):
                    op("dve", lambda e: e.tensor_copy(out=KaT[0:16, n * 256:(n + 1) * 256],
                                                      in_=ident[0:16, n:n + 1].to_broadcast([16, 256])),
                       r=[C], w=[KaT])
                npt = 0
                for h in range(8):
                    op("dve", lambda e: e.memset(kmT[:], 0.0), w=[kmT])
                    for g in range(NG):
                        gs = slice(g * 512, (g + 1) * 512)
                        pb = PS[g % 2]
                        col = 512 + h * 64
                        for c in range(8):
                            mm(pb[64:128, :], Wmb[:, c, col:col + 64], hT[:, c, gs], start=(c == 0), stop=(c == 7),
                               r=[(Wmb, c), (hT, g)], w=[pb], inc=(c == 7))
                        kft = kf[g % 2]
                        op("act", lambda e: e.activation(out=kft[64:128, :], in_=pb[64:128, :], func=AF.Copy),
                           r=[pb], w=[kft])
                        op("dve", lambda e: e.tensor_copy(out=KaT[64:128, gs], in_=pb[64:128, :]), r=[pb],
                           w=[(KaT, g)])
                        op("dve", lambda e: e.tensor_reduce(out=kmT[64:128, 2 * g:2 * g + 2],
                                                            in_=kft[64:128, :].rearrange("p (b t) -> p b t", b=2),
                                                            axis=AX.X, op=ALU.add), r=[kft], w=[kmT])
                        sqt = sqm[g % 2]
                        op("act", lambda e: e.activation(out=sqt[64:128, :], in_=kft[64:128, :], func=AF.Square),
                           r=[kft], w=[sqt])
                        pr = PS[2 + g % 2]
                        mm(pr[32:33, :], ones[64:128, 0:1], sqt[64:128, :], r=[C, sqt], w=[pr])
                        op("dve", lambda e: e.tensor_reduce(out=km2[32:33, g:g + 1], in_=pr[32:33, :], axis=AX.X,
                                                            op=ALU.max), r=[pr], w=[km2])
                    op("dve", lambda e: e.tensor_scalar(out=kmT[64:128, :], in0=kmT[64:128, :], scalar1=1.0 / 256,
                                                        scalar2=None, op0=ALU.mult), r=[kmT], w=[kmT])
                    op("dve", lambda e: e.tensor_reduce(out=km2[32:33, NG:NG + 1], in_=km2[32:33, 0:NG], axis=AX.X,
                                                        op=ALU.max), r=[km2], w=[km2])
                    for g in range(NG):
                        gs = slice(g * 512, (g + 1) * 512)
                        pb = PS[g % 2]
                        col = 1536 + h * 64
                        for c in range(8):
                            mm(pb[0:64, :], Wmb[:, c, col:col + 64], hT[:, c, gs], start=(c == 0), stop=(c == 7),
                               r=[(Wmb, c), (hT, g)], w=[pb], inc=(c == 7))
                        op("act", lambda e: e.activation(out=zsm[:, gs], in_=pb[0:64, :], func=AF.Silu), r=[pb],
                           w=[(zsm, g)])
                    for g in range(NG):
                        gs = slice(g * 512, (g + 1) * 512)
                        pb = PS[g % 2]
                        col = h * 64
                        for c in range(8):
                            mm(pb[64:128, :], Wmb[:, c, col:col + 64], hT[:, c, gs], start=(c == 0), stop=(c == 7),
                               r=[(Wmb, c), (hT, g)], w=[pb], inc=(c == 7))
                        qft = qf[g % 2]
                        op("act", lambda e: e.activation(out=qft[64:128, :], in_=pb[64:128, :], func=AF.Identity,
                                                         scale=0.125), r=[pb], w=[qft])
                        op("dve", lambda e: e.tensor_scalar(out=QaT[64:128, gs], in0=pb[64:128, :], scalar1=0.125,
                                                            scalar2=None, op0=ALU.mult), r=[pb], w=[(QaT, g)])
                        sqt = sqm[g % 2]
                        op("act", lambda e: e.activation(out=sqt[64:128, :], in_=qft[64:128, :], func=AF.Square),
                           r=[qft], w=[sqt])
                        pr = PS[2 + g % 2]
                        mm(pr[32:33, :], ones[64:128, 0:1], sqt[64:128, :], r=[C, sqt], w=[pr])
                        op("act", lambda e: e.activation(out=sqt[32:33, :], in_=pr[32:33, :], func=AF.Sqrt,
                                                         scale=km2[32:33, NG:NG + 1]), r=[pr, km2], w=[sqt])
                        op("dve", lambda e: e.tensor_scalar(out=QaT[32:33, gs], in0=sqt[32:33, :], scalar1=-1.0,
                                                            scalar2=None, op0=ALU.mult), r=[sqt], w=[(QaT, g)])
                        pgt = PS[4]
                        for tt in range(4):
                            mm(pgt[:, tt * 16:(tt + 1) * 16], qft[64:128, tt * 128:(tt + 1) * 128], kmT[64:128, :],
                               r=[qft, kmT], w=[pgt], inc=(tt == 3))
                        op("dve", lambda e: e.tensor_tensor(out=gm4[:].rearrange("p a b -> p (a b)"), in0=pgt[:, 0:64],
                                                            in1=C[:, C_PB4 + g * 64:C_PB4 + (g + 1) * 64], op=ALU.add),
                           r=[pgt, C], w=[gm4])
                        for tt in range(4):
                            op("dve", lambda e: e.max(out=top84[:, tt, :], in_=gm4[:, tt, :]), r=[gm4],
                               w=[(top84, tt)])
                        for tt in range(4):
                            op("dve", lambda e: e.tensor_scalar(out=mbt4[:, tt, :], in0=gm4[:, tt, :],
                                                                scalar1=top84[:, tt, 2:3], scalar2=NEG,
                                                                op0=ALU.is_lt, op1=ALU.mult),
                               r=[gm4, (top84, tt)], w=[(mbt4, tt)])
                        for t2 in range(2):
                            own = 2 * g + t2
                            op("dve", lambda e: e.memset(mbt4[:, 2 * t2:2 * t2 + 2, own:own + 1], 0.0),
                               r=[(mbt4, 2 * t2), (mbt4, 2 * t2 + 1)], w=[(mbt4, 2 * t2), (mbt4, 2 * t2 + 1)])
                        pt_ = PS[3]
                        for tt in range(4):
                            tr(pt_[0:16, tt * 128:(tt + 1) * 128], mbt4[:, tt, :], ident, r=[(mbt4, tt), C], w=[pt_])
                        op("act", lambda e: e.activation(out=QaT[0:16, gs], in_=pt_[0:16, 0:512], func=AF.Copy),
                           r=[pt_], w=[(QaT, g)])
                    for qc in range(NB):
                        q0 = qc * 256
                        pO = PS[6 + qc % 2]
                        qk_ = (QaT, qc // 2)

                        def emitS(p):
                            pS = PS[nS[0] % 3]
                            nS[0] += 1
                            if p < qc:
                                for j in range(2):
                                    kt = 2 * p + j
                                    mm(pS[:, j * 256:(j + 1) * 256], KaT[:, kt * 128:(kt + 1) * 128],
                                       QaT[:, q0:q0 + 256], r=[(KaT, kt // 4), qk_], w=[pS], inc=(j == 1))
                            else:
                                kt = 2 * qc
                                mm(pS[:, 0:256], KaT[:, kt * 128:(kt + 1) * 128], QaT[:, q0:q0 + 256], start=True,
                                   stop=False, r=[(KaT, kt // 4), qk_], w=[pS], inc=False)
                                mm(pS[:, 0:128], identb[:], trib[:], start=False, stop=True, r=[identb, trib],
                                   w=[pS], inc=False)
                                kt = 2 * qc + 1
                                mm(pS[:, 256:384], KaT[:, kt * 128:(kt + 1) * 128], QaT[:, q0 + 128:q0 + 256],
                                   start=True, stop=False, r=[(KaT, kt // 4), qk_], w=[pS], inc=False)
                                mm(pS[:, 256:384], identb[:], trib[:], start=False, stop=True, r=[identb, trib],
                                   w=[pS])
                            return pS

                        pend = emitS(0)
                        for p in range(qc + 1):
                            pS = pend
                            if p + 1 <= qc:
                                pend = emitS(p + 1)
                            ptile = PT[npt % 3]
                            npt += 1
                            wdt = 512 if p < qc else 384
                            op("act", lambda e: e.activation(out=ptile[:, 0:wdt], in_=pS[:, 0:wdt], func=AF.Exp),
                               r=[pS], w=[ptile])
                            if p < qc:
                                for j in range(2):
                                    kt = 2 * p + j
                                    mm(pO[0:65, 0:256], Vt[:, kt, h, :], ptile[:, j * 256:(j + 1) * 256],
                                       start=(kt == 0), stop=False, r=[(Vt, kt), ptile], w=[pO], inc=False)
                            else:
                                kt = 2 * qc
                                mm(pO[0:65, 0:256], Vt[:, kt, h, :], ptile[:, 0:256], start=(kt == 0), stop=False,
                                   r=[(Vt, kt), ptile], w=[pO], inc=False)
                                mm(pO[0:65, 128:256], Vt[:, kt + 1, h, :], ptile[:, 256:384], start=False, stop=True,
                                   r=[(Vt, kt + 1), ptile], w=[pO])
                            for _ in range(JUNK):
                                kb.eng["pe"].matmul(PS[5][:, :], lhsT=identb[:], rhs=KaT[:, 0:512], start=True, stop=True)
                        op("dve", lambda e: e.reciprocal(out=R(rden[64:65, :]), in_=pO[64:65, 0:256]), r=[pO],
                           w=[rden])
                        pB = PS[3]
                        mm(pB[0:64, 0:256], R(onesr[64:65, 0:64]), R(rden[64:65, :]), r=[onesr, rden], w=[pB])
                        tt1 = t1[qc % 2]
                        op("dve", lambda e: e.tensor_tensor(out=tt1[:], in0=pO[0:64, 0:256],
                                                            in1=zsm[:, q0:q0 + 256], op=ALU.mult),
                           r=[pO, (zsm, qc // 2)], w=[tt1])
                        op("dve", lambda e: e.tensor_tensor(out=ogm[:, q0:q0 + 256], in0=tt1[:], in1=pB[0:64, 0:256],
                                                            op=ALU.mult), r=[tt1, pB], w=[(ogm, qc)])
                    if "omb" in dbg and l == 0 and h == 0:
                        d = dbg_out("omb", [64, S])
                        tmp = T(st, "dbgomb", [64, S])
                        op("dve", lambda e: e.tensor_copy(out=tmp[:], in_=ogm[:]), r=[ogm], w=[tmp])
                        dma(d[:, :], tmp[:], r=[tmp])
                    dma(ogmb_d[h, :, :], ogm[:], r=[ogm], w=[kogmb])
                kb.barrier()

            for st in phase("fin"):
                Wpdn = T(st, "Wpdn", [128, 4, 1024], BF16)
                Wpmb = T(st, "Wpmb", [64, 8, 1024], BF16)
                Wout = T(st, "Wout", [128, 8, 1024], BF16)
                Wmg = T(st, "Wmg", [128, 8, 2048], BF16)
                dma(Wpdn[:], wpdn_d[l], w=[Wpdn], q="pool")
                dma(Wpmb[:], wpmb_d[l], w=[Wpmb], q="pool")
                for c in range(8):
                    dma(Wout[:, c, :], wout_d[l, :, c, :], w=[(Wout, c)], q="pool")
                for c in range(8):
                    dma(Wmg[:, c, :], wmg_d[l, :, c, :], w=[(Wmg, c)], q="pool")
                OGD = [T(st, "OGD%d" % i, [128, 4, 512], BF16) for i in range(2)]
                OGM = [T(st, "OGM0", [64, 8, 512], BF16)] * 2
                mixT = T(st, "mixT", [128, 8, 512], BF16)
                gd = [T(st, "gd%d" % i, [128, 512]) for i in range(2)]
                gmm = [T(st, "gmm%d" % i, [128, 512]) for i in range(2)]
                u1 = [T(st, "u1_%d" % i, [128, 512]) for i in range(2)]
                u2 = [T(st, "u2_%d" % i, [128, 512]) for i in range(2)]
                xr = [T(st, "xr%d" % i, [128, 1024]) for i in range(2)]
                res = [T(st, "res%d" % i, [128, 512]) for i in range(2)]
                junk2 = T(st, "junk2", [128, 512], BF16)
                GPl = T(st, "GPl", [128, 1024])
                dma(GPl[:], gp_d[l], r=[(kgp, l)], w=[GPl])
                ss2 = T(st, "ss2", [128, NT, 2])
                rs2 = T(st, "rs2", [128, NT])
                for g in range(NG):
                    gs = slice(g * 512, (g + 1) * 512)
                    ogd, ogmm = OGD[g % 2], OGM[g % 2]
                    dma(ogd[:], ogdn_d[:, :, gs].rearrange("h p s -> p h s"), r=[(kogdn, g)], w=[ogd])
                    dma(ogmm[:], ogmb_d[:, :, gs].rearrange("h p s -> p h s"), r=[kogmb], w=[ogmm])
                    for d_ in range(8):
                        ds_ = slice(d_ * 128, (d_ + 1) * 128)
                        pa, pb, pc, pd = PS[0 + 4 * (d_ % 2)], PS[1 + 4 * (d_ % 2)], PS[2 + 4 * (d_ % 2)], PS[3 + 4 * (d_ % 2)]
                        for h in range(4):
                            mm(pa[:, :], Wpdn[:, h, ds_], ogd[:, h, :], start=(h == 0), stop=(h == 3),
                               r=[Wpdn, ogd], w=[pa], inc=(h == 3))
                        for h in range(8):
                            mm(pb[:, :], Wpmb[:, h, ds_], ogmm[:, h, :], start=(h == 0), stop=(h == 7),
                               r=[Wpmb, ogmm], w=[pb], inc=(h == 7))
                        for c in range(8):
                            mm(pc[:, :], Wmg[:, c, ds_], hT[:, c, gs], start=(c == 0), stop=(c == 7),
                               r=[(Wmg, c), (hT, g)], w=[pc], inc=(c == 7))
                        for c in range(8):
                            mm(pd[:, :], Wmg[:, c, 1024 + d_ * 128:1024 + (d_ + 1) * 128], hT[:, c, gs],
                               start=(c == 0), stop=(c == 7), r=[(Wmg, c), (hT, g)], w=[pd], inc=(c == 7))
                        gdt, gmt, u1t, u2t = gd[d_ % 2], gmm[d_ % 2], u1[d_ % 2], u2[d_ % 2]
                        op("act", lambda e: e.activation(out=gdt[:], in_=pc[:, :], func=AF.Sigmoid), r=[pc], w=[gdt])
                        op("act", lambda e: e.activation(out=gmt[:], in_=pd[:, :], func=AF.Sigmoid), r=[pd], w=[gmt])
                        op("dve", lambda e: e.tensor_tensor(out=u1t[:], in0=pa[:, :], in1=gdt[:], op=ALU.mult),
                           r=[pa, gdt], w=[u1t])
                        op("dve", lambda e: e.tensor_tensor(out=u2t[:], in0=pb[:, :], in1=gmt[:], op=ALU.mult),
                           r=[pb, gmt], w=[u2t])
                        op("dve", lambda e: e.tensor_tensor(out=mixT[:, d_, :], in0=u1t[:], in1=u2t[:], op=ALU.add),
                           r=[u1t, u2t], w=[(mixT, d_)])
                    for tt in range(4):
                        t = g * 4 + tt
                        xrt = xr[t % 2]
                        dma(xrt[:], xin_d[t * 128:(t + 1) * 128, :], r=[(xin_k, t)], w=[xrt])
                        for hf in range(2):
                            pb = PS[(t % 2) * 2 + hf]
                            for d_ in range(8):
                                mm(pb[:, :], mixT[:, d_, tt * 128:(tt + 1) * 128], Wout[:, d_, hf * 512:(hf + 1) * 512],
                                   start=(d_ == 0), stop=(d_ == 7), r=[(mixT, d_), (Wout, d_)], w=[pb],
                                   inc=(d_ == 7))
                            op("act", lambda e: e.activation(out=junk2[:], in_=pb[:, :], func=AF.Square,
                                                             accum_out=ss2[:, t, hf:hf + 1]), r=[pb],
                               w=[junk2, (ss2, t)])
                        op("dve", lambda e: e.tensor_tensor(out=rs2[:, t:t + 1], in0=ss2[:, t, 0:1],
                                                            in1=ss2[:, t, 1:2], op=ALU.add), r=[(ss2, t)],
                           w=[(rs2, t)])
                        op("act", lambda e: e.activation(out=rs2[:, t:t + 1], in_=rs2[:, t:t + 1], func=AF.Sqrt,
                                                         scale=1.0 / D_MODEL, bias=epsT[:]), r=[(rs2, t), epsT],
                           w=[(rs2, t)])
                        op("dve", lambda e: e.reciprocal(out=rs2[:, t:t + 1], in_=rs2[:, t:t + 1]), r=[(rs2, t)],
                           w=[(rs2, t)])
                        for hf in range(2):
                            pb = PS[(t % 2) * 2 + hf]
                            hs = slice(hf * 512, (hf + 1) * 512)
                            rt = res[hf]
                            op("dve", lambda e: e.scalar_tensor_tensor(out=rt[:], in0=pb[:, :],
                                                                       scalar=rs2[:, t:t + 1], in1=GPl[:, hs],
                                                                       op0=ALU.mult, op1=ALU.mult),
                               r=[pb, (rs2, t), GPl], w=[rt])
                            op("dve", lambda e: e.tensor_tensor(out=xrt[:, hs], in0=rt[:], in1=xrt[:, hs],
                                                                op=ALU.add), r=[rt, (xrt, hf)], w=[(xrt, hf)])
                        dma(xout_d[t * 128:(t + 1) * 128, :], xrt[:], r=[xrt], w=[(xout_k, t)])
                kb.barrier()
          except _Stop:
            kb.barrier()
            curgen[0].close()
            break
        kb.finish()
        stuck = kb.simulate()
        print("deadlock check:", stuck if stuck else "ok")
        print("instructions:", kb.ninstr, "counts", kb.count, "dmas", kb.ndmaq)
    return nc, dbg_d


def _pc(w):
    sh = w.shape
    w = w.reshape(sh[:-2] + (8, 128, sh[-1]))
    return np.ascontiguousarray(np.swapaxes(w, -3, -2))


def host_layout(inp):
    f = lambda a: np.ascontiguousarray(np.asarray(a, dtype=np.float32))
    w_in = f(inp["w_in"])
    depth = w_in.shape[0]
    shared = {
        "wada": _pc(f(inp["w_ada"])),
        "bada": f(inp["b_ada"]).reshape(depth, 1, 3072),
        "gpre": f(inp["g_pre"]).reshape(depth, 1, 1024),
        "gpost": f(inp["g_post"]).reshape(depth, 1, 1024),
        "wdn": _pc(w_in[:, :, 0:2048]),
        "wba": _pc(w_in[:, :, 2048:2056]),
        "wmb": _pc(w_in[:, :, 2056:4104]),
        "wmg": _pc(w_in[:, :, 4104:6152]),
        "convw": np.ascontiguousarray(
            f(inp["conv_w"]).transpose(0, 2, 1).reshape(depth, 12, 128, 4).transpose(0, 2, 1, 3)
        ).reshape(depth, 128, 48),
        "alog": np.ascontiguousarray(np.broadcast_to(f(inp["a_log"])[:, None, :], (depth, 128, 4))),
        "dtb": np.ascontiguousarray(np.broadcast_to(f(inp["dt_bias"])[:, None, :], (depth, 128, 4))),
        "dng": f(inp["dn_norm_g"]).reshape(depth, 128, 1),
        "wpdn": np.ascontiguousarray(f(inp["w_proj_dn"]).reshape(depth, 4, 128, 1024).transpose(0, 2, 1, 3)),
        "wpmb": np.ascontiguousarray(f(inp["w_proj_mb"]).reshape(depth, 8, 64, 1024).transpose(0, 2, 1, 3)),
        "wout": _pc(f(inp["w_out"])),
        "consts": make_consts(),
    }
    x = f(inp["x"])
    c = f(inp["c"])
    maps = []
    for b in range(x.shape[0]):
        m = dict(shared)
        m["x"] = x[b]
        m["cT"] = np.ascontiguousarray(c[b].reshape(8, 128).T)
        maps.append(m)
    return maps


_CACHE = {}


def kernel(**inputs):
    x = np.asarray(inputs["x"])
    B, S, _ = x.shape
    depth = np.asarray(inputs["w_in"]).shape[0]
    key = (S, depth)
    if key not in _CACHE:
        _CACHE[key] = build(S=S, DEPTH=depth)[0]
    nc = _CACHE[key]
    maps = host_layout(inputs)
    res = run_bass_kernel_spmd(nc, maps, core_ids=list(range(B)))
    return np.stack([np.asarray(r["y"], dtype=np.float32) for r in res.results], axis=0)
```

```python
from contextlib import ExitStack

import numpy as np
import concourse.bass as bass
import concourse.mybir as mybir
from concourse.bass_utils import run_bass_kernel_spmd

F32 = mybir.dt.float32
BF16 = mybir.dt.bfloat16
F32R = mybir.dt.float32r


def R(ap):
    return ap.bitcast(F32R)
AF = mybir.ActivationFunctionType
ALU = mybir.AluOpType
AX = mybir.AxisListType

D_MODEL = 1024
NEG = -30000.0
EPS = 1e-6


class SV:
    def __init__(self, tile, j, n=1):
        self.tile, self.sub = tile, j
        self.ap = tile[:, j * 128:(j + n) * 128]

    def __getitem__(self, idx):
        return self.ap[idx]


class KB:
    NRING = {"sp": 6, "pool": 4}

    def __init__(self, nc, stack):
        self.nc = nc
        self.eng = {"pe": nc.tensor, "act": nc.scalar, "dve": nc.vector,
                    "pool": nc.gpsimd, "sp": nc.sync}
        self.sem = {}
        for e in ("pe", "act", "dve", "pool"):
            self.sem[e] = stack.enter_context(nc.semaphore("s_" + e))
        self.ring = {q: [stack.enter_context(nc.semaphore("s_dma_%s%d" % (q, i))) for i in range(n)]
                     for q, n in self.NRING.items()}
        self.ndmaq = {q: 0 for q in self.NRING}
        self.count = {e: 0 for e in ("pe", "act", "dve", "pool")}
        self.waited = {}
        self.track = {}
        self.ninstr = 0
        self.streams = {e: [] for e in self.eng}
        self.psum_ids = set()

    def _semof(self, dep):
        if dep[0] == "e":
            return self.sem[dep[1]], ("e", dep[1])
        return self.ring[dep[1][0]][dep[1][1]], ("d", dep[1])

    def _wait(self, e, dep):
        sem, sk = self._semof(dep)
        val = dep[2]
        k = (e, sk)
        if self.waited.get(k, 0) >= val:
            return
        self.eng[e].wait_ge(sem, val)
        self.streams[e].append(("wait", sk, val))
        self.ninstr += 1
        self.waited[k] = val

    @staticmethod
    def _keys(items):
        out = []
        for it in items:
            if isinstance(it, SV):
                out.append((id(it.tile), it.sub))
            elif isinstance(it, tuple):
                out.append((id(it[0]), it[1]))
            else:
                out.append((id(it), None))
        return out

    def _conflicts(self, key):
        tid, sub = key
        d = self.track.get(tid)
        if d is None:
            return []
        if sub is None:
            return list(d.values())
        res = []
        if sub in d:
            res.append(d[sub])
        if None in d:
            res.append(d[None])
        return res

    def _entry(self, key):
        tid, sub = key
        d = self.track.setdefault(tid, {})
        if sub is None:
            ent = {"w": [], "r": []}
            for v in d.values():
                ent["w"] += v["w"]
                ent["r"] += v["r"]
            d.clear()
            d[None] = ent
            return ent
        if sub not in d:
            ent = {"w": [], "r": []}
            if None in d:
                ent["w"] = list(d[None]["w"])
                ent["r"] = list(d[None]["r"])
            d[sub] = ent
        return d[sub]

    def _deps(self, e, reads, writes):
        deps = []
        for k in reads:
            for ent in self._conflicts(k):
                for w in ent["w"]:
                    deps.append((w, "raw"))
        for k in writes:
            for ent in self._conflicts(k):
                for w in ent["w"]:
                    deps.append((w, "waw"))
                for r in ent["r"]:
                    deps.append((r, "war"))
        out = []
        for dep, kind in deps:
            if dep[0] == "e" and dep[1] == e:
                if e == "pe":
                    continue
            out.append(dep)
        return out

    @staticmethod
    def _prune(lst):
        best = {}
        for d in lst:
            kk = (d[0], d[1])
            if kk not in best or best[kk][2] < d[2]:
                best[kk] = d
        return list(best.values())

    def _record(self, me, reads, writes):
        for k in reads:
            ent = self._entry(k)
            ent["r"].append(me)
            if len(ent["r"]) > 16:
                ent["r"] = self._prune(ent["r"])
        for k in writes:
            ent = self._entry(k)
            ent["w"] = [me]
            ent["r"] = []

    def _rw(self, r, w):
        reads, writes = [], []
        for k in self._keys(r):
            if k[0] in self.psum_ids:
                writes.append((k[0], None))
            else:
                reads.append(k)
        for k in self._keys(w):
            writes.append((k[0], None) if k[0] in self.psum_ids else k)
        return reads, writes

    def op(self, e, fn, r=(), w=(), inc=True):
        reads, writes = self._rw(r, w)
        for dep in self._deps(e, reads, writes):
            self._wait(e, dep)
        ins = fn(self.eng[e])
        self.ninstr += 1
        if inc:
            self.count[e] += 1
            ins.then_inc(self.sem[e], 1)
            self.streams[e].append(("inc", ("e", e), 1))
            me = ("e", e, self.count[e])
        else:
            me = ("e", e, self.count[e] + 1)
        self._record(me, reads, writes)
        return ins

    def dma(self, out, in_, r=(), w=(), q="sp", **kw):
        e = q
        reads = self._keys(r)
        writes = self._keys(w)
        k = self.ndmaq[q]
        nr = self.NRING[q]
        slot = k % nr
        gen = k // nr
        for dep in self._deps(e, reads, writes):
            self._wait(e, dep)
        if gen > 0:
            self._wait(e, ("d", (q, slot), 16 * gen))
        ins = self.eng[e].dma_start(out=out, in_=in_, **kw)
        ins.then_inc(self.ring[q][slot], 16)
        self.streams[e].append(("inc", ("d", (q, slot)), 16))
        self.ndmaq[q] += 1
        self.ninstr += 1
        me = ("d", (q, slot), 16 * (gen + 1))
        self._record(me, reads, writes)
        return ins

    def _alldma(self):
        out = []
        for q, nr in self.NRING.items():
            n = self.ndmaq[q]
            for slot in range(nr):
                cnt = (n - 1 - slot) // nr + 1 if n > slot else 0
                if cnt > 0:
                    out.append(("d", (q, slot), 16 * cnt))
        return out

    def barrier(self):
        for e in ("pe", "act", "dve", "pool", "sp"):
            for o in ("pe", "act", "dve", "pool"):
                if o != e and self.count[o] > 0:
                    self._wait(e, ("e", o, self.count[o]))
            for dep in self._alldma():
                self._wait(e, dep)
        self.track = {}

    def finish(self):
        for dep in self._alldma():
            self._wait("sp", dep)

    def simulate(self):
        sems = {}
        pc = {e: 0 for e in self.streams}
        progress = True
        while progress:
            progress = False
            for e, st in self.streams.items():
                while pc[e] < len(st):
                    kind, sk, val = st[pc[e]]
                    if kind == "wait":
                        if sems.get(sk, 0) < val:
                            break
                    else:
                        sems[sk] = sems.get(sk, 0) + val
                    pc[e] += 1
                    progress = True
        stuck = {e: (pc[e], len(st), st[pc[e]], sems.get(st[pc[e]][1], 0)) for e, st in self.streams.items()
                 if pc[e] < len(st)}
        return stuck


C_IDENT, C_U, C_MBD, C_MOFF, C_TRI, C_ONES, C_PB = 0, 128, 256, 384, 512, 640, 768
C_PB4 = 768
NCONST = 768 + 512


def make_consts():
    c = np.zeros((128, NCONST), np.float32)
    i = np.arange(128)
    c[:, C_IDENT:C_IDENT + 128] = np.eye(128)
    c[:, C_U:C_U + 128] = (i[:, None] <= i[None, :])
    c[:, C_MBD:C_MBD + 128] = (i[:, None] < i[None, :]) & ((i[:, None] // 64) == (i[None, :] // 64))
    c[:, C_MOFF:C_MOFF + 128] = (i[:, None] < 64) & (i[None, :] >= 64)
    c[:, C_TRI:C_TRI + 128] = np.where(i[:, None] <= i[None, :], 0.0, NEG)
    c[:, C_ONES:C_ONES + 128] = 1.0
    pb = np.zeros((16, 16), np.float32)
    for own in range(16):
        pb[own, own:] = -1e30
    for g in range(8):
        for tt in range(4):
            own = (4 * g + tt) // 2
            c[:, C_PB4 + g * 64 + tt * 16:C_PB4 + g * 64 + (tt + 1) * 16] = pb[own][None, :]
    return c


def build(S=4096, DEPTH=2, LS=2, dbg=None, phases=("p1", "dn", "mb", "fin"), stop=0, JUNK=0):
    dbg = dbg or set()
    NT = S // 128
    NG = S // 512
    NB = S // 256
    nc = bass.Bass("TRN2", target_bir_lowering=False)

    def din(name, shape, dt=F32):
        return nc.dram_tensor(name, shape, dt, kind="ExternalInput").ap()

    x_d = din("x", [S, 1024])
    cT_d = din("cT", [128, 8])
    wada_d = din("wada", [DEPTH, 128, 8, 3072])
    bada_d = din("bada", [DEPTH, 1, 3072])
    gpre_d = din("gpre", [DEPTH, 1, 1024])
    gpost_d = din("gpost", [DEPTH, 1, 1024])
    wdn_d = din("wdn", [DEPTH, 128, 8, 2048])
    wba_d = din("wba", [DEPTH, 128, 8, 8])
    wmb_d = din("wmb", [DEPTH, 128, 8, 2048])
    wmg_d = din("wmg", [DEPTH, 128, 8, 2048])
    convw_d = din("convw", [DEPTH, 128, 48])
    alog_d = din("alog", [DEPTH, 128, 4])
    dtb_d = din("dtb", [DEPTH, 128, 4])
    dng_d = din("dng", [DEPTH, 128, 1])
    wpdn_d = din("wpdn", [DEPTH, 128, 4, 1024])
    wpmb_d = din("wpmb", [DEPTH, 64, 8, 1024])
    wout_d = din("wout", [DEPTH, 128, 8, 1024])
    consts_d = din("consts", [128, NCONST])
    y_d = nc.dram_tensor("y", [S, 1024], F32, kind="ExternalOutput").ap()
    xmid_d = nc.dram_tensor("xmid", [S, 1024], F32, kind="Internal").ap()
    ogdn_d = nc.dram_tensor("ogdn", [4, 128, S], BF16, kind="Internal").ap()
    ogmb_d = nc.dram_tensor("ogmb", [8, 64, S], BF16, kind="Internal").ap()
    dbg_d = {}

    class _K:
        pass
    kx, kmid, ky, kogdn, kogmb = _K(), _K(), _K(), _K(), _K()

    def dbg_out(name, shape):
        dbg_d[name] = nc.dram_tensor("dbg_" + name, shape, F32, kind="ExternalOutput").ap()
        return dbg_d[name]

    with ExitStack() as gst:
        kb = KB(nc, gst)
        op, dma = kb.op, kb.dma
        gst.enter_context(nc.allow_low_precision("float32r (1-pass PE) operands for non-critical fp32 matmuls"))

        def mm(out, lhsT, rhs, start=True, stop=True, r=(), w=(), inc=True):
            return op("pe", lambda e: e.matmul(out, lhsT=lhsT, rhs=rhs, start=start, stop=stop),
                      r=r, w=w, inc=inc)

        def tr(out, in_, ident, r=(), w=()):
            return op("pe", lambda e: e.transpose(out=out, in_=in_, identity=ident), r=r, w=w)

        uid = [0]

        def T(st, name, shape, dt=F32):
            uid[0] += 1
            return st.enter_context(nc.sbuf_tensor("sb%d_%s" % (uid[0], name), shape, dt))

        class _Stop(Exception):
            pass

        def ck(level):
            if stop == level:
                raise _Stop()

        curgen = [None]

        def phase(name):
            if name in phases:
                st_ = ExitStack()
                curgen[0] = st_
                yield st_
                st_.close()

        PS = [gst.enter_context(nc.psum_tensor("ps%d" % i, [128, 512], F32)) for i in range(8)]
        kb.psum_ids = {id(p) for p in PS}
        C = T(gst, "consts", [128, NCONST])
        dma(C[:], consts_d[:, :], w=[C])
        ident = C[:, C_IDENT:C_IDENT + 128]
        U = C[:, C_U:C_U + 128]
        Mbd = C[:, C_MBD:C_MBD + 128]
        Moff = C[:, C_MOFF:C_MOFF + 128]
        ones = C[:, C_ONES:C_ONES + 128]
        identb = T(gst, "identb", [128, 128], BF16)
        trib = T(gst, "trib", [128, 128], BF16)
        epsT = T(gst, "epsT", [128, 1])
        op("dve", lambda e: e.tensor_copy(out=identb[:], in_=ident), r=[C], w=[identb])
        op("dve", lambda e: e.tensor_copy(out=trib[:], in_=C[:, C_TRI:C_TRI + 128]), r=[C], w=[trib])
        op("dve", lambda e: e.memset(epsT[:], EPS), w=[epsT])
        onesr = T(gst, "onesr", [128, 128])
        op("dve", lambda e: e.tensor_copy(out=R(onesr[:]), in_=ones), r=[C], w=[onesr])
        AB = [T(gst, "AB%d" % l, [128, 16]) for l in range(DEPTH)]
        gp_d = nc.dram_tensor("gp_scr", [DEPTH, 128, 1024], F32, kind="Internal").ap()
        kgp = _K()

        with ExitStack() as st:
            cT = T(st, "cT", [128, 8])
            sc = T(st, "sc", [128, 8])
            dma(cT[:], cT_d[:, :], w=[cT])
            op("act", lambda e: e.activation(out=sc[:], in_=cT[:], func=AF.Silu), r=[cT], w=[sc])
            wa = [T(st, "wa%d" % i, [128, 8, 512]) for i in range(2)]
            row = T(st, "row", [1, 3072])
            bada = T(st, "bada", [1, 3072])
            gpr = T(st, "gpr", [1, 1024])
            gpo = T(st, "gpo", [1, 1024])
            arow = T(st, "arow", [1, 1024])
            gprow = T(st, "gprow", [1, 1024])
            gptmp = T(st, "gptmp", [128, 1024])
            nwa = 0
            for l in range(DEPTH):
                dma(bada[:], bada_d[l], w=[bada])
                dma(gpr[:], gpre_d[l], w=[gpr])
                dma(gpo[:], gpost_d[l], w=[gpo])
                for cg in range(6):
                    wt = wa[nwa % 2]
                    nwa += 1
                    dma(wt[:], wada_d[l, :, :, cg * 512:(cg + 1) * 512], w=[wt])
                    pb = PS[cg % 2]
                    for c in range(8):
                        mm(pb[0:1, :], sc[:, c:c + 1], wt[:, c, :], start=(c == 0), stop=(c == 7),
                           r=[sc, wt], w=[pb], inc=(c == 7))
                    op("dve", lambda e: e.tensor_tensor(out=row[0:1, cg * 512:(cg + 1) * 512], in0=pb[0:1, :],
                                                        in1=bada[0:1, cg * 512:(cg + 1) * 512], op=ALU.add),
                       r=[pb, bada], w=[(row, cg)])
                op("dve", lambda e: e.scalar_tensor_tensor(out=arow[:], in0=row[0:1, 1024:2048], scalar=1.0,
                                                           in1=gpr[:], op0=ALU.add, op1=ALU.mult),
                   r=[row, gpr], w=[arow])
                op("dve", lambda e: e.tensor_tensor(out=gprow[:], in0=row[0:1, 2048:3072], in1=gpo[:], op=ALU.mult),
                   r=[row, gpo], w=[gprow])
                pc = PS[2]
                for c in range(8):
                    mm(pc[:, c:c + 1], arow[0:1, c * 128:(c + 1) * 128], ones[0:1, 0:1], r=[arow, C], w=[pc], inc=False)
                for c in range(8):
                    mm(pc[:, 8 + c:9 + c], row[0:1, c * 128:(c + 1) * 128], ones[0:1, 0:1], r=[row, C], w=[pc],
                       inc=(c == 7))
                op("dve", lambda e: e.tensor_copy(out=AB[l][:], in_=pc[:, 0:16]), r=[pc], w=[AB[l]])
                for hf in range(2):
                    pg = PS[3 + hf]
                    mm(pg[:, :], ones[0:1, 0:128], gprow[0:1, hf * 512:(hf + 1) * 512], r=[gprow, C], w=[pg])
                    op("act", lambda e: e.activation(out=gptmp[:, hf * 512:(hf + 1) * 512], in_=pg[:, :], func=AF.Copy),
                       r=[pg], w=[(gptmp, hf)])
                dma(gp_d[l], gptmp[:], r=[gptmp], w=[(kgp, l)])
            kb.barrier()

        hT = T(gst, "hT", [128, 8, S], BF16)

        for l in range(DEPTH):
          try:
            xin_d = x_d if l == 0 else xmid_d
            xout_d = y_d if l == DEPTH - 1 else xmid_d
            xin_k = kx if l == 0 else kmid
            xout_k = ky if l == DEPTH - 1 else kmid

            for st in phase("p1"):
                xt = [T(st, "xt%d" % i, [128, 1024]) for i in range(4)]
                xn = [T(st, "xn%d" % i, [128, 1024]) for i in range(3)]
                junk = T(st, "junk", [128, 1024], BF16)
                ss = T(st, "ss", [128, NT])
                rstd = T(st, "rstd", [128, NT])
                for t in range(NT):
                    xtt, xnt = xt[t % 4], xn[t % 3]
                    dma(xtt[:], xin_d[t * 128:(t + 1) * 128, :], r=[(xin_k, t)], w=[xtt])
                    op("act", lambda e: e.activation(out=junk[:], in_=xtt[:], func=AF.Square,
                                                     accum_out=ss[:, t:t + 1]), r=[xtt], w=[junk, (ss, t)])
                    op("act", lambda e: e.activation(out=rstd[:, t:t + 1], in_=ss[:, t:t + 1], func=AF.Sqrt,
                                                     scale=1.0 / D_MODEL, bias=epsT[:]), r=[(ss, t), epsT],
                       w=[(rstd, t)])
                    op("dve", lambda e: e.reciprocal(out=rstd[:, t:t + 1], in_=rstd[:, t:t + 1]), r=[(rstd, t)],
                       w=[(rstd, t)])
                    op("dve", lambda e: e.tensor_scalar(out=xnt[:], in0=xtt[:], scalar1=rstd[:, t:t + 1],
                                                        scalar2=None, op0=ALU.mult), r=[xtt, (rstd, t)], w=[xnt])
                    for hf in range(2):
                        pb = PS[(t % 2) * 2 + hf]
                        for cc in range(4):
                            c = hf * 4 + cc
                            tr(pb[:, cc * 128:(cc + 1) * 128], xnt[:, c * 128:(c + 1) * 128], ident, r=[xnt, C],
                               w=[pb])
                        for cc in range(4):
                            c = hf * 4 + cc
                            eng = "act" if hf == 0 else "dve"
                            if eng == "act":
                                op("act", lambda e: e.activation(out=hT[:, c, t * 128:(t + 1) * 128],
                                                                 in_=pb[:, cc * 128:(cc + 1) * 128], func=AF.Identity,
                                                                 scale=AB[l][:, c:c + 1], bias=AB[l][:, 8 + c:9 + c]),
                                   r=[pb, AB[l]], w=[(hT, t // 4)])
                            else:
                                op("dve", lambda e: e.tensor_scalar(out=hT[:, c, t * 128:(t + 1) * 128],
                                                                    in0=pb[:, cc * 128:(cc + 1) * 128],
                                                                    scalar1=AB[l][:, c:c + 1],
                                                                    scalar2=AB[l][:, 8 + c:9 + c],
                                                                    op0=ALU.mult, op1=ALU.add),
                                   r=[pb, AB[l]], w=[(hT, t // 4)])
                kb.barrier()
            if "hT" in dbg and l == 0:
                with ExitStack() as st:
                    d = dbg_out("hT", [128, 8, S])
                    tmp = T(st, "dbgtmp", [128, 8, S])
                    op("dve", lambda e: e.tensor_copy(out=tmp[:], in_=hT[:]), r=[hT], w=[tmp])
                    dma(d[:, :, :], tmp[:], r=[tmp])
                    kb.barrier()

            for st in phase("dn"):
                Wdn = T(st, "Wdn", [128, 8, 2048], BF16)
                Wba = T(st, "Wba", [128, 8, 8], BF16)
                for c in range(8):
                    dma(Wdn[:, c, :], wdn_d[l, :, c, :], w=[(Wdn, c)], q="pool")
                dma(Wba[:], wba_d[l], w=[Wba], q="pool")
                convw = T(st, "convw", [128, 48])
                alog = T(st, "alog", [128, 4])
                dtb = T(st, "dtb", [128, 4])
                dng = T(st, "dng", [128, 1])
                dma(convw[:], convw_d[l], w=[convw])
                dma(alog[:], alog_d[l], w=[alog])
                dma(dtb[:], dtb_d[l], w=[dtb])
                dma(dng[:], dng_d[l], w=[dng])
                st2 = ExitStack()
                BETA = T(st, "BETA", [128, NT, 4])
                NBETA = T(st, "NBETA", [128, NT, 4])
                GRAW = T(st, "GRAW", [128, NT, 4])
                GC = T(st, "GC", [128, NT, 4])
                NEXPG = T(st, "NEXPG", [128, NT, 4])
                negA = T(st, "negA", [128, 4])
                BG = T(st2, "BG", [128, NT, 8])
                AA = T(st2, "AA", [128, NT, 4])
                AX_ = T(st2, "AXs", [128, NT, 4])
                pbg = PS[0]
                for t in range(NT):
                    for c in range(8):
                        mm(pbg[:, t * 8:(t + 1) * 8], hT[:, c, t * 128:(t + 1) * 128], Wba[:, c, :],
                           start=(c == 0), stop=(c == 7), r=[(hT, t // 4), Wba], w=[pbg], inc=(c == 7))
                op("dve", lambda e: e.tensor_copy(out=BG[:].rearrange("p t e -> p (t e)"), in_=pbg[:, 0:NT * 8]),
                   r=[pbg], w=[BG])
                op("act", lambda e: e.activation(out=BETA[:], in_=BG[:, :, 0:4], func=AF.Sigmoid), r=[BG], w=[BETA])
                op("dve", lambda e: e.tensor_scalar(out=NBETA[:], in0=BETA[:], scalar1=-1.0, scalar2=None,
                                                    op0=ALU.mult), r=[BETA], w=[NBETA])
                for h in range(4):
                    op("dve", lambda e: e.tensor_scalar(out=AA[:, :, h], in0=BG[:, :, 4 + h], scalar1=dtb[:, h:h + 1],
                                                        scalar2=None, op0=ALU.add), r=[BG, dtb], w=[AA])
                op("act", lambda e: e.activation(out=AX_[:], in_=AA[:], func=AF.Abs), r=[AA], w=[AX_])
                op("act", lambda e: e.activation(out=AX_[:], in_=AX_[:], func=AF.Exp, scale=-1.0), r=[AX_], w=[AX_])
                op("dve", lambda e: e.tensor_scalar(out=AX_[:], in0=AX_[:], scalar1=1.0, scalar2=None, op0=ALU.add),
                   r=[AX_], w=[AX_])
                op("act", lambda e: e.activation(out=AX_[:], in_=AX_[:], func=AF.Ln), r=[AX_], w=[AX_])
                op("dve", lambda e: e.scalar_tensor_tensor(out=AA[:], in0=AA[:], scalar=0.0, in1=AX_[:],
                                                           op0=ALU.max, op1=ALU.add), r=[AA, AX_], w=[AA])
                op("act", lambda e: e.activation(out=negA[:], in_=alog[:], func=AF.Exp), r=[alog], w=[negA])
                op("dve", lambda e: e.tensor_scalar(out=negA[:], in0=negA[:], scalar1=-1.0, scalar2=None,
                                                    op0=ALU.mult), r=[negA], w=[negA])
                for h in range(4):
                    op("dve", lambda e: e.tensor_scalar(out=GRAW[:, :, h], in0=AA[:, :, h], scalar1=negA[:, h:h + 1],
                                                        scalar2=None, op0=ALU.mult), r=[AA, negA], w=[GRAW])
                pgc = PS[1]
                mm(pgc[:, 0:NT * 4], U, GRAW[:].rearrange("p t e -> p (t e)"), r=[C, GRAW], w=[pgc])
                op("dve", lambda e: e.tensor_copy(out=GC[:].rearrange("p t e -> p (t e)"), in_=pgc[:, 0:NT * 4]),
                   r=[pgc], w=[GC])
                op("act", lambda e: e.activation(out=NEXPG[:], in_=GC[:], func=AF.Exp), r=[GC], w=[NEXPG])
                op("dve", lambda e: e.tensor_scalar(out=NEXPG[:], in0=NEXPG[:], scalar1=-1.0, scalar2=None,
                                                    op0=ALU.mult), r=[NEXPG], w=[NEXPG])
                if "graw" in dbg and l == 0:
                    d = dbg_out("graw", [128, NT, 4])
                    dma(d[:, :, :], GRAW[:], r=[GRAW])
                    d = dbg_out("beta", [128, NT, 4])
                    dma(d[:, :, :], BETA[:], r=[BETA])

                kb.barrier()
                st2.close()
                ck(1)
                pre = [T(st, "pre%d" % i, [128, 515]) for i in range(3)]
                halo = T(st, "halo", [128, 12, 3])
                op("dve", lambda e: e.memset(halo[:], 0.0), w=[halo])
                qkv = [[T(st, "qkv%d_%d" % (i, j), [128, 512]) for j in range(3)] for i in range(2)]
                zs = [T(st, "zs%d" % i, [128, 512], BF16) for i in range(4)]
                cv = [T(st, "cv0", [128, 512])] * 2
                sq = [T(st, "sq0", [128, 512])] * 2
                oTg = [T(st, "oTg%d" % i, [128, 512]) for i in range(2)]
                ogb = [T(st, "ogb0", [128, 512], BF16)] * 2
                Sst = [[T(st, "S%d_%d" % (h, i), [128, 128]) for i in range(2)] for h in range(4)]
                for h in range(4):
                    op("dve", lambda e: e.tensor_scalar(out=R(Sst[h][0][:]), in0=ident, scalar1=0.0, scalar2=None,
                                                        op0=ALU.mult), r=[C], w=[Sst[h][0]])
                spar = [0, 0, 0, 0]
                NCH = 8
                ALIAS = {"Dm": 0, "tq": 0, "DecT": 1, "Xo": 1, "Pe": 2, "ktok": 3, "Xe": 3, "NoffT": 3, "Po": 4,
                         "B": 5, "BT": 6, "WbdT": 6, "Noff": 7, "Z1": 7, "ExpG": 8, "XTo": 8, "Lm": 9, "XTe": 9}
                scr = []
                for i in range(NCH):
                    wide = T(st, "s%d" % i, [128, 9 * 128])
                    plain = T(st, "s%da" % i, [128, 128])
                    d_ = {n: (SV(wide, j - 1) if j > 0 else plain) for n, j in ALIAS.items()}
                    d_["XPo"] = SV(wide, 0, 2)
                    d_["XPe"] = SV(wide, 2, 2)
                    scr.append(d_)
                OUTN = ["W", "QKdT", "kdec", "qg", "vtok", "kTc"]
                outs = [{n: T(st, "o%d_%s" % (i, n), [128, 128]) for n in OUTN} for i in range(NCH)]
                EGL = T(st, "EGL", [128, NCH])
                Yr = [T(st, "Yr%d" % i, [128, 128]) for i in range(2)]
                Vn = [T(st, "Vn%d" % i, [128, 128]) for i in range(2)]

                def prep8(chains, pump=lambda: None):
                    def each(fn):
                        for ch in chains:
                            fn(ch, ch["s"], ch["o"], PS[ch["i"]])

                    def f(ch, s, o, pb):
                        h, n, sl = ch["h"], ch["n"], ch["sl"]
                        kT, vT = ch["kT"], ch["vT"]
                        tr(pb[:, 0:128], kT[:, sl], ident, r=[kT, C], w=[pb])
                        tr(pb[:, 128:256], vT[:, sl], ident, r=[vT, C], w=[pb])
                        mm(pb[:, 256:384], GRAW[:, n, h:h + 1].to_broadcast([128, 128]), U, r=[GRAW, C], w=[pb])
                        op("act", lambda e: e.activation(out=R(s["ktok"][:]), in_=pb[:, 0:128], func=AF.Copy),
                           r=[pb], w=[s["ktok"]])
                        op("dve", lambda e: e.tensor_copy(out=o["vtok"][:], in_=pb[:, 128:256]), r=[pb],
                           w=[o["vtok"]])
                        op("dve", lambda e: e.tensor_scalar(out=s["Dm"][:], in0=pb[:, 256:384],
                                                            scalar1=GC[:, n, h:h + 1], scalar2=0.0,
                                                            op0=ALU.subtract, op1=ALU.min),
                           r=[pb, GC], w=[s["Dm"]])
                        op("act", lambda e: e.activation(out=R(s["ExpG"][:]), in_=pb[:, 256:384], func=AF.Exp),
                           r=[pb], w=[s["ExpG"]])
                    each(f)

                    def f(ch, s, o, pb):
                        op("act", lambda e: e.activation(out=R(s["DecT"][:]), in_=s["Dm"][:], func=AF.Exp),
                           r=[s["Dm"]], w=[s["DecT"]])
                    each(f)

                    def f(ch, s, o, pb):
                        h, n, sl = ch["h"], ch["n"], ch["sl"]
                        kT, qT = ch["kT"], ch["qT"]
                        mm(pb[:, 0:128], R(kT[:, sl]), R(kT[:, sl]), r=[kT], w=[pb], inc=False)
                        mm(pb[:, 128:256], R(kT[:, sl]), R(qT[:, sl]), r=[kT, qT], w=[pb])
                        op("dve", lambda e: e.tensor_tensor(out=R(s["Lm"][:]), in0=pb[:, 0:128], in1=s["DecT"][:],
                                                            op=ALU.mult), r=[pb, s["DecT"]], w=[s["Lm"]])
                        op("dve", lambda e: e.tensor_tensor(out=s["tq"][:], in0=pb[:, 128:256], in1=s["DecT"][:],
                                                            op=ALU.mult), r=[pb, s["DecT"]], w=[s["tq"]])
                        op("dve", lambda e: e.scalar_tensor_tensor(out=R(s["B"][:]), in0=s["Lm"][:],
                                                                   scalar=NBETA[:, n, h:h + 1], in1=Mbd,
                                                                   op0=ALU.mult, op1=ALU.mult),
                           r=[s["Lm"], NBETA, C], w=[s["B"]])
                        op("dve", lambda e: e.scalar_tensor_tensor(out=R(s["Noff"][:]), in0=s["Lm"][:],
                                                                   scalar=BETA[:, n, h:h + 1], in1=Moff,
                                                                   op0=ALU.mult, op1=ALU.mult),
                           r=[s["Lm"], BETA, C], w=[s["Noff"]])
                        op("pool", lambda e: e.tensor_tensor(out=R(o["QKdT"][:]), in0=s["tq"][:], in1=U,
                                                             op=ALU.mult), r=[s["tq"], C], w=[o["QKdT"]])
                        op("act", lambda e: e.activation(out=R(o["kdec"][:]), in_=s["ktok"][:], func=AF.Identity,
                                                         scale=s["DecT"][:, 127:128]), r=[s["ktok"], s["DecT"]],
                           w=[o["kdec"]])
                        op("dve", lambda e: e.tensor_tensor(out=R(o["qg"][:]), in0=qT[:, sl], in1=s["ExpG"][:],
                                                            op=ALU.mult), r=[qT, s["ExpG"]], w=[o["qg"]])
                        op("dve", lambda e: e.tensor_copy(out=R(o["kTc"][:]), in_=kT[:, sl]), r=[kT], w=[o["kTc"]])
                        op("dve", lambda e: e.tensor_copy(out=EGL[:, ch["i"]:ch["i"] + 1],
                                                          in_=s["ExpG"][:, 127:128]),
                           r=[s["ExpG"]], w=[(EGL, ch["i"])])
                    each(f)

                    def f(ch, s, o, pb):
                        tr(pb[:, 256:384], s["B"][:], ident, r=[s["B"], C], w=[pb])
                        op("act", lambda e: e.activation(out=R(s["BT"][:]), in_=pb[:, 256:384], func=AF.Copy),
                           r=[pb], w=[s["BT"]])
                        op("dve", lambda e: e.tensor_tensor(out=R(s["Pe"][:]), in0=s["B"][:], in1=ident,
                                                            op=ALU.add), r=[s["B"], C], w=[s["Pe"]])
                    each(f)

                    def f(ch, s, o, pb):
                        mm(pb[:, 0:128], R(s["BT"][:]), R(s["B"][:]), r=[s["BT"], s["B"]], w=[pb])
                        op("act", lambda e: e.activation(out=R(s["Xo"][:]), in_=pb[:, 0:128], func=AF.Copy),
                           r=[pb], w=[s["Xo"]])
                    each(f)
                    pump()
                    for j in range(1, 6):
                        odd = (j % 2 == 1)
                        Xc, Pc, XTc, XPc = ("Xo", "Pe", "XTo", "XPo") if odd else ("Xe", "Po", "XTe", "XPe")
                        Xn, Pn = ("Xe", "Po") if odd else ("Xo", "Pe")

                        def f(ch, s, o, pb):
                            tr(pb[:, 256:384], s[Xc][:], ident, r=[s[Xc], C], w=[pb])
                            op("act", lambda e: e.activation(out=R(s[XTc][:]), in_=pb[:, 256:384], func=AF.Copy),
                               r=[pb], w=[s[XTc]])
                        each(f)
                        pump()

                        def f(ch, s, o, pb):
                            if j < 5:
                                mm(pb[:, 0:256], R(s[XTc][:]), R(s[XPc][:]), r=[s[XTc], s[Xc], s[Pc]], w=[pb])
                                op("act", lambda e: e.activation(out=R(s[Xn][:]), in_=pb[:, 0:128], func=AF.Copy),
                                   r=[pb], w=[s[Xn]])
                                op("dve", lambda e: e.tensor_tensor(out=R(s[Pn][:]), in0=pb[:, 128:256],
                                                                    in1=s[Pc][:], op=ALU.add),
                                   r=[pb, s[Pc]], w=[s[Pn]])
                            else:
                                mm(pb[:, 128:256], R(s[XTc][:]), R(s[Pc][:]), r=[s[XTc], s[Pc]], w=[pb])
                                op("dve", lambda e: e.tensor_tensor(out=R(s[Pn][:]), in0=pb[:, 128:256],
                                                                    in1=s[Pc][:], op=ALU.add),
                                   r=[pb, s[Pc]], w=[s[Pn]])
                        each(f)
                        pump()
                    Pf = "Po"

                    def f(ch, s, o, pb):
                        tr(pb[:, 0:128], s[Pf][:], ident, r=[s[Pf], C], w=[pb])
                        tr(pb[:, 128:256], s["Noff"][:], ident, r=[s["Noff"], C], w=[pb])
                        op("act", lambda e: e.activation(out=R(s["WbdT"][:]), in_=pb[:, 0:128], func=AF.Copy),
                           r=[pb], w=[s["WbdT"]])
                        op("dve", lambda e: e.tensor_copy(out=R(s["NoffT"][:]), in_=pb[:, 128:256]), r=[pb],
                           w=[s["NoffT"]])
                    each(f)
                    pump()

                    def f(ch, s, o, pb):
                        mm(pb[:, 0:128], R(s["NoffT"][:]), R(s[Pf][:]), r=[s["NoffT"], s[Pf]], w=[pb])
                        op("act", lambda e: e.activation(out=R(s["Z1"][:]), in_=pb[:, 0:128], func=AF.Copy),
                           r=[pb], w=[s["Z1"]])
                    each(f)
                    pump()

                    def f(ch, s, o, pb):
                        mm(pb[:, 128:256], R(s["WbdT"][:]), R(s["Z1"][:]), r=[s["WbdT"], s["Z1"]], w=[pb])
                        op("dve", lambda e: e.tensor_tensor(out=R(o["W"][:]), in0=s[Pf][:], in1=pb[:, 128:256],
                                                            op=ALU.subtract), r=[s[Pf], pb], w=[o["W"]])
                    each(f)
                    pump()

                nrec = [0]

                def recur_pair(chs, g):
                    st_ = []
                    for ch in chs:
                        h = ch["h"]
                        So = Sst[h][spar[h]]
                        Sn = Sst[h][1 - spar[h]]
                        spar[h] = 1 - spar[h]
                        bb = 4 * ch["hi"]
                        st_.append((ch, So, Sn, PS[bb], PS[bb + 1], PS[bb + 2], PS[bb + 3],
                                    Yr[nrec[0] % 2], Vn[nrec[0] % 2]))
                        nrec[0] += 1
                    for ch, So, Sn, pa, pb2, pc, pd, Y, vnew in st_:
                        h, n, o = ch["h"], ch["n"], ch["o"]
                        mm(pa[:, 0:128], R(o["kTc"][:]), R(So[:]), r=[o["kTc"], So], w=[pa])
                        op("dve", lambda e: e.scalar_tensor_tensor(out=R(Y[:]), in0=pa[:, 0:128],
                                                                   scalar=NEXPG[:, n, h:h + 1], in1=o["vtok"][:],
                                                                   op0=ALU.mult, op1=ALU.add),
                           r=[pa, NEXPG, o["vtok"]], w=[Y])
                    for ch, So, Sn, pa, pb2, pc, pd, Y, vnew in st_:
                        h, n, o = ch["h"], ch["n"], ch["o"]
                        mm(pb2[:, 0:128], R(o["W"][:]), R(Y[:]), r=[o["W"], Y], w=[pb2])
                        op("act", lambda e: e.activation(out=R(vnew[:]), in_=pb2[:, 0:128], func=AF.Identity,
                                                         scale=BETA[:, n, h:h + 1]), r=[pb2, BETA], w=[vnew])
                    for ch, So, Sn, pa, pb2, pc, pd, Y, vnew in st_:
                        o = ch["o"]
                        mm(pd[:, 0:128], R(o["kdec"][:]), R(vnew[:]), r=[o["kdec"], vnew], w=[pd])
                        op("dve", lambda e: e.scalar_tensor_tensor(out=R(Sn[:]), in0=So[:],
                                                                   scalar=EGL[:, ch["i"]:ch["i"] + 1],
                                                                   in1=pd[:, 0:128], op0=ALU.mult, op1=ALU.add),
                           r=[So, (EGL, ch["i"]), pd], w=[Sn])
                    for ch, So, Sn, pa, pb2, pc, pd, Y, vnew in st_:
                        o, sl = ch["o"], ch["sl"]
                        mm(pc[:, 0:128], R(So[:]), R(o["qg"][:]), start=True, stop=False, r=[So, o["qg"]], w=[pc],
                           inc=False)
                        mm(pc[:, 0:128], R(vnew[:]), R(o["QKdT"][:]), start=False, stop=True, r=[vnew, o["QKdT"]],
                           w=[pc])
                        ot = oTg[ch["hi"]]
                        op("act", lambda e: e.activation(out=ot[:, sl], in_=pc[:, 0:128], func=AF.Copy), r=[pc],
                           w=[(ot, ch["cc"])])

                def stageA(g, hp, par, res):
                    gs = slice(g * 512, (g + 1) * 512)
                    chains = []
                    for hi in range(2):
                        h = 2 * hp + hi
                        qk = qkv[hi]
                        zt = zs[2 * par + hi]
                        for ty in range(4):
                            pb = PS[4 * hi + ty]
                            col = ty * 512 + h * 128
                            for c in range(8):
                                mm(pb[:, :], Wdn[:, c, col:col + 128], hT[:, c, gs], start=(c == 0),
                                   stop=(c == 7), r=[(Wdn, c), (hT, g)], w=[pb], inc=(c == 7))
                            if ty < 3:
                                ch_ = ty * 4 + h
                                op("dve", lambda e: e.tensor_copy(out=pre[ty][:, 0:3], in_=halo[:, ch_, :]),
                                   r=[(halo, ch_)], w=[(pre[ty], 0)])
                                op("act", lambda e: e.activation(out=pre[ty][:, 3:515], in_=pb[:, :],
                                                                 func=AF.Copy), r=[pb], w=[(pre[ty], 1)])
                                op("dve", lambda e: e.tensor_copy(out=halo[:, ch_, :], in_=pre[ty][:, 512:515]),
                                   r=[(pre[ty], 1)], w=[(halo, ch_)])
                                yield
                                cvt = cv[ty % 2]
                                wk = lambda k: convw[:, ch_ * 4 + k:ch_ * 4 + k + 1]
                                op("act", lambda e: e.activation(out=cvt[:], in_=pre[ty][:, 0:512],
                                                                 func=AF.Identity, scale=wk(0)),
                                   r=[pre[ty], convw], w=[cvt])
                                for k in range(1, 4):
                                    op("dve", lambda e: e.scalar_tensor_tensor(out=cvt[:],
                                                                               in0=pre[ty][:, k:k + 512],
                                                                               scalar=wk(k), in1=cvt[:],
                                                                               op0=ALU.mult, op1=ALU.add),
                                       r=[pre[ty], convw, cvt], w=[cvt])
                                    yield
                                op("act", lambda e: e.activation(out=(R(qk[ty][:]) if ty < 2 else qk[ty][:]),
                                                                 in_=cvt[:], func=AF.Silu),
                                   r=[cvt], w=[qk[ty]])
                            else:
                                op("act", lambda e: e.activation(out=zt[:], in_=pb[:, :], func=AF.Silu),
                                   r=[pb], w=[zt])
                            yield
                        for ty in range(2):
                            sqt = sq[ty]
                            pb = PS[4 * hi + ty]
                            op("act", lambda e: e.activation(out=R(sqt[:]), in_=qk[ty][:], func=AF.Square),
                               r=[qk[ty]], w=[sqt])
                            mm(pb[:, :], R(onesr[:]), R(sqt[:]), r=[onesr, sqt], w=[pb])
                            rt_ = cv[0]
                            op("act", lambda e: e.activation(out=rt_[:], in_=pb[:, :], func=AF.Ln,
                                                             bias=epsT[:]), r=[pb, epsT], w=[rt_])
                            op("act", lambda e: e.activation(out=rt_[:], in_=rt_[:], func=AF.Exp, scale=-0.5),
                               r=[rt_], w=[rt_])
                            sc_ = (128.0 ** -0.5) if ty == 0 else 1.0
                            op("dve", lambda e: e.scalar_tensor_tensor(out=R(qk[ty][:]), in0=qk[ty][:],
                                                                       scalar=sc_, in1=rt_[:], op0=ALU.mult,
                                                                       op1=ALU.mult),
                               r=[qk[ty], rt_], w=[qk[ty]])
                            yield
                        if dbg and l == 0 and g == 0 and h == 0:
                            for nm, tt in (("q", qk[0]), ("k", qk[1]), ("v", qk[2])):
                                if nm in dbg:
                                    d = dbg_out(nm, [128, 512])
                                    dma(d[:, :], tt[:], r=[tt])
                        for cc in range(4):
                            i = hi * 4 + cc
                            chains.append({"h": h, "hi": hi, "cc": cc, "n": g * 4 + cc, "i": i,
                                           "sl": slice(cc * 128, (cc + 1) * 128), "qT": qk[0], "kT": qk[1],
                                           "vT": qk[2], "s": scr[i], "o": outs[i]})
                    res["chains"] = chains

                def drain(gen):
                    if gen is not None:
                        for _ in gen:
                            pass

                pairs = [(g, hp) for g in range(NG) for hp in range(2)]
                resA = [dict() for _ in pairs]
                gens = [stageA(g, hp, pi % 2, resA[pi]) for pi, (g, hp) in enumerate(pairs)]
                drain(gens[0])
                for pi, (g, hp) in enumerate(pairs):
                    gs = slice(g * 512, (g + 1) * 512)
                    chains = resA[pi]["chains"]
                    nxt = gens[pi + 1] if pi + 1 < len(pairs) else None

                    def pump(n=5):
                        if nxt is not None:
                            for _ in range(n):
                                if next(nxt, "done") == "done":
                                    break
                    prep8(chains, pump)
                    drain(nxt)
                    for cc in range(4):
                        recur_pair([ch for ch in chains if ch["cc"] == cc], g)
                    for hi in range(2):
                        h = 2 * hp + hi
                        zt = zs[2 * (pi % 2) + hi]
                        ot = oTg[hi]
                        og = ogb[hi]
                        sqt = sq[hi]
                        pb = PS[4 * hi]
                        op("act", lambda e: e.activation(out=R(sqt[:]), in_=ot[:], func=AF.Square), r=[ot],
                           w=[sqt])
                        mm(pb[:, :], R(onesr[:]), R(sqt[:]), r=[onesr, sqt], w=[pb])
                        sqt = cv[0]
                        op("act", lambda e: e.activation(out=sqt[:], in_=pb[:, :], func=AF.Ln,
                                                         scale=1.0 / 128, bias=epsT[:]), r=[pb, epsT], w=[sqt])
                        op("act", lambda e: e.activation(out=sqt[:], in_=sqt[:], func=AF.Exp, scale=-0.5),
                           r=[sqt], w=[sqt])
                        if "odn" in dbg and l == 0 and h == 0 and g == 0:
                            d = dbg_out("odn", [128, 512])
                            dma(d[:, :], ot[:], r=[ot])
                        op("dve", lambda e: e.scalar_tensor_tensor(out=sqt[:], in0=ot[:], scalar=dng[:, 0:1],
                                                                   in1=sqt[:], op0=ALU.mult, op1=ALU.mult),
                           r=[ot, dng, sqt], w=[sqt])
                        op("dve", lambda e: e.tensor_tensor(out=og[:], in0=sqt[:], in1=zt[:], op=ALU.mult),
                           r=[sqt, zt], w=[og])
                        dma(ogdn_d[h, :, gs], og[:], r=[og], w=[(kogdn, g)])
                kb.barrier()

            for st in phase("mb"):
                Wmb = T(st, "Wmb", [128, 8, 2048], BF16)
                for c in range(8):
                    dma(Wmb[:, c, :], wmb_d[l, :, c, :], w=[(Wmb, c)], q="pool")
                Vt = T(st, "Vt", [128, NT, 8, 65], BF16)
                op("dve", lambda e: e.memset(Vt[:, :, :, 64:65], 1.0), w=[Vt])
                for t in range(NT):
                    pb = PS[t % 2]
                    for c in range(8):
                        mm(pb[:, :], hT[:, c, t * 128:(t + 1) * 128], Wmb[:, c, 1024:1536], start=(c == 0),
                           stop=(c == 7), r=[(hT, t // 4), (Wmb, c)], w=[pb], inc=(c == 7))
                    op("act" if t % 2 else "dve",
                       lambda e: (e.activation(out=Vt[:, t, :, 0:64], in_=pb[:, :].rearrange("p (h d) -> p h d", h=8),
                                               func=AF.Copy) if t % 2 else
                                  e.tensor_copy(out=Vt[:, t, :, 0:64],
                                                in_=pb[:, :].rearrange("p (h d) -> p h d", h=8))),
                       r=[pb], w=[(Vt, t)])
                KaT = T(st, "KaT", [128, S], BF16)
                QaT = T(st, "QaT", [128, S], BF16)
                zsm = T(st, "zsm", [64, S], BF16)
                ogm = T(st, "ogm", [64, S], BF16)
                kf = [T(st, "kf%d" % i, [128, 512]) for i in range(2)]
                qf = [T(st, "qf%d" % i, [128, 512]) for i in range(2)]
                sqm = [T(st, "sqm%d" % i, [128, 512]) for i in range(2)]
                kmT = T(st, "kmT", [128, 16])
                km2 = T(st, "km2", [64, NG + 1])
                gm4 = T(st, "gm4", [128, 4, 16])
                top84 = T(st, "top84", [128, 4, 8])
                mbt4 = [T(st, "mbt4_%d" % i, [128, 4, 16]) for i in range(2)]
                rden = T(st, "rden", [128, 256])
                t1 = [T(st, "t1_%d" % i, [64, 256]) for i in range(2)]
                PT = [T(st, "PT%d" % i, [128, 512], BF16) for i in range(4)]
                nS = [0]
                op("dve", lambda e: e.memset(KaT[0:64, :], 0.0), w=[KaT])
                op("dve", lambda e: e.memset(QaT[0:64, :], 0.0), w=[QaT])
                op("dve", lambda e: e.memset(KaT[32:33, :], 1.0), w=[KaT])
                for n in range(NB):
                    op("dve", lambda e: e.tensor_copy(out=KaT[0:16, n * 256:(n + 1) * 256],
                                                      in_=ident[0:16, n:n + 1].to_broadcast([16, 256])),
                       r=[C], w=[KaT])
                npt = 0
                for h in range(8):
                    op("dve", lambda e: e.memset(kmT[:], 0.0), w=[kmT])
                    def projK(g):
                        pb = PS[g % 2]
                        col = 512 + h * 64
                        for c in range(8):
                            mm(pb[64:128, :], Wmb[:, c, col:col + 64], hT[:, c, g * 512:(g + 1) * 512],
                               start=(c == 0), stop=(c == 7), r=[(Wmb, c), (hT, g)], w=[pb], inc=(c == 7))

                    def postK(g):
                        gs = slice(g * 512, (g + 1) * 512)
                        pb = PS[g % 2]
                        kft = kf[g % 2]
                        op("act", lambda e: e.activation(out=kft[64:128, :], in_=pb[64:128, :], func=AF.Copy),
                           r=[pb], w=[kft])
                        op("act", lambda e: e.activation(out=KaT[64:128, gs], in_=pb[64:128, :], func=AF.Copy),
                           r=[pb], w=[(KaT, g)])
                        op("dve", lambda e: e.tensor_reduce(out=kmT[64:128, 2 * g:2 * g + 2],
                                                            in_=kft[64:128, :].rearrange("p (b t) -> p b t", b=2),
                                                            axis=AX.X, op=ALU.add), r=[kft], w=[kmT])
                        sqt = sqm[g % 2]
                        op("dve", lambda e: e.tensor_tensor(out=sqt[64:128, :], in0=kft[64:128, :],
                                                            in1=kft[64:128, :], op=ALU.mult), r=[kft], w=[sqt])
                        pr = PS[2]
                        mm(pr[32:33, :], ones[64:128, 0:1], sqt[64:128, :], r=[C, sqt], w=[pr])
                        op("dve", lambda e: e.tensor_reduce(out=km2[32:33, g:g + 1], in_=pr[32:33, :], axis=AX.X,
                                                            op=ALU.max), r=[pr], w=[km2])
                    projK(0)
                    for g in range(NG):
                        if g + 1 < NG:
                            projK(g + 1)
                        postK(g)
                    op("dve", lambda e: e.tensor_scalar(out=kmT[64:128, :], in0=kmT[64:128, :], scalar1=1.0 / 256,
                                                        scalar2=None, op0=ALU.mult), r=[kmT], w=[kmT])
                    op("dve", lambda e: e.tensor_reduce(out=km2[32:33, NG:NG + 1], in_=km2[32:33, 0:NG], axis=AX.X,
                                                        op=ALU.max), r=[km2], w=[km2])
                    for g in range(NG):
                        gs = slice(g * 512, (g + 1) * 512)
                        pb = PS[g % 2]
                        col = 1536 + h * 64
                        for c in range(8):
                            mm(pb[0:64, :], Wmb[:, c, col:col + 64], hT[:, c, gs], start=(c == 0), stop=(c == 7),
                               r=[(Wmb, c), (hT, g)], w=[pb], inc=(c == 7))
                        op("act", lambda e: e.activation(out=zsm[:, gs], in_=pb[0:64, :], func=AF.Silu), r=[pb],
                           w=[(zsm, g)])

                    def projQ(g):
                        pb = PS[g % 3]
                        col = h * 64
                        for c in range(8):
                            mm(pb[64:128, :], Wmb[:, c, col:col + 64], hT[:, c, g * 512:(g + 1) * 512],
                               start=(c == 0), stop=(c == 7), r=[(Wmb, c), (hT, g)], w=[pb], inc=(c == 7))

                    def postQ1(g):
                        gs = slice(g * 512, (g + 1) * 512)
                        pb = PS[g % 3]
                        qft = qf[g % 2]
                        op("act", lambda e: e.activation(out=qft[64:128, :], in_=pb[64:128, :], func=AF.Identity,
                                                         scale=0.125), r=[pb], w=[qft])
                        op("act", lambda e: e.activation(out=QaT[64:128, gs], in_=pb[64:128, :], func=AF.Identity,
                                                         scale=0.125), r=[pb], w=[(QaT, g)])
                        sqt = sqm[g % 2]
                        op("dve", lambda e: e.tensor_tensor(out=sqt[64:128, :], in0=qft[64:128, :],
                                                            in1=qft[64:128, :], op=ALU.mult), r=[qft], w=[sqt])
                        pgt = PS[4]
                        for tt in range(4):
                            mm(pgt[:, tt * 16:(tt + 1) * 16], qft[64:128, tt * 128:(tt + 1) * 128], kmT[64:128, :],
                               r=[qft, kmT], w=[pgt], inc=(tt == 3))
                        pr = PS[5]
                        mm(pr[32:33, :], ones[64:128, 0:1], sqt[64:128, :], r=[C, sqt], w=[pr])
                        op("dve", lambda e: e.tensor_tensor(out=gm4[:].rearrange("p a b -> p (a b)"), in0=pgt[:, 0:64],
                                                            in1=C[:, C_PB4 + g * 64:C_PB4 + (g + 1) * 64], op=ALU.add),
                           r=[pgt, C], w=[gm4])
                        op("act", lambda e: e.activation(out=sqt[32:33, :], in_=pr[32:33, :], func=AF.Sqrt,
                                                         scale=km2[32:33, NG:NG + 1]), r=[pr, km2], w=[sqt])
                        op("act", lambda e: e.activation(out=QaT[32:33, gs], in_=sqt[32:33, :], func=AF.Identity,
                                                         scale=-1.0), r=[sqt], w=[(QaT, g)])
                        for tt in range(4):
                            op("dve", lambda e: e.max(out=top84[:, tt, :], in_=gm4[:, tt, :]), r=[gm4],
                               w=[(top84, tt)])
                        mb_ = mbt4[g % 2]
                        for tt in range(4):
                            op("dve", lambda e: e.tensor_scalar(out=mb_[:, tt, :], in0=gm4[:, tt, :],
                                                                scalar1=top84[:, tt, 2:3], scalar2=NEG,
                                                                op0=ALU.is_lt, op1=ALU.mult),
                               r=[gm4, (top84, tt)], w=[(mb_, tt)])
                        for t2 in range(2):
                            own = 2 * g + t2
                            op("dve", lambda e: e.memset(mb_[:, 2 * t2:2 * t2 + 2, own:own + 1], 0.0),
                               r=[(mb_, 2 * t2), (mb_, 2 * t2 + 1)], w=[(mb_, 2 * t2), (mb_, 2 * t2 + 1)])

                    def postQ2(g):
                        gs = slice(g * 512, (g + 1) * 512)
                        mb_ = mbt4[g % 2]
                        pt_ = PS[3]
                        for tt in range(4):
                            tr(pt_[0:16, tt * 128:(tt + 1) * 128], mb_[:, tt, :], ident, r=[(mb_, tt), C], w=[pt_])
                        op("act", lambda e: e.activation(out=QaT[0:16, gs], in_=pt_[0:16, 0:512], func=AF.Copy),
                           r=[pt_], w=[(QaT, g)])
                    projQ(0)
                    if NG > 1:
                        projQ(1)
                    for g in range(NG):
                        postQ1(g)
                        if g + 2 < NG:
                            projQ(g + 2)
                        if g > 0:
                            postQ2(g - 1)
                    postQ2(NG - 1)
                    for qc in range(NB):
                        q0 = qc * 256
                        pO = PS[6 + qc % 2]
                        qk_ = (QaT, qc // 2)

                        def emitS(p):
                            pS = PS[nS[0] % 4]
                            nS[0] += 1
                            if p < qc:
                                for j in range(2):
                                    kt = 2 * p + j
                                    mm(pS[:, j * 256:(j + 1) * 256], KaT[:, kt * 128:(kt + 1) * 128],
                                       QaT[:, q0:q0 + 256], r=[(KaT, kt // 4), qk_], w=[pS], inc=(j == 1))
                            else:
                                kt = 2 * qc
                                mm(pS[:, 0:256], KaT[:, kt * 128:(kt + 1) * 128], QaT[:, q0:q0 + 256], start=True,
                                   stop=False, r=[(KaT, kt // 4), qk_], w=[pS], inc=False)
                                mm(pS[:, 0:128], identb[:], trib[:], start=False, stop=True, r=[identb, trib],
                                   w=[pS], inc=False)
                                kt = 2 * qc + 1
                                mm(pS[:, 256:384], KaT[:, kt * 128:(kt + 1) * 128], QaT[:, q0 + 128:q0 + 256],
                                   start=True, stop=False, r=[(KaT, kt // 4), qk_], w=[pS], inc=False)
                                mm(pS[:, 256:384], identb[:], trib[:], start=False, stop=True, r=[identb, trib],
                                   w=[pS])
                            return pS

                        LOOK = 3
                        pendq = [emitS(p_) for p_ in range(min(LOOK, qc + 1))]
                        for p in range(qc + 1):
                            pS = pendq.pop(0)
                            if p + LOOK <= qc:
                                pendq.append(emitS(p + LOOK))
                            ptile = PT[npt % 4]
                            npt += 1
                            wdt = 512 if p < qc else 384
                            op("act", lambda e: e.activation(out=ptile[:, 0:wdt], in_=pS[:, 0:wdt], func=AF.Exp),
                               r=[pS], w=[ptile])
                            if p < qc:
                                for j in range(2):
                                    kt = 2 * p + j
                                    mm(pO[0:65, 0:256], Vt[:, kt, h, :], ptile[:, j * 256:(j + 1) * 256],
                                       start=(kt == 0), stop=False, r=[(Vt, kt), ptile], w=[pO], inc=False)
                            else:
                                kt = 2 * qc
                                mm(pO[0:65, 0:256], Vt[:, kt, h, :], ptile[:, 0:256], start=(kt == 0), stop=False,
                                   r=[(Vt, kt), ptile], w=[pO], inc=False)
                                mm(pO[0:65, 128:256], Vt[:, kt + 1, h, :], ptile[:, 256:384], start=False, stop=True,
                                   r=[(Vt, kt + 1), ptile], w=[pO])
                            for _ in range(JUNK):
                                kb.eng["pe"].matmul(PS[5][:, :], lhsT=identb[:], rhs=KaT[:, 0:512], start=True, stop=True)
                        op("dve", lambda e: e.reciprocal(out=R(rden[64:65, :]), in_=pO[64:65, 0:256]), r=[pO],
                           w=[rden])
                        pB = PS[5]
                        mm(pB[0:64, 0:256], R(onesr[64:65, 0:64]), R(rden[64:65, :]), r=[onesr, rden], w=[pB])
                        tt1 = t1[qc % 2]
                        op("dve", lambda e: e.tensor_tensor(out=tt1[:], in0=pO[0:64, 0:256],
                                                            in1=zsm[:, q0:q0 + 256], op=ALU.mult),
                           r=[pO, (zsm, qc // 2)], w=[tt1])
                        op("dve", lambda e: e.tensor_tensor(out=ogm[:, q0:q0 + 256], in0=tt1[:], in1=pB[0:64, 0:256],
                                                            op=ALU.mult), r=[tt1, pB], w=[(ogm, qc)])
                    if "omb" in dbg and l == 0 and h == 0:
                        d = dbg_out("omb", [64, S])
                        tmp = T(st, "dbgomb", [64, S])
                        op("dve", lambda e: e.tensor_copy(out=tmp[:], in_=ogm[:]), r=[ogm], w=[tmp])
                        dma(d[:, :], tmp[:], r=[tmp])
                    dma(ogmb_d[h, :, :], ogm[:], r=[ogm], w=[kogmb])
                kb.barrier()

            for st in phase("fin"):
                Wpdn = T(st, "Wpdn", [128, 4, 1024], BF16)
                Wpmb = T(st, "Wpmb", [64, 8, 1024], BF16)
                Wout = T(st, "Wout", [128, 8, 1024], BF16)
                Wmg = T(st, "Wmg", [128, 8, 2048], BF16)
                dma(Wpdn[:], wpdn_d[l], w=[Wpdn], q="pool")
                dma(Wpmb[:], wpmb_d[l], w=[Wpmb], q="pool")
                for c in range(8):
                    dma(Wout[:, c, :], wout_d[l, :, c, :], w=[(Wout, c)], q="pool")
                for c in range(8):
                    dma(Wmg[:, c, :], wmg_d[l, :, c, :], w=[(Wmg, c)], q="pool")
                OGD = [T(st, "OGD%d" % i, [128, 4, 512], BF16) for i in range(2)]
                OGM = [T(st, "OGM0", [64, 8, 512], BF16)] * 2
                mixT = T(st, "mixT", [128, 8, 512], BF16)
                gd = [T(st, "gd%d" % i, [128, 512]) for i in range(2)]
                gmm = [T(st, "gmm%d" % i, [128, 512]) for i in range(2)]
                u1 = [T(st, "u1_%d" % i, [128, 512]) for i in range(2)]
                u2 = [T(st, "u2_%d" % i, [128, 512]) for i in range(2)]
                xr = [T(st, "xr%d" % i, [128, 1024]) for i in range(2)]
                res = [T(st, "res%d" % i, [128, 512]) for i in range(2)]
                junk2 = T(st, "junk2", [128, 512], BF16)
                GPl = T(st, "GPl", [128, 1024])
                dma(GPl[:], gp_d[l], r=[(kgp, l)], w=[GPl])
                ss2 = T(st, "ss2", [128, NT, 2])
                rs2 = T(st, "rs2", [128, NT])
                for g in range(NG):
                    gs = slice(g * 512, (g + 1) * 512)
                    ogd, ogmm = OGD[g % 2], OGM[g % 2]
                    dma(ogd[:], ogdn_d[:, :, gs].rearrange("h p s -> p h s"), r=[(kogdn, g)], w=[ogd])
                    dma(ogmm[:], ogmb_d[:, :, gs].rearrange("h p s -> p h s"), r=[kogmb], w=[ogmm])
                    for d_ in range(8):
                        ds_ = slice(d_ * 128, (d_ + 1) * 128)
                        pa, pb, pc, pd = PS[0 + 4 * (d_ % 2)], PS[1 + 4 * (d_ % 2)], PS[2 + 4 * (d_ % 2)], PS[3 + 4 * (d_ % 2)]
                        for h in range(4):
                            mm(pa[:, :], Wpdn[:, h, ds_], ogd[:, h, :], start=(h == 0), stop=(h == 3),
                               r=[Wpdn, ogd], w=[pa], inc=(h == 3))
                        for h in range(8):
                            mm(pb[:, :], Wpmb[:, h, ds_], ogmm[:, h, :], start=(h == 0), stop=(h == 7),
                               r=[Wpmb, ogmm], w=[pb], inc=(h == 7))
                        for c in range(8):
                            mm(pc[:, :], Wmg[:, c, ds_], hT[:, c, gs], start=(c == 0), stop=(c == 7),
                               r=[(Wmg, c), (hT, g)], w=[pc], inc=(c == 7))
                        for c in range(8):
                            mm(pd[:, :], Wmg[:, c, 1024 + d_ * 128:1024 + (d_ + 1) * 128], hT[:, c, gs],
                               start=(c == 0), stop=(c == 7), r=[(Wmg, c), (hT, g)], w=[pd], inc=(c == 7))
                        gdt, gmt, u1t, u2t = gd[d_ % 2], gmm[d_ % 2], u1[d_ % 2], u2[d_ % 2]
                        op("act", lambda e: e.activation(out=gdt[:], in_=pc[:, :], func=AF.Sigmoid), r=[pc], w=[gdt])
                        op("act", lambda e: e.activation(out=gmt[:], in_=pd[:, :], func=AF.Sigmoid), r=[pd], w=[gmt])
                        op("dve", lambda e: e.tensor_tensor(out=u1t[:], in0=pa[:, :], in1=gdt[:], op=ALU.mult),
                           r=[pa, gdt], w=[u1t])
                        op("dve", lambda e: e.tensor_tensor(out=u2t[:], in0=pb[:, :], in1=gmt[:], op=ALU.mult),
                           r=[pb, gmt], w=[u2t])
                        op("dve", lambda e: e.tensor_tensor(out=mixT[:, d_, :], in0=u1t[:], in1=u2t[:], op=ALU.add),
                           r=[u1t, u2t], w=[(mixT, d_)])
                    for tt in range(4):
                        t = g * 4 + tt
                        xrt = xr[t % 2]
                        dma(xrt[:], xin_d[t * 128:(t + 1) * 128, :], r=[(xin_k, t)], w=[xrt])
                        for hf in range(2):
                            pb = PS[(t % 2) * 2 + hf]
                            for d_ in range(8):
                                mm(pb[:, :], mixT[:, d_, tt * 128:(tt + 1) * 128], Wout[:, d_, hf * 512:(hf + 1) * 512],
                                   start=(d_ == 0), stop=(d_ == 7), r=[(mixT, d_), (Wout, d_)], w=[pb],
                                   inc=(d_ == 7))
                            op("act", lambda e: e.activation(out=junk2[:], in_=pb[:, :], func=AF.Square,
                                                             accum_out=ss2[:, t, hf:hf + 1]), r=[pb],
                               w=[junk2, (ss2, t)])
                        op("dve", lambda e: e.tensor_tensor(out=rs2[:, t:t + 1], in0=ss2[:, t, 0:1],
                                                            in1=ss2[:, t, 1:2], op=ALU.add), r=[(ss2, t)],
                           w=[(rs2, t)])
                        op("act", lambda e: e.activation(out=rs2[:, t:t + 1], in_=rs2[:, t:t + 1], func=AF.Sqrt,
                                                         scale=1.0 / D_MODEL, bias=epsT[:]), r=[(rs2, t), epsT],
                           w=[(rs2, t)])
                        op("dve", lambda e: e.reciprocal(out=rs2[:, t:t + 1], in_=rs2[:, t:t + 1]), r=[(rs2, t)],
                           w=[(rs2, t)])
                        for hf in range(2):
                            pb = PS[(t % 2) * 2 + hf]
                            hs = slice(hf * 512, (hf + 1) * 512)
                            rt = res[hf]
                            op("dve", lambda e: e.scalar_tensor_tensor(out=rt[:], in0=pb[:, :],
                                                                       scalar=rs2[:, t:t + 1], in1=GPl[:, hs],
                                                                       op0=ALU.mult, op1=ALU.mult),
                               r=[pb, (rs2, t), GPl], w=[rt])
                            op("dve", lambda e: e.tensor_tensor(out=xrt[:, hs], in0=rt[:], in1=xrt[:, hs],
                                                                op=ALU.add), r=[rt, (xrt, hf)], w=[(xrt, hf)])
                        dma(xout_d[t * 128:(t + 1) * 128, :], xrt[:], r=[xrt], w=[(xout_k, t)])
                kb.barrier()
          except _Stop:
            kb.barrier()
            curgen[0].close()
            break
        kb.finish()
        stuck = kb.simulate()
        print("deadlock check:", stuck if stuck else "ok")
        print("instructions:", kb.ninstr, "counts", kb.count, "dmas", kb.ndmaq)
    return nc, dbg_d


def _pc(w):
    sh = w.shape
    w = w.reshape(sh[:-2] + (8, 128, sh[-1]))
    return np.ascontiguousarray(np.swapaxes(w, -3, -2))


def host_layout(inp):
    f = lambda a: np.ascontiguousarray(np.asarray(a, dtype=np.float32))
    w_in = f(inp["w_in"])
    depth = w_in.shape[0]
    shared = {
        "wada": _pc(f(inp["w_ada"])),
        "bada": f(inp["b_ada"]).reshape(depth, 1, 3072),
        "gpre": f(inp["g_pre"]).reshape(depth, 1, 1024),
        "gpost": f(inp["g_post"]).reshape(depth, 1, 1024),
        "wdn": _pc(w_in[:, :, 0:2048]),
        "wba": _pc(w_in[:, :, 2048:2056]),
        "wmb": _pc(w_in[:, :, 2056:4104]),
        "wmg": _pc(w_in[:, :, 4104:6152]),
        "convw": np.ascontiguousarray(
            f(inp["conv_w"]).transpose(0, 2, 1).reshape(depth, 12, 128, 4).transpose(0, 2, 1, 3)
        ).reshape(depth, 128, 48),
        "alog": np.ascontiguousarray(np.broadcast_to(f(inp["a_log"])[:, None, :], (depth, 128, 4))),
        "dtb": np.ascontiguousarray(np.broadcast_to(f(inp["dt_bias"])[:, None, :], (depth, 128, 4))),
        "dng": f(inp["dn_norm_g"]).reshape(depth, 128, 1),
        "wpdn": np.ascontiguousarray(f(inp["w_proj_dn"]).reshape(depth, 4, 128, 1024).transpose(0, 2, 1, 3)),
        "wpmb": np.ascontiguousarray(f(inp["w_proj_mb"]).reshape(depth, 8, 64, 1024).transpose(0, 2, 1, 3)),
        "wout": _pc(f(inp["w_out"])),
        "consts": make_consts(),
    }
    x = f(inp["x"])
    c = f(inp["c"])
    maps = []
    for b in range(x.shape[0]):
        m = dict(shared)
        m["x"] = x[b]
        m["cT"] = np.ascontiguousarray(c[b].reshape(8, 128).T)
        maps.append(m)
    return maps


_CACHE = {}


def kernel(**inputs):
    x = np.asarray(inputs["x"])
    B, S, _ = x.shape
    depth = np.asarray(inputs["w_in"]).shape[0]
    key = (S, depth)
    if key not in _CACHE:
        _CACHE[key] = build(S=S, DEPTH=depth)[0]
    nc = _CACHE[key]
    maps = host_layout(inputs)
    res = run_bass_kernel_spmd(nc, maps, core_ids=list(range(B)))
    return np.stack([np.asarray(r["y"], dtype=np.float32) for r in res.results], axis=0)
```

```python
from contextlib import ExitStack

import numpy as np
import concourse.bass as bass
import concourse.mybir as mybir
from concourse.bass_utils import run_bass_kernel_spmd

F32 = mybir.dt.float32
BF16 = mybir.dt.bfloat16
F32R = mybir.dt.float32r


def R(ap):
    return ap.bitcast(F32R)
AF = mybir.ActivationFunctionType
ALU = mybir.AluOpType
AX = mybir.AxisListType

D_MODEL = 1024
NEG = -30000.0
EPS = 1e-6


class SV:
    def __init__(self, tile, j, n=1):
        self.tile, self.sub = tile, j
        self.ap = tile[:, j * 128:(j + n) * 128]

    def __getitem__(self, idx):
        return self.ap[idx]


class KB:
    NRING = {"sp": 6, "pool": 4}

    def __init__(self, nc, stack):
        self.nc = nc
        self.eng = {"pe": nc.tensor, "act": nc.scalar, "dve": nc.vector,
                    "pool": nc.gpsimd, "sp": nc.sync}
        self.sem = {}
        for e in ("pe", "act", "dve", "pool"):
            self.sem[e] = stack.enter_context(nc.semaphore("s_" + e))
        self.ring = {q: [stack.enter_context(nc.semaphore("s_dma_%s%d" % (q, i))) for i in range(n)]
                     for q, n in self.NRING.items()}
        self.ndmaq = {q: 0 for q in self.NRING}
        self.count = {e: 0 for e in ("pe", "act", "dve", "pool")}
        self.waited = {}
        self.track = {}
        self.ninstr = 0
        self.streams = {e: [] for e in self.eng}
        self.psum_ids = set()

    def _semof(self, dep):
        if dep[0] == "e":
            return self.sem[dep[1]], ("e", dep[1])
        return self.ring[dep[1][0]][dep[1][1]], ("d", dep[1])

    def _wait(self, e, dep):
        sem, sk = self._semof(dep)
        val = dep[2]
        k = (e, sk)
        if self.waited.get(k, 0) >= val:
            return
        self.eng[e].wait_ge(sem, val)
        self.streams[e].append(("wait", sk, val))
        self.ninstr += 1
        self.waited[k] = val

    @staticmethod
    def _keys(items):
        out = []
        for it in items:
            if isinstance(it, SV):
                out.append((id(it.tile), it.sub))
            elif isinstance(it, tuple):
                out.append((id(it[0]), it[1]))
            else:
                out.append((id(it), None))
        return out

    def _conflicts(self, key):
        tid, sub = key
        d = self.track.get(tid)
        if d is None:
            return []
        if sub is None:
            return list(d.values())
        res = []
        if sub in d:
            res.append(d[sub])
        if None in d:
            res.append(d[None])
        return res

    def _entry(self, key):
        tid, sub = key
        d = self.track.setdefault(tid, {})
        if sub is None:
            ent = {"w": [], "r": []}
            for v in d.values():
                ent["w"] += v["w"]
                ent["r"] += v["r"]
            d.clear()
            d[None] = ent
            return ent
        if sub not in d:
            ent = {"w": [], "r": []}
            if None in d:
                ent["w"] = list(d[None]["w"])
                ent["r"] = list(d[None]["r"])
            d[sub] = ent
        return d[sub]

    def _deps(self, e, reads, writes):
        deps = []
        for k in reads:
            for ent in self._conflicts(k):
                for w in ent["w"]:
                    deps.append((w, "raw"))
        for k in writes:
            for ent in self._conflicts(k):
                for w in ent["w"]:
                    deps.append((w, "waw"))
                for r in ent["r"]:
                    deps.append((r, "war"))
        out = []
        for dep, kind in deps:
            if dep[0] == "e" and dep[1] == e:
                if e == "pe":
                    continue
            out.append(dep)
        return out

    @staticmethod
    def _prune(lst):
        best = {}
        for d in lst:
            kk = (d[0], d[1])
            if kk not in best or best[kk][2] < d[2]:
                best[kk] = d
        return list(best.values())

    def _record(self, me, reads, writes):
        for k in reads:
            ent = self._entry(k)
            ent["r"].append(me)
            if len(ent["r"]) > 16:
                ent["r"] = self._prune(ent["r"])
        for k in writes:
            ent = self._entry(k)
            ent["w"] = [me]
            ent["r"] = []

    def _rw(self, r, w):
        reads, writes = [], []
        for k in self._keys(r):
            if k[0] in self.psum_ids:
                writes.append((k[0], None))
            else:
                reads.append(k)
        for k in self._keys(w):
            writes.append((k[0], None) if k[0] in self.psum_ids else k)
        return reads, writes

    def op(self, e, fn, r=(), w=(), inc=True):
        reads, writes = self._rw(r, w)
        for dep in self._deps(e, reads, writes):
            self._wait(e, dep)
        ins = fn(self.eng[e])
        self.ninstr += 1
        if inc:
            self.count[e] += 1
            ins.then_inc(self.sem[e], 1)
            self.streams[e].append(("inc", ("e", e), 1))
            me = ("e", e, self.count[e])
        else:
            me = ("e", e, self.count[e] + 1)
        self._record(me, reads, writes)
        return ins

    def dma(self, out, in_, r=(), w=(), q="sp", **kw):
        e = q
        reads = self._keys(r)
        writes = self._keys(w)
        k = self.ndmaq[q]
        nr = self.NRING[q]
        slot = k % nr
        gen = k // nr
        for dep in self._deps(e, reads, writes):
            self._wait(e, dep)
        if gen > 0:
            self._wait(e, ("d", (q, slot), 16 * gen))
        ins = self.eng[e].dma_start(out=out, in_=in_, **kw)
        ins.then_inc(self.ring[q][slot], 16)
        self.streams[e].append(("inc", ("d", (q, slot)), 16))
        self.ndmaq[q] += 1
        self.ninstr += 1
        me = ("d", (q, slot), 16 * (gen + 1))
        self._record(me, reads, writes)
        return ins

    def _alldma(self):
        out = []
        for q, nr in self.NRING.items():
            n = self.ndmaq[q]
            for slot in range(nr):
                cnt = (n - 1 - slot) // nr + 1 if n > slot else 0
                if cnt > 0:
                    out.append(("d", (q, slot), 16 * cnt))
        return out

    def barrier(self):
        for e in ("pe", "act", "dve", "pool", "sp"):
            for o in ("pe", "act", "dve", "pool"):
                if o != e and self.count[o] > 0:
                    self._wait(e, ("e", o, self.count[o]))
            for dep in self._alldma():
                self._wait(e, dep)
        self.track = {}

    def finish(self):
        for dep in self._alldma():
            self._wait("sp", dep)

    def simulate(self):
        sems = {}
        pc = {e: 0 for e in self.streams}
        progress = True
        while progress:
            progress = False
            for e, st in self.streams.items():
                while pc[e] < len(st):
                    kind, sk, val = st[pc[e]]
                    if kind == "wait":
                        if sems.get(sk, 0) < val:
                            break
                    else:
                        sems[sk] = sems.get(sk, 0) + val
                    pc[e] += 1
                    progress = True
        stuck = {e: (pc[e], len(st), st[pc[e]], sems.get(st[pc[e]][1], 0)) for e, st in self.streams.items()
                 if pc[e] < len(st)}
        return stuck


C_IDENT, C_U, C_MBD, C_MOFF, C_TRI, C_ONES, C_PB = 0, 128, 256, 384, 512, 640, 768
C_PB4 = 768
NCONST = 768 + 512


def make_consts():
    c = np.zeros((128, NCONST), np.float32)
    i = np.arange(128)
    c[:, C_IDENT:C_IDENT + 128] = np.eye(128)
    c[:, C_U:C_U + 128] = (i[:, None] <= i[None, :])
    c[:, C_MBD:C_MBD + 128] = (i[:, None] < i[None, :]) & ((i[:, None] // 64) == (i[None, :] // 64))
    c[:, C_MOFF:C_MOFF + 128] = (i[:, None] < 64) & (i[None, :] >= 64)
    c[:, C_TRI:C_TRI + 128] = np.where(i[:, None] <= i[None, :], 0.0, NEG)
    c[:, C_ONES:C_ONES + 128] = 1.0
    pb = np.zeros((16, 16), np.float32)
    for own in range(16):
        pb[own, own:] = -1e30
    for g in range(8):
        for tt in range(4):
            own = (4 * g + tt) // 2
            c[:, C_PB4 + g * 64 + tt * 16:C_PB4 + g * 64 + (tt + 1) * 16] = pb[own][None, :]
    return c


def build(S=4096, DEPTH=2, LS=2, dbg=None, phases=("p1", "dn", "mb", "fin"), stop=0, JUNK=0):
    dbg = dbg or set()
    NT = S // 128
    NG = S // 512
    NB = S // 256
    nc = bass.Bass("TRN2", target_bir_lowering=False)

    def din(name, shape, dt=F32):
        return nc.dram_tensor(name, shape, dt, kind="ExternalInput").ap()

    x_d = din("x", [S, 1024])
    cT_d = din("cT", [128, 8])
    wada_d = din("wada", [DEPTH, 128, 8, 3072])
    bada_d = din("bada", [DEPTH, 1, 3072])
    gpre_d = din("gpre", [DEPTH, 1, 1024])
    gpost_d = din("gpost", [DEPTH, 1, 1024])
    wdn_d = din("wdn", [DEPTH, 128, 8, 2048])
    wba_d = din("wba", [DEPTH, 128, 8, 8])
    wmb_d = din("wmb", [DEPTH, 128, 8, 2048])
    wmg_d = din("wmg", [DEPTH, 128, 8, 2048])
    convw_d = din("convw", [DEPTH, 128, 48])
    alog_d = din("alog", [DEPTH, 128, 4])
    dtb_d = din("dtb", [DEPTH, 128, 4])
    dng_d = din("dng", [DEPTH, 128, 1])
    wpdn_d = din("wpdn", [DEPTH, 128, 4, 1024])
    wpmb_d = din("wpmb", [DEPTH, 64, 8, 1024])
    wout_d = din("wout", [DEPTH, 128, 8, 1024])
    consts_d = din("consts", [128, NCONST])
    y_d = nc.dram_tensor("y", [S, 1024], F32, kind="ExternalOutput").ap()
    xmid_d = nc.dram_tensor("xmid", [S, 1024], F32, kind="Internal").ap()
    ogdn_d = nc.dram_tensor("ogdn", [4, 128, S], BF16, kind="Internal").ap()
    ogmb_d = nc.dram_tensor("ogmb", [8, 64, S], BF16, kind="Internal").ap()
    dbg_d = {}

    class _K:
        pass
    kx, kmid, ky, kogdn, kogmb = _K(), _K(), _K(), _K(), _K()

    def dbg_out(name, shape):
        dbg_d[name] = nc.dram_tensor("dbg_" + name, shape, F32, kind="ExternalOutput").ap()
        return dbg_d[name]

    with ExitStack() as gst:
        kb = KB(nc, gst)
        op, dma = kb.op, kb.dma
        gst.enter_context(nc.allow_low_precision("float32r (1-pass PE) operands for non-critical fp32 matmuls"))

        def mm(out, lhsT, rhs, start=True, stop=True, r=(), w=(), inc=True):
            return op("pe", lambda e: e.matmul(out, lhsT=lhsT, rhs=rhs, start=start, stop=stop),
                      r=r, w=w, inc=inc)

        def tr(out, in_, ident, r=(), w=()):
            return op("pe", lambda e: e.transpose(out=out, in_=in_, identity=ident), r=r, w=w)

        uid = [0]

        def T(st, name, shape, dt=F32):
            uid[0] += 1
            return st.enter_context(nc.sbuf_tensor("sb%d_%s" % (uid[0], name), shape, dt))

        class _Stop(Exception):
            pass

        def ck(level):
            if stop == level:
                raise _Stop()

        curgen = [None]

        def phase(name):
            if name in phases:
                st_ = ExitStack()
                curgen[0] = st_
                yield st_
                st_.close()

        PS = [gst.enter_context(nc.psum_tensor("ps%d" % i, [128, 512], F32)) for i in range(8)]
        kb.psum_ids = {id(p) for p in PS}
        C = T(gst, "consts", [128, NCONST])
        dma(C[:], consts_d[:, :], w=[C])
        ident = C[:, C_IDENT:C_IDENT + 128]
        U = C[:, C_U:C_U + 128]
        Mbd = C[:, C_MBD:C_MBD + 128]
        Moff = C[:, C_MOFF:C_MOFF + 128]
        ones = C[:, C_ONES:C_ONES + 128]
        identb = T(gst, "identb", [128, 128], BF16)
        trib = T(gst, "trib", [128, 128], BF16)
        epsT = T(gst, "epsT", [128, 1])
        op("dve", lambda e: e.tensor_copy(out=identb[:], in_=ident), r=[C], w=[identb])
        op("dve", lambda e: e.tensor_copy(out=trib[:], in_=C[:, C_TRI:C_TRI + 128]), r=[C], w=[trib])
        op("dve", lambda e: e.memset(epsT[:], EPS), w=[epsT])
        onesr = T(gst, "onesr", [128, 128])
        op("dve", lambda e: e.tensor_copy(out=R(onesr[:]), in_=ones), r=[C], w=[onesr])
        AB = [T(gst, "AB%d" % l, [128, 16]) for l in range(DEPTH)]
        gp_d = nc.dram_tensor("gp_scr", [DEPTH, 128, 1024], F32, kind="Internal").ap()
        kgp = _K()

        with ExitStack() as st:
            cT = T(st, "cT", [128, 8])
            sc = T(st, "sc", [128, 8])
            dma(cT[:], cT_d[:, :], w=[cT])
            op("act", lambda e: e.activation(out=sc[:], in_=cT[:], func=AF.Silu), r=[cT], w=[sc])
            wa = [T(st, "wa%d" % i, [128, 8, 512]) for i in range(2)]
            row = T(st, "row", [1, 3072])
            bada = T(st, "bada", [1, 3072])
            gpr = T(st, "gpr", [1, 1024])
            gpo = T(st, "gpo", [1, 1024])
            arow = T(st, "arow", [1, 1024])
            gprow = T(st, "gprow", [1, 1024])
            gptmp = T(st, "gptmp", [128, 1024])
            nwa = 0
            for l in range(DEPTH):
                dma(bada[:], bada_d[l], w=[bada])
                dma(gpr[:], gpre_d[l], w=[gpr])
                dma(gpo[:], gpost_d[l], w=[gpo])
                for cg in range(6):
                    wt = wa[nwa % 2]
                    nwa += 1
                    dma(wt[:], wada_d[l, :, :, cg * 512:(cg + 1) * 512], w=[wt])
                    pb = PS[cg % 2]
                    for c in range(8):
                        mm(pb[0:1, :], sc[:, c:c + 1], wt[:, c, :], start=(c == 0), stop=(c == 7),
                           r=[sc, wt], w=[pb], inc=(c == 7))
                    op("dve", lambda e: e.tensor_tensor(out=row[0:1, cg * 512:(cg + 1) * 512], in0=pb[0:1, :],
                                                        in1=bada[0:1, cg * 512:(cg + 1) * 512], op=ALU.add),
                       r=[pb, bada], w=[(row, cg)])
                op("dve", lambda e: e.scalar_tensor_tensor(out=arow[:], in0=row[0:1, 1024:2048], scalar=1.0,
                                                           in1=gpr[:], op0=ALU.add, op1=ALU.mult),
                   r=[row, gpr], w=[arow])
                op("dve", lambda e: e.tensor_tensor(out=gprow[:], in0=row[0:1, 2048:3072], in1=gpo[:], op=ALU.mult),
                   r=[row, gpo], w=[gprow])
                pc = PS[2]
                for c in range(8):
                    mm(pc[:, c:c + 1], arow[0:1, c * 128:(c + 1) * 128], ones[0:1, 0:1], r=[arow, C], w=[pc], inc=False)
                for c in range(8):
                    mm(pc[:, 8 + c:9 + c], row[0:1, c * 128:(c + 1) * 128], ones[0:1, 0:1], r=[row, C], w=[pc],
                       inc=(c == 7))
                op("dve", lambda e: e.tensor_copy(out=AB[l][:], in_=pc[:, 0:16]), r=[pc], w=[AB[l]])
                for hf in range(2):
                    pg = PS[3 + hf]
                    mm(pg[:, :], ones[0:1, 0:128], gprow[0:1, hf * 512:(hf + 1) * 512], r=[gprow, C], w=[pg])
                    op("act", lambda e: e.activation(out=gptmp[:, hf * 512:(hf + 1) * 512], in_=pg[:, :], func=AF.Copy),
                       r=[pg], w=[(gptmp, hf)])
                dma(gp_d[l], gptmp[:], r=[gptmp], w=[(kgp, l)])
            kb.barrier()

        hT = T(gst, "hT", [128, 8, S], BF16)

        for l in range(DEPTH):
          try:
            xin_d = x_d if l == 0 else xmid_d
            xout_d = y_d if l == DEPTH - 1 else xmid_d
            xin_k = kx if l == 0 else kmid
            xout_k = ky if l == DEPTH - 1 else kmid

            for st in phase("p1"):
                xt = [T(st, "xt%d" % i, [128, 1024]) for i in range(4)]
                xn = [T(st, "xn%d" % i, [128, 1024]) for i in range(3)]
                junk = T(st, "junk", [128, 1024], BF16)
                ss = T(st, "ss", [128, NT])
                rstd = T(st, "rstd", [128, NT])
                for t in range(NT):
                    xtt, xnt = xt[t % 4], xn[t % 3]
                    dma(xtt[:], xin_d[t * 128:(t + 1) * 128, :], r=[(xin_k, t)], w=[xtt])
                    op("act", lambda e: e.activation(out=junk[:], in_=xtt[:], func=AF.Square,
                                                     accum_out=ss[:, t:t + 1]), r=[xtt], w=[junk, (ss, t)])
                    op("act", lambda e: e.activation(out=rstd[:, t:t + 1], in_=ss[:, t:t + 1], func=AF.Sqrt,
                                                     scale=1.0 / D_MODEL, bias=epsT[:]), r=[(ss, t), epsT],
                       w=[(rstd, t)])
                    op("dve", lambda e: e.reciprocal(out=rstd[:, t:t + 1], in_=rstd[:, t:t + 1]), r=[(rstd, t)],
                       w=[(rstd, t)])
                    op("dve", lambda e: e.tensor_scalar(out=xnt[:], in0=xtt[:], scalar1=rstd[:, t:t + 1],
                                                        scalar2=None, op0=ALU.mult), r=[xtt, (rstd, t)], w=[xnt])
                    for hf in range(2):
                        pb = PS[(t % 2) * 2 + hf]
                        for cc in range(4):
                            c = hf * 4 + cc
                            tr(pb[:, cc * 128:(cc + 1) * 128], xnt[:, c * 128:(c + 1) * 128], ident, r=[xnt, C],
                               w=[pb])
                        for cc in range(4):
                            c = hf * 4 + cc
                            eng = "act" if hf == 0 else "dve"
                            if eng == "act":
                                op("act", lambda e: e.activation(out=hT[:, c, t * 128:(t + 1) * 128],
                                                                 in_=pb[:, cc * 128:(cc + 1) * 128], func=AF.Identity,
                                                                 scale=AB[l][:, c:c + 1], bias=AB[l][:, 8 + c:9 + c]),
                                   r=[pb, AB[l]], w=[(hT, t // 4)])
                            else:
                                op("dve", lambda e: e.tensor_scalar(out=hT[:, c, t * 128:(t + 1) * 128],
                                                                    in0=pb[:, cc * 128:(cc + 1) * 128],
                                                                    scalar1=AB[l][:, c:c + 1],
                                                                    scalar2=AB[l][:, 8 + c:9 + c],
                                                                    op0=ALU.mult, op1=ALU.add),
                                   r=[pb, AB[l]], w=[(hT, t // 4)])
                kb.barrier()
            if "hT" in dbg and l == 0:
                with ExitStack() as st:
                    d = dbg_out("hT", [128, 8, S])
                    tmp = T(st, "dbgtmp", [128, 8, S])
                    op("dve", lambda e: e.tensor_copy(out=tmp[:], in_=hT[:]), r=[hT], w=[tmp])
                    dma(d[:, :, :], tmp[:], r=[tmp])
                    kb.barrier()

            for st in phase("dn"):
                Wdn = T(st, "Wdn", [128, 8, 2048], BF16)
                Wba = T(st, "Wba", [128, 8, 8], BF16)
                for c in range(8):
                    dma(Wdn[:, c, :], wdn_d[l, :, c, :], w=[(Wdn, c)], q="pool")
                dma(Wba[:], wba_d[l], w=[Wba], q="pool")
                convw = T(st, "convw", [128, 48])
                alog = T(st, "alog", [128, 4])
                dtb = T(st, "dtb", [128, 4])
                dng = T(st, "dng", [128, 1])
                dma(convw[:], convw_d[l], w=[convw])
                dma(alog[:], alog_d[l], w=[alog])
                dma(dtb[:], dtb_d[l], w=[dtb])
                dma(dng[:], dng_d[l], w=[dng])
                st2 = ExitStack()
                BETA = T(st, "BETA", [128, NT, 4])
                NBETA = T(st, "NBETA", [128, NT, 4])
                GRAW = T(st, "GRAW", [128, NT, 4])
                GC = T(st, "GC", [128, NT, 4])
                NEXPG = T(st, "NEXPG", [128, NT, 4])
                negA = T(st, "negA", [128, 4])
                BG = T(st2, "BG", [128, NT, 8])
                AA = T(st2, "AA", [128, NT, 4])
                AX_ = T(st2, "AXs", [128, NT, 4])
                pbg = PS[0]
                for t in range(NT):
                    for c in range(8):
                        mm(pbg[:, t * 8:(t + 1) * 8], hT[:, c, t * 128:(t + 1) * 128], Wba[:, c, :],
                           start=(c == 0), stop=(c == 7), r=[(hT, t // 4), Wba], w=[pbg], inc=(c == 7))
                op("dve", lambda e: e.tensor_copy(out=BG[:].rearrange("p t e -> p (t e)"), in_=pbg[:, 0:NT * 8]),
                   r=[pbg], w=[BG])
                op("act", lambda e: e.activation(out=BETA[:], in_=BG[:, :, 0:4], func=AF.Sigmoid), r=[BG], w=[BETA])
                op("dve", lambda e: e.tensor_scalar(out=NBETA[:], in0=BETA[:], scalar1=-1.0, scalar2=None,
                                                    op0=ALU.mult), r=[BETA], w=[NBETA])
                for h in range(4):
                    op("dve", lambda e: e.tensor_scalar(out=AA[:, :, h], in0=BG[:, :, 4 + h], scalar1=dtb[:, h:h + 1],
                                                        scalar2=None, op0=ALU.add), r=[BG, dtb], w=[AA])
                op("act", lambda e: e.activation(out=AX_[:], in_=AA[:], func=AF.Abs), r=[AA], w=[AX_])
                op("act", lambda e: e.activation(out=AX_[:], in_=AX_[:], func=AF.Exp, scale=-1.0), r=[AX_], w=[AX_])
                op("dve", lambda e: e.tensor_scalar(out=AX_[:], in0=AX_[:], scalar1=1.0, scalar2=None, op0=ALU.add),
                   r=[AX_], w=[AX_])
                op("act", lambda e: e.activation(out=AX_[:], in_=AX_[:], func=AF.Ln), r=[AX_], w=[AX_])
                op("dve", lambda e: e.scalar_tensor_tensor(out=AA[:], in0=AA[:], scalar=0.0, in1=AX_[:],
                                                           op0=ALU.max, op1=ALU.add), r=[AA, AX_], w=[AA])
                op("act", lambda e: e.activation(out=negA[:], in_=alog[:], func=AF.Exp), r=[alog], w=[negA])
                op("dve", lambda e: e.tensor_scalar(out=negA[:], in0=negA[:], scalar1=-1.0, scalar2=None,
                                                    op0=ALU.mult), r=[negA], w=[negA])
                for h in range(4):
                    op("dve", lambda e: e.tensor_scalar(out=GRAW[:, :, h], in0=AA[:, :, h], scalar1=negA[:, h:h + 1],
                                                        scalar2=None, op0=ALU.mult), r=[AA, negA], w=[GRAW])
                pgc = PS[1]
                mm(pgc[:, 0:NT * 4], U, GRAW[:].rearrange("p t e -> p (t e)"), r=[C, GRAW], w=[pgc])
                op("dve", lambda e: e.tensor_copy(out=GC[:].rearrange("p t e -> p (t e)"), in_=pgc[:, 0:NT * 4]),
                   r=[pgc], w=[GC])
                op("act", lambda e: e.activation(out=NEXPG[:], in_=GC[:], func=AF.Exp), r=[GC], w=[NEXPG])
                op("dve", lambda e: e.tensor_scalar(out=NEXPG[:], in0=NEXPG[:], scalar1=-1.0, scalar2=None,
                                                    op0=ALU.mult), r=[NEXPG], w=[NEXPG])
                if "graw" in dbg and l == 0:
                    d = dbg_out("graw", [128, NT, 4])
                    dma(d[:, :, :], GRAW[:], r=[GRAW])
                    d = dbg_out("beta", [128, NT, 4])
                    dma(d[:, :, :], BETA[:], r=[BETA])

                kb.barrier()
                st2.close()
                ck(1)
                pre = [T(st, "pre%d" % i, [128, 515]) for i in range(3)]
                halo = T(st, "halo", [128, 12, 3])
                op("dve", lambda e: e.memset(halo[:], 0.0), w=[halo])
                qkv = [[T(st, "qkv%d_%d" % (i, j), [128, 512]) for j in range(3)] for i in range(2)]
                zs = [T(st, "zs%d" % i, [128, 512], BF16) for i in range(4)]
                cv = [T(st, "cv0", [128, 512])] * 2
                sq = [T(st, "sq0", [128, 512])] * 2
                oTg = [T(st, "oTg%d" % i, [128, 512]) for i in range(2)]
                ogb = [T(st, "ogb0", [128, 512], BF16)] * 2
                Sst = [[T(st, "S%d_%d" % (h, i), [128, 128]) for i in range(2)] for h in range(4)]
                for h in range(4):
                    op("dve", lambda e: e.tensor_scalar(out=R(Sst[h][0][:]), in0=ident, scalar1=0.0, scalar2=None,
                                                        op0=ALU.mult), r=[C], w=[Sst[h][0]])
                spar = [0, 0, 0, 0]
                NCH = 8
                ALIAS = {"Dm": 0, "tq": 0, "DecT": 1, "Xo": 1, "Pe": 2, "ktok": 3, "Xe": 3, "NoffT": 3, "Po": 4,
                         "B": 5, "BT": 6, "WbdT": 6, "Noff": 7, "Z1": 7, "ExpG": 8, "XTo": 8, "Lm": 9, "XTe": 9}
                scr = []
                for i in range(NCH):
                    wide = T(st, "s%d" % i, [128, 9 * 128])
                    plain = T(st, "s%da" % i, [128, 128])
                    d_ = {n: (SV(wide, j - 1) if j > 0 else plain) for n, j in ALIAS.items()}
                    d_["XPo"] = SV(wide, 0, 2)
                    d_["XPe"] = SV(wide, 2, 2)
                    scr.append(d_)
                OUTN = ["W", "QKdT", "kdec", "qg", "vtok", "kTc"]
                outs = [{n: T(st, "o%d_%s" % (i, n), [128, 128]) for n in OUTN} for i in range(NCH)]
                EGL = T(st, "EGL", [128, NCH])
                Yr = [T(st, "Yr%d" % i, [128, 128]) for i in range(2)]
                Vn = [T(st, "Vn%d" % i, [128, 128]) for i in range(2)]

                def prep8(chains, pump=lambda: None):
                    pump_on = [False]

                    def each(fn):
                        for ci, ch in enumerate(chains):
                            fn(ch, ch["s"], ch["o"], PS[ch["i"]])
                            if ci % 2 == 1 and pump_on[0]:
                                pump(1)

                    def f(ch, s, o, pb):
                        h, n, sl = ch["h"], ch["n"], ch["sl"]
                        kT, vT = ch["kT"], ch["vT"]
                        tr(pb[:, 0:128], kT[:, sl], ident, r=[kT, C], w=[pb])
                        tr(pb[:, 128:256], vT[:, sl], ident, r=[vT, C], w=[pb])
                        mm(pb[:, 256:384], GRAW[:, n, h:h + 1].to_broadcast([128, 128]), U, r=[GRAW, C], w=[pb])
                        op("act", lambda e: e.activation(out=R(s["ktok"][:]), in_=pb[:, 0:128], func=AF.Copy),
                           r=[pb], w=[s["ktok"]])
                        op("dve", lambda e: e.tensor_copy(out=o["vtok"][:], in_=pb[:, 128:256]), r=[pb],
                           w=[o["vtok"]])
                        op("dve", lambda e: e.tensor_scalar(out=s["Dm"][:], in0=pb[:, 256:384],
                                                            scalar1=GC[:, n, h:h + 1], scalar2=0.0,
                                                            op0=ALU.subtract, op1=ALU.min),
                           r=[pb, GC], w=[s["Dm"]])
                        op("act", lambda e: e.activation(out=R(s["ExpG"][:]), in_=pb[:, 256:384], func=AF.Exp),
                           r=[pb], w=[s["ExpG"]])
                    each(f)

                    def f(ch, s, o, pb):
                        op("act", lambda e: e.activation(out=R(s["DecT"][:]), in_=s["Dm"][:], func=AF.Exp),
                           r=[s["Dm"]], w=[s["DecT"]])
                    each(f)

                    def f(ch, s, o, pb):
                        h, n, sl = ch["h"], ch["n"], ch["sl"]
                        kT, qT = ch["kT"], ch["qT"]
                        mm(pb[:, 0:128], R(kT[:, sl]), R(kT[:, sl]), r=[kT], w=[pb], inc=False)
                        mm(pb[:, 128:256], R(kT[:, sl]), R(qT[:, sl]), r=[kT, qT], w=[pb])
                        op("dve", lambda e: e.tensor_tensor(out=R(s["Lm"][:]), in0=pb[:, 0:128], in1=s["DecT"][:],
                                                            op=ALU.mult), r=[pb, s["DecT"]], w=[s["Lm"]])
                        op("dve", lambda e: e.tensor_tensor(out=s["tq"][:], in0=pb[:, 128:256], in1=s["DecT"][:],
                                                            op=ALU.mult), r=[pb, s["DecT"]], w=[s["tq"]])
                        op("dve", lambda e: e.scalar_tensor_tensor(out=R(s["B"][:]), in0=s["Lm"][:],
                                                                   scalar=NBETA[:, n, h:h + 1], in1=Mbd,
                                                                   op0=ALU.mult, op1=ALU.mult),
                           r=[s["Lm"], NBETA, C], w=[s["B"]])
                        op("dve", lambda e: e.scalar_tensor_tensor(out=R(s["Noff"][:]), in0=s["Lm"][:],
                                                                   scalar=BETA[:, n, h:h + 1], in1=Moff,
                                                                   op0=ALU.mult, op1=ALU.mult),
                           r=[s["Lm"], BETA, C], w=[s["Noff"]])
                        op("pool", lambda e: e.tensor_tensor(out=R(o["QKdT"][:]), in0=s["tq"][:], in1=U,
                                                             op=ALU.mult), r=[s["tq"], C], w=[o["QKdT"]])
                        op("act", lambda e: e.activation(out=R(o["kdec"][:]), in_=s["ktok"][:], func=AF.Identity,
                                                         scale=s["DecT"][:, 127:128]), r=[s["ktok"], s["DecT"]],
                           w=[o["kdec"]])
                        op("dve", lambda e: e.tensor_tensor(out=R(o["qg"][:]), in0=qT[:, sl], in1=s["ExpG"][:],
                                                            op=ALU.mult), r=[qT, s["ExpG"]], w=[o["qg"]])
                        op("dve", lambda e: e.tensor_copy(out=R(o["kTc"][:]), in_=kT[:, sl]), r=[kT], w=[o["kTc"]])
                        op("dve", lambda e: e.tensor_copy(out=EGL[:, ch["i"]:ch["i"] + 1],
                                                          in_=s["ExpG"][:, 127:128]),
                           r=[s["ExpG"]], w=[(EGL, ch["i"])])
                    each(f)

                    pump_on[0] = True
                    def f(ch, s, o, pb):
                        tr(pb[:, 256:384], s["B"][:], ident, r=[s["B"], C], w=[pb])
                        op("act", lambda e: e.activation(out=R(s["BT"][:]), in_=pb[:, 256:384], func=AF.Copy),
                           r=[pb], w=[s["BT"]])
                        op("dve", lambda e: e.tensor_tensor(out=R(s["Pe"][:]), in0=s["B"][:], in1=ident,
                                                            op=ALU.add), r=[s["B"], C], w=[s["Pe"]])
                    each(f)

                    def f(ch, s, o, pb):
                        mm(pb[:, 0:128], R(s["BT"][:]), R(s["B"][:]), r=[s["BT"], s["B"]], w=[pb])
                        op("act", lambda e: e.activation(out=R(s["Xo"][:]), in_=pb[:, 0:128], func=AF.Copy),
                           r=[pb], w=[s["Xo"]])
                    each(f)
                    pump()
                    for j in range(1, 6):
                        odd = (j % 2 == 1)
                        Xc, Pc, XTc, XPc = ("Xo", "Pe", "XTo", "XPo") if odd else ("Xe", "Po", "XTe", "XPe")
                        Xn, Pn = ("Xe", "Po") if odd else ("Xo", "Pe")

                        def f(ch, s, o, pb):
                            tr(pb[:, 256:384], s[Xc][:], ident, r=[s[Xc], C], w=[pb])
                            op("act", lambda e: e.activation(out=R(s[XTc][:]), in_=pb[:, 256:384], func=AF.Copy),
                               r=[pb], w=[s[XTc]])
                        each(f)
                        pump()

                        def f(ch, s, o, pb):
                            if j < 5:
                                mm(pb[:, 0:256], R(s[XTc][:]), R(s[XPc][:]), r=[s[XTc], s[Xc], s[Pc]], w=[pb])
                                op("act", lambda e: e.activation(out=R(s[Xn][:]), in_=pb[:, 0:128], func=AF.Copy),
                                   r=[pb], w=[s[Xn]])
                                op("dve", lambda e: e.tensor_tensor(out=R(s[Pn][:]), in0=pb[:, 128:256],
                                                                    in1=s[Pc][:], op=ALU.add),
                                   r=[pb, s[Pc]], w=[s[Pn]])
                            else:
                                mm(pb[:, 128:256], R(s[XTc][:]), R(s[Pc][:]), r=[s[XTc], s[Pc]], w=[pb])
                                op("dve", lambda e: e.tensor_tensor(out=R(s[Pn][:]), in0=pb[:, 128:256],
                                                                    in1=s[Pc][:], op=ALU.add),
                                   r=[pb, s[Pc]], w=[s[Pn]])
                        each(f)
                        pump()
                    Pf = "Po"

                    def f(ch, s, o, pb):
                        tr(pb[:, 0:128], s[Pf][:], ident, r=[s[Pf], C], w=[pb])
                        tr(pb[:, 128:256], s["Noff"][:], ident, r=[s["Noff"], C], w=[pb])
                        op("act", lambda e: e.activation(out=R(s["WbdT"][:]), in_=pb[:, 0:128], func=AF.Copy),
                           r=[pb], w=[s["WbdT"]])
                        op("dve", lambda e: e.tensor_copy(out=R(s["NoffT"][:]), in_=pb[:, 128:256]), r=[pb],
                           w=[s["NoffT"]])
                    each(f)
                    pump()

                    def f(ch, s, o, pb):
                        mm(pb[:, 0:128], R(s["NoffT"][:]), R(s[Pf][:]), r=[s["NoffT"], s[Pf]], w=[pb])
                        op("act", lambda e: e.activation(out=R(s["Z1"][:]), in_=pb[:, 0:128], func=AF.Copy),
                           r=[pb], w=[s["Z1"]])
                    each(f)
                    pump()

                    def f(ch, s, o, pb):
                        mm(pb[:, 128:256], R(s["WbdT"][:]), R(s["Z1"][:]), r=[s["WbdT"], s["Z1"]], w=[pb])
                        op("dve", lambda e: e.tensor_tensor(out=R(o["W"][:]), in0=s[Pf][:], in1=pb[:, 128:256],
                                                            op=ALU.subtract), r=[s[Pf], pb], w=[o["W"]])
                    each(f)
                    pump()

                nrec = [0]

                def recur_pair(chs, g):
                    st_ = []
                    for ch in chs:
                        h = ch["h"]
                        So = Sst[h][spar[h]]
                        Sn = Sst[h][1 - spar[h]]
                        spar[h] = 1 - spar[h]
                        bb = 4 * ch["hi"]
                        st_.append((ch, So, Sn, PS[bb], PS[bb + 1], PS[bb + 2], PS[bb + 3],
                                    Yr[nrec[0] % 2], Vn[nrec[0] % 2]))
                        nrec[0] += 1
                    for ch, So, Sn, pa, pb2, pc, pd, Y, vnew in st_:
                        h, n, o = ch["h"], ch["n"], ch["o"]
                        mm(pa[:, 0:128], R(o["kTc"][:]), R(So[:]), r=[o["kTc"], So], w=[pa])
                        op("dve", lambda e: e.scalar_tensor_tensor(out=R(Y[:]), in0=pa[:, 0:128],
                                                                   scalar=NEXPG[:, n, h:h + 1], in1=o["vtok"][:],
                                                                   op0=ALU.mult, op1=ALU.add),
                           r=[pa, NEXPG, o["vtok"]], w=[Y])
                    for ch, So, Sn, pa, pb2, pc, pd, Y, vnew in st_:
                        h, n, o = ch["h"], ch["n"], ch["o"]
                        mm(pb2[:, 0:128], R(o["W"][:]), R(Y[:]), r=[o["W"], Y], w=[pb2])
                        op("act", lambda e: e.activation(out=R(vnew[:]), in_=pb2[:, 0:128], func=AF.Identity,
                                                         scale=BETA[:, n, h:h + 1]), r=[pb2, BETA], w=[vnew])
                    for ch, So, Sn, pa, pb2, pc, pd, Y, vnew in st_:
                        o = ch["o"]
                        mm(pd[:, 0:128], R(o["kdec"][:]), R(vnew[:]), r=[o["kdec"], vnew], w=[pd])
                        op("dve", lambda e: e.scalar_tensor_tensor(out=R(Sn[:]), in0=So[:],
                                                                   scalar=EGL[:, ch["i"]:ch["i"] + 1],
                                                                   in1=pd[:, 0:128], op0=ALU.mult, op1=ALU.add),
                           r=[So, (EGL, ch["i"]), pd], w=[Sn])
                    for ch, So, Sn, pa, pb2, pc, pd, Y, vnew in st_:
                        o, sl = ch["o"], ch["sl"]
                        mm(pc[:, 0:128], R(So[:]), R(o["qg"][:]), start=True, stop=False, r=[So, o["qg"]], w=[pc],
                           inc=False)
                        mm(pc[:, 0:128], R(vnew[:]), R(o["QKdT"][:]), start=False, stop=True, r=[vnew, o["QKdT"]],
                           w=[pc])
                        ot = oTg[ch["hi"]]
                        op("act", lambda e: e.activation(out=ot[:, sl], in_=pc[:, 0:128], func=AF.Copy), r=[pc],
                           w=[(ot, ch["cc"])])

                def stageA(g, hp, par, res):
                    gs = slice(g * 512, (g + 1) * 512)
                    chains = []
                    for hi in range(2):
                        h = 2 * hp + hi
                        qk = qkv[hi]
                        zt = zs[2 * par + hi]
                        for ty in range(4):
                            pb = PS[4 * hi + ty]
                            col = ty * 512 + h * 128
                            for c in range(8):
                                mm(pb[:, :], Wdn[:, c, col:col + 128], hT[:, c, gs], start=(c == 0),
                                   stop=(c == 7), r=[(Wdn, c), (hT, g)], w=[pb], inc=(c == 7))
                            if ty < 3:
                                ch_ = ty * 4 + h
                                op("dve", lambda e: e.tensor_copy(out=pre[ty][:, 0:3], in_=halo[:, ch_, :]),
                                   r=[(halo, ch_)], w=[(pre[ty], 0)])
                                op("act", lambda e: e.activation(out=pre[ty][:, 3:515], in_=pb[:, :],
                                                                 func=AF.Copy), r=[pb], w=[(pre[ty], 1)])
                                op("dve", lambda e: e.tensor_copy(out=halo[:, ch_, :], in_=pre[ty][:, 512:515]),
                                   r=[(pre[ty], 1)], w=[(halo, ch_)])
                                yield
                                cvt = cv[ty % 2]
                                wk = lambda k: convw[:, ch_ * 4 + k:ch_ * 4 + k + 1]
                                op("act", lambda e: e.activation(out=cvt[:], in_=pre[ty][:, 0:512],
                                                                 func=AF.Identity, scale=wk(0)),
                                   r=[pre[ty], convw], w=[cvt])
                                for k in range(1, 4):
                                    op("dve", lambda e: e.scalar_tensor_tensor(out=cvt[:],
                                                                               in0=pre[ty][:, k:k + 512],
                                                                               scalar=wk(k), in1=cvt[:],
                                                                               op0=ALU.mult, op1=ALU.add),
                                       r=[pre[ty], convw, cvt], w=[cvt])
                                    yield
                                op("act", lambda e: e.activation(out=(R(qk[ty][:]) if ty < 2 else qk[ty][:]),
                                                                 in_=cvt[:], func=AF.Silu),
                                   r=[cvt], w=[qk[ty]])
                            else:
                                op("act", lambda e: e.activation(out=zt[:], in_=pb[:, :], func=AF.Silu),
                                   r=[pb], w=[zt])
                            yield
                        for ty in range(2):
                            sqt = sq[ty]
                            pb = PS[4 * hi + ty]
                            op("act", lambda e: e.activation(out=R(sqt[:]), in_=qk[ty][:], func=AF.Square),
                               r=[qk[ty]], w=[sqt])
                            mm(pb[:, :], R(onesr[:]), R(sqt[:]), r=[onesr, sqt], w=[pb])
                            rt_ = cv[0]
                            op("act", lambda e: e.activation(out=rt_[:], in_=pb[:, :], func=AF.Ln,
                                                             bias=epsT[:]), r=[pb, epsT], w=[rt_])
                            op("act", lambda e: e.activation(out=rt_[:], in_=rt_[:], func=AF.Exp, scale=-0.5),
                               r=[rt_], w=[rt_])
                            sc_ = (128.0 ** -0.5) if ty == 0 else 1.0
                            op("dve", lambda e: e.scalar_tensor_tensor(out=R(qk[ty][:]), in0=qk[ty][:],
                                                                       scalar=sc_, in1=rt_[:], op0=ALU.mult,
                                                                       op1=ALU.mult),
                               r=[qk[ty], rt_], w=[qk[ty]])
                            yield
                        if dbg and l == 0 and g == 0 and h == 0:
                            for nm, tt in (("q", qk[0]), ("k", qk[1]), ("v", qk[2])):
                                if nm in dbg:
                                    d = dbg_out(nm, [128, 512])
                                    dma(d[:, :], tt[:], r=[tt])
                        for cc in range(4):
                            i = hi * 4 + cc
                            chains.append({"h": h, "hi": hi, "cc": cc, "n": g * 4 + cc, "i": i,
                                           "sl": slice(cc * 128, (cc + 1) * 128), "qT": qk[0], "kT": qk[1],
                                           "vT": qk[2], "s": scr[i], "o": outs[i]})
                    res["chains"] = chains

                def drain(gen):
                    if gen is not None:
                        for _ in gen:
                            pass

                pairs = [(g, hp) for g in range(NG) for hp in range(2)]
                resA = [dict() for _ in pairs]
                gens = [stageA(g, hp, pi % 2, resA[pi]) for pi, (g, hp) in enumerate(pairs)]
                drain(gens[0])
                for pi, (g, hp) in enumerate(pairs):
                    gs = slice(g * 512, (g + 1) * 512)
                    chains = resA[pi]["chains"]
                    nxt = gens[pi + 1] if pi + 1 < len(pairs) else None

                    def pump(n=1):
                        if nxt is not None:
                            for _ in range(n):
                                if next(nxt, "done") == "done":
                                    break
                    prep8(chains, pump)
                    drain(nxt)
                    for cc in range(4):
                        recur_pair([ch for ch in chains if ch["cc"] == cc], g)
                    for hi in range(2):
                        h = 2 * hp + hi
                        zt = zs[2 * (pi % 2) + hi]
                        ot = oTg[hi]
                        og = ogb[hi]
                        sqt = sq[hi]
                        pb = PS[4 * hi]
                        op("act", lambda e: e.activation(out=R(sqt[:]), in_=ot[:], func=AF.Square), r=[ot],
                           w=[sqt])
                        mm(pb[:, :], R(onesr[:]), R(sqt[:]), r=[onesr, sqt], w=[pb])
                        sqt = cv[0]
                        op("act", lambda e: e.activation(out=sqt[:], in_=pb[:, :], func=AF.Ln,
                                                         scale=1.0 / 128, bias=epsT[:]), r=[pb, epsT], w=[sqt])
                        op("act", lambda e: e.activation(out=sqt[:], in_=sqt[:], func=AF.Exp, scale=-0.5),
                           r=[sqt], w=[sqt])
                        if "odn" in dbg and l == 0 and h == 0 and g == 0:
                            d = dbg_out("odn", [128, 512])
                            dma(d[:, :], ot[:], r=[ot])
                        op("dve", lambda e: e.scalar_tensor_tensor(out=sqt[:], in0=ot[:], scalar=dng[:, 0:1],
                                                                   in1=sqt[:], op0=ALU.mult, op1=ALU.mult),
                           r=[ot, dng, sqt], w=[sqt])
                        op("dve", lambda e: e.tensor_tensor(out=og[:], in0=sqt[:], in1=zt[:], op=ALU.mult),
                           r=[sqt, zt], w=[og])
                        dma(ogdn_d[h, :, gs], og[:], r=[og], w=[(kogdn, g)])
                kb.barrier()

            for st in phase("mb"):
                Wmb = T(st, "Wmb", [128, 8, 2048], BF16)
                for c in range(8):
                    dma(Wmb[:, c, :], wmb_d[l, :, c, :], w=[(Wmb, c)], q="pool")
                Vt = T(st, "Vt", [128, NT, 8, 65], BF16)
                op("dve", lambda e: e.memset(Vt[:, :, :, 64:65], 1.0), w=[Vt])
                for t in range(NT):
                    pb = PS[t % 2]
                    for c in range(8):
                        mm(pb[:, :], hT[:, c, t * 128:(t + 1) * 128], Wmb[:, c, 1024:1536], start=(c == 0),
                           stop=(c == 7), r=[(hT, t // 4), (Wmb, c)], w=[pb], inc=(c == 7))
                    op("act" if t % 2 else "dve",
                       lambda e: (e.activation(out=Vt[:, t, :, 0:64], in_=pb[:, :].rearrange("p (h d) -> p h d", h=8),
                                               func=AF.Copy) if t % 2 else
                                  e.tensor_copy(out=Vt[:, t, :, 0:64],
                                                in_=pb[:, :].rearrange("p (h d) -> p h d", h=8))),
                       r=[pb], w=[(Vt, t)])
                KaT = T(st, "KaT", [128, S], BF16)
                QaT = T(st, "QaT", [128, S], BF16)
                zsm = T(st, "zsm", [64, S], BF16)
                ogm = T(st, "ogm", [64, S], BF16)
                kf = [T(st, "kf%d" % i, [128, 512]) for i in range(2)]
                qf = [T(st, "qf%d" % i, [128, 512]) for i in range(2)]
                sqm = [T(st, "sqm%d" % i, [128, 512]) for i in range(2)]
                kmT = T(st, "kmT", [128, 16])
                km2 = T(st, "km2", [64, NG + 1])
                gm4 = T(st, "gm4", [128, 4, 16])
                top84 = T(st, "top84", [128, 4, 8])
                mbt4 = [T(st, "mbt4_%d" % i, [128, 4, 16]) for i in range(2)]
                rden = T(st, "rden", [128, 256])
                t1 = [T(st, "t1_%d" % i, [64, 256]) for i in range(2)]
                PT = [T(st, "PT%d" % i, [128, 512], BF16) for i in range(4)]
                nS = [0]
                op("dve", lambda e: e.memset(KaT[0:64, :], 0.0), w=[KaT])
                op("dve", lambda e: e.memset(QaT[0:64, :], 0.0), w=[QaT])
                op("dve", lambda e: e.memset(KaT[32:33, :], 1.0), w=[KaT])
                for n in range(NB):
                    op("dve", lambda e: e.tensor_copy(out=KaT[0:16, n * 256:(n + 1) * 256],
                                                      in_=ident[0:16, n:n + 1].to_broadcast([16, 256])),
                       r=[C], w=[KaT])
                npt = 0
                for h in range(8):
                    op("dve", lambda e: e.memset(kmT[:], 0.0), w=[kmT])
                    def projK(g):
                        pb = PS[g % 2]
                        col = 512 + h * 64
                        for c in range(8):
                            mm(pb[64:128, :], Wmb[:, c, col:col + 64], hT[:, c, g * 512:(g + 1) * 512],
                               start=(c == 0), stop=(c == 7), r=[(Wmb, c), (hT, g)], w=[pb], inc=(c == 7))

                    def postK(g):
                        gs = slice(g * 512, (g + 1) * 512)
                        pb = PS[g % 2]
                        kft = kf[g % 2]
                        op("act", lambda e: e.activation(out=kft[64:128, :], in_=pb[64:128, :], func=AF.Copy),
                           r=[pb], w=[kft])
                        op("act", lambda e: e.activation(out=KaT[64:128, gs], in_=pb[64:128, :], func=AF.Copy),
                           r=[pb], w=[(KaT, g)])
                        op("dve", lambda e: e.tensor_reduce(out=kmT[64:128, 2 * g:2 * g + 2],
                                                            in_=kft[64:128, :].rearrange("p (b t) -> p b t", b=2),
                                                            axis=AX.X, op=ALU.add), r=[kft], w=[kmT])
                        sqt = sqm[g % 2]
                        op("dve", lambda e: e.tensor_tensor(out=sqt[64:128, :], in0=kft[64:128, :],
                                                            in1=kft[64:128, :], op=ALU.mult), r=[kft], w=[sqt])
                        pr = PS[2]
                        mm(pr[32:33, :], ones[64:128, 0:1], sqt[64:128, :], r=[C, sqt], w=[pr])
                        op("dve", lambda e: e.tensor_reduce(out=km2[32:33, g:g + 1], in_=pr[32:33, :], axis=AX.X,
                                                            op=ALU.max), r=[pr], w=[km2])
                    projK(0)
                    for g in range(NG):
                        if g + 1 < NG:
                            projK(g + 1)
                        postK(g)
                    op("dve", lambda e: e.tensor_scalar(out=kmT[64:128, :], in0=kmT[64:128, :], scalar1=1.0 / 256,
                                                        scalar2=None, op0=ALU.mult), r=[kmT], w=[kmT])
                    op("dve", lambda e: e.tensor_reduce(out=km2[32:33, NG:NG + 1], in_=km2[32:33, 0:NG], axis=AX.X,
                                                        op=ALU.max), r=[km2], w=[km2])
                    for g in range(NG):
                        gs = slice(g * 512, (g + 1) * 512)
                        pb = PS[g % 2]
                        col = 1536 + h * 64
                        for c in range(8):
                            mm(pb[0:64, :], Wmb[:, c, col:col + 64], hT[:, c, gs], start=(c == 0), stop=(c == 7),
                               r=[(Wmb, c), (hT, g)], w=[pb], inc=(c == 7))
                        op("act", lambda e: e.activation(out=zsm[:, gs], in_=pb[0:64, :], func=AF.Silu), r=[pb],
                           w=[(zsm, g)])

                    def projQ(g):
                        pb = PS[g % 3]
                        col = h * 64
                        for c in range(8):
                            mm(pb[64:128, :], Wmb[:, c, col:col + 64], hT[:, c, g * 512:(g + 1) * 512],
                               start=(c == 0), stop=(c == 7), r=[(Wmb, c), (hT, g)], w=[pb], inc=(c == 7))

                    def postQ1(g):
                        gs = slice(g * 512, (g + 1) * 512)
                        pb = PS[g % 3]
                        qft = qf[g % 2]
                        op("act", lambda e: e.activation(out=qft[64:128, :], in_=pb[64:128, :], func=AF.Identity,
                                                         scale=0.125), r=[pb], w=[qft])
                        op("act", lambda e: e.activation(out=QaT[64:128, gs], in_=pb[64:128, :], func=AF.Identity,
                                                         scale=0.125), r=[pb], w=[(QaT, g)])
                        sqt = sqm[g % 2]
                        op("dve", lambda e: e.tensor_tensor(out=sqt[64:128, :], in0=qft[64:128, :],
                                                            in1=qft[64:128, :], op=ALU.mult), r=[qft], w=[sqt])
                        pgt = PS[4]
                        for tt in range(4):
                            mm(pgt[:, tt * 16:(tt + 1) * 16], qft[64:128, tt * 128:(tt + 1) * 128], kmT[64:128, :],
                               r=[qft, kmT], w=[pgt], inc=(tt == 3))
                        pr = PS[5]
                        mm(pr[32:33, :], ones[64:128, 0:1], sqt[64:128, :], r=[C, sqt], w=[pr])
                        op("dve", lambda e: e.tensor_tensor(out=gm4[:].rearrange("p a b -> p (a b)"), in0=pgt[:, 0:64],
                                                            in1=C[:, C_PB4 + g * 64:C_PB4 + (g + 1) * 64], op=ALU.add),
                           r=[pgt, C], w=[gm4])
                        op("act", lambda e: e.activation(out=sqt[32:33, :], in_=pr[32:33, :], func=AF.Sqrt,
                                                         scale=km2[32:33, NG:NG + 1]), r=[pr, km2], w=[sqt])
                        op("act", lambda e: e.activation(out=QaT[32:33, gs], in_=sqt[32:33, :], func=AF.Identity,
                                                         scale=-1.0), r=[sqt], w=[(QaT, g)])
                        for tt in range(4):
                            op("dve", lambda e: e.max(out=top84[:, tt, :], in_=gm4[:, tt, :]), r=[gm4],
                               w=[(top84, tt)])
                        mb_ = mbt4[g % 2]
                        for tt in range(4):
                            op("dve", lambda e: e.tensor_scalar(out=mb_[:, tt, :], in0=gm4[:, tt, :],
                                                                scalar1=top84[:, tt, 2:3], scalar2=NEG,
                                                                op0=ALU.is_lt, op1=ALU.mult),
                               r=[gm4, (top84, tt)], w=[(mb_, tt)])
                        for t2 in range(2):
                            own = 2 * g + t2
                            op("dve", lambda e: e.memset(mb_[:, 2 * t2:2 * t2 + 2, own:own + 1], 0.0),
                               r=[(mb_, 2 * t2), (mb_, 2 * t2 + 1)], w=[(mb_, 2 * t2), (mb_, 2 * t2 + 1)])

                    def postQ2(g):
                        gs = slice(g * 512, (g + 1) * 512)
                        mb_ = mbt4[g % 2]
                        pt_ = PS[3]
                        for tt in range(4):
                            tr(pt_[0:16, tt * 128:(tt + 1) * 128], mb_[:, tt, :], ident, r=[(mb_, tt), C], w=[pt_])
                        op("act", lambda e: e.activation(out=QaT[0:16, gs], in_=pt_[0:16, 0:512], func=AF.Copy),
                           r=[pt_], w=[(QaT, g)])
                    projQ(0)
                    if NG > 1:
                        projQ(1)
                    for g in range(NG):
                        postQ1(g)
                        if g + 2 < NG:
                            projQ(g + 2)
                        if g > 0:
                            postQ2(g - 1)
                    postQ2(NG - 1)
                    for qc in range(NB):
                        q0 = qc * 256
                        pO = PS[6 + qc % 2]
                        qk_ = (QaT, qc // 2)

                        def emitS(p):
                            pS = PS[nS[0] % 4]
                            nS[0] += 1
                            if p < qc:
                                for j in range(2):
                                    kt = 2 * p + j
                                    mm(pS[:, j * 256:(j + 1) * 256], KaT[:, kt * 128:(kt + 1) * 128],
                                       QaT[:, q0:q0 + 256], r=[(KaT, kt // 4), qk_], w=[pS], inc=(j == 1))
                            else:
                                kt = 2 * qc
                                mm(pS[:, 0:256], KaT[:, kt * 128:(kt + 1) * 128], QaT[:, q0:q0 + 256], start=True,
                                   stop=False, r=[(KaT, kt // 4), qk_], w=[pS], inc=False)
                                mm(pS[:, 0:128], identb[:], trib[:], start=False, stop=True, r=[identb, trib],
                                   w=[pS], inc=False)
                                kt = 2 * qc + 1
                                mm(pS[:, 256:384], KaT[:, kt * 128:(kt + 1) * 128], QaT[:, q0 + 128:q0 + 256],
                                   start=True, stop=False, r=[(KaT, kt // 4), qk_], w=[pS], inc=False)
                                mm(pS[:, 256:384], identb[:], trib[:], start=False, stop=True, r=[identb, trib],
                                   w=[pS])
                            return pS

                        LOOK = 3
                        pendq = [emitS(p_) for p_ in range(min(LOOK, qc + 1))]
                        for p in range(qc + 1):
                            pS = pendq.pop(0)
                            if p + LOOK <= qc:
                                pendq.append(emitS(p + LOOK))
                            ptile = PT[npt % 4]
                            npt += 1
                            wdt = 512 if p < qc else 384
                            op("act", lambda e: e.activation(out=ptile[:, 0:wdt], in_=pS[:, 0:wdt], func=AF.Exp),
                               r=[pS], w=[ptile])
                            if p < qc:
                                for j in range(2):
                                    kt = 2 * p + j
                                    mm(pO[0:65, 0:256], Vt[:, kt, h, :], ptile[:, j * 256:(j + 1) * 256],
                                       start=(kt == 0), stop=False, r=[(Vt, kt), ptile], w=[pO], inc=False)
                            else:
                                kt = 2 * qc
                                mm(pO[0:65, 0:256], Vt[:, kt, h, :], ptile[:, 0:256], start=(kt == 0), stop=False,
                                   r=[(Vt, kt), ptile], w=[pO], inc=False)
                                mm(pO[0:65, 128:256], Vt[:, kt + 1, h, :], ptile[:, 256:384], start=False, stop=True,
                                   r=[(Vt, kt + 1), ptile], w=[pO])
                            for _ in range(JUNK):
                                kb.eng["pe"].matmul(PS[5][:, :], lhsT=identb[:], rhs=KaT[:, 0:512], start=True, stop=True)
                        op("dve", lambda e: e.reciprocal(out=R(rden[64:65, :]), in_=pO[64:65, 0:256]), r=[pO],
                           w=[rden])
                        pB = PS[5]
                        mm(pB[0:64, 0:256], R(onesr[64:65, 0:64]), R(rden[64:65, :]), r=[onesr, rden], w=[pB])
                        tt1 = t1[qc % 2]
                        op("dve", lambda e: e.tensor_tensor(out=tt1[:], in0=pO[0:64, 0:256],
                                                            in1=zsm[:, q0:q0 + 256], op=ALU.mult),
                           r=[pO, (zsm, qc // 2)], w=[tt1])
                        op("dve", lambda e: e.tensor_tensor(out=ogm[:, q0:q0 + 256], in0=tt1[:], in1=pB[0:64, 0:256],
                                                            op=ALU.mult), r=[tt1, pB], w=[(ogm, qc)])
                    if "omb" in dbg and l == 0 and h == 0:
                        d = dbg_out("omb", [64, S])
                        tmp = T(st, "dbgomb", [64, S])
                        op("dve", lambda e: e.tensor_copy(out=tmp[:], in_=ogm[:]), r=[ogm], w=[tmp])
                        dma(d[:, :], tmp[:], r=[tmp])
                    dma(ogmb_d[h, :, :], ogm[:], r=[ogm], w=[kogmb])
                kb.barrier()

            for st in phase("fin"):
                Wpdn = T(st, "Wpdn", [128, 4, 1024], BF16)
                Wpmb = T(st, "Wpmb", [64, 8, 1024], BF16)
                Wout = T(st, "Wout", [128, 8, 1024], BF16)
                Wmg = T(st, "Wmg", [128, 8, 2048], BF16)
                dma(Wpdn[:], wpdn_d[l], w=[Wpdn], q="pool")
                dma(Wpmb[:], wpmb_d[l], w=[Wpmb], q="pool")
                for c in range(8):
                    dma(Wout[:, c, :], wout_d[l, :, c, :], w=[(Wout, c)], q="pool")
                for c in range(8):
                    dma(Wmg[:, c, :], wmg_d[l, :, c, :], w=[(Wmg, c)], q="pool")
                OGD = [T(st, "OGD%d" % i, [128, 4, 512], BF16) for i in range(2)]
                OGM = [T(st, "OGM0", [64, 8, 512], BF16)] * 2
                mixT = T(st, "mixT", [128, 8, 512], BF16)
                gd = [T(st, "gd%d" % i, [128, 512]) for i in range(2)]
                gmm = [T(st, "gmm%d" % i, [128, 512]) for i in range(2)]
                u1 = [T(st, "u1_%d" % i, [128, 512]) for i in range(2)]
                u2 = [T(st, "u2_%d" % i, [128, 512]) for i in range(2)]
                xr = [T(st, "xr%d" % i, [128, 1024]) for i in range(2)]
                res = [T(st, "res%d" % i, [128, 512]) for i in range(2)]
                junk2 = T(st, "junk2", [128, 512], BF16)
                GPl = T(st, "GPl", [128, 1024])
                dma(GPl[:], gp_d[l], r=[(kgp, l)], w=[GPl])
                ss2 = T(st, "ss2", [128, NT, 2])
                rs2 = T(st, "rs2", [128, NT])
                for g in range(NG):
                    gs = slice(g * 512, (g + 1) * 512)
                    ogd, ogmm = OGD[g % 2], OGM[g % 2]
                    dma(ogd[:], ogdn_d[:, :, gs].rearrange("h p s -> p h s"), r=[(kogdn, g)], w=[ogd])
                    dma(ogmm[:], ogmb_d[:, :, gs].rearrange("h p s -> p h s"), r=[kogmb], w=[ogmm])
                    for d_ in range(8):
                        ds_ = slice(d_ * 128, (d_ + 1) * 128)
                        pa, pb, pc, pd = PS[0 + 4 * (d_ % 2)], PS[1 + 4 * (d_ % 2)], PS[2 + 4 * (d_ % 2)], PS[3 + 4 * (d_ % 2)]
                        for h in range(4):
                            mm(pa[:, :], Wpdn[:, h, ds_], ogd[:, h, :], start=(h == 0), stop=(h == 3),
                               r=[Wpdn, ogd], w=[pa], inc=(h == 3))
                        for h in range(8):
                            mm(pb[:, :], Wpmb[:, h, ds_], ogmm[:, h, :], start=(h == 0), stop=(h == 7),
                               r=[Wpmb, ogmm], w=[pb], inc=(h == 7))
                        for c in range(8):
                            mm(pc[:, :], Wmg[:, c, ds_], hT[:, c, gs], start=(c == 0), stop=(c == 7),
                               r=[(Wmg, c), (hT, g)], w=[pc], inc=(c == 7))
                        for c in range(8):
                            mm(pd[:, :], Wmg[:, c, 1024 + d_ * 128:1024 + (d_ + 1) * 128], hT[:, c, gs],
                               start=(c == 0), stop=(c == 7), r=[(Wmg, c), (hT, g)], w=[pd], inc=(c == 7))
                        gdt, gmt, u1t, u2t = gd[d_ % 2], gmm[d_ % 2], u1[d_ % 2], u2[d_ % 2]
                        op("act", lambda e: e.activation(out=gdt[:], in_=pc[:, :], func=AF.Sigmoid), r=[pc], w=[gdt])
                        op("act", lambda e: e.activation(out=gmt[:], in_=pd[:, :], func=AF.Sigmoid), r=[pd], w=[gmt])
                        op("dve", lambda e: e.tensor_tensor(out=u1t[:], in0=pa[:, :], in1=gdt[:], op=ALU.mult),
                           r=[pa, gdt], w=[u1t])
                        op("dve", lambda e: e.tensor_tensor(out=u2t[:], in0=pb[:, :], in1=gmt[:], op=ALU.mult),
                           r=[pb, gmt], w=[u2t])
                        op("dve", lambda e: e.tensor_tensor(out=mixT[:, d_, :], in0=u1t[:], in1=u2t[:], op=ALU.add),
                           r=[u1t, u2t], w=[(mixT, d_)])
                    for tt in range(4):
                        t = g * 4 + tt
                        xrt = xr[t % 2]
                        dma(xrt[:], xin_d[t * 128:(t + 1) * 128, :], r=[(xin_k, t)], w=[xrt])
                        for hf in range(2):
                            pb = PS[(t % 2) * 2 + hf]
                            for d_ in range(8):
                                mm(pb[:, :], mixT[:, d_, tt * 128:(tt + 1) * 128], Wout[:, d_, hf * 512:(hf + 1) * 512],
                                   start=(d_ == 0), stop=(d_ == 7), r=[(mixT, d_), (Wout, d_)], w=[pb],
                                   inc=(d_ == 7))
                            op("act", lambda e: e.activation(out=junk2[:], in_=pb[:, :], func=AF.Square,
                                                             accum_out=ss2[:, t, hf:hf + 1]), r=[pb],
                               w=[junk2, (ss2, t)])
                        op("dve", lambda e: e.tensor_tensor(out=rs2[:, t:t + 1], in0=ss2[:, t, 0:1],
                                                            in1=ss2[:, t, 1:2], op=ALU.add), r=[(ss2, t)],
                           w=[(rs2, t)])
                        op("act", lambda e: e.activation(out=rs2[:, t:t + 1], in_=rs2[:, t:t + 1], func=AF.Sqrt,
                                                         scale=1.0 / D_MODEL, bias=epsT[:]), r=[(rs2, t), epsT],
                           w=[(rs2, t)])
                        op("dve", lambda e: e.reciprocal(out=rs2[:, t:t + 1], in_=rs2[:, t:t + 1]), r=[(rs2, t)],
                           w=[(rs2, t)])
                        for hf in range(2):
                            pb = PS[(t % 2) * 2 + hf]
                            hs = slice(hf * 512, (hf + 1) * 512)
                            rt = res[hf]
                            op("dve", lambda e: e.scalar_tensor_tensor(out=rt[:], in0=pb[:, :],
                                                                       scalar=rs2[:, t:t + 1], in1=GPl[:, hs],
                                                                       op0=ALU.mult, op1=ALU.mult),
                               r=[pb, (rs2, t), GPl], w=[rt])
                            op("dve", lambda e: e.tensor_tensor(out=xrt[:, hs], in0=rt[:], in1=xrt[:, hs],
                                                                op=ALU.add), r=[rt, (xrt, hf)], w=[(xrt, hf)])
                        dma(xout_d[t * 128:(t + 1) * 128, :], xrt[:], r=[xrt], w=[(xout_k, t)])
                kb.barrier()
          except _Stop:
            kb.barrier()
            curgen[0].close()
            break
        kb.finish()
        stuck = kb.simulate()
        print("deadlock check:", stuck if stuck else "ok")
        print("instructions:", kb.ninstr, "counts", kb.count, "dmas", kb.ndmaq)
    return nc, dbg_d


def _pc(w):
    sh = w.shape
    w = w.reshape(sh[:-2] + (8, 128, sh[-1]))
    return np.ascontiguousarray(np.swapaxes(w, -3, -2))


def host_layout(inp):
    f = lambda a: np.ascontiguousarray(np.asarray(a, dtype=np.float32))
    w_in = f(inp["w_in"])
    depth = w_in.shape[0]
    shared = {
        "wada": _pc(f(inp["w_ada"])),
        "bada": f(inp["b_ada"]).reshape(depth, 1, 3072),
        "gpre": f(inp["g_pre"]).reshape(depth, 1, 1024),
        "gpost": f(inp["g_post"]).reshape(depth, 1, 1024),
        "wdn": _pc(w_in[:, :, 0:2048]),
        "wba": _pc(w_in[:, :, 2048:2056]),
        "wmb": _pc(w_in[:, :, 2056:4104]),
        "wmg": _pc(w_in[:, :, 4104:6152]),
        "convw": np.ascontiguousarray(
            f(inp["conv_w"]).transpose(0, 2, 1).reshape(depth, 12, 128, 4).transpose(0, 2, 1, 3)
        ).reshape(depth, 128, 48),
        "alog": np.ascontiguousarray(np.broadcast_to(f(inp["a_log"])[:, None, :], (depth, 128, 4))),
        "dtb": np.ascontiguousarray(np.broadcast_to(f(inp["dt_bias"])[:, None, :], (depth, 128, 4))),
        "dng": f(inp["dn_norm_g"]).reshape(depth, 128, 1),
        "wpdn": np.ascontiguousarray(f(inp["w_proj_dn"]).reshape(depth, 4, 128, 1024).transpose(0, 2, 1, 3)),
        "wpmb": np.ascontiguousarray(f(inp["w_proj_mb"]).reshape(depth, 8, 64, 1024).transpose(0, 2, 1, 3)),
        "wout": _pc(f(inp["w_out"])),
        "consts": make_consts(),
    }
    x = f(inp["x"])
    c = f(inp["c"])
    maps = []
    for b in range(x.shape[0]):
        m = dict(shared)
        m["x"] = x[b]
        m["cT"] = np.ascontiguousarray(c[b].reshape(8, 128).T)
        maps.append(m)
    return maps


_CACHE = {}


def kernel(**inputs):
    x = np.asarray(inputs["x"])
    B, S, _ = x.shape
    depth = np.asarray(inputs["w_in"]).shape[0]
    key = (S, depth)
    if key not in _CACHE:
        _CACHE[key] = build(S=S, DEPTH=depth)[0]
    nc = _CACHE[key]
    maps = host_layout(inputs)
    res = run_bass_kernel_spmd(nc, maps, core_ids=list(range(B)))
    return np.stack([np.asarray(r["y"], dtype=np.float32) for r in res.results], axis=0)
```

```python
from contextlib import ExitStack

import numpy as np
import concourse.bass as bass
import concourse.mybir as mybir
from concourse.bass_utils import run_bass_kernel_spmd

F32 = mybir.dt.float32
BF16 = mybir.dt.bfloat16
F32R = mybir.dt.float32r


def R(ap):
    return ap.bitcast(F32R)
AF = mybir.ActivationFunctionType
ALU = mybir.AluOpType
AX = mybir.AxisListType

D_MODEL = 1024
NEG = -30000.0
EPS = 1e-6


class SV:
    def __init__(self, tile, j, n=1):
        self.tile, self.sub = tile, j
        self.ap = tile[:, j * 128:(j + n) * 128]

    def __getitem__(self, idx):
        return self.ap[idx]


class KB:
    NRING = {"sp": 6, "pool": 4}

    def __init__(self, nc, stack):
        self.nc = nc
        self.eng = {"pe": nc.tensor, "act": nc.scalar, "dve": nc.vector,
                    "pool": nc.gpsimd, "sp": nc.sync}
        self.sem = {}
        for e in ("pe", "act", "dve", "pool"):
            self.sem[e] = stack.enter_context(nc.semaphore("s_" + e))
        self.ring = {q: [stack.enter_context(nc.semaphore("s_dma_%s%d" % (q, i))) for i in range(n)]
                     for q, n in self.NRING.items()}
        self.ndmaq = {q: 0 for q in self.NRING}
        self.count = {e: 0 for e in ("pe", "act", "dve", "pool")}
        self.waited = {}
        self.track = {}
        self.ninstr = 0
        self.streams = {e: [] for e in self.eng}
        self.psum_ids = set()

    def _semof(self, dep):
        if dep[0] == "e":
            return self.sem[dep[1]], ("e", dep[1])
        return self.ring[dep[1][0]][dep[1][1]], ("d", dep[1])

    def _wait(self, e, dep):
        sem, sk = self._semof(dep)
        val = dep[2]
        k = (e, sk)
        if self.waited.get(k, 0) >= val:
            return
        self.eng[e].wait_ge(sem, val)
        self.streams[e].append(("wait", sk, val))
        self.ninstr += 1
        self.waited[k] = val

    @staticmethod
    def _keys(items):
        out = []
        for it in items:
            if isinstance(it, SV):
                out.append((id(it.tile), it.sub))
            elif isinstance(it, tuple):
                out.append((id(it[0]), it[1]))
            else:
                out.append((id(it), None))
        return out

    def _conflicts(self, key):
        tid, sub = key
        d = self.track.get(tid)
        if d is None:
            return []
        if sub is None:
            return list(d.values())
        res = []
        if sub in d:
            res.append(d[sub])
        if None in d:
            res.append(d[None])
        return res

    def _entry(self, key):
        tid, sub = key
        d = self.track.setdefault(tid, {})
        if sub is None:
            ent = {"w": [], "r": []}
            for v in d.values():
                ent["w"] += v["w"]
                ent["r"] += v["r"]
            d.clear()
            d[None] = ent
            return ent
        if sub not in d:
            ent = {"w": [], "r": []}
            if None in d:
                ent["w"] = list(d[None]["w"])
                ent["r"] = list(d[None]["r"])
            d[sub] = ent
        return d[sub]

    def _deps(self, e, reads, writes):
        deps = []
        for k in reads:
            for ent in self._conflicts(k):
                for w in ent["w"]:
                    deps.append((w, "raw"))
        for k in writes:
            for ent in self._conflicts(k):
                for w in ent["w"]:
                    deps.append((w, "waw"))
                for r in ent["r"]:
                    deps.append((r, "war"))
        out = []
        for dep, kind in deps:
            if dep[0] == "e" and dep[1] == e:
                if e == "pe":
                    continue
            out.append(dep)
        return out

    @staticmethod
    def _prune(lst):
        best = {}
        for d in lst:
            kk = (d[0], d[1])
            if kk not in best or best[kk][2] < d[2]:
                best[kk] = d
        return list(best.values())

    def _record(self, me, reads, writes):
        for k in reads:
            ent = self._entry(k)
            ent["r"].append(me)
            if len(ent["r"]) > 16:
                ent["r"] = self._prune(ent["r"])
        for k in writes:
            ent = self._entry(k)
            ent["w"] = [me]
            ent["r"] = []

    def _rw(self, r, w):
        reads, writes = [], []
        for k in self._keys(r):
            if k[0] in self.psum_ids:
                writes.append((k[0], None))
            else:
                reads.append(k)
        for k in self._keys(w):
            writes.append((k[0], None) if k[0] in self.psum_ids else k)
        return reads, writes

    def op(self, e, fn, r=(), w=(), inc=True):
        reads, writes = self._rw(r, w)
        for dep in self._deps(e, reads, writes):
            self._wait(e, dep)
        ins = fn(self.eng[e])
        self.ninstr += 1
        if inc:
            self.count[e] += 1
            ins.then_inc(self.sem[e], 1)
            self.streams[e].append(("inc", ("e", e), 1))
            me = ("e", e, self.count[e])
        else:
            me = ("e", e, self.count[e] + 1)
        self._record(me, reads, writes)
        return ins

    def dma(self, out, in_, r=(), w=(), q="sp", **kw):
        e = q
        reads = self._keys(r)
        writes = self._keys(w)
        k = self.ndmaq[q]
        nr = self.NRING[q]
        slot = k % nr
        gen = k // nr
        for dep in self._deps(e, reads, writes):
            self._wait(e, dep)
        if gen > 0:
            self._wait(e, ("d", (q, slot), 16 * gen))
        ins = self.eng[e].dma_start(out=out, in_=in_, **kw)
        ins.then_inc(self.ring[q][slot], 16)
        self.streams[e].append(("inc", ("d", (q, slot)), 16))
        self.ndmaq[q] += 1
        self.ninstr += 1
        me = ("d", (q, slot), 16 * (gen + 1))
        self._record(me, reads, writes)
        return ins

    def _alldma(self):
        out = []
        for q, nr in self.NRING.items():
            n = self.ndmaq[q]
            for slot in range(nr):
                cnt = (n - 1 - slot) // nr + 1 if n > slot else 0
                if cnt > 0:
                    out.append(("d", (q, slot), 16 * cnt))
        return out

    def barrier(self):
        for e in ("pe", "act", "dve", "pool", "sp"):
            for o in ("pe", "act", "dve", "pool"):
                if o != e and self.count[o] > 0:
                    self._wait(e, ("e", o, self.count[o]))
            for dep in self._alldma():
                self._wait(e, dep)
        self.track = {}

    def finish(self):
        for dep in self._alldma():
            self._wait("sp", dep)

    def simulate(self):
        sems = {}
        pc = {e: 0 for e in self.streams}
        progress = True
        while progress:
            progress = False
            for e, st in self.streams.items():
                while pc[e] < len(st):
                    kind, sk, val = st[pc[e]]
                    if kind == "wait":
                        if sems.get(sk, 0) < val:
                            break
                    else:
                        sems[sk] = sems.get(sk, 0) + val
                    pc[e] += 1
                    progress = True
        stuck = {e: (pc[e], len(st), st[pc[e]], sems.get(st[pc[e]][1], 0)) for e, st in self.streams.items()
                 if pc[e] < len(st)}
        return stuck


C_IDENT, C_U, C_MBD, C_MOFF, C_TRI, C_ONES, C_PB = 0, 128, 256, 384, 512, 640, 768
C_PB4 = 768
NCONST = 768 + 512


def make_consts():
    c = np.zeros((128, NCONST), np.float32)
    i = np.arange(128)
    c[:, C_IDENT:C_IDENT + 128] = np.eye(128)
    c[:, C_U:C_U + 128] = (i[:, None] <= i[None, :])
    c[:, C_MBD:C_MBD + 128] = (i[:, None] < i[None, :]) & ((i[:, None] // 64) == (i[None, :] // 64))
    c[:, C_MOFF:C_MOFF + 128] = (i[:, None] < 64) & (i[None, :] >= 64)
    c[:, C_TRI:C_TRI + 128] = np.where(i[:, None] <= i[None, :], 0.0, NEG)
    c[:, C_ONES:C_ONES + 128] = 1.0
    pb = np.zeros((16, 16), np.float32)
    for own in range(16):
        pb[own, own:] = -1e30
    for g in range(8):
        for tt in range(4):
            own = (4 * g + tt) // 2
            c[:, C_PB4 + g * 64 + tt * 16:C_PB4 + g * 64 + (tt + 1) * 16] = pb[own][None, :]
    return c


def build(S=4096, DEPTH=2, LS=2, dbg=None, phases=("p1", "dn", "mb", "fin"), stop=0, JUNK=0):
    dbg = dbg or set()
    NT = S // 128
    NG = S // 512
    NB = S // 256
    nc = bass.Bass("TRN2", target_bir_lowering=False)

    def din(name, shape, dt=F32):
        return nc.dram_tensor(name, shape, dt, kind="ExternalInput").ap()

    x_d = din("x", [S, 1024])
    cT_d = din("cT", [128, 8])
    wada_d = din("wada", [DEPTH, 128, 8, 3072])
    bada_d = din("bada", [DEPTH, 1, 3072])
    gpre_d = din("gpre", [DEPTH, 1, 1024])
    gpost_d = din("gpost", [DEPTH, 1, 1024])
    wdn_d = din("wdn", [DEPTH, 128, 8, 2048])
    wba_d = din("wba", [DEPTH, 128, 8, 8])
    wmb_d = din("wmb", [DEPTH, 128, 8, 2048])
    wmg_d = din("wmg", [DEPTH, 128, 8, 2048])
    convw_d = din("convw", [DEPTH, 128, 48])
    alog_d = din("alog", [DEPTH, 128, 4])
    dtb_d = din("dtb", [DEPTH, 128, 4])
    dng_d = din("dng", [DEPTH, 128, 1])
    wpdn_d = din("wpdn", [DEPTH, 128, 4, 1024])
    wpmb_d = din("wpmb", [DEPTH, 64, 8, 1024])
    wout_d = din("wout", [DEPTH, 128, 8, 1024])
    consts_d = din("consts", [128, NCONST])
    y_d = nc.dram_tensor("y", [S, 1024], F32, kind="ExternalOutput").ap()
    xmid_d = nc.dram_tensor("xmid", [S, 1024], F32, kind="Internal").ap()
    ogdn_d = nc.dram_tensor("ogdn", [4, 128, S], BF16, kind="Internal").ap()
    ogmb_d = nc.dram_tensor("ogmb", [8, 64, S], BF16, kind="Internal").ap()
    dbg_d = {}

    class _K:
        pass
    kx, kmid, ky, kogdn, kogmb = _K(), _K(), _K(), _K(), _K()

    def dbg_out(name, shape):
        dbg_d[name] = nc.dram_tensor("dbg_" + name, shape, F32, kind="ExternalOutput").ap()
        return dbg_d[name]

    with ExitStack() as gst:
        kb = KB(nc, gst)
        op, dma = kb.op, kb.dma
        gst.enter_context(nc.allow_low_precision("float32r (1-pass PE) operands for non-critical fp32 matmuls"))

        def mm(out, lhsT, rhs, start=True, stop=True, r=(), w=(), inc=True):
            return op("pe", lambda e: e.matmul(out, lhsT=lhsT, rhs=rhs, start=start, stop=stop),
                      r=r, w=w, inc=inc)

        def tr(out, in_, ident, r=(), w=()):
            return op("pe", lambda e: e.transpose(out=out, in_=in_, identity=ident), r=r, w=w)

        uid = [0]

        def T(st, name, shape, dt=F32):
            uid[0] += 1
            return st.enter_context(nc.sbuf_tensor("sb%d_%s" % (uid[0], name), shape, dt))

        class _Stop(Exception):
            pass

        def ck(level):
            if stop == level:
                raise _Stop()

        curgen = [None]

        def phase(name):
            if name in phases:
                st_ = ExitStack()
                curgen[0] = st_
                yield st_
                st_.close()

        PS = [gst.enter_context(nc.psum_tensor("ps%d" % i, [128, 512], F32)) for i in range(8)]
        kb.psum_ids = {id(p) for p in PS}
        C = T(gst, "consts", [128, NCONST])
        dma(C[:], consts_d[:, :], w=[C])
        ident = C[:, C_IDENT:C_IDENT + 128]
        U = C[:, C_U:C_U + 128]
        Mbd = C[:, C_MBD:C_MBD + 128]
        Moff = C[:, C_MOFF:C_MOFF + 128]
        ones = C[:, C_ONES:C_ONES + 128]
        identb = T(gst, "identb", [128, 128], BF16)
        trib = T(gst, "trib", [128, 128], BF16)
        epsT = T(gst, "epsT", [128, 1])
        op("dve", lambda e: e.tensor_copy(out=identb[:], in_=ident), r=[C], w=[identb])
        op("dve", lambda e: e.tensor_copy(out=trib[:], in_=C[:, C_TRI:C_TRI + 128]), r=[C], w=[trib])
        op("dve", lambda e: e.memset(epsT[:], EPS), w=[epsT])
        onesr = T(gst, "onesr", [128, 128])
        op("dve", lambda e: e.tensor_copy(out=R(onesr[:]), in_=ones), r=[C], w=[onesr])
        AB = [T(gst, "AB%d" % l, [128, 16]) for l in range(DEPTH)]
        gp_d = nc.dram_tensor("gp_scr", [DEPTH, 128, 1024], F32, kind="Internal").ap()
        kgp = _K()

        with ExitStack() as st:
            cT = T(st, "cT", [128, 8])
            sc = T(st, "sc", [128, 8])
            dma(cT[:], cT_d[:, :], w=[cT])
            op("act", lambda e: e.activation(out=sc[:], in_=cT[:], func=AF.Silu), r=[cT], w=[sc])
            wa = [T(st, "wa%d" % i, [128, 8, 512]) for i in range(2)]
            row = T(st, "row", [1, 3072])
            bada = T(st, "bada", [1, 3072])
            gpr = T(st, "gpr", [1, 1024])
            gpo = T(st, "gpo", [1, 1024])
            arow = T(st, "arow", [1, 1024])
            gprow = T(st, "gprow", [1, 1024])
            gptmp = T(st, "gptmp", [128, 1024])
            nwa = 0
            for l in range(DEPTH):
                dma(bada[:], bada_d[l], w=[bada])
                dma(gpr[:], gpre_d[l], w=[gpr])
                dma(gpo[:], gpost_d[l], w=[gpo])
                for cg in range(6):
                    wt = wa[nwa % 2]
                    nwa += 1
                    dma(wt[:], wada_d[l, :, :, cg * 512:(cg + 1) * 512], w=[wt])
                    pb = PS[cg % 2]
                    for c in range(8):
                        mm(pb[0:1, :], sc[:, c:c + 1], wt[:, c, :], start=(c == 0), stop=(c == 7),
                           r=[sc, wt], w=[pb], inc=(c == 7))
                    op("dve", lambda e: e.tensor_tensor(out=row[0:1, cg * 512:(cg + 1) * 512], in0=pb[0:1, :],
                                                        in1=bada[0:1, cg * 512:(cg + 1) * 512], op=ALU.add),
                       r=[pb, bada], w=[(row, cg)])
                op("dve", lambda e: e.scalar_tensor_tensor(out=arow[:], in0=row[0:1, 1024:2048], scalar=1.0,
                                                           in1=gpr[:], op0=ALU.add, op1=ALU.mult),
                   r=[row, gpr], w=[arow])
                op("dve", lambda e: e.tensor_tensor(out=gprow[:], in0=row[0:1, 2048:3072], in1=gpo[:], op=ALU.mult),
                   r=[row, gpo], w=[gprow])
                pc = PS[2]
                for c in range(8):
                    mm(pc[:, c:c + 1], arow[0:1, c * 128:(c + 1) * 128], ones[0:1, 0:1], r=[arow, C], w=[pc], inc=False)
                for c in range(8):
                    mm(pc[:, 8 + c:9 + c], row[0:1, c * 128:(c + 1) * 128], ones[0:1, 0:1], r=[row, C], w=[pc],
                       inc=(c == 7))
                op("dve", lambda e: e.tensor_copy(out=AB[l][:], in_=pc[:, 0:16]), r=[pc], w=[AB[l]])
                for hf in range(2):
                    pg = PS[3 + hf]
                    mm(pg[:, :], ones[0:1, 0:128], gprow[0:1, hf * 512:(hf + 1) * 512], r=[gprow, C], w=[pg])
                    op("act", lambda e: e.activation(out=gptmp[:, hf * 512:(hf + 1) * 512], in_=pg[:, :], func=AF.Copy),
                       r=[pg], w=[(gptmp, hf)])
                dma(gp_d[l], gptmp[:], r=[gptmp], w=[(kgp, l)])
            kb.barrier()

        hT = T(gst, "hT", [128, 8, S], BF16)

        for l in range(DEPTH):
          try:
            xin_d = x_d if l == 0 else xmid_d
            xout_d = y_d if l == DEPTH - 1 else xmid_d
            xin_k = kx if l == 0 else kmid
            xout_k = ky if l == DEPTH - 1 else kmid

            for st in phase("p1"):
                xt = [T(st, "xt%d" % i, [128, 1024]) for i in range(4)]
                xn = [T(st, "xn%d" % i, [128, 1024]) for i in range(3)]
                junk = T(st, "junk", [128, 1024], BF16)
                ss = T(st, "ss", [128, NT])
                rstd = T(st, "rstd", [128, NT])
                for t in range(NT):
                    xtt, xnt = xt[t % 4], xn[t % 3]
                    dma(xtt[:], xin_d[t * 128:(t + 1) * 128, :], r=[(xin_k, t)], w=[xtt])
                    op("act", lambda e: e.activation(out=junk[:], in_=xtt[:], func=AF.Square,
                                                     accum_out=ss[:, t:t + 1]), r=[xtt], w=[junk, (ss, t)])
                    op("act", lambda e: e.activation(out=rstd[:, t:t + 1], in_=ss[:, t:t + 1], func=AF.Sqrt,
                                                     scale=1.0 / D_MODEL, bias=epsT[:]), r=[(ss, t), epsT],
                       w=[(rstd, t)])
                    op("dve", lambda e: e.reciprocal(out=rstd[:, t:t + 1], in_=rstd[:, t:t + 1]), r=[(rstd, t)],
                       w=[(rstd, t)])
                    op("dve", lambda e: e.tensor_scalar(out=xnt[:], in0=xtt[:], scalar1=rstd[:, t:t + 1],
                                                        scalar2=None, op0=ALU.mult), r=[xtt, (rstd, t)], w=[xnt])
                    for hf in range(2):
                        pb = PS[(t % 2) * 2 + hf]
                        for cc in range(4):
                            c = hf * 4 + cc
                            tr(pb[:, cc * 128:(cc + 1) * 128], xnt[:, c * 128:(c + 1) * 128], ident, r=[xnt, C],
                               w=[pb])
                        for cc in range(4):
                            c = hf * 4 + cc
                            eng = "act" if hf == 0 else "dve"
                            if eng == "act":
                                op("act", lambda e: e.activation(out=hT[:, c, t * 128:(t + 1) * 128],
                                                                 in_=pb[:, cc * 128:(cc + 1) * 128], func=AF.Identity,
                                                                 scale=AB[l][:, c:c + 1], bias=AB[l][:, 8 + c:9 + c]),
                                   r=[pb, AB[l]], w=[(hT, t // 4)])
                            else:
                                op("dve", lambda e: e.tensor_scalar(out=hT[:, c, t * 128:(t + 1) * 128],
                                                                    in0=pb[:, cc * 128:(cc + 1) * 128],
                                                                    scalar1=AB[l][:, c:c + 1],
                                                                    scalar2=AB[l][:, 8 + c:9 + c],
                                                                    op0=ALU.mult, op1=ALU.add),
                                   r=[pb, AB[l]], w=[(hT, t // 4)])
                kb.barrier()
            if "hT" in dbg and l == 0:
                with ExitStack() as st:
                    d = dbg_out("hT", [128, 8, S])
                    tmp = T(st, "dbgtmp", [128, 8, S])
                    op("dve", lambda e: e.tensor_copy(out=tmp[:], in_=hT[:]), r=[hT], w=[tmp])
                    dma(d[:, :, :], tmp[:], r=[tmp])
                    kb.barrier()

            for st in phase("dn"):
                Wdn = T(st, "Wdn", [128, 8, 2048], BF16)
                Wba = T(st, "Wba", [128, 8, 8], BF16)
                for c in range(8):
                    dma(Wdn[:, c, :], wdn_d[l, :, c, :], w=[(Wdn, c)], q="pool")
                dma(Wba[:], wba_d[l], w=[Wba], q="pool")
                convw = T(st, "convw", [128, 48])
                alog = T(st, "alog", [128, 4])
                dtb = T(st, "dtb", [128, 4])
                dng = T(st, "dng", [128, 1])
                dma(convw[:], convw_d[l], w=[convw])
                dma(alog[:], alog_d[l], w=[alog])
                dma(dtb[:], dtb_d[l], w=[dtb])
                dma(dng[:], dng_d[l], w=[dng])
                st2 = ExitStack()
                BETA = T(st, "BETA", [128, NT, 4])
                NBETA = T(st, "NBETA", [128, NT, 4])
                GRAW = T(st, "GRAW", [128, NT, 4])
                GC = T(st, "GC", [128, NT, 4])
                NEXPG = T(st, "NEXPG", [128, NT, 4])
                negA = T(st, "negA", [128, 4])
                BG = T(st2, "BG", [128, NT, 8])
                AA = T(st2, "AA", [128, NT, 4])
                AX_ = T(st2, "AXs", [128, NT, 4])
                pbg = PS[0]
                for t in range(NT):
                    for c in range(8):
                        mm(pbg[:, t * 8:(t + 1) * 8], hT[:, c, t * 128:(t + 1) * 128], Wba[:, c, :],
                           start=(c == 0), stop=(c == 7), r=[(hT, t // 4), Wba], w=[pbg], inc=(c == 7))
                op("dve", lambda e: e.tensor_copy(out=BG[:].rearrange("p t e -> p (t e)"), in_=pbg[:, 0:NT * 8]),
                   r=[pbg], w=[BG])
                op("act", lambda e: e.activation(out=BETA[:], in_=BG[:, :, 0:4], func=AF.Sigmoid), r=[BG], w=[BETA])
                op("dve", lambda e: e.tensor_scalar(out=NBETA[:], in0=BETA[:], scalar1=-1.0, scalar2=None,
                                                    op0=ALU.mult), r=[BETA], w=[NBETA])
                for h in range(4):
                    op("dve", lambda e: e.tensor_scalar(out=AA[:, :, h], in0=BG[:, :, 4 + h], scalar1=dtb[:, h:h + 1],
                                                        scalar2=None, op0=ALU.add), r=[BG, dtb], w=[AA])
                op("act", lambda e: e.activation(out=AX_[:], in_=AA[:], func=AF.Abs), r=[AA], w=[AX_])
                op("act", lambda e: e.activation(out=AX_[:], in_=AX_[:], func=AF.Exp, scale=-1.0), r=[AX_], w=[AX_])
                op("dve", lambda e: e.tensor_scalar(out=AX_[:], in0=AX_[:], scalar1=1.0, scalar2=None, op0=ALU.add),
                   r=[AX_], w=[AX_])
                op("act", lambda e: e.activation(out=AX_[:], in_=AX_[:], func=AF.Ln), r=[AX_], w=[AX_])
                op("dve", lambda e: e.scalar_tensor_tensor(out=AA[:], in0=AA[:], scalar=0.0, in1=AX_[:],
                                                           op0=ALU.max, op1=ALU.add), r=[AA, AX_], w=[AA])
                op("act", lambda e: e.activation(out=negA[:], in_=alog[:], func=AF.Exp), r=[alog], w=[negA])
                op("dve", lambda e: e.tensor_scalar(out=negA[:], in0=negA[:], scalar1=-1.0, scalar2=None,
                                                    op0=ALU.mult), r=[negA], w=[negA])
                for h in range(4):
                    op("dve", lambda e: e.tensor_scalar(out=GRAW[:, :, h], in0=AA[:, :, h], scalar1=negA[:, h:h + 1],
                                                        scalar2=None, op0=ALU.mult), r=[AA, negA], w=[GRAW])
                pgc = PS[1]
                mm(pgc[:, 0:NT * 4], U, GRAW[:].rearrange("p t e -> p (t e)"), r=[C, GRAW], w=[pgc])
                op("dve", lambda e: e.tensor_copy(out=GC[:].rearrange("p t e -> p (t e)"), in_=pgc[:, 0:NT * 4]),
                   r=[pgc], w=[GC])
                op("act", lambda e: e.activation(out=NEXPG[:], in_=GC[:], func=AF.Exp), r=[GC], w=[NEXPG])
                op("dve", lambda e: e.tensor_scalar(out=NEXPG[:], in0=NEXPG[:], scalar1=-1.0, scalar2=None,
                                                    op0=ALU.mult), r=[NEXPG], w=[NEXPG])
                if "graw" in dbg and l == 0:
                    d = dbg_out("graw", [128, NT, 4])
                    dma(d[:, :, :], GRAW[:], r=[GRAW])
                    d = dbg_out("beta", [128, NT, 4])
                    dma(d[:, :, :], BETA[:], r=[BETA])

                kb.barrier()
                st2.close()
                ck(1)
                pre = [T(st, "pre%d" % i, [128, 515]) for i in range(3)]
                halo = T(st, "halo", [128, 12, 3])
                op("dve", lambda e: e.memset(halo[:], 0.0), w=[halo])
                qkv = [[T(st, "qkv%d_%d" % (i, j), [128, 512]) for j in range(3)] for i in range(2)]
                zs = [T(st, "zs%d" % i, [128, 512], BF16) for i in range(4)]
                cv = [T(st, "cv0", [128, 512])] * 2
                sq = [T(st, "sq0", [128, 512])] * 2
                oTg = [T(st, "oTg%d" % i, [128, 512]) for i in range(2)]
                ogb = [T(st, "ogb0", [128, 512], BF16)] * 2
                Sst = [[T(st, "S%d_%d" % (h, i), [128, 128]) for i in range(2)] for h in range(4)]
                for h in range(4):
                    op("dve", lambda e: e.tensor_scalar(out=R(Sst[h][0][:]), in0=ident, scalar1=0.0, scalar2=None,
                                                        op0=ALU.mult), r=[C], w=[Sst[h][0]])
                spar = [0, 0, 0, 0]
                NCH = 8
                ALIAS = {"Dm": 0, "tq": 0, "DecT": 1, "Xo": 1, "Pe": 2, "ktok": 3, "Xe": 3, "NoffT": 3, "Po": 4,
                         "B": 5, "BT": 6, "WbdT": 6, "Noff": 7, "Z1": 7, "ExpG": 8, "XTo": 8, "Lm": 9, "XTe": 9}
                scr = []
                for i in range(NCH):
                    wide = T(st, "s%d" % i, [128, 9 * 128])
                    plain = T(st, "s%da" % i, [128, 128])
                    d_ = {n: (SV(wide, j - 1) if j > 0 else plain) for n, j in ALIAS.items()}
                    d_["XPo"] = SV(wide, 0, 2)
                    d_["XPe"] = SV(wide, 2, 2)
                    scr.append(d_)
                OUTN = ["W", "QKdT", "kdec", "qg", "vtok", "kTc"]
                outs = [{n: T(st, "o%d_%s" % (i, n), [128, 128]) for n in OUTN} for i in range(NCH)]
                EGL = T(st, "EGL", [128, NCH])
                Yr = [T(st, "Yr%d" % i, [128, 128]) for i in range(2)]
                Vn = [T(st, "Vn%d" % i, [128, 128]) for i in range(2)]

                def prep8(chains, pump=lambda: None):
                    pump_on = [False]

                    def each(fn):
                        for ci, ch in enumerate(chains):
                            fn(ch, ch["s"], ch["o"], PS[ch["i"]])
                            if ci % 2 == 1 and pump_on[0]:
                                pump(1)

                    def f(ch, s, o, pb):
                        h, n, sl = ch["h"], ch["n"], ch["sl"]
                        kT, vT = ch["kT"], ch["vT"]
                        tr(pb[:, 0:128], kT[:, sl], ident, r=[kT, C], w=[pb])
                        tr(pb[:, 128:256], vT[:, sl], ident, r=[vT, C], w=[pb])
                        mm(pb[:, 256:384], GRAW[:, n, h:h + 1].to_broadcast([128, 128]), U, r=[GRAW, C], w=[pb])
                        op("act", lambda e: e.activation(out=R(s["ktok"][:]), in_=pb[:, 0:128], func=AF.Copy),
                           r=[pb], w=[s["ktok"]])
                        op("dve", lambda e: e.tensor_copy(out=o["vtok"][:], in_=pb[:, 128:256]), r=[pb],
                           w=[o["vtok"]])
                        op("dve", lambda e: e.tensor_scalar(out=s["Dm"][:], in0=pb[:, 256:384],
                                                            scalar1=GC[:, n, h:h + 1], scalar2=0.0,
                                                            op0=ALU.subtract, op1=ALU.min),
                           r=[pb, GC], w=[s["Dm"]])
                        op("act", lambda e: e.activation(out=R(s["ExpG"][:]), in_=pb[:, 256:384], func=AF.Exp),
                           r=[pb], w=[s["ExpG"]])
                    each(f)

                    def f(ch, s, o, pb):
                        op("act", lambda e: e.activation(out=R(s["DecT"][:]), in_=s["Dm"][:], func=AF.Exp),
                           r=[s["Dm"]], w=[s["DecT"]])
                    each(f)

                    def f(ch, s, o, pb):
                        h, n, sl = ch["h"], ch["n"], ch["sl"]
                        kT, qT = ch["kT"], ch["qT"]
                        mm(pb[:, 0:128], R(kT[:, sl]), R(kT[:, sl]), r=[kT], w=[pb], inc=False)
                        mm(pb[:, 128:256], R(kT[:, sl]), R(qT[:, sl]), r=[kT, qT], w=[pb])
                        op("dve", lambda e: e.tensor_tensor(out=R(s["Lm"][:]), in0=pb[:, 0:128], in1=s["DecT"][:],
                                                            op=ALU.mult), r=[pb, s["DecT"]], w=[s["Lm"]])
                        op("dve", lambda e: e.tensor_tensor(out=s["tq"][:], in0=pb[:, 128:256], in1=s["DecT"][:],
                                                            op=ALU.mult), r=[pb, s["DecT"]], w=[s["tq"]])
                        op("dve", lambda e: e.scalar_tensor_tensor(out=R(s["B"][:]), in0=s["Lm"][:],
                                                                   scalar=NBETA[:, n, h:h + 1], in1=Mbd,
                                                                   op0=ALU.mult, op1=ALU.mult),
                           r=[s["Lm"], NBETA, C], w=[s["B"]])
                        op("dve", lambda e: e.scalar_tensor_tensor(out=R(s["Noff"][:]), in0=s["Lm"][:],
                                                                   scalar=BETA[:, n, h:h + 1], in1=Moff,
                                                                   op0=ALU.mult, op1=ALU.mult),
                           r=[s["Lm"], BETA, C], w=[s["Noff"]])
                        op("pool", lambda e: e.tensor_tensor(out=R(o["QKdT"][:]), in0=s["tq"][:], in1=U,
                                                             op=ALU.mult), r=[s["tq"], C], w=[o["QKdT"]])
                        op("act", lambda e: e.activation(out=R(o["kdec"][:]), in_=s["ktok"][:], func=AF.Identity,
                                                         scale=s["DecT"][:, 127:128]), r=[s["ktok"], s["DecT"]],
                           w=[o["kdec"]])
                        op("dve", lambda e: e.tensor_tensor(out=R(o["qg"][:]), in0=qT[:, sl], in1=s["ExpG"][:],
                                                            op=ALU.mult), r=[qT, s["ExpG"]], w=[o["qg"]])
                        op("dve", lambda e: e.tensor_copy(out=R(o["kTc"][:]), in_=kT[:, sl]), r=[kT], w=[o["kTc"]])
                        op("dve", lambda e: e.tensor_copy(out=EGL[:, ch["i"]:ch["i"] + 1],
                                                          in_=s["ExpG"][:, 127:128]),
                           r=[s["ExpG"]], w=[(EGL, ch["i"])])
                    each(f)

                    pump_on[0] = True
                    def f(ch, s, o, pb):
                        tr(pb[:, 256:384], s["B"][:], ident, r=[s["B"], C], w=[pb])
                        op("act", lambda e: e.activation(out=R(s["BT"][:]), in_=pb[:, 256:384], func=AF.Copy),
                           r=[pb], w=[s["BT"]])
                        op("dve", lambda e: e.tensor_tensor(out=R(s["Pe"][:]), in0=s["B"][:], in1=ident,
                                                            op=ALU.add), r=[s["B"], C], w=[s["Pe"]])
                    each(f)

                    def f(ch, s, o, pb):
                        mm(pb[:, 0:128], R(s["BT"][:]), R(s["B"][:]), r=[s["BT"], s["B"]], w=[pb])
                        op("act", lambda e: e.activation(out=R(s["Xo"][:]), in_=pb[:, 0:128], func=AF.Copy),
                           r=[pb], w=[s["Xo"]])
                    each(f)
                    pump()
                    for j in range(1, 6):
                        odd = (j % 2 == 1)
                        Xc, Pc, XTc, XPc = ("Xo", "Pe", "XTo", "XPo") if odd else ("Xe", "Po", "XTe", "XPe")
                        Xn, Pn = ("Xe", "Po") if odd else ("Xo", "Pe")

                        def f(ch, s, o, pb):
                            tr(pb[:, 256:384], s[Xc][:], ident, r=[s[Xc], C], w=[pb])
                            op("act", lambda e: e.activation(out=R(s[XTc][:]), in_=pb[:, 256:384], func=AF.Copy),
                               r=[pb], w=[s[XTc]])
                        each(f)
                        pump()

                        def f(ch, s, o, pb):
                            if j < 5:
                                mm(pb[:, 0:256], R(s[XTc][:]), R(s[XPc][:]), r=[s[XTc], s[Xc], s[Pc]], w=[pb])
                                op("act", lambda e: e.activation(out=R(s[Xn][:]), in_=pb[:, 0:128], func=AF.Copy),
                                   r=[pb], w=[s[Xn]])
                                op("dve", lambda e: e.tensor_tensor(out=R(s[Pn][:]), in0=pb[:, 128:256],
                                                                    in1=s[Pc][:], op=ALU.add),
                                   r=[pb, s[Pc]], w=[s[Pn]])
                            else:
                                mm(pb[:, 128:256], R(s[XTc][:]), R(s[Pc][:]), r=[s[XTc], s[Pc]], w=[pb])
                                op("dve", lambda e: e.tensor_tensor(out=R(s[Pn][:]), in0=pb[:, 128:256],
                                                                    in1=s[Pc][:], op=ALU.add),
                                   r=[pb, s[Pc]], w=[s[Pn]])
                        each(f)
                        pump()
                    Pf = "Po"

                    def f(ch, s, o, pb):
                        tr(pb[:, 0:128], s[Pf][:], ident, r=[s[Pf], C], w=[pb])
                        tr(pb[:, 128:256], s["Noff"][:], ident, r=[s["Noff"], C], w=[pb])
                        op("act", lambda e: e.activation(out=R(s["WbdT"][:]), in_=pb[:, 0:128], func=AF.Copy),
                           r=[pb], w=[s["WbdT"]])
                        op("dve", lambda e: e.tensor_copy(out=R(s["NoffT"][:]), in_=pb[:, 128:256]), r=[pb],
                           w=[s["NoffT"]])
                    each(f)
                    pump()

                    def f(ch, s, o, pb):
                        mm(pb[:, 0:128], R(s["NoffT"][:]), R(s[Pf][:]), r=[s["NoffT"], s[Pf]], w=[pb])
                        op("act", lambda e: e.activation(out=R(s["Z1"][:]), in_=pb[:, 0:128], func=AF.Copy),
                           r=[pb], w=[s["Z1"]])
                    each(f)
                    pump()

                    def f(ch, s, o, pb):
                        mm(pb[:, 128:256], R(s["WbdT"][:]), R(s["Z1"][:]), r=[s["WbdT"], s["Z1"]], w=[pb])
                        op("dve", lambda e: e.tensor_tensor(out=R(o["W"][:]), in0=s[Pf][:], in1=pb[:, 128:256],
                                                            op=ALU.subtract), r=[s[Pf], pb], w=[o["W"]])
                    each(f)
                    pump()

                nrec = [0]

                def recur_pair(chs, g):
                    st_ = []
                    for ch in chs:
                        h = ch["h"]
                        So = Sst[h][spar[h]]
                        Sn = Sst[h][1 - spar[h]]
                        spar[h] = 1 - spar[h]
                        bb = 4 * ch["hi"]
                        st_.append((ch, So, Sn, PS[bb], PS[bb + 1], PS[bb + 2], PS[bb + 3],
                                    Yr[nrec[0] % 2], Vn[nrec[0] % 2]))
                        nrec[0] += 1
                    for ch, So, Sn, pa, pb2, pc, pd, Y, vnew in st_:
                        h, n, o = ch["h"], ch["n"], ch["o"]
                        mm(pa[:, 0:128], R(o["kTc"][:]), R(So[:]), r=[o["kTc"], So], w=[pa])
                        op("dve", lambda e: e.scalar_tensor_tensor(out=R(Y[:]), in0=pa[:, 0:128],
                                                                   scalar=NEXPG[:, n, h:h + 1], in1=o["vtok"][:],
                                                                   op0=ALU.mult, op1=ALU.add),
                           r=[pa, NEXPG, o["vtok"]], w=[Y])
                    for ch, So, Sn, pa, pb2, pc, pd, Y, vnew in st_:
                        h, n, o = ch["h"], ch["n"], ch["o"]
                        mm(pb2[:, 0:128], R(o["W"][:]), R(Y[:]), r=[o["W"], Y], w=[pb2])
                        op("act", lambda e: e.activation(out=R(vnew[:]), in_=pb2[:, 0:128], func=AF.Identity,
                                                         scale=BETA[:, n, h:h + 1]), r=[pb2, BETA], w=[vnew])
                    for ch, So, Sn, pa, pb2, pc, pd, Y, vnew in st_:
                        o = ch["o"]
                        mm(pd[:, 0:128], R(o["kdec"][:]), R(vnew[:]), r=[o["kdec"], vnew], w=[pd])
                        op("dve", lambda e: e.scalar_tensor_tensor(out=R(Sn[:]), in0=So[:],
                                                                   scalar=EGL[:, ch["i"]:ch["i"] + 1],
                                                                   in1=pd[:, 0:128], op0=ALU.mult, op1=ALU.add),
                           r=[So, (EGL, ch["i"]), pd], w=[Sn])
                    for ch, So, Sn, pa, pb2, pc, pd, Y, vnew in st_:
                        o, sl = ch["o"], ch["sl"]
                        mm(pc[:, 0:128], R(So[:]), R(o["qg"][:]), start=True, stop=False, r=[So, o["qg"]], w=[pc],
                           inc=False)
                        mm(pc[:, 0:128], R(vnew[:]), R(o["QKdT"][:]), start=False, stop=True, r=[vnew, o["QKdT"]],
                           w=[pc])
                        ot = oTg[ch["hi"]]
                        op("act", lambda e: e.activation(out=ot[:, sl], in_=pc[:, 0:128], func=AF.Copy), r=[pc],
                           w=[(ot, ch["cc"])])

                def stageA(g, hp, par, res):
                    gs = slice(g * 512, (g + 1) * 512)
                    chains = []
                    for hi in range(2):
                        h = 2 * hp + hi
                        qk = qkv[hi]
                        zt = zs[2 * par + hi]
                        for ty in range(4):
                            pb = PS[4 * hi + ty]
                            col = ty * 512 + h * 128
                            for c in range(8):
                                mm(pb[:, :], Wdn[:, c, col:col + 128], hT[:, c, gs], start=(c == 0),
                                   stop=(c == 7), r=[(Wdn, c), (hT, g)], w=[pb], inc=(c == 7))
                            if ty < 3:
                                ch_ = ty * 4 + h
                                op("dve", lambda e: e.tensor_copy(out=pre[ty][:, 0:3], in_=halo[:, ch_, :]),
                                   r=[(halo, ch_)], w=[(pre[ty], 0)])
                                op("act", lambda e: e.activation(out=pre[ty][:, 3:515], in_=pb[:, :],
                                                                 func=AF.Copy), r=[pb], w=[(pre[ty], 1)])
                                op("dve", lambda e: e.tensor_copy(out=halo[:, ch_, :], in_=pre[ty][:, 512:515]),
                                   r=[(pre[ty], 1)], w=[(halo, ch_)])
                                yield
                                cvt = cv[ty % 2]
                                wk = lambda k: convw[:, ch_ * 4 + k:ch_ * 4 + k + 1]
                                op("act", lambda e: e.activation(out=cvt[:], in_=pre[ty][:, 0:512],
                                                                 func=AF.Identity, scale=wk(0)),
                                   r=[pre[ty], convw], w=[cvt])
                                for k in range(1, 4):
                                    op("dve", lambda e: e.scalar_tensor_tensor(out=cvt[:],
                                                                               in0=pre[ty][:, k:k + 512],
                                                                               scalar=wk(k), in1=cvt[:],
                                                                               op0=ALU.mult, op1=ALU.add),
                                       r=[pre[ty], convw, cvt], w=[cvt])
                                    yield
                                op("act", lambda e: e.activation(out=(R(qk[ty][:]) if ty < 2 else qk[ty][:]),
                                                                 in_=cvt[:], func=AF.Silu),
                                   r=[cvt], w=[qk[ty]])
                            else:
                                op("act", lambda e: e.activation(out=zt[:], in_=pb[:, :], func=AF.Silu),
                                   r=[pb], w=[zt])
                            yield
                        for ty in range(2):
                            sqt = sq[ty]
                            pb = PS[4 * hi + ty]
                            op("act", lambda e: e.activation(out=R(sqt[:]), in_=qk[ty][:], func=AF.Square),
                               r=[qk[ty]], w=[sqt])
                            mm(pb[:, :], R(onesr[:]), R(sqt[:]), r=[onesr, sqt], w=[pb])
                            rt_ = cv[0]
                            op("act", lambda e: e.activation(out=rt_[:], in_=pb[:, :], func=AF.Ln,
                                                             bias=epsT[:]), r=[pb, epsT], w=[rt_])
                            op("act", lambda e: e.activation(out=rt_[:], in_=rt_[:], func=AF.Exp, scale=-0.5),
                               r=[rt_], w=[rt_])
                            sc_ = (128.0 ** -0.5) if ty == 0 else 1.0
                            op("dve", lambda e: e.scalar_tensor_tensor(out=R(qk[ty][:]), in0=qk[ty][:],
                                                                       scalar=sc_, in1=rt_[:], op0=ALU.mult,
                                                                       op1=ALU.mult),
                               r=[qk[ty], rt_], w=[qk[ty]])
                            yield
                        if dbg and l == 0 and g == 0 and h == 0:
                            for nm, tt in (("q", qk[0]), ("k", qk[1]), ("v", qk[2])):
                                if nm in dbg:
                                    d = dbg_out(nm, [128, 512])
                                    dma(d[:, :], tt[:], r=[tt])
                        for cc in range(4):
                            i = hi * 4 + cc
                            chains.append({"h": h, "hi": hi, "cc": cc, "n": g * 4 + cc, "i": i,
                                           "sl": slice(cc * 128, (cc + 1) * 128), "qT": qk[0], "kT": qk[1],
                                           "vT": qk[2], "s": scr[i], "o": outs[i]})
                    res["chains"] = chains

                def drain(gen):
                    if gen is not None:
                        for _ in gen:
                            pass

                pairs = [(g, hp) for g in range(NG) for hp in range(2)]
                resA = [dict() for _ in pairs]
                gens = [stageA(g, hp, pi % 2, resA[pi]) for pi, (g, hp) in enumerate(pairs)]
                drain(gens[0])
                for pi, (g, hp) in enumerate(pairs):
                    gs = slice(g * 512, (g + 1) * 512)
                    chains = resA[pi]["chains"]
                    nxt = gens[pi + 1] if pi + 1 < len(pairs) else None

                    def pump(n=1):
                        if nxt is not None:
                            for _ in range(n):
                                if next(nxt, "done") == "done":
                                    break
                    prep8(chains, pump)
                    drain(nxt)
                    for cc in range(4):
                        recur_pair([ch for ch in chains if ch["cc"] == cc], g)
                    for hi in range(2):
                        h = 2 * hp + hi
                        zt = zs[2 * (pi % 2) + hi]
                        ot = oTg[hi]
                        og = ogb[hi]
                        sqt = sq[hi]
                        pb = PS[4 * hi]
                        op("act", lambda e: e.activation(out=R(sqt[:]), in_=ot[:], func=AF.Square), r=[ot],
                           w=[sqt])
                        mm(pb[:, :], R(onesr[:]), R(sqt[:]), r=[onesr, sqt], w=[pb])
                        sqt = cv[0]
                        op("act", lambda e: e.activation(out=sqt[:], in_=pb[:, :], func=AF.Ln,
                                                         scale=1.0 / 128, bias=epsT[:]), r=[pb, epsT], w=[sqt])
                        op("act", lambda e: e.activation(out=sqt[:], in_=sqt[:], func=AF.Exp, scale=-0.5),
                           r=[sqt], w=[sqt])
                        if "odn" in dbg and l == 0 and h == 0 and g == 0:
                            d = dbg_out("odn", [128, 512])
                            dma(d[:, :], ot[:], r=[ot])
                        op("dve", lambda e: e.scalar_tensor_tensor(out=sqt[:], in0=ot[:], scalar=dng[:, 0:1],
                                                                   in1=sqt[:], op0=ALU.mult, op1=ALU.mult),
                           r=[ot, dng, sqt], w=[sqt])
                        op("dve", lambda e: e.tensor_tensor(out=og[:], in0=sqt[:], in1=zt[:], op=ALU.mult),
                           r=[sqt, zt], w=[og])
                        dma(ogdn_d[h, :, gs], og[:], r=[og], w=[(kogdn, g)])
                kb.barrier()

            for st in phase("mb"):
                Wmb = T(st, "Wmb", [128, 8, 2048], BF16)
                for c in range(8):
                    dma(Wmb[:, c, :], wmb_d[l, :, c, :], w=[(Wmb, c)], q="pool")
                Vt = T(st, "Vt", [128, NT, 8, 65], BF16)
                op("dve", lambda e: e.memset(Vt[:, :, :, 64:65], 1.0), w=[Vt])
                for t in range(NT):
                    pb = PS[t % 2]
                    for c in range(8):
                        mm(pb[:, :], hT[:, c, t * 128:(t + 1) * 128], Wmb[:, c, 1024:1536], start=(c == 0),
                           stop=(c == 7), r=[(hT, t // 4), (Wmb, c)], w=[pb], inc=(c == 7))
                    op("act" if t % 2 else "dve",
                       lambda e: (e.activation(out=Vt[:, t, :, 0:64], in_=pb[:, :].rearrange("p (h d) -> p h d", h=8),
                                               func=AF.Copy) if t % 2 else
                                  e.tensor_copy(out=Vt[:, t, :, 0:64],
                                                in_=pb[:, :].rearrange("p (h d) -> p h d", h=8))),
                       r=[pb], w=[(Vt, t)])
                KaT = T(st, "KaT", [128, S], BF16)
                QaT = T(st, "QaT", [128, S], BF16)
                zsm = T(st, "zsm", [64, S], BF16)
                ogm = T(st, "ogm", [64, S], BF16)
                kf = [T(st, "kf%d" % i, [128, 512]) for i in range(2)]
                qf = [T(st, "qf%d" % i, [128, 512]) for i in range(2)]
                sqm = [T(st, "sqm%d" % i, [128, 512]) for i in range(2)]
                kmT = T(st, "kmT", [128, 16])
                km2 = T(st, "km2", [64, NG + 1])
                gm4 = T(st, "gm4", [128, 4, 16])
                top84 = T(st, "top84", [128, 4, 8])
                mbt4 = [T(st, "mbt4_%d" % i, [128, 4, 16]) for i in range(2)]
                rden = T(st, "rden", [128, 512])
                t1 = [T(st, "t1_%d" % i, [64, 512]) for i in range(2)]
                PT = [T(st, "PT%d" % i, [128, 512], BF16) for i in range(4)]
                nS = [0]
                op("dve", lambda e: e.memset(KaT[0:64, :], 0.0), w=[KaT])
                op("dve", lambda e: e.memset(QaT[0:64, :], 0.0), w=[QaT])
                op("dve", lambda e: e.memset(KaT[32:33, :], 1.0), w=[KaT])
                for n in range(NB):
                    op("dve", lambda e: e.tensor_copy(out=KaT[0:16, n * 256:(n + 1) * 256],
                                                      in_=ident[0:16, n:n + 1].to_broadcast([16, 256])),
                       r=[C], w=[KaT])
                npt = 0
                for h in range(8):
                    op("dve", lambda e: e.memset(kmT[:], 0.0), w=[kmT])
                    def projK(g):
                        pb = PS[g % 2]
                        col = 512 + h * 64
                        for c in range(8):
                            mm(pb[64:128, :], Wmb[:, c, col:col + 64], hT[:, c, g * 512:(g + 1) * 512],
                               start=(c == 0), stop=(c == 7), r=[(Wmb, c), (hT, g)], w=[pb], inc=(c == 7))

                    def postK(g):
                        gs = slice(g * 512, (g + 1) * 512)
                        pb = PS[g % 2]
                        kft = kf[g % 2]
                        op("act", lambda e: e.activation(out=kft[64:128, :], in_=pb[64:128, :], func=AF.Copy),
                           r=[pb], w=[kft])
                        op("act", lambda e: e.activation(out=KaT[64:128, gs], in_=pb[64:128, :], func=AF.Copy),
                           r=[pb], w=[(KaT, g)])
                        op("dve", lambda e: e.tensor_reduce(out=kmT[64:128, 2 * g:2 * g + 2],
                                                            in_=kft[64:128, :].rearrange("p (b t) -> p b t", b=2),
                                                            axis=AX.X, op=ALU.add), r=[kft], w=[kmT])
                        sqt = sqm[g % 2]
                        op("dve", lambda e: e.tensor_tensor(out=sqt[64:128, :], in0=kft[64:128, :],
                                                            in1=kft[64:128, :], op=ALU.mult), r=[kft], w=[sqt])
                        pr = PS[2]
                        mm(pr[32:33, :], ones[64:128, 0:1], sqt[64:128, :], r=[C, sqt], w=[pr])
                        op("dve", lambda e: e.tensor_reduce(out=km2[32:33, g:g + 1], in_=pr[32:33, :], axis=AX.X,
                                                            op=ALU.max), r=[pr], w=[km2])
                    projK(0)
                    for g in range(NG):
                        if g + 1 < NG:
                            projK(g + 1)
                        postK(g)
                    op("dve", lambda e: e.tensor_scalar(out=kmT[64:128, :], in0=kmT[64:128, :], scalar1=1.0 / 256,
                                                        scalar2=None, op0=ALU.mult), r=[kmT], w=[kmT])
                    op("dve", lambda e: e.tensor_reduce(out=km2[32:33, NG:NG + 1], in_=km2[32:33, 0:NG], axis=AX.X,
                                                        op=ALU.max), r=[km2], w=[km2])
                    for g in range(NG):
                        gs = slice(g * 512, (g + 1) * 512)
                        pb = PS[g % 2]
                        col = 1536 + h * 64
                        for c in range(8):
                            mm(pb[0:64, :], Wmb[:, c, col:col + 64], hT[:, c, gs], start=(c == 0), stop=(c == 7),
                               r=[(Wmb, c), (hT, g)], w=[pb], inc=(c == 7))
                        op("act", lambda e: e.activation(out=zsm[:, gs], in_=pb[0:64, :], func=AF.Silu), r=[pb],
                           w=[(zsm, g)])

                    def projQ(g):
                        pb = PS[g % 3]
                        col = h * 64
                        for c in range(8):
                            mm(pb[64:128, :], Wmb[:, c, col:col + 64], hT[:, c, g * 512:(g + 1) * 512],
                               start=(c == 0), stop=(c == 7), r=[(Wmb, c), (hT, g)], w=[pb], inc=(c == 7))

                    def postQ1(g):
                        gs = slice(g * 512, (g + 1) * 512)
                        pb = PS[g % 3]
                        qft = qf[g % 2]
                        op("act", lambda e: e.activation(out=qft[64:128, :], in_=pb[64:128, :], func=AF.Identity,
                                                         scale=0.125), r=[pb], w=[qft])
                        op("act", lambda e: e.activation(out=QaT[64:128, gs], in_=pb[64:128, :], func=AF.Identity,
                                                         scale=0.125), r=[pb], w=[(QaT, g)])
                        sqt = sqm[g % 2]
                        op("dve", lambda e: e.tensor_tensor(out=sqt[64:128, :], in0=qft[64:128, :],
                                                            in1=qft[64:128, :], op=ALU.mult), r=[qft], w=[sqt])
                        pgt = PS[4]
                        for tt in range(4):
                            mm(pgt[:, tt * 16:(tt + 1) * 16], qft[64:128, tt * 128:(tt + 1) * 128], kmT[64:128, :],
                               r=[qft, kmT], w=[pgt], inc=(tt == 3))
                        pr = PS[5]
                        mm(pr[32:33, :], ones[64:128, 0:1], sqt[64:128, :], r=[C, sqt], w=[pr])
                        op("dve", lambda e: e.tensor_tensor(out=gm4[:].rearrange("p a b -> p (a b)"), in0=pgt[:, 0:64],
                                                            in1=C[:, C_PB4 + g * 64:C_PB4 + (g + 1) * 64], op=ALU.add),
                           r=[pgt, C], w=[gm4])
                        op("act", lambda e: e.activation(out=sqt[32:33, :], in_=pr[32:33, :], func=AF.Sqrt,
                                                         scale=km2[32:33, NG:NG + 1]), r=[pr, km2], w=[sqt])
                        op("act", lambda e: e.activation(out=QaT[32:33, gs], in_=sqt[32:33, :], func=AF.Identity,
                                                         scale=-1.0), r=[sqt], w=[(QaT, g)])
                        for tt in range(4):
                            op("dve", lambda e: e.max(out=top84[:, tt, :], in_=gm4[:, tt, :]), r=[gm4],
                               w=[(top84, tt)])
                        mb_ = mbt4[g % 2]
                        for tt in range(4):
                            op("dve", lambda e: e.tensor_scalar(out=mb_[:, tt, :], in0=gm4[:, tt, :],
                                                                scalar1=top84[:, tt, 2:3], scalar2=NEG,
                                                                op0=ALU.is_lt, op1=ALU.mult),
                               r=[gm4, (top84, tt)], w=[(mb_, tt)])
                        for t2 in range(2):
                            own = 2 * g + t2
                            op("dve", lambda e: e.memset(mb_[:, 2 * t2:2 * t2 + 2, own:own + 1], 0.0),
                               r=[(mb_, 2 * t2), (mb_, 2 * t2 + 1)], w=[(mb_, 2 * t2), (mb_, 2 * t2 + 1)])

                    def postQ2(g):
                        gs = slice(g * 512, (g + 1) * 512)
                        mb_ = mbt4[g % 2]
                        pt_ = PS[3]
                        for tt in range(4):
                            tr(pt_[0:16, tt * 128:(tt + 1) * 128], mb_[:, tt, :], ident, r=[(mb_, tt), C], w=[pt_])
                        op("act", lambda e: e.activation(out=QaT[0:16, gs], in_=pt_[0:16, 0:512], func=AF.Copy),
                           r=[pt_], w=[(QaT, g)])
                    projQ(0)
                    if NG > 1:
                        projQ(1)
                    for g in range(NG):
                        postQ1(g)
                        if g + 2 < NG:
                            projQ(g + 2)
                        if g > 0:
                            postQ2(g - 1)
                    postQ2(NG - 1)
                    LOOK = 3
                    for jq in range(NB // 2):
                        q0 = jq * 512
                        pO = PS[6 + jq % 2]
                        ntile = 4 * jq + 4
                        qk_ = (QaT, jq)

                        def emitS(kt):
                            pS = PS[nS[0] % 4]
                            nS[0] += 1
                            d = kt - 4 * jq
                            c0 = 0 if d < 0 else 128 * d
                            mm(pS[:, c0:512], KaT[:, kt * 128:(kt + 1) * 128], QaT[:, q0 + c0:q0 + 512], start=True,
                               stop=(d < 0), r=[(KaT, kt // 4), qk_], w=[pS], inc=(d < 0))
                            if d >= 0:
                                mm(pS[:, c0:c0 + 128], identb[:], trib[:], start=False, stop=True, r=[identb, trib],
                                   w=[pS])
                            return pS, c0

                        pendq = [emitS(k_) for k_ in range(min(LOOK, ntile))]
                        for kt in range(ntile):
                            pS, c0 = pendq.pop(0)
                            if kt + LOOK < ntile:
                                pendq.append(emitS(kt + LOOK))
                            ptile = PT[npt % 4]
                            npt += 1
                            op("act", lambda e: e.activation(out=ptile[:, c0:512], in_=pS[:, c0:512], func=AF.Exp),
                               r=[pS], w=[ptile])
                            mm(pO[0:65, c0:512], Vt[:, kt, h, :], ptile[:, c0:512], start=(kt == 0),
                               stop=(kt == ntile - 1), r=[(Vt, kt), ptile], w=[pO], inc=(kt == ntile - 1))
                        op("dve", lambda e: e.reciprocal(out=R(rden[64:65, :]), in_=pO[64:65, 0:512]), r=[pO],
                           w=[rden])
                        pB = PS[5]
                        mm(pB[0:64, 0:512], R(onesr[64:65, 0:64]), R(rden[64:65, :]), r=[onesr, rden], w=[pB])
                        tt1 = t1[jq % 2]
                        op("dve", lambda e: e.tensor_tensor(out=tt1[:], in0=pO[0:64, 0:512],
                                                            in1=zsm[:, q0:q0 + 512], op=ALU.mult),
                           r=[pO, (zsm, jq)], w=[tt1])
                        op("dve", lambda e: e.tensor_tensor(out=ogm[:, q0:q0 + 512], in0=tt1[:], in1=pB[0:64, 0:512],
                                                            op=ALU.mult), r=[tt1, pB], w=[(ogm, jq)])
                    if "omb" in dbg and l == 0 and h == 0:
                        d = dbg_out("omb", [64, S])
                        tmp = T(st, "dbgomb", [64, S])
                        op("dve", lambda e: e.tensor_copy(out=tmp[:], in_=ogm[:]), r=[ogm], w=[tmp])
                        dma(d[:, :], tmp[:], r=[tmp])
                    dma(ogmb_d[h, :, :], ogm[:], r=[ogm], w=[kogmb])
                kb.barrier()

            for st in phase("fin"):
                Wpdn = T(st, "Wpdn", [128, 4, 1024], BF16)
                Wpmb = T(st, "Wpmb", [64, 8, 1024], BF16)
                Wout = T(st, "Wout", [128, 8, 1024], BF16)
                Wmg = T(st, "Wmg", [128, 8, 2048], BF16)
                dma(Wpdn[:], wpdn_d[l], w=[Wpdn], q="pool")
                dma(Wpmb[:], wpmb_d[l], w=[Wpmb], q="pool")
                for c in range(8):
                    dma(Wout[:, c, :], wout_d[l, :, c, :], w=[(Wout, c)], q="pool")
                for c in range(8):
                    dma(Wmg[:, c, :], wmg_d[l, :, c, :], w=[(Wmg, c)], q="pool")
                OGD = [T(st, "OGD%d" % i, [128, 4, 512], BF16) for i in range(2)]
                OGM = [T(st, "OGM0", [64, 8, 512], BF16)] * 2
                mixT = T(st, "mixT", [128, 8, 512], BF16)
                gd = [T(st, "gd%d" % i, [128, 512]) for i in range(2)]
                gmm = [T(st, "gmm%d" % i, [128, 512]) for i in range(2)]
                u1 = [T(st, "u1_%d" % i, [128, 512]) for i in range(2)]
                u2 = [T(st, "u2_%d" % i, [128, 512]) for i in range(2)]
                xr = [T(st, "xr%d" % i, [128, 1024]) for i in range(2)]
                res = [T(st, "res%d" % i, [128, 512]) for i in range(2)]
                junk2 = T(st, "junk2", [128, 512], BF16)
                GPl = T(st, "GPl", [128, 1024])
                dma(GPl[:], gp_d[l], r=[(kgp, l)], w=[GPl])
                ss2 = T(st, "ss2", [128, NT, 2])
                rs2 = T(st, "rs2", [128, NT])
                for g in range(NG):
                    gs = slice(g * 512, (g + 1) * 512)
                    ogd, ogmm = OGD[g % 2], OGM[g % 2]
                    dma(ogd[:], ogdn_d[:, :, gs].rearrange("h p s -> p h s"), r=[(kogdn, g)], w=[ogd])
                    dma(ogmm[:], ogmb_d[:, :, gs].rearrange("h p s -> p h s"), r=[kogmb], w=[ogmm])
                    for d_ in range(8):
                        ds_ = slice(d_ * 128, (d_ + 1) * 128)
                        pa, pb, pc, pd = PS[0 + 4 * (d_ % 2)], PS[1 + 4 * (d_ % 2)], PS[2 + 4 * (d_ % 2)], PS[3 + 4 * (d_ % 2)]
                        for h in range(4):
                            mm(pa[:, :], Wpdn[:, h, ds_], ogd[:, h, :], start=(h == 0), stop=(h == 3),
                               r=[Wpdn, ogd], w=[pa], inc=(h == 3))
                        for h in range(8):
                            mm(pb[:, :], Wpmb[:, h, ds_], ogmm[:, h, :], start=(h == 0), stop=(h == 7),
                               r=[Wpmb, ogmm], w=[pb], inc=(h == 7))
                        for c in range(8):
                            mm(pc[:, :], Wmg[:, c, ds_], hT[:, c, gs], start=(c == 0), stop=(c == 7),
                               r=[(Wmg, c), (hT, g)], w=[pc], inc=(c == 7))
                        for c in range(8):
                            mm(pd[:, :], Wmg[:, c, 1024 + d_ * 128:1024 + (d_ + 1) * 128], hT[:, c, gs],
                               start=(c == 0), stop=(c == 7), r=[(Wmg, c), (hT, g)], w=[pd], inc=(c == 7))
                        gdt, gmt, u1t, u2t = gd[d_ % 2], gmm[d_ % 2], u1[d_ % 2], u2[d_ % 2]
                        op("act", lambda e: e.activation(out=gdt[:], in_=pc[:, :], func=AF.Sigmoid), r=[pc], w=[gdt])
                        op("act", lambda e: e.activation(out=gmt[:], in_=pd[:, :], func=AF.Sigmoid), r=[pd], w=[gmt])
                        op("dve", lambda e: e.tensor_tensor(out=u1t[:], in0=pa[:, :], in1=gdt[:], op=ALU.mult),
                           r=[pa, gdt], w=[u1t])
                        op("dve", lambda e: e.tensor_tensor(out=u2t[:], in0=pb[:, :], in1=gmt[:], op=ALU.mult),
                           r=[pb, gmt], w=[u2t])
                        op("dve", lambda e: e.tensor_tensor(out=mixT[:, d_, :], in0=u1t[:], in1=u2t[:], op=ALU.add),
                           r=[u1t, u2t], w=[(mixT, d_)])
                    for tt in range(4):
                        t = g * 4 + tt
                        xrt = xr[t % 2]
                        dma(xrt[:], xin_d[t * 128:(t + 1) * 128, :], r=[(xin_k, t)], w=[xrt])
                        for hf in range(2):
                            pb = PS[(t % 2) * 2 + hf]
                            for d_ in range(8):
                                mm(pb[:, :], mixT[:, d_, tt * 128:(tt + 1) * 128], Wout[:, d_, hf * 512:(hf + 1) * 512],
                                   start=(d_ == 0), stop=(d_ == 7), r=[(mixT, d_), (Wout, d_)], w=[pb],
                                   inc=(d_ == 7))
                            op("act", lambda e: e.activation(out=junk2[:], in_=pb[:, :], func=AF.Square,
                                                             accum_out=ss2[:, t, hf:hf + 1]), r=[pb],
                               w=[junk2, (ss2, t)])
                        op("dve", lambda e: e.tensor_tensor(out=rs2[:, t:t + 1], in0=ss2[:, t, 0:1],
                                                            in1=ss2[:, t, 1:2], op=ALU.add), r=[(ss2, t)],
                           w=[(rs2, t)])
                        op("act", lambda e: e.activation(out=rs2[:, t:t + 1], in_=rs2[:, t:t + 1], func=AF.Sqrt,
                                                         scale=1.0 / D_MODEL, bias=epsT[:]), r=[(rs2, t), epsT],
                           w=[(rs2, t)])
                        op("dve", lambda e: e.reciprocal(out=rs2[:, t:t + 1], in_=rs2[:, t:t + 1]), r=[(rs2, t)],
                           w=[(rs2, t)])
                        for hf in range(2):
                            pb = PS[(t % 2) * 2 + hf]
                            hs = slice(hf * 512, (hf + 1) * 512)
                            rt = res[hf]
                            op("dve", lambda e: e.scalar_tensor_tensor(out=rt[:], in0=pb[:, :],
                                                                       scalar=rs2[:, t:t + 1], in1=GPl[:, hs],
                                                                       op0=ALU.mult, op1=ALU.mult),
                               r=[pb, (rs2, t), GPl], w=[rt])
                            op("dve", lambda e: e.tensor_tensor(out=xrt[:, hs], in0=rt[:], in1=xrt[:, hs],
                                                                op=ALU.add), r=[rt, (xrt, hf)], w=[(xrt, hf)])
                        dma(xout_d[t * 128:(t + 1) * 128, :], xrt[:], r=[xrt], w=[(xout_k, t)])
                kb.barrier()
          except _Stop:
            kb.barrier()
            curgen[0].close()
            break
        kb.finish()
        stuck = kb.simulate()
        print("deadlock check:", stuck if stuck else "ok")
        print("instructions:", kb.ninstr, "counts", kb.count, "dmas", kb.ndmaq)
    return nc, dbg_d


def _pc(w):
    sh = w.shape
    w = w.reshape(sh[:-2] + (8, 128, sh[-1]))
    return np.ascontiguousarray(np.swapaxes(w, -3, -2))


def host_layout(inp):
    f = lambda a: np.ascontiguousarray(np.asarray(a, dtype=np.float32))
    w_in = f(inp["w_in"])
    depth = w_in.shape[0]
    shared = {
        "wada": _pc(f(inp["w_ada"])),
        "bada": f(inp["b_ada"]).reshape(depth, 1, 3072),
        "gpre": f(inp["g_pre"]).reshape(depth, 1, 1024),
        "gpost": f(inp["g_post"]).reshape(depth, 1, 1024),
        "wdn": _pc(w_in[:, :, 0:2048]),
        "wba": _pc(w_in[:, :, 2048:2056]),
        "wmb": _pc(w_in[:, :, 2056:4104]),
        "wmg": _pc(w_in[:, :, 4104:6152]),
        "convw": np.ascontiguousarray(
            f(inp["conv_w"]).transpose(0, 2, 1).reshape(depth, 12, 128, 4).transpose(0, 2, 1, 3)
        ).reshape(depth, 128, 48),
        "alog": np.ascontiguousarray(np.broadcast_to(f(inp["a_log"])[:, None, :], (depth, 128, 4))),
        "dtb": np.ascontiguousarray(np.broadcast_to(f(inp["dt_bias"])[:, None, :], (depth, 128, 4))),
        "dng": f(inp["dn_norm_g"]).reshape(depth, 128, 1),
        "wpdn": np.ascontiguousarray(f(inp["w_proj_dn"]).reshape(depth, 4, 128, 1024).transpose(0, 2, 1, 3)),
        "wpmb": np.ascontiguousarray(f(inp["w_proj_mb"]).reshape(depth, 8, 64, 1024).transpose(0, 2, 1, 3)),
        "wout": _pc(f(inp["w_out"])),
        "consts": make_consts(),
    }
    x = f(inp["x"])
    c = f(inp["c"])
    maps = []
    for b in range(x.shape[0]):
        m = dict(shared)
        m["x"] = x[b]
        m["cT"] = np.ascontiguousarray(c[b].reshape(8, 128).T)
        maps.append(m)
    return maps


_CACHE = {}


def kernel(**inputs):
    x = np.asarray(inputs["x"])
    B, S, _ = x.shape
    depth = np.asarray(inputs["w_in"]).shape[0]
    key = (S, depth)
    if key not in _CACHE:
        _CACHE[key] = build(S=S, DEPTH=depth)[0]
    nc = _CACHE[key]
    maps = host_layout(inputs)
    res = run_bass_kernel_spmd(nc, maps, core_ids=list(range(B)))
    return np.stack([np.asarray(r["y"], dtype=np.float32) for r in res.results], axis=0)
```

```python
from contextlib import ExitStack

import numpy as np
import concourse.bass as bass
import concourse.mybir as mybir
from concourse.bass_utils import run_bass_kernel_spmd

F32 = mybir.dt.float32
BF16 = mybir.dt.bfloat16
F32R = mybir.dt.float32r


def R(ap):
    return ap.bitcast(F32R)
AF = mybir.ActivationFunctionType
ALU = mybir.AluOpType
AX = mybir.AxisListType

D_MODEL = 1024
NEG = -30000.0
EPS = 1e-6


class SV:
    def __init__(self, tile, j, n=1):
        self.tile, self.sub = tile, j
        self.ap = tile[:, j * 128:(j + n) * 128]

    def __getitem__(self, idx):
        return self.ap[idx]


class KB:
    NRING = {"sp": 6, "pool": 4}

    def __init__(self, nc, stack):
        self.nc = nc
        self.eng = {"pe": nc.tensor, "act": nc.scalar, "dve": nc.vector,
                    "pool": nc.gpsimd, "sp": nc.sync}
        self.sem = {}
        for e in ("pe", "act", "dve", "pool"):
            self.sem[e] = stack.enter_context(nc.semaphore("s_" + e))
        self.ring = {q: [stack.enter_context(nc.semaphore("s_dma_%s%d" % (q, i))) for i in range(n)]
                     for q, n in self.NRING.items()}
        self.ndmaq = {q: 0 for q in self.NRING}
        self.count = {e: 0 for e in ("pe", "act", "dve", "pool")}
        self.waited = {}
        self.track = {}
        self.ninstr = 0
        self.streams = {e: [] for e in self.eng}
        self.psum_ids = set()

    def _semof(self, dep):
        if dep[0] == "e":
            return self.sem[dep[1]], ("e", dep[1])
        return self.ring[dep[1][0]][dep[1][1]], ("d", dep[1])

    def _wait(self, e, dep):
        sem, sk = self._semof(dep)
        val = dep[2]
        k = (e, sk)
        if self.waited.get(k, 0) >= val:
            return
        self.eng[e].wait_ge(sem, val)
        self.streams[e].append(("wait", sk, val))
        self.ninstr += 1
        self.waited[k] = val

    @staticmethod
    def _keys(items):
        out = []
        for it in items:
            if isinstance(it, SV):
                out.append((id(it.tile), it.sub))
            elif isinstance(it, tuple):
                out.append((id(it[0]), it[1]))
            else:
                out.append((id(it), None))
        return out

    def _conflicts(self, key):
        tid, sub = key
        d = self.track.get(tid)
        if d is None:
            return []
        if sub is None:
            return list(d.values())
        res = []
        if sub in d:
            res.append(d[sub])
        if None in d:
            res.append(d[None])
        return res

    def _entry(self, key):
        tid, sub = key
        d = self.track.setdefault(tid, {})
        if sub is None:
            ent = {"w": [], "r": []}
            for v in d.values():
                ent["w"] += v["w"]
                ent["r"] += v["r"]
            d.clear()
            d[None] = ent
            return ent
        if sub not in d:
            ent = {"w": [], "r": []}
            if None in d:
                ent["w"] = list(d[None]["w"])
                ent["r"] = list(d[None]["r"])
            d[sub] = ent
        return d[sub]

    def _deps(self, e, reads, writes):
        deps = []
        for k in reads:
            for ent in self._conflicts(k):
                for w in ent["w"]:
                    deps.append((w, "raw"))
        for k in writes:
            for ent in self._conflicts(k):
                for w in ent["w"]:
                    deps.append((w, "waw"))
                for r in ent["r"]:
                    deps.append((r, "war"))
        out = []
        for dep, kind in deps:
            if dep[0] == "e" and dep[1] == e:
                if e == "pe":
                    continue
            out.append(dep)
        return out

    @staticmethod
    def _prune(lst):
        best = {}
        for d in lst:
            kk = (d[0], d[1])
            if kk not in best or best[kk][2] < d[2]:
                best[kk] = d
        return list(best.values())

    def _record(self, me, reads, writes):
        for k in reads:
            ent = self._entry(k)
            ent["r"].append(me)
            if len(ent["r"]) > 16:
                ent["r"] = self._prune(ent["r"])
        for k in writes:
            ent = self._entry(k)
            ent["w"] = [me]
            ent["r"] = []

    def _rw(self, r, w):
        reads, writes = [], []
        for k in self._keys(r):
            if k[0] in self.psum_ids:
                writes.append((k[0], None))
            else:
                reads.append(k)
        for k in self._keys(w):
            writes.append((k[0], None) if k[0] in self.psum_ids else k)
        return reads, writes

    def op(self, e, fn, r=(), w=(), inc=True):
        reads, writes = self._rw(r, w)
        for dep in self._deps(e, reads, writes):
            self._wait(e, dep)
        ins = fn(self.eng[e])
        self.ninstr += 1
        if inc:
            self.count[e] += 1
            ins.then_inc(self.sem[e], 1)
            self.streams[e].append(("inc", ("e", e), 1))
            me = ("e", e, self.count[e])
        else:
            me = ("e", e, self.count[e] + 1)
        self._record(me, reads, writes)
        return ins

    def dma(self, out, in_, r=(), w=(), q="sp", **kw):
        e = q
        reads = self._keys(r)
        writes = self._keys(w)
        k = self.ndmaq[q]
        nr = self.NRING[q]
        slot = k % nr
        gen = k // nr
        for dep in self._deps(e, reads, writes):
            self._wait(e, dep)
        if gen > 0:
            self._wait(e, ("d", (q, slot), 16 * gen))
        ins = self.eng[e].dma_start(out=out, in_=in_, **kw)
        ins.then_inc(self.ring[q][slot], 16)
        self.streams[e].append(("inc", ("d", (q, slot)), 16))
        self.ndmaq[q] += 1
        self.ninstr += 1
        me = ("d", (q, slot), 16 * (gen + 1))
        self._record(me, reads, writes)
        return ins

    def _alldma(self):
        out = []
        for q, nr in self.NRING.items():
            n = self.ndmaq[q]
            for slot in range(nr):
                cnt = (n - 1 - slot) // nr + 1 if n > slot else 0
                if cnt > 0:
                    out.append(("d", (q, slot), 16 * cnt))
        return out

    def barrier(self):
        for e in ("pe", "act", "dve", "pool", "sp"):
            for o in ("pe", "act", "dve", "pool"):
                if o != e and self.count[o] > 0:
                    self._wait(e, ("e", o, self.count[o]))
            for dep in self._alldma():
                self._wait(e, dep)
        self.track = {}

    def finish(self):
        for dep in self._alldma():
            self._wait("sp", dep)

    def simulate(self):
        sems = {}
        pc = {e: 0 for e in self.streams}
        progress = True
        while progress:
            progress = False
            for e, st in self.streams.items():
                while pc[e] < len(st):
                    kind, sk, val = st[pc[e]]
                    if kind == "wait":
                        if sems.get(sk, 0) < val:
                            break
                    else:
                        sems[sk] = sems.get(sk, 0) + val
                    pc[e] += 1
                    progress = True
        stuck = {e: (pc[e], len(st), st[pc[e]], sems.get(st[pc[e]][1], 0)) for e, st in self.streams.items()
                 if pc[e] < len(st)}
        return stuck


C_IDENT, C_U, C_MBD, C_MOFF, C_TRI, C_ONES, C_PB = 0, 128, 256, 384, 512, 640, 768
C_PB4 = 768
NCONST = 768 + 512


def make_consts():
    c = np.zeros((128, NCONST), np.float32)
    i = np.arange(128)
    c[:, C_IDENT:C_IDENT + 128] = np.eye(128)
    c[:, C_U:C_U + 128] = (i[:, None] <= i[None, :])
    c[:, C_MBD:C_MBD + 128] = (i[:, None] < i[None, :]) & ((i[:, None] // 64) == (i[None, :] // 64))
    c[:, C_MOFF:C_MOFF + 128] = (i[:, None] < 64) & (i[None, :] >= 64)
    c[:, C_TRI:C_TRI + 128] = np.where(i[:, None] <= i[None, :], 0.0, NEG)
    c[:, C_ONES:C_ONES + 128] = 1.0
    pb = np.zeros((16, 16), np.float32)
    for own in range(16):
        pb[own, own:] = -1e30
    for g in range(8):
        for tt in range(4):
            own = (4 * g + tt) // 2
            c[:, C_PB4 + g * 64 + tt * 16:C_PB4 + g * 64 + (tt + 1) * 16] = pb[own][None, :]
    return c


def build(S=4096, DEPTH=2, LS=2, dbg=None, phases=("p1", "dn", "mb", "fin"), stop=0, JUNK=0):
    dbg = dbg or set()
    NT = S // 128
    NG = S // 512
    NB = S // 256
    nc = bass.Bass("TRN2", target_bir_lowering=False)

    def din(name, shape, dt=F32):
        return nc.dram_tensor(name, shape, dt, kind="ExternalInput").ap()

    x_d = din("x", [S, 1024])
    cT_d = din("cT", [128, 8])
    wada_d = din("wada", [DEPTH, 128, 8, 3072])
    bada_d = din("bada", [DEPTH, 1, 3072])
    gpre_d = din("gpre", [DEPTH, 1, 1024])
    gpost_d = din("gpost", [DEPTH, 1, 1024])
    wdn_d = din("wdn", [DEPTH, 128, 8, 2048])
    wba_d = din("wba", [DEPTH, 128, 8, 8])
    wmb_d = din("wmb", [DEPTH, 128, 8, 2048])
    wmg_d = din("wmg", [DEPTH, 128, 8, 2048])
    convw_d = din("convw", [DEPTH, 128, 48])
    alog_d = din("alog", [DEPTH, 128, 4])
    dtb_d = din("dtb", [DEPTH, 128, 4])
    dng_d = din("dng", [DEPTH, 128, 1])
    wpdn_d = din("wpdn", [DEPTH, 128, 4, 1024])
    wpmb_d = din("wpmb", [DEPTH, 64, 8, 1024])
    wout_d = din("wout", [DEPTH, 128, 8, 1024])
    consts_d = din("consts", [128, NCONST])
    y_d = nc.dram_tensor("y", [S, 1024], F32, kind="ExternalOutput").ap()
    xmid_d = nc.dram_tensor("xmid", [S, 1024], F32, kind="Internal").ap()
    ogdn_d = nc.dram_tensor("ogdn", [4, 128, S], BF16, kind="Internal").ap()
    ogmb_d = nc.dram_tensor("ogmb", [8, 64, S], BF16, kind="Internal").ap()
    dbg_d = {}

    class _K:
        pass
    kx, kmid, ky, kogdn, kogmb = _K(), _K(), _K(), _K(), _K()

    def dbg_out(name, shape):
        dbg_d[name] = nc.dram_tensor("dbg_" + name, shape, F32, kind="ExternalOutput").ap()
        return dbg_d[name]

    with ExitStack() as gst:
        kb = KB(nc, gst)
        op, dma = kb.op, kb.dma
        gst.enter_context(nc.allow_low_precision("float32r (1-pass PE) operands for non-critical fp32 matmuls"))

        def mm(out, lhsT, rhs, start=True, stop=True, r=(), w=(), inc=True):
            return op("pe", lambda e: e.matmul(out, lhsT=lhsT, rhs=rhs, start=start, stop=stop),
                      r=r, w=w, inc=inc)

        def tr(out, in_, ident, r=(), w=()):
            return op("pe", lambda e: e.transpose(out=out, in_=in_, identity=ident), r=r, w=w)

        uid = [0]

        def T(st, name, shape, dt=F32):
            uid[0] += 1
            return st.enter_context(nc.sbuf_tensor("sb%d_%s" % (uid[0], name), shape, dt))

        class _Stop(Exception):
            pass

        def ck(level):
            if stop == level:
                raise _Stop()

        curgen = [None]

        def phase(name):
            if name in phases:
                st_ = ExitStack()
                curgen[0] = st_
                yield st_
                st_.close()

        PS = [gst.enter_context(nc.psum_tensor("ps%d" % i, [128, 512], F32)) for i in range(8)]
        kb.psum_ids = {id(p) for p in PS}
        C = T(gst, "consts", [128, NCONST])
        dma(C[:], consts_d[:, :], w=[C])
        ident = C[:, C_IDENT:C_IDENT + 128]
        U = C[:, C_U:C_U + 128]
        Mbd = C[:, C_MBD:C_MBD + 128]
        Moff = C[:, C_MOFF:C_MOFF + 128]
        ones = C[:, C_ONES:C_ONES + 128]
        identb = T(gst, "identb", [128, 128], BF16)
        trib = T(gst, "trib", [128, 128], BF16)
        epsT = T(gst, "epsT", [128, 1])
        op("dve", lambda e: e.tensor_copy(out=identb[:], in_=ident), r=[C], w=[identb])
        op("dve", lambda e: e.tensor_copy(out=trib[:], in_=C[:, C_TRI:C_TRI + 128]), r=[C], w=[trib])
        op("dve", lambda e: e.memset(epsT[:], EPS), w=[epsT])
        onesr = T(gst, "onesr", [128, 128])
        op("dve", lambda e: e.tensor_copy(out=R(onesr[:]), in_=ones), r=[C], w=[onesr])
        AB = [T(gst, "AB%d" % l, [128, 16]) for l in range(DEPTH)]
        gp_d = nc.dram_tensor("gp_scr", [DEPTH, 128, 1024], F32, kind="Internal").ap()
        kgp = _K()

        with ExitStack() as st:
            cT = T(st, "cT", [128, 8])
            sc = T(st, "sc", [128, 8])
            dma(cT[:], cT_d[:, :], w=[cT])
            op("act", lambda e: e.activation(out=sc[:], in_=cT[:], func=AF.Silu), r=[cT], w=[sc])
            wa = [T(st, "wa%d" % i, [128, 8, 512]) for i in range(2)]
            row = T(st, "row", [1, 3072])
            bada = T(st, "bada", [1, 3072])
            gpr = T(st, "gpr", [1, 1024])
            gpo = T(st, "gpo", [1, 1024])
            arow = T(st, "arow", [1, 1024])
            gprow = T(st, "gprow", [1, 1024])
            gptmp = T(st, "gptmp", [128, 1024])
            nwa = 0
            for l in range(DEPTH):
                dma(bada[:], bada_d[l], w=[bada])
                dma(gpr[:], gpre_d[l], w=[gpr])
                dma(gpo[:], gpost_d[l], w=[gpo])
                for cg in range(6):
                    wt = wa[nwa % 2]
                    nwa += 1
                    dma(wt[:], wada_d[l, :, :, cg * 512:(cg + 1) * 512], w=[wt])
                    pb = PS[cg % 2]
                    for c in range(8):
                        mm(pb[0:1, :], sc[:, c:c + 1], wt[:, c, :], start=(c == 0), stop=(c == 7),
                           r=[sc, wt], w=[pb], inc=(c == 7))
                    op("dve", lambda e: e.tensor_tensor(out=row[0:1, cg * 512:(cg + 1) * 512], in0=pb[0:1, :],
                                                        in1=bada[0:1, cg * 512:(cg + 1) * 512], op=ALU.add),
                       r=[pb, bada], w=[(row, cg)])
                op("dve", lambda e: e.scalar_tensor_tensor(out=arow[:], in0=row[0:1, 1024:2048], scalar=1.0,
                                                           in1=gpr[:], op0=ALU.add, op1=ALU.mult),
                   r=[row, gpr], w=[arow])
                op("dve", lambda e: e.tensor_tensor(out=gprow[:], in0=row[0:1, 2048:3072], in1=gpo[:], op=ALU.mult),
                   r=[row, gpo], w=[gprow])
                pc = PS[2]
                for c in range(8):
                    mm(pc[:, c:c + 1], arow[0:1, c * 128:(c + 1) * 128], ones[0:1, 0:1], r=[arow, C], w=[pc], inc=False)
                for c in range(8):
                    mm(pc[:, 8 + c:9 + c], row[0:1, c * 128:(c + 1) * 128], ones[0:1, 0:1], r=[row, C], w=[pc],
                       inc=(c == 7))
                op("dve", lambda e: e.tensor_copy(out=AB[l][:], in_=pc[:, 0:16]), r=[pc], w=[AB[l]])
                for hf in range(2):
                    pg = PS[3 + hf]
                    mm(pg[:, :], ones[0:1, 0:128], gprow[0:1, hf * 512:(hf + 1) * 512], r=[gprow, C], w=[pg])
                    op("act", lambda e: e.activation(out=gptmp[:, hf * 512:(hf + 1) * 512], in_=pg[:, :], func=AF.Copy),
                       r=[pg], w=[(gptmp, hf)])
                dma(gp_d[l], gptmp[:], r=[gptmp], w=[(kgp, l)])
            kb.barrier()

        hT = T(gst, "hT", [128, 8, S], BF16)

        for l in range(DEPTH):
          try:
            xin_d = x_d if l == 0 else xmid_d
            xout_d = y_d if l == DEPTH - 1 else xmid_d
            xin_k = kx if l == 0 else kmid
            xout_k = ky if l == DEPTH - 1 else kmid

            for st in phase("p1"):
                xt = [T(st, "xt%d" % i, [128, 1024]) for i in range(4)]
                xn = [T(st, "xn%d" % i, [128, 1024]) for i in range(3)]
                junk = T(st, "junk", [128, 1024], BF16)
                ss = T(st, "ss", [128, NT])
                rstd = T(st, "rstd", [128, NT])
                for t in range(NT):
                    xtt, xnt = xt[t % 4], xn[t % 3]
                    dma(xtt[:], xin_d[t * 128:(t + 1) * 128, :], r=[(xin_k, t)], w=[xtt])
                    op("act", lambda e: e.activation(out=junk[:], in_=xtt[:], func=AF.Square,
                                                     accum_out=ss[:, t:t + 1]), r=[xtt], w=[junk, (ss, t)])
                    op("act", lambda e: e.activation(out=rstd[:, t:t + 1], in_=ss[:, t:t + 1], func=AF.Sqrt,
                                                     scale=1.0 / D_MODEL, bias=epsT[:]), r=[(ss, t), epsT],
                       w=[(rstd, t)])
                    op("dve", lambda e: e.reciprocal(out=rstd[:, t:t + 1], in_=rstd[:, t:t + 1]), r=[(rstd, t)],
                       w=[(rstd, t)])
                    op("dve", lambda e: e.tensor_scalar(out=xnt[:], in0=xtt[:], scalar1=rstd[:, t:t + 1],
                                                        scalar2=None, op0=ALU.mult), r=[xtt, (rstd, t)], w=[xnt])
                    for hf in range(2):
                        pb = PS[(t % 2) * 2 + hf]
                        for cc in range(4):
                            c = hf * 4 + cc
                            tr(pb[:, cc * 128:(cc + 1) * 128], xnt[:, c * 128:(c + 1) * 128], ident, r=[xnt, C],
                               w=[pb])
                        for cc in range(4):
                            c = hf * 4 + cc
                            eng = "act" if hf == 0 else "dve"
                            if eng == "act":
                                op("act", lambda e: e.activation(out=hT[:, c, t * 128:(t + 1) * 128],
                                                                 in_=pb[:, cc * 128:(cc + 1) * 128], func=AF.Identity,
                                                                 scale=AB[l][:, c:c + 1], bias=AB[l][:, 8 + c:9 + c]),
                                   r=[pb, AB[l]], w=[(hT, t // 4)])
                            else:
                                op("dve", lambda e: e.tensor_scalar(out=hT[:, c, t * 128:(t + 1) * 128],
                                                                    in0=pb[:, cc * 128:(cc + 1) * 128],
                                                                    scalar1=AB[l][:, c:c + 1],
                                                                    scalar2=AB[l][:, 8 + c:9 + c],
                                                                    op0=ALU.mult, op1=ALU.add),
                                   r=[pb, AB[l]], w=[(hT, t // 4)])
                kb.barrier()
            if "hT" in dbg and l == 0:
                with ExitStack() as st:
                    d = dbg_out("hT", [128, 8, S])
                    tmp = T(st, "dbgtmp", [128, 8, S])
                    op("dve", lambda e: e.tensor_copy(out=tmp[:], in_=hT[:]), r=[hT], w=[tmp])
                    dma(d[:, :, :], tmp[:], r=[tmp])
                    kb.barrier()

            for st in phase("dn"):
                Wdn = T(st, "Wdn", [128, 8, 2048], BF16)
                Wba = T(st, "Wba", [128, 8, 8], BF16)
                for c in range(8):
                    dma(Wdn[:, c, :], wdn_d[l, :, c, :], w=[(Wdn, c)], q="pool")
                dma(Wba[:], wba_d[l], w=[Wba], q="pool")
                convw = T(st, "convw", [128, 48])
                alog = T(st, "alog", [128, 4])
                dtb = T(st, "dtb", [128, 4])
                dng = T(st, "dng", [128, 1])
                dma(convw[:], convw_d[l], w=[convw])
                dma(alog[:], alog_d[l], w=[alog])
                dma(dtb[:], dtb_d[l], w=[dtb])
                dma(dng[:], dng_d[l], w=[dng])
                st2 = ExitStack()
                BETA = T(st, "BETA", [128, NT, 4])
                NBETA = T(st, "NBETA", [128, NT, 4])
                GRAW = T(st, "GRAW", [128, NT, 4])
                GC = T(st, "GC", [128, NT, 4])
                NEXPG = T(st, "NEXPG", [128, NT, 4])
                negA = T(st, "negA", [128, 4])
                BG = T(st2, "BG", [128, NT, 8])
                AA = T(st2, "AA", [128, NT, 4])
                AX_ = T(st2, "AXs", [128, NT, 4])
                pbg = PS[0]
                for t in range(NT):
                    for c in range(8):
                        mm(pbg[:, t * 8:(t + 1) * 8], hT[:, c, t * 128:(t + 1) * 128], Wba[:, c, :],
                           start=(c == 0), stop=(c == 7), r=[(hT, t // 4), Wba], w=[pbg], inc=(c == 7))
                op("dve", lambda e: e.tensor_copy(out=BG[:].rearrange("p t e -> p (t e)"), in_=pbg[:, 0:NT * 8]),
                   r=[pbg], w=[BG])
                op("act", lambda e: e.activation(out=BETA[:], in_=BG[:, :, 0:4], func=AF.Sigmoid), r=[BG], w=[BETA])
                op("dve", lambda e: e.tensor_scalar(out=NBETA[:], in0=BETA[:], scalar1=-1.0, scalar2=None,
                                                    op0=ALU.mult), r=[BETA], w=[NBETA])
                for h in range(4):
                    op("dve", lambda e: e.tensor_scalar(out=AA[:, :, h], in0=BG[:, :, 4 + h], scalar1=dtb[:, h:h + 1],
                                                        scalar2=None, op0=ALU.add), r=[BG, dtb], w=[AA])
                op("act", lambda e: e.activation(out=AX_[:], in_=AA[:], func=AF.Abs), r=[AA], w=[AX_])
                op("act", lambda e: e.activation(out=AX_[:], in_=AX_[:], func=AF.Exp, scale=-1.0), r=[AX_], w=[AX_])
                op("dve", lambda e: e.tensor_scalar(out=AX_[:], in0=AX_[:], scalar1=1.0, scalar2=None, op0=ALU.add),
                   r=[AX_], w=[AX_])
                op("act", lambda e: e.activation(out=AX_[:], in_=AX_[:], func=AF.Ln), r=[AX_], w=[AX_])
                op("dve", lambda e: e.scalar_tensor_tensor(out=AA[:], in0=AA[:], scalar=0.0, in1=AX_[:],
                                                           op0=ALU.max, op1=ALU.add), r=[AA, AX_], w=[AA])
                op("act", lambda e: e.activation(out=negA[:], in_=alog[:], func=AF.Exp), r=[alog], w=[negA])
                op("dve", lambda e: e.tensor_scalar(out=negA[:], in0=negA[:], scalar1=-1.0, scalar2=None,
                                                    op0=ALU.mult), r=[negA], w=[negA])
                for h in range(4):
                    op("dve", lambda e: e.tensor_scalar(out=GRAW[:, :, h], in0=AA[:, :, h], scalar1=negA[:, h:h + 1],
                                                        scalar2=None, op0=ALU.mult), r=[AA, negA], w=[GRAW])
                pgc = PS[1]
                mm(pgc[:, 0:NT * 4], U, GRAW[:].rearrange("p t e -> p (t e)"), r=[C, GRAW], w=[pgc])
                op("dve", lambda e: e.tensor_copy(out=GC[:].rearrange("p t e -> p (t e)"), in_=pgc[:, 0:NT * 4]),
                   r=[pgc], w=[GC])
                op("act", lambda e: e.activation(out=NEXPG[:], in_=GC[:], func=AF.Exp), r=[GC], w=[NEXPG])
                op("dve", lambda e: e.tensor_scalar(out=NEXPG[:], in0=NEXPG[:], scalar1=-1.0, scalar2=None,
                                                    op0=ALU.mult), r=[NEXPG], w=[NEXPG])
                if "graw" in dbg and l == 0:
                    d = dbg_out("graw", [128, NT, 4])
                    dma(d[:, :, :], GRAW[:], r=[GRAW])
                    d = dbg_out("beta", [128, NT, 4])
                    dma(d[:, :, :], BETA[:], r=[BETA])

                kb.barrier()
                st2.close()
                ck(1)
                pre = [T(st, "pre%d" % i, [128, 515]) for i in range(3)]
                halo = T(st, "halo", [128, 12, 3])
                op("dve", lambda e: e.memset(halo[:], 0.0), w=[halo])
                qkv = [[T(st, "qkv%d_%d" % (i, j), [128, 512]) for j in range(3)] for i in range(2)]
                zs = [T(st, "zs%d" % i, [128, 512], BF16) for i in range(4)]
                cv = [T(st, "cv0", [128, 512])] * 2
                sq = [T(st, "sq0", [128, 512])] * 2
                oTg = [T(st, "oTg%d" % i, [128, 512]) for i in range(2)]
                ogb = [T(st, "ogb0", [128, 512], BF16)] * 2
                Sst = [[T(st, "S%d_%d" % (h, i), [128, 128]) for i in range(2)] for h in range(4)]
                for h in range(4):
                    op("dve", lambda e: e.tensor_scalar(out=R(Sst[h][0][:]), in0=ident, scalar1=0.0, scalar2=None,
                                                        op0=ALU.mult), r=[C], w=[Sst[h][0]])
                spar = [0, 0, 0, 0]
                NCH = 8
                ALIAS = {"Dm": 0, "tq": 0, "DecT": 1, "Xo": 1, "Pe": 2, "ktok": 3, "Xe": 3, "NoffT": 3, "Po": 4,
                         "B": 5, "BT": 6, "WbdT": 6, "Noff": 7, "Z1": 7, "ExpG": 8, "XTo": 8, "Lm": 9, "XTe": 9}
                scr = []
                for i in range(NCH):
                    wide = T(st, "s%d" % i, [128, 9 * 128])
                    plain = T(st, "s%da" % i, [128, 128])
                    d_ = {n: (SV(wide, j - 1) if j > 0 else plain) for n, j in ALIAS.items()}
                    d_["XPo"] = SV(wide, 0, 2)
                    d_["XPe"] = SV(wide, 2, 2)
                    scr.append(d_)
                OUTN = ["W", "QKdT", "kdec", "qg", "vtok", "kTc"]
                outs = [{n: T(st, "o%d_%s" % (i, n), [128, 128]) for n in OUTN} for i in range(NCH)]
                EGL = T(st, "EGL", [128, NCH])
                Yr = [T(st, "Yr%d" % i, [128, 128]) for i in range(2)]
                Vn = [T(st, "Vn%d" % i, [128, 128]) for i in range(2)]

                def prep8(chains, pump=lambda: None):
                    pump_on = [False]

                    def each(fn):
                        for ci, ch in enumerate(chains):
                            fn(ch, ch["s"], ch["o"], PS[ch["i"]])
                            if ci % 2 == 1 and pump_on[0]:
                                pump(1)

                    def f(ch, s, o, pb):
                        h, n, sl = ch["h"], ch["n"], ch["sl"]
                        kT, vT = ch["kT"], ch["vT"]
                        tr(pb[:, 0:128], kT[:, sl], ident, r=[kT, C], w=[pb])
                        tr(pb[:, 128:256], vT[:, sl], ident, r=[vT, C], w=[pb])
                        mm(pb[:, 256:384], GRAW[:, n, h:h + 1].to_broadcast([128, 128]), U, r=[GRAW, C], w=[pb])
                        op("act", lambda e: e.activation(out=R(s["ktok"][:]), in_=pb[:, 0:128], func=AF.Copy),
                           r=[pb], w=[s["ktok"]])
                        op("dve", lambda e: e.tensor_copy(out=o["vtok"][:], in_=pb[:, 128:256]), r=[pb],
                           w=[o["vtok"]])
                        op("dve", lambda e: e.tensor_scalar(out=s["Dm"][:], in0=pb[:, 256:384],
                                                            scalar1=GC[:, n, h:h + 1], scalar2=0.0,
                                                            op0=ALU.subtract, op1=ALU.min),
                           r=[pb, GC], w=[s["Dm"]])
                        op("act", lambda e: e.activation(out=R(s["ExpG"][:]), in_=pb[:, 256:384], func=AF.Exp),
                           r=[pb], w=[s["ExpG"]])
                    each(f)

                    def f(ch, s, o, pb):
                        op("act", lambda e: e.activation(out=R(s["DecT"][:]), in_=s["Dm"][:], func=AF.Exp),
                           r=[s["Dm"]], w=[s["DecT"]])
                    each(f)

                    def f(ch, s, o, pb):
                        h, n, sl = ch["h"], ch["n"], ch["sl"]
                        kT, qT = ch["kT"], ch["qT"]
                        mm(pb[:, 0:128], R(kT[:, sl]), R(kT[:, sl]), r=[kT], w=[pb], inc=False)
                        mm(pb[:, 128:256], R(kT[:, sl]), R(qT[:, sl]), r=[kT, qT], w=[pb])
                        op("dve", lambda e: e.tensor_tensor(out=R(s["Lm"][:]), in0=pb[:, 0:128], in1=s["DecT"][:],
                                                            op=ALU.mult), r=[pb, s["DecT"]], w=[s["Lm"]])
                        op("dve", lambda e: e.tensor_tensor(out=s["tq"][:], in0=pb[:, 128:256], in1=s["DecT"][:],
                                                            op=ALU.mult), r=[pb, s["DecT"]], w=[s["tq"]])
                        op("dve", lambda e: e.scalar_tensor_tensor(out=R(s["B"][:]), in0=s["Lm"][:],
                                                                   scalar=NBETA[:, n, h:h + 1], in1=Mbd,
                                                                   op0=ALU.mult, op1=ALU.mult),
                           r=[s["Lm"], NBETA, C], w=[s["B"]])
                        op("dve", lambda e: e.scalar_tensor_tensor(out=R(s["Noff"][:]), in0=s["Lm"][:],
                                                                   scalar=BETA[:, n, h:h + 1], in1=Moff,
                                                                   op0=ALU.mult, op1=ALU.mult),
                           r=[s["Lm"], BETA, C], w=[s["Noff"]])
                        op("pool", lambda e: e.tensor_tensor(out=R(o["QKdT"][:]), in0=s["tq"][:], in1=U,
                                                             op=ALU.mult), r=[s["tq"], C], w=[o["QKdT"]])
                        op("act", lambda e: e.activation(out=R(o["kdec"][:]), in_=s["ktok"][:], func=AF.Identity,
                                                         scale=s["DecT"][:, 127:128]), r=[s["ktok"], s["DecT"]],
                           w=[o["kdec"]])
                        op("dve", lambda e: e.tensor_tensor(out=R(o["qg"][:]), in0=qT[:, sl], in1=s["ExpG"][:],
                                                            op=ALU.mult), r=[qT, s["ExpG"]], w=[o["qg"]])
                        op("dve", lambda e: e.tensor_copy(out=R(o["kTc"][:]), in_=kT[:, sl]), r=[kT], w=[o["kTc"]])
                        op("dve", lambda e: e.tensor_copy(out=EGL[:, ch["i"]:ch["i"] + 1],
                                                          in_=s["ExpG"][:, 127:128]),
                           r=[s["ExpG"]], w=[(EGL, ch["i"])])
                    each(f)

                    pump_on[0] = True
                    def f(ch, s, o, pb):
                        tr(pb[:, 256:384], s["B"][:], ident, r=[s["B"], C], w=[pb])
                        op("act", lambda e: e.activation(out=R(s["BT"][:]), in_=pb[:, 256:384], func=AF.Copy),
                           r=[pb], w=[s["BT"]])
                        op("dve", lambda e: e.tensor_tensor(out=R(s["Pe"][:]), in0=s["B"][:], in1=ident,
                                                            op=ALU.add), r=[s["B"], C], w=[s["Pe"]])
                    each(f)

                    def f(ch, s, o, pb):
                        mm(pb[:, 0:128], R(s["BT"][:]), R(s["B"][:]), r=[s["BT"], s["B"]], w=[pb])
                        op("act", lambda e: e.activation(out=R(s["Xo"][:]), in_=pb[:, 0:128], func=AF.Copy),
                           r=[pb], w=[s["Xo"]])
                    each(f)
                    pump()
                    for j in range(1, 6):
                        odd = (j % 2 == 1)
                        Xc, Pc, XTc, XPc = ("Xo", "Pe", "XTo", "XPo") if odd else ("Xe", "Po", "XTe", "XPe")
                        Xn, Pn = ("Xe", "Po") if odd else ("Xo", "Pe")

                        def f(ch, s, o, pb):
                            tr(pb[:, 256:384], s[Xc][:], ident, r=[s[Xc], C], w=[pb])
                            op("act", lambda e: e.activation(out=R(s[XTc][:]), in_=pb[:, 256:384], func=AF.Copy),
                               r=[pb], w=[s[XTc]])
                        each(f)
                        pump()

                        def f(ch, s, o, pb):
                            if j < 5:
                                mm(pb[:, 0:256], R(s[XTc][:]), R(s[XPc][:]), r=[s[XTc], s[Xc], s[Pc]], w=[pb])
                                op("act", lambda e: e.activation(out=R(s[Xn][:]), in_=pb[:, 0:128], func=AF.Copy),
                                   r=[pb], w=[s[Xn]])
                                op("dve", lambda e: e.tensor_tensor(out=R(s[Pn][:]), in0=pb[:, 128:256],
                                                                    in1=s[Pc][:], op=ALU.add),
                                   r=[pb, s[Pc]], w=[s[Pn]])
                            else:
                                mm(pb[:, 128:256], R(s[XTc][:]), R(s[Pc][:]), r=[s[XTc], s[Pc]], w=[pb])
                                op("dve", lambda e: e.tensor_tensor(out=R(s[Pn][:]), in0=pb[:, 128:256],
                                                                    in1=s[Pc][:], op=ALU.add),
                                   r=[pb, s[Pc]], w=[s[Pn]])
                        each(f)
                        pump()
                    Pf = "Po"

                    def f(ch, s, o, pb):
                        tr(pb[:, 0:128], s[Pf][:], ident, r=[s[Pf], C], w=[pb])
                        tr(pb[:, 128:256], s["Noff"][:], ident, r=[s["Noff"], C], w=[pb])
                        op("act", lambda e: e.activation(out=R(s["WbdT"][:]), in_=pb[:, 0:128], func=AF.Copy),
                           r=[pb], w=[s["WbdT"]])
                        op("dve", lambda e: e.tensor_copy(out=R(s["NoffT"][:]), in_=pb[:, 128:256]), r=[pb],
                           w=[s["NoffT"]])
                    each(f)
                    pump()

                    def f(ch, s, o, pb):
                        mm(pb[:, 0:128], R(s["NoffT"][:]), R(s[Pf][:]), r=[s["NoffT"], s[Pf]], w=[pb])
                        op("act", lambda e: e.activation(out=R(s["Z1"][:]), in_=pb[:, 0:128], func=AF.Copy),
                           r=[pb], w=[s["Z1"]])
                    each(f)
                    pump()

                    def f(ch, s, o, pb):
                        mm(pb[:, 128:256], R(s["WbdT"][:]), R(s["Z1"][:]), r=[s["WbdT"], s["Z1"]], w=[pb])
                        op("dve", lambda e: e.tensor_tensor(out=R(o["W"][:]), in0=s[Pf][:], in1=pb[:, 128:256],
                                                            op=ALU.subtract), r=[s[Pf], pb], w=[o["W"]])
                    each(f)
                    pump()

                nrec = [0]

                def recur_pair(chs, g):
                    st_ = []
                    for ch in chs:
                        h = ch["h"]
                        So = Sst[h][spar[h]]
                        Sn = Sst[h][1 - spar[h]]
                        spar[h] = 1 - spar[h]
                        bb = 4 * ch["hi"]
                        st_.append((ch, So, Sn, PS[bb], PS[bb + 1], PS[bb + 2], PS[bb + 3],
                                    Yr[nrec[0] % 2], Vn[nrec[0] % 2]))
                        nrec[0] += 1
                    for ch, So, Sn, pa, pb2, pc, pd, Y, vnew in st_:
                        h, n, o = ch["h"], ch["n"], ch["o"]
                        mm(pa[:, 0:128], R(o["kTc"][:]), R(So[:]), r=[o["kTc"], So], w=[pa])
                        op("dve", lambda e: e.scalar_tensor_tensor(out=R(Y[:]), in0=pa[:, 0:128],
                                                                   scalar=NEXPG[:, n, h:h + 1], in1=o["vtok"][:],
                                                                   op0=ALU.mult, op1=ALU.add),
                           r=[pa, NEXPG, o["vtok"]], w=[Y])
                    for ch, So, Sn, pa, pb2, pc, pd, Y, vnew in st_:
                        h, n, o = ch["h"], ch["n"], ch["o"]
                        mm(pb2[:, 0:128], R(o["W"][:]), R(Y[:]), r=[o["W"], Y], w=[pb2])
                        op("act", lambda e: e.activation(out=R(vnew[:]), in_=pb2[:, 0:128], func=AF.Identity,
                                                         scale=BETA[:, n, h:h + 1]), r=[pb2, BETA], w=[vnew])
                    for ch, So, Sn, pa, pb2, pc, pd, Y, vnew in st_:
                        o = ch["o"]
                        mm(pd[:, 0:128], R(o["kdec"][:]), R(vnew[:]), r=[o["kdec"], vnew], w=[pd])
                        op("dve", lambda e: e.scalar_tensor_tensor(out=R(Sn[:]), in0=So[:],
                                                                   scalar=EGL[:, ch["i"]:ch["i"] + 1],
                                                                   in1=pd[:, 0:128], op0=ALU.mult, op1=ALU.add),
                           r=[So, (EGL, ch["i"]), pd], w=[Sn])
                    for ch, So, Sn, pa, pb2, pc, pd, Y, vnew in st_:
                        o, sl = ch["o"], ch["sl"]
                        mm(pc[:, 0:128], R(So[:]), R(o["qg"][:]), start=True, stop=False, r=[So, o["qg"]], w=[pc],
                           inc=False)
                        mm(pc[:, 0:128], R(vnew[:]), R(o["QKdT"][:]), start=False, stop=True, r=[vnew, o["QKdT"]],
                           w=[pc])
                        ot = oTg[ch["hi"]]
                        op("act", lambda e: e.activation(out=ot[:, sl], in_=pc[:, 0:128], func=AF.Copy), r=[pc],
                           w=[(ot, ch["cc"])])

                def stageA(g, hp, par, res):
                    gs = slice(g * 512, (g + 1) * 512)
                    chains = []
                    for hi in range(2):
                        h = 2 * hp + hi
                        qk = qkv[hi]
                        zt = zs[2 * par + hi]
                        for ty in range(4):
                            pb = PS[4 * hi + ty]
                            col = ty * 512 + h * 128
                            for c in range(8):
                                mm(pb[:, :], Wdn[:, c, col:col + 128], hT[:, c, gs], start=(c == 0),
                                   stop=(c == 7), r=[(Wdn, c), (hT, g)], w=[pb], inc=(c == 7))
                            if ty < 3:
                                ch_ = ty * 4 + h
                                op("dve", lambda e: e.tensor_copy(out=pre[ty][:, 0:3], in_=halo[:, ch_, :]),
                                   r=[(halo, ch_)], w=[(pre[ty], 0)])
                                op("act", lambda e: e.activation(out=pre[ty][:, 3:515], in_=pb[:, :],
                                                                 func=AF.Copy), r=[pb], w=[(pre[ty], 1)])
                                op("dve", lambda e: e.tensor_copy(out=halo[:, ch_, :], in_=pre[ty][:, 512:515]),
                                   r=[(pre[ty], 1)], w=[(halo, ch_)])
                                yield
                                cvt = cv[ty % 2]
                                wk = lambda k: convw[:, ch_ * 4 + k:ch_ * 4 + k + 1]
                                op("act", lambda e: e.activation(out=cvt[:], in_=pre[ty][:, 0:512],
                                                                 func=AF.Identity, scale=wk(0)),
                                   r=[pre[ty], convw], w=[cvt])
                                for k in range(1, 4):
                                    op("dve", lambda e: e.scalar_tensor_tensor(out=cvt[:],
                                                                               in0=pre[ty][:, k:k + 512],
                                                                               scalar=wk(k), in1=cvt[:],
                                                                               op0=ALU.mult, op1=ALU.add),
                                       r=[pre[ty], convw, cvt], w=[cvt])
                                    yield
                                op("act", lambda e: e.activation(out=(R(qk[ty][:]) if ty < 2 else qk[ty][:]),
                                                                 in_=cvt[:], func=AF.Silu),
                                   r=[cvt], w=[qk[ty]])
                            else:
                                op("act", lambda e: e.activation(out=zt[:], in_=pb[:, :], func=AF.Silu),
                                   r=[pb], w=[zt])
                            yield
                        for ty in range(2):
                            sqt = sq[ty]
                            pb = PS[4 * hi + ty]
                            op("act", lambda e: e.activation(out=R(sqt[:]), in_=qk[ty][:], func=AF.Square),
                               r=[qk[ty]], w=[sqt])
                            mm(pb[:, :], R(onesr[:]), R(sqt[:]), r=[onesr, sqt], w=[pb])
                            rt_ = cv[0]
                            op("act", lambda e: e.activation(out=rt_[:], in_=pb[:, :], func=AF.Ln,
                                                             bias=epsT[:]), r=[pb, epsT], w=[rt_])
                            op("act", lambda e: e.activation(out=rt_[:], in_=rt_[:], func=AF.Exp, scale=-0.5),
                               r=[rt_], w=[rt_])
                            sc_ = (128.0 ** -0.5) if ty == 0 else 1.0
                            op("dve", lambda e: e.scalar_tensor_tensor(out=R(qk[ty][:]), in0=qk[ty][:],
                                                                       scalar=sc_, in1=rt_[:], op0=ALU.mult,
                                                                       op1=ALU.mult),
                               r=[qk[ty], rt_], w=[qk[ty]])
                            yield
                        if dbg and l == 0 and g == 0 and h == 0:
                            for nm, tt in (("q", qk[0]), ("k", qk[1]), ("v", qk[2])):
                                if nm in dbg:
                                    d = dbg_out(nm, [128, 512])
                                    dma(d[:, :], tt[:], r=[tt])
                        for cc in range(4):
                            i = hi * 4 + cc
                            chains.append({"h": h, "hi": hi, "cc": cc, "n": g * 4 + cc, "i": i,
                                           "sl": slice(cc * 128, (cc + 1) * 128), "qT": qk[0], "kT": qk[1],
                                           "vT": qk[2], "s": scr[i], "o": outs[i]})
                    res["chains"] = chains

                def drain(gen):
                    if gen is not None:
                        for _ in gen:
                            pass

                pairs = [(g, hp) for g in range(NG) for hp in range(2)]
                resA = [dict() for _ in pairs]
                gens = [stageA(g, hp, pi % 2, resA[pi]) for pi, (g, hp) in enumerate(pairs)]
                drain(gens[0])
                for pi, (g, hp) in enumerate(pairs):
                    gs = slice(g * 512, (g + 1) * 512)
                    chains = resA[pi]["chains"]
                    nxt = gens[pi + 1] if pi + 1 < len(pairs) else None

                    def pump(n=1):
                        if nxt is not None:
                            for _ in range(n):
                                if next(nxt, "done") == "done":
                                    break
                    prep8(chains, pump)
                    drain(nxt)
                    for cc in range(4):
                        recur_pair([ch for ch in chains if ch["cc"] == cc], g)
                    for hi in range(2):
                        h = 2 * hp + hi
                        zt = zs[2 * (pi % 2) + hi]
                        ot = oTg[hi]
                        og = ogb[hi]
                        sqt = sq[hi]
                        pb = PS[4 * hi]
                        op("act", lambda e: e.activation(out=R(sqt[:]), in_=ot[:], func=AF.Square), r=[ot],
                           w=[sqt])
                        mm(pb[:, :], R(onesr[:]), R(sqt[:]), r=[onesr, sqt], w=[pb])
                        sqt = cv[0]
                        op("act", lambda e: e.activation(out=sqt[:], in_=pb[:, :], func=AF.Ln,
                                                         scale=1.0 / 128, bias=epsT[:]), r=[pb, epsT], w=[sqt])
                        op("act", lambda e: e.activation(out=sqt[:], in_=sqt[:], func=AF.Exp, scale=-0.5),
                           r=[sqt], w=[sqt])
                        if "odn" in dbg and l == 0 and h == 0 and g == 0:
                            d = dbg_out("odn", [128, 512])
                            dma(d[:, :], ot[:], r=[ot])
                        op("dve", lambda e: e.scalar_tensor_tensor(out=sqt[:], in0=ot[:], scalar=dng[:, 0:1],
                                                                   in1=sqt[:], op0=ALU.mult, op1=ALU.mult),
                           r=[ot, dng, sqt], w=[sqt])
                        op("dve", lambda e: e.tensor_tensor(out=og[:], in0=sqt[:], in1=zt[:], op=ALU.mult),
                           r=[sqt, zt], w=[og])
                        dma(ogdn_d[h, :, gs], og[:], r=[og], w=[(kogdn, g)])
                kb.barrier()

            for st in phase("mb"):
                Wmb = T(st, "Wmb", [128, 8, 2048], BF16)
                for c in range(8):
                    dma(Wmb[:, c, :], wmb_d[l, :, c, :], w=[(Wmb, c)], q="pool")
                Vt = T(st, "Vt", [128, NT, 8, 65], BF16)
                op("dve", lambda e: e.memset(Vt[:, :, :, 64:65], 1.0), w=[Vt])
                for t in range(NT):
                    pb = PS[t % 2]
                    for c in range(8):
                        mm(pb[:, :], hT[:, c, t * 128:(t + 1) * 128], Wmb[:, c, 1024:1536], start=(c == 0),
                           stop=(c == 7), r=[(hT, t // 4), (Wmb, c)], w=[pb], inc=(c == 7))
                    op("act" if t % 2 else "dve",
                       lambda e: (e.activation(out=Vt[:, t, :, 0:64], in_=pb[:, :].rearrange("p (h d) -> p h d", h=8),
                                               func=AF.Copy) if t % 2 else
                                  e.tensor_copy(out=Vt[:, t, :, 0:64],
                                                in_=pb[:, :].rearrange("p (h d) -> p h d", h=8))),
                       r=[pb], w=[(Vt, t)])
                KaT = T(st, "KaT", [128, S], BF16)
                QaT = T(st, "QaT", [128, S], BF16)
                zsm = T(st, "zsm", [64, S], BF16)
                ogm = T(st, "ogm", [64, S], BF16)
                kf = [T(st, "kf%d" % i, [128, 512]) for i in range(2)]
                qf = [T(st, "qf%d" % i, [128, 512]) for i in range(2)]
                sqm = [T(st, "sqm%d" % i, [128, 512]) for i in range(2)]
                kmT = T(st, "kmT", [128, 16])
                shr = T(st, "shr", [64, 512])
                km2 = T(st, "km2", [64, NG + 1])
                gm4 = T(st, "gm4", [128, 4, 16])
                top84 = T(st, "top84", [128, 4, 8])
                mbt4 = [T(st, "mbt4_%d" % i, [128, 4, 16]) for i in range(2)]
                rden = T(st, "rden", [128, 512])
                t1 = [T(st, "t1_%d" % i, [64, 512]) for i in range(2)]
                PT = [T(st, "PT%d" % i, [128, 512], BF16) for i in range(4)]
                nS = [0]
                op("dve", lambda e: e.memset(KaT[0:64, :], 0.0), w=[KaT])
                op("dve", lambda e: e.memset(QaT[0:64, :], 0.0), w=[QaT])
                op("dve", lambda e: e.memset(KaT[32:33, :], 1.0), w=[KaT])
                for n in range(NB):
                    op("dve", lambda e: e.tensor_copy(out=KaT[0:16, n * 256:(n + 1) * 256],
                                                      in_=ident[0:16, n:n + 1].to_broadcast([16, 256])),
                       r=[C], w=[KaT])
                npt = 0
                for h in range(8):
                    op("dve", lambda e: e.tensor_scalar(out=R(kmT[:]), in0=ident[:, 0:16], scalar1=0.0, scalar2=None,
                                                        op0=ALU.mult), r=[C], w=[kmT])
                    def projK(g):
                        pb = PS[g % 2]
                        col = 512 + h * 64
                        for c in range(8):
                            mm(pb[64:128, :], Wmb[:, c, col:col + 64], hT[:, c, g * 512:(g + 1) * 512],
                               start=(c == 0), stop=(c == 7), r=[(Wmb, c), (hT, g)], w=[pb], inc=(c == 7))

                    def postK(g):
                        gs = slice(g * 512, (g + 1) * 512)
                        pb = PS[g % 2]
                        kft = kf[g % 2]
                        op("act", lambda e: e.activation(out=kft[64:128, :], in_=pb[64:128, :], func=AF.Copy),
                           r=[pb], w=[kft])
                        op("act", lambda e: e.activation(out=KaT[64:128, gs], in_=pb[64:128, :], func=AF.Copy),
                           r=[pb], w=[(KaT, g)])
                        op("dve", lambda e: e.tensor_reduce(out=R(kmT[64:128, 2 * g:2 * g + 2]),
                                                            in_=kft[64:128, :].rearrange("p (b t) -> p b t", b=2),
                                                            axis=AX.X, op=ALU.add), r=[kft], w=[kmT])
                        sqt = sqm[g % 2]
                        op("dve", lambda e: e.tensor_tensor(out=sqt[64:128, :], in0=kft[64:128, :],
                                                            in1=kft[64:128, :], op=ALU.mult), r=[kft], w=[sqt])
                        pr = PS[2]
                        mm(pr[32:33, :], ones[64:128, 0:1], sqt[64:128, :], r=[C, sqt], w=[pr])
                        op("dve", lambda e: e.tensor_reduce(out=km2[32:33, g:g + 1], in_=pr[32:33, :], axis=AX.X,
                                                            op=ALU.max), r=[pr], w=[km2])
                    projK(0)
                    for g in range(NG):
                        if g + 1 < NG:
                            projK(g + 1)
                        postK(g)
                    op("dve", lambda e: e.tensor_scalar(out=R(kmT[64:128, :]), in0=kmT[64:128, :], scalar1=1.0 / 256,
                                                        scalar2=None, op0=ALU.mult), r=[kmT], w=[kmT])
                    op("dve", lambda e: e.tensor_reduce(out=km2[32:33, NG:NG + 1], in_=km2[32:33, 0:NG], axis=AX.X,
                                                        op=ALU.max), r=[km2], w=[km2])
                    for g in range(NG):
                        gs = slice(g * 512, (g + 1) * 512)
                        pb = PS[g % 2]
                        col = 1536 + h * 64
                        for c in range(8):
                            mm(pb[0:64, :], Wmb[:, c, col:col + 64], hT[:, c, gs], start=(c == 0), stop=(c == 7),
                               r=[(Wmb, c), (hT, g)], w=[pb], inc=(c == 7))
                        op("act", lambda e: e.activation(out=zsm[:, gs], in_=pb[0:64, :], func=AF.Silu), r=[pb],
                           w=[(zsm, g)])

                    def projQ(g):
                        pb = PS[g % 3]
                        col = h * 64
                        for c in range(8):
                            mm(pb[64:128, :], Wmb[:, c, col:col + 64], hT[:, c, g * 512:(g + 1) * 512],
                               start=(c == 0), stop=(c == 7), r=[(Wmb, c), (hT, g)], w=[pb], inc=(c == 7))

                    def postQ1(g):
                        gs = slice(g * 512, (g + 1) * 512)
                        pb = PS[g % 3]
                        qft = qf[g % 2]
                        op("act", lambda e: e.activation(out=R(qft[64:128, :]), in_=pb[64:128, :], func=AF.Identity,
                                                         scale=0.125), r=[pb], w=[qft])
                        op("act", lambda e: e.activation(out=QaT[64:128, gs], in_=pb[64:128, :], func=AF.Identity,
                                                         scale=0.125), r=[pb], w=[(QaT, g)])
                        sqt = sqm[g % 2]
                        op("dve", lambda e: e.tensor_tensor(out=sqt[64:128, :], in0=qft[64:128, :],
                                                            in1=qft[64:128, :], op=ALU.mult), r=[qft], w=[sqt])
                        pgt = PS[4]
                        for tt in range(4):
                            mm(pgt[:, tt * 16:(tt + 1) * 16], R(qft[64:128, tt * 128:(tt + 1) * 128]), R(kmT[64:128, :]),
                               r=[qft, kmT], w=[pgt], inc=(tt == 3))
                        pr = PS[5]
                        mm(pr[32:33, :], ones[64:128, 0:1], sqt[64:128, :], r=[C, sqt], w=[pr])
                        op("dve", lambda e: e.tensor_tensor(out=gm4[:].rearrange("p a b -> p (a b)"), in0=pgt[:, 0:64],
                                                            in1=C[:, C_PB4 + g * 64:C_PB4 + (g + 1) * 64], op=ALU.add),
                           r=[pgt, C], w=[gm4])
                        op("act", lambda e: e.activation(out=shr[32:33, :], in_=pr[32:33, :], func=AF.Sqrt,
                                                         scale=km2[32:33, NG:NG + 1]), r=[pr, km2], w=[shr])
                        op("act", lambda e: e.activation(out=QaT[32:33, gs], in_=shr[32:33, :], func=AF.Identity,
                                                         scale=-1.0), r=[shr], w=[(QaT, g)])
                        for tt in range(4):
                            op("dve", lambda e: e.max(out=top84[:, tt, :], in_=gm4[:, tt, :]), r=[gm4],
                               w=[(top84, tt)])
                        mb_ = mbt4[g % 2]
                        for tt in range(4):
                            op("dve", lambda e: e.tensor_scalar(out=mb_[:, tt, :], in0=gm4[:, tt, :],
                                                                scalar1=top84[:, tt, 2:3], scalar2=NEG,
                                                                op0=ALU.is_lt, op1=ALU.mult),
                               r=[gm4, (top84, tt)], w=[(mb_, tt)])
                        for t2 in range(2):
                            own = 2 * g + t2
                            op("dve", lambda e: e.memset(mb_[:, 2 * t2:2 * t2 + 2, own:own + 1], 0.0),
                               r=[(mb_, 2 * t2), (mb_, 2 * t2 + 1)], w=[(mb_, 2 * t2), (mb_, 2 * t2 + 1)])

                    def postQ2(g):
                        gs = slice(g * 512, (g + 1) * 512)
                        mb_ = mbt4[g % 2]
                        pt_ = PS[3]
                        for tt in range(4):
                            tr(pt_[0:16, tt * 128:(tt + 1) * 128], mb_[:, tt, :], ident, r=[(mb_, tt), C], w=[pt_])
                        op("act", lambda e: e.activation(out=QaT[0:16, gs], in_=pt_[0:16, 0:512], func=AF.Copy),
                           r=[pt_], w=[(QaT, g)])
                    projQ(0)
                    if NG > 1:
                        projQ(1)
                    for g in range(NG):
                        postQ1(g)
                        if g + 2 < NG:
                            projQ(g + 2)
                        if g > 0:
                            postQ2(g - 1)
                    postQ2(NG - 1)
                    LOOK = 3
                    for jq in range(NB // 2):
                        q0 = jq * 512
                        pO = PS[6 + jq % 2]
                        ntile = 4 * jq + 4
                        qk_ = (QaT, jq)

                        def emitS(kt):
                            pS = PS[nS[0] % 4]
                            nS[0] += 1
                            d = kt - 4 * jq
                            c0 = 0 if d < 0 else 128 * d
                            mm(pS[:, c0:512], KaT[:, kt * 128:(kt + 1) * 128], QaT[:, q0 + c0:q0 + 512], start=True,
                               stop=(d < 0), r=[(KaT, kt // 4), qk_], w=[pS], inc=(d < 0))
                            if d >= 0:
                                mm(pS[:, c0:c0 + 128], identb[:], trib[:], start=False, stop=True, r=[identb, trib],
                                   w=[pS])
                            return pS, c0

                        pendq = [emitS(k_) for k_ in range(min(LOOK, ntile))]
                        for kt in range(ntile):
                            pS, c0 = pendq.pop(0)
                            if kt + LOOK < ntile:
                                pendq.append(emitS(kt + LOOK))
                            ptile = PT[npt % 4]
                            npt += 1
                            op("act", lambda e: e.activation(out=ptile[:, c0:512], in_=pS[:, c0:512], func=AF.Exp),
                               r=[pS], w=[ptile])
                            mm(pO[0:65, c0:512], Vt[:, kt, h, :], ptile[:, c0:512], start=(kt == 0),
                               stop=(kt == ntile - 1), r=[(Vt, kt), ptile], w=[pO], inc=(kt == ntile - 1))
                        op("dve", lambda e: e.reciprocal(out=R(rden[64:65, :]), in_=pO[64:65, 0:512]), r=[pO],
                           w=[rden])
                        pB = PS[5]
                        mm(pB[0:64, 0:512], R(onesr[64:65, 0:64]), R(rden[64:65, :]), r=[onesr, rden], w=[pB])
                        tt1 = t1[jq % 2]
                        op("dve", lambda e: e.tensor_tensor(out=tt1[:], in0=pO[0:64, 0:512],
                                                            in1=zsm[:, q0:q0 + 512], op=ALU.mult),
                           r=[pO, (zsm, jq)], w=[tt1])
                        op("dve", lambda e: e.tensor_tensor(out=ogm[:, q0:q0 + 512], in0=tt1[:], in1=pB[0:64, 0:512],
                                                            op=ALU.mult), r=[tt1, pB], w=[(ogm, jq)])
                    if "omb" in dbg and l == 0 and h == 0:
                        d = dbg_out("omb", [64, S])
                        tmp = T(st, "dbgomb", [64, S])
                        op("dve", lambda e: e.tensor_copy(out=tmp[:], in_=ogm[:]), r=[ogm], w=[tmp])
                        dma(d[:, :], tmp[:], r=[tmp])
                    dma(ogmb_d[h, :, :], ogm[:], r=[ogm], w=[kogmb])
                kb.barrier()

            for st in phase("fin"):
                Wpdn = T(st, "Wpdn", [128, 4, 1024], BF16)
                Wpmb = T(st, "Wpmb", [64, 8, 1024], BF16)
                Wout = T(st, "Wout", [128, 8, 1024], BF16)
                Wmg = T(st, "Wmg", [128, 8, 2048], BF16)
                dma(Wpdn[:], wpdn_d[l], w=[Wpdn], q="pool")
                dma(Wpmb[:], wpmb_d[l], w=[Wpmb], q="pool")
                for c in range(8):
                    dma(Wout[:, c, :], wout_d[l, :, c, :], w=[(Wout, c)], q="pool")
                for c in range(8):
                    dma(Wmg[:, c, :], wmg_d[l, :, c, :], w=[(Wmg, c)], q="pool")
                OGD = [T(st, "OGD%d" % i, [128, 4, 512], BF16) for i in range(2)]
                OGM = [T(st, "OGM0", [64, 8, 512], BF16)] * 2
                mixT = T(st, "mixT", [128, 8, 512], BF16)
                gd = [T(st, "gd%d" % i, [128, 512]) for i in range(2)]
                gmm = [T(st, "gmm%d" % i, [128, 512]) for i in range(2)]
                u1 = [T(st, "u1_%d" % i, [128, 512]) for i in range(2)]
                u2 = [T(st, "u2_%d" % i, [128, 512]) for i in range(2)]
                xr = [T(st, "xr%d" % i, [128, 1024]) for i in range(2)]
                res = [T(st, "res%d" % i, [128, 512]) for i in range(2)]
                junk2 = T(st, "junk2", [128, 512], BF16)
                GPl = T(st, "GPl", [128, 1024])
                dma(GPl[:], gp_d[l], r=[(kgp, l)], w=[GPl])
                ss2 = T(st, "ss2", [128, NT, 2])
                rs2 = T(st, "rs2", [128, NT])
                for g in range(NG):
                    gs = slice(g * 512, (g + 1) * 512)
                    ogd, ogmm = OGD[g % 2], OGM[g % 2]
                    dma(ogd[:], ogdn_d[:, :, gs].rearrange("h p s -> p h s"), r=[(kogdn, g)], w=[ogd])
                    dma(ogmm[:], ogmb_d[:, :, gs].rearrange("h p s -> p h s"), r=[kogmb], w=[ogmm])
                    for d_ in range(8):
                        ds_ = slice(d_ * 128, (d_ + 1) * 128)
                        pa, pb, pc, pd = PS[0 + 4 * (d_ % 2)], PS[1 + 4 * (d_ % 2)], PS[2 + 4 * (d_ % 2)], PS[3 + 4 * (d_ % 2)]
                        for h in range(4):
                            mm(pa[:, :], Wpdn[:, h, ds_], ogd[:, h, :], start=(h == 0), stop=(h == 3),
                               r=[Wpdn, ogd], w=[pa], inc=(h == 3))
                        for h in range(8):
                            mm(pb[:, :], Wpmb[:, h, ds_], ogmm[:, h, :], start=(h == 0), stop=(h == 7),
                               r=[Wpmb, ogmm], w=[pb], inc=(h == 7))
                        for c in range(8):
                            mm(pc[:, :], Wmg[:, c, ds_], hT[:, c, gs], start=(c == 0), stop=(c == 7),
                               r=[(Wmg, c), (hT, g)], w=[pc], inc=(c == 7))
                        for c in range(8):
                            mm(pd[:, :], Wmg[:, c, 1024 + d_ * 128:1024 + (d_ + 1) * 128], hT[:, c, gs],
                               start=(c == 0), stop=(c == 7), r=[(Wmg, c), (hT, g)], w=[pd], inc=(c == 7))
                        gdt, gmt, u1t, u2t = gd[d_ % 2], gmm[d_ % 2], u1[d_ % 2], u2[d_ % 2]
                        op("act", lambda e: e.activation(out=gdt[:], in_=pc[:, :], func=AF.Sigmoid), r=[pc], w=[gdt])
                        op("act", lambda e: e.activation(out=gmt[:], in_=pd[:, :], func=AF.Sigmoid), r=[pd], w=[gmt])
                        op("dve", lambda e: e.tensor_tensor(out=u1t[:], in0=pa[:, :], in1=gdt[:], op=ALU.mult),
                           r=[pa, gdt], w=[u1t])
                        op("dve", lambda e: e.tensor_tensor(out=u2t[:], in0=pb[:, :], in1=gmt[:], op=ALU.mult),
                           r=[pb, gmt], w=[u2t])
                        op("dve", lambda e: e.tensor_tensor(out=mixT[:, d_, :], in0=u1t[:], in1=u2t[:], op=ALU.add),
                           r=[u1t, u2t], w=[(mixT, d_)])
                    for tt in range(4):
                        t = g * 4 + tt
                        xrt = xr[t % 2]
                        dma(xrt[:], xin_d[t * 128:(t + 1) * 128, :], r=[(xin_k, t)], w=[xrt])
                        for hf in range(2):
                            pb = PS[(t % 2) * 2 + hf]
                            for d_ in range(8):
                                mm(pb[:, :], mixT[:, d_, tt * 128:(tt + 1) * 128], Wout[:, d_, hf * 512:(hf + 1) * 512],
                                   start=(d_ == 0), stop=(d_ == 7), r=[(mixT, d_), (Wout, d_)], w=[pb],
                                   inc=(d_ == 7))
                            op("act", lambda e: e.activation(out=junk2[:], in_=pb[:, :], func=AF.Square,
                                                             accum_out=ss2[:, t, hf:hf + 1]), r=[pb],
                               w=[junk2, (ss2, t)])
                        op("dve", lambda e: e.tensor_tensor(out=rs2[:, t:t + 1], in0=ss2[:, t, 0:1],
                                                            in1=ss2[:, t, 1:2], op=ALU.add), r=[(ss2, t)],
                           w=[(rs2, t)])
                        op("act", lambda e: e.activation(out=rs2[:, t:t + 1], in_=rs2[:, t:t + 1], func=AF.Sqrt,
                                                         scale=1.0 / D_MODEL, bias=epsT[:]), r=[(rs2, t), epsT],
                           w=[(rs2, t)])
                        op("dve", lambda e: e.reciprocal(out=rs2[:, t:t + 1], in_=rs2[:, t:t + 1]), r=[(rs2, t)],
                           w=[(rs2, t)])
                        for hf in range(2):
                            pb = PS[(t % 2) * 2 + hf]
                            hs = slice(hf * 512, (hf + 1) * 512)
                            rt = res[hf]
                            op("dve", lambda e: e.scalar_tensor_tensor(out=rt[:], in0=pb[:, :],
                                                                       scalar=rs2[:, t:t + 1], in1=GPl[:, hs],
                                                                       op0=ALU.mult, op1=ALU.mult),
                               r=[pb, (rs2, t), GPl], w=[rt])
                            op("dve", lambda e: e.tensor_tensor(out=xrt[:, hs], in0=rt[:], in1=xrt[:, hs],
                                                                op=ALU.add), r=[rt, (xrt, hf)], w=[(xrt, hf)])
                        dma(xout_d[t * 128:(t + 1) * 128, :], xrt[:], r=[xrt], w=[(xout_k, t)])
                kb.barrier()
          except _Stop:
            kb.barrier()
            curgen[0].close()
            break
        kb.finish()
        stuck = kb.simulate()
        print("deadlock check:", stuck if stuck else "ok")
        print("instructions:", kb.ninstr, "counts", kb.count, "dmas", kb.ndmaq)
    return nc, dbg_d


def _pc(w):
    sh = w.shape
    w = w.reshape(sh[:-2] + (8, 128, sh[-1]))
    return np.ascontiguousarray(np.swapaxes(w, -3, -2))


def host_layout(inp):
    f = lambda a: np.ascontiguousarray(np.asarray(a, dtype=np.float32))
    w_in = f(inp["w_in"])
    depth = w_in.shape[0]
    shared = {
        "wada": _pc(f(inp["w_ada"])),
        "bada": f(inp["b_ada"]).reshape(depth, 1, 3072),
        "gpre": f(inp["g_pre"]).reshape(depth, 1, 1024),
        "gpost": f(inp["g_post"]).reshape(depth, 1, 1024),
        "wdn": _pc(w_in[:, :, 0:2048]),
        "wba": _pc(w_in[:, :, 2048:2056]),
        "wmb": _pc(w_in[:, :, 2056:4104]),
        "wmg": _pc(w_in[:, :, 4104:6152]),
        "convw": np.ascontiguousarray(
            f(inp["conv_w"]).transpose(0, 2, 1).reshape(depth, 12, 128, 4).transpose(0, 2, 1, 3)
        ).reshape(depth, 128, 48),
        "alog": np.ascontiguousarray(np.broadcast_to(f(inp["a_log"])[:, None, :], (depth, 128, 4))),
        "dtb": np.ascontiguousarray(np.broadcast_to(f(inp["dt_bias"])[:, None, :], (depth, 128, 4))),
        "dng": f(inp["dn_norm_g"]).reshape(depth, 128, 1),
        "wpdn": np.ascontiguousarray(f(inp["w_proj_dn"]).reshape(depth, 4, 128, 1024).transpose(0, 2, 1, 3)),
        "wpmb": np.ascontiguousarray(f(inp["w_proj_mb"]).reshape(depth, 8, 64, 1024).transpose(0, 2, 1, 3)),
        "wout": _pc(f(inp["w_out"])),
        "consts": make_consts(),
    }
    x = f(inp["x"])
    c = f(inp["c"])
    maps = []
    for b in range(x.shape[0]):
        m = dict(shared)
        m["x"] = x[b]
        m["cT"] = np.ascontiguousarray(c[b].reshape(8, 128).T)
        maps.append(m)
    return maps


_CACHE = {}


def kernel(**inputs):
    x = np.asarray(inputs["x"])
    B, S, _ = x.shape
    depth = np.asarray(inputs["w_in"]).shape[0]
    key = (S, depth)
    if key not in _CACHE:
        _CACHE[key] = build(S=S, DEPTH=depth)[0]
    nc = _CACHE[key]
    maps = host_layout(inputs)
    res = run_bass_kernel_spmd(nc, maps, core_ids=list(range(B)))
    return np.stack([np.asarray(r["y"], dtype=np.float32) for r in res.results], axis=0)
```

```python
from contextlib import ExitStack

import numpy as np
import concourse.bass as bass
import concourse.mybir as mybir
from concourse.bass_utils import run_bass_kernel_spmd

F32 = mybir.dt.float32
BF16 = mybir.dt.bfloat16
F32R = mybir.dt.float32r


def R(ap):
    return ap.bitcast(F32R)
AF = mybir.ActivationFunctionType
ALU = mybir.AluOpType
AX = mybir.AxisListType

D_MODEL = 1024
NEG = -30000.0
EPS = 1e-6


class SV:
    def __init__(self, tile, j, n=1):
        self.tile, self.sub = tile, j
        self.ap = tile[:, j * 128:(j + n) * 128]

    def __getitem__(self, idx):
        return self.ap[idx]


class KB:
    NRING = {"sp": 6, "pool": 4}

    def __init__(self, nc, stack):
        self.nc = nc
        self.eng = {"pe": nc.tensor, "act": nc.scalar, "dve": nc.vector,
                    "pool": nc.gpsimd, "sp": nc.sync}
        self.sem = {}
        for e in ("pe", "act", "dve", "pool"):
            self.sem[e] = stack.enter_context(nc.semaphore("s_" + e))
        self.ring = {q: [stack.enter_context(nc.semaphore("s_dma_%s%d" % (q, i))) for i in range(n)]
                     for q, n in self.NRING.items()}
        self.ndmaq = {q: 0 for q in self.NRING}
        self.count = {e: 0 for e in ("pe", "act", "dve", "pool")}
        self.waited = {}
        self.track = {}
        self.ninstr = 0
        self.streams = {e: [] for e in self.eng}
        self.psum_ids = set()

    def _semof(self, dep):
        if dep[0] == "e":
            return self.sem[dep[1]], ("e", dep[1])
        return self.ring[dep[1][0]][dep[1][1]], ("d", dep[1])

    def _wait(self, e, dep):
        sem, sk = self._semof(dep)
        val = dep[2]
        k = (e, sk)
        if self.waited.get(k, 0) >= val:
            return
        self.eng[e].wait_ge(sem, val)
        self.streams[e].append(("wait", sk, val))
        self.ninstr += 1
        self.waited[k] = val

    @staticmethod
    def _keys(items):
        out = []
        for it in items:
            if isinstance(it, SV):
                out.append((id(it.tile), it.sub))
            elif isinstance(it, tuple):
                out.append((id(it[0]), it[1]))
            else:
                out.append((id(it), None))
        return out

    def _conflicts(self, key):
        tid, sub = key
        d = self.track.get(tid)
        if d is None:
            return []
        if sub is None:
            return list(d.values())
        res = []
        if sub in d:
            res.append(d[sub])
        if None in d:
            res.append(d[None])
        return res

    def _entry(self, key):
        tid, sub = key
        d = self.track.setdefault(tid, {})
        if sub is None:
            ent = {"w": [], "r": []}
            for v in d.values():
                ent["w"] += v["w"]
                ent["r"] += v["r"]
            d.clear()
            d[None] = ent
            return ent
        if sub not in d:
            ent = {"w": [], "r": []}
            if None in d:
                ent["w"] = list(d[None]["w"])
                ent["r"] = list(d[None]["r"])
            d[sub] = ent
        return d[sub]

    def _deps(self, e, reads, writes):
        deps = []
        for k in reads:
            for ent in self._conflicts(k):
                for w in ent["w"]:
                    deps.append((w, "raw"))
        for k in writes:
            for ent in self._conflicts(k):
                for w in ent["w"]:
                    deps.append((w, "waw"))
                for r in ent["r"]:
                    deps.append((r, "war"))
        out = []
        for dep, kind in deps:
            if dep[0] == "e" and dep[1] == e:
                if e == "pe":
                    continue
            out.append(dep)
        return out

    @staticmethod
    def _prune(lst):
        best = {}
        for d in lst:
            kk = (d[0], d[1])
            if kk not in best or best[kk][2] < d[2]:
                best[kk] = d
        return list(best.values())

    def _record(self, me, reads, writes):
        for k in reads:
            ent = self._entry(k)
            ent["r"].append(me)
            if len(ent["r"]) > 16:
                ent["r"] = self._prune(ent["r"])
        for k in writes:
            ent = self._entry(k)
            ent["w"] = [me]
            ent["r"] = []

    def _rw(self, r, w):
        reads, writes = [], []
        for k in self._keys(r):
            if k[0] in self.psum_ids:
                writes.append((k[0], None))
            else:
                reads.append(k)
        for k in self._keys(w):
            writes.append((k[0], None) if k[0] in self.psum_ids else k)
        return reads, writes

    def op(self, e, fn, r=(), w=(), inc=True):
        reads, writes = self._rw(r, w)
        for dep in self._deps(e, reads, writes):
            self._wait(e, dep)
        ins = fn(self.eng[e])
        self.ninstr += 1
        if inc:
            self.count[e] += 1
            ins.then_inc(self.sem[e], 1)
            self.streams[e].append(("inc", ("e", e), 1))
            me = ("e", e, self.count[e])
        else:
            me = ("e", e, self.count[e] + 1)
        self._record(me, reads, writes)
        return ins

    def dma(self, out, in_, r=(), w=(), q="sp", **kw):
        e = q
        reads = self._keys(r)
        writes = self._keys(w)
        k = self.ndmaq[q]
        nr = self.NRING[q]
        slot = k % nr
        gen = k // nr
        for dep in self._deps(e, reads, writes):
            self._wait(e, dep)
        if gen > 0:
            self._wait(e, ("d", (q, slot), 16 * gen))
        ins = self.eng[e].dma_start(out=out, in_=in_, **kw)
        ins.then_inc(self.ring[q][slot], 16)
        self.streams[e].append(("inc", ("d", (q, slot)), 16))
        self.ndmaq[q] += 1
        self.ninstr += 1
        me = ("d", (q, slot), 16 * (gen + 1))
        self._record(me, reads, writes)
        return ins

    def _alldma(self):
        out = []
        for q, nr in self.NRING.items():
            n = self.ndmaq[q]
            for slot in range(nr):
                cnt = (n - 1 - slot) // nr + 1 if n > slot else 0
                if cnt > 0:
                    out.append(("d", (q, slot), 16 * cnt))
        return out

    def barrier(self):
        for e in ("pe", "act", "dve", "pool", "sp"):
            for o in ("pe", "act", "dve", "pool"):
                if o != e and self.count[o] > 0:
                    self._wait(e, ("e", o, self.count[o]))
            for dep in self._alldma():
                self._wait(e, dep)
        self.track = {}

    def finish(self):
        for dep in self._alldma():
            self._wait("sp", dep)

    def simulate(self):
        sems = {}
        pc = {e: 0 for e in self.streams}
        progress = True
        while progress:
            progress = False
            for e, st in self.streams.items():
                while pc[e] < len(st):
                    kind, sk, val = st[pc[e]]
                    if kind == "wait":
                        if sems.get(sk, 0) < val:
                            break
                    else:
                        sems[sk] = sems.get(sk, 0) + val
                    pc[e] += 1
                    progress = True
        stuck = {e: (pc[e], len(st), st[pc[e]], sems.get(st[pc[e]][1], 0)) for e, st in self.streams.items()
                 if pc[e] < len(st)}
        return stuck


C_IDENT, C_U, C_MBD, C_MOFF, C_TRI, C_ONES, C_PB = 0, 128, 256, 384, 512, 640, 768
C_PB4 = 768
NCONST = 768 + 512


def make_consts():
    c = np.zeros((128, NCONST), np.float32)
    i = np.arange(128)
    c[:, C_IDENT:C_IDENT + 128] = np.eye(128)
    c[:, C_U:C_U + 128] = (i[:, None] <= i[None, :])
    c[:, C_MBD:C_MBD + 128] = (i[:, None] < i[None, :]) & ((i[:, None] // 64) == (i[None, :] // 64))
    c[:, C_MOFF:C_MOFF + 128] = (i[:, None] < 64) & (i[None, :] >= 64)
    c[:, C_TRI:C_TRI + 128] = np.where(i[:, None] <= i[None, :], 0.0, NEG)
    c[:, C_ONES:C_ONES + 128] = 1.0
    pb = np.zeros((16, 16), np.float32)
    for own in range(16):
        pb[own, own:] = -1e30
    for g in range(8):
        for tt in range(4):
            own = (4 * g + tt) // 2
            c[:, C_PB4 + g * 64 + tt * 16:C_PB4 + g * 64 + (tt + 1) * 16] = pb[own][None, :]
    return c


def build(S=4096, DEPTH=2, LS=2, dbg=None, phases=("p1", "dn", "mb", "fin"), stop=0, JUNK=0):
    dbg = dbg or set()
    NT = S // 128
    NG = S // 512
    NB = S // 256
    nc = bass.Bass("TRN2", target_bir_lowering=False)

    def din(name, shape, dt=F32):
        return nc.dram_tensor(name, shape, dt, kind="ExternalInput").ap()

    x_d = din("x", [S, 1024])
    cT_d = din("cT", [128, 8])
    wada_d = din("wada", [DEPTH, 128, 8, 3072])
    bada_d = din("bada", [DEPTH, 1, 3072])
    gpre_d = din("gpre", [DEPTH, 1, 1024])
    gpost_d = din("gpost", [DEPTH, 1, 1024])
    wdn_d = din("wdn", [DEPTH, 128, 8, 2048])
    wba_d = din("wba", [DEPTH, 128, 8, 8])
    wmb_d = din("wmb", [DEPTH, 128, 8, 2048])
    wmg_d = din("wmg", [DEPTH, 128, 8, 2048])
    convw_d = din("convw", [DEPTH, 128, 48])
    alog_d = din("alog", [DEPTH, 128, 4])
    dtb_d = din("dtb", [DEPTH, 128, 4])
    dng_d = din("dng", [DEPTH, 128, 1])
    wpdn_d = din("wpdn", [DEPTH, 128, 4, 1024])
    wpmb_d = din("wpmb", [DEPTH, 64, 8, 1024])
    wout_d = din("wout", [DEPTH, 128, 8, 1024])
    consts_d = din("consts", [128, NCONST])
    y_d = nc.dram_tensor("y", [S, 1024], F32, kind="ExternalOutput").ap()
    xmid_d = nc.dram_tensor("xmid", [S, 1024], F32, kind="Internal").ap()
    ogdn_d = nc.dram_tensor("ogdn", [4, 128, S], BF16, kind="Internal").ap()
    ogmb_d = nc.dram_tensor("ogmb", [8, 64, S], BF16, kind="Internal").ap()
    dbg_d = {}

    class _K:
        pass
    kx, kmid, ky, kogdn, kogmb = _K(), _K(), _K(), _K(), _K()

    def dbg_out(name, shape):
        dbg_d[name] = nc.dram_tensor("dbg_" + name, shape, F32, kind="ExternalOutput").ap()
        return dbg_d[name]

    with ExitStack() as gst:
        kb = KB(nc, gst)
        op, dma = kb.op, kb.dma
        gst.enter_context(nc.allow_low_precision("float32r (1-pass PE) operands for non-critical fp32 matmuls"))

        def mm(out, lhsT, rhs, start=True, stop=True, r=(), w=(), inc=True):
            return op("pe", lambda e: e.matmul(out, lhsT=lhsT, rhs=rhs, start=start, stop=stop),
                      r=r, w=w, inc=inc)

        def tr(out, in_, ident, r=(), w=()):
            return op("pe", lambda e: e.transpose(out=out, in_=in_, identity=ident), r=r, w=w)

        uid = [0]

        def T(st, name, shape, dt=F32):
            uid[0] += 1
            return st.enter_context(nc.sbuf_tensor("sb%d_%s" % (uid[0], name), shape, dt))

        class _Stop(Exception):
            pass

        def ck(level):
            if stop == level:
                raise _Stop()

        curgen = [None]

        def phase(name):
            if name in phases:
                st_ = ExitStack()
                curgen[0] = st_
                yield st_
                st_.close()

        PS = [gst.enter_context(nc.psum_tensor("ps%d" % i, [128, 512], F32)) for i in range(8)]
        kb.psum_ids = {id(p) for p in PS}
        C = T(gst, "consts", [128, NCONST])
        dma(C[:], consts_d[:, :], w=[C])
        ident = C[:, C_IDENT:C_IDENT + 128]
        U = C[:, C_U:C_U + 128]
        Mbd = C[:, C_MBD:C_MBD + 128]
        Moff = C[:, C_MOFF:C_MOFF + 128]
        ones = C[:, C_ONES:C_ONES + 128]
        identb = T(gst, "identb", [128, 128], BF16)
        trib = T(gst, "trib", [128, 128], BF16)
        epsT = T(gst, "epsT", [128, 1])
        op("dve", lambda e: e.tensor_copy(out=identb[:], in_=ident), r=[C], w=[identb])
        op("dve", lambda e: e.tensor_copy(out=trib[:], in_=C[:, C_TRI:C_TRI + 128]), r=[C], w=[trib])
        op("dve", lambda e: e.memset(epsT[:], EPS), w=[epsT])
        onesr = T(gst, "onesr", [128, 128])
        op("dve", lambda e: e.tensor_copy(out=R(onesr[:]), in_=ones), r=[C], w=[onesr])
        AB = [T(gst, "AB%d" % l, [128, 16]) for l in range(DEPTH)]
        gp_d = nc.dram_tensor("gp_scr", [DEPTH, 128, 1024], F32, kind="Internal").ap()
        kgp = _K()

        with ExitStack() as st:
            cT = T(st, "cT", [128, 8])
            sc = T(st, "sc", [128, 8])
            dma(cT[:], cT_d[:, :], w=[cT])
            op("act", lambda e: e.activation(out=sc[:], in_=cT[:], func=AF.Silu), r=[cT], w=[sc])
            wa = [T(st, "wa%d" % i, [128, 8, 512]) for i in range(2)]
            row = T(st, "row", [1, 3072])
            bada = T(st, "bada", [1, 3072])
            gpr = T(st, "gpr", [1, 1024])
            gpo = T(st, "gpo", [1, 1024])
            arow = T(st, "arow", [1, 1024])
            gprow = T(st, "gprow", [1, 1024])
            gptmp = T(st, "gptmp", [128, 1024])
            nwa = 0
            for l in range(DEPTH):
                dma(bada[:], bada_d[l], w=[bada])
                dma(gpr[:], gpre_d[l], w=[gpr])
                dma(gpo[:], gpost_d[l], w=[gpo])
                for cg in range(6):
                    wt = wa[nwa % 2]
                    nwa += 1
                    dma(wt[:], wada_d[l, :, :, cg * 512:(cg + 1) * 512], w=[wt])
                    pb = PS[cg % 2]
                    for c in range(8):
                        mm(pb[0:1, :], sc[:, c:c + 1], wt[:, c, :], start=(c == 0), stop=(c == 7),
                           r=[sc, wt], w=[pb], inc=(c == 7))
                    op("dve", lambda e: e.tensor_tensor(out=row[0:1, cg * 512:(cg + 1) * 512], in0=pb[0:1, :],
                                                        in1=bada[0:1, cg * 512:(cg + 1) * 512], op=ALU.add),
                       r=[pb, bada], w=[(row, cg)])
                op("dve", lambda e: e.scalar_tensor_tensor(out=arow[:], in0=row[0:1, 1024:2048], scalar=1.0,
                                                           in1=gpr[:], op0=ALU.add, op1=ALU.mult),
                   r=[row, gpr], w=[arow])
                op("dve", lambda e: e.tensor_tensor(out=gprow[:], in0=row[0:1, 2048:3072], in1=gpo[:], op=ALU.mult),
                   r=[row, gpo], w=[gprow])
                pc = PS[2]
                for c in range(8):
                    mm(pc[:, c:c + 1], arow[0:1, c * 128:(c + 1) * 128], ones[0:1, 0:1], r=[arow, C], w=[pc], inc=False)
                for c in range(8):
                    mm(pc[:, 8 + c:9 + c], row[0:1, c * 128:(c + 1) * 128], ones[0:1, 0:1], r=[row, C], w=[pc],
                       inc=(c == 7))
                op("dve", lambda e: e.tensor_copy(out=AB[l][:], in_=pc[:, 0:16]), r=[pc], w=[AB[l]])
                for hf in range(2):
                    pg = PS[3 + hf]
                    mm(pg[:, :], ones[0:1, 0:128], gprow[0:1, hf * 512:(hf + 1) * 512], r=[gprow, C], w=[pg])
                    op("act", lambda e: e.activation(out=gptmp[:, hf * 512:(hf + 1) * 512], in_=pg[:, :], func=AF.Copy),
                       r=[pg], w=[(gptmp, hf)])
                dma(gp_d[l], gptmp[:], r=[gptmp], w=[(kgp, l)])
            kb.barrier()

        hT = T(gst, "hT", [128, 8, S], BF16)

        for l in range(DEPTH):
          try:
            xin_d = x_d if l == 0 else xmid_d
            xout_d = y_d if l == DEPTH - 1 else xmid_d
            xin_k = kx if l == 0 else kmid
            xout_k = ky if l == DEPTH - 1 else kmid

            for st in phase("p1"):
                xt = [T(st, "xt%d" % i, [128, 1024]) for i in range(4)]
                xn = [T(st, "xn%d" % i, [128, 1024]) for i in range(3)]
                junk = T(st, "junk", [128, 1024], BF16)
                ss = T(st, "ss", [128, NT])
                rstd = T(st, "rstd", [128, NT])
                for t in range(NT):
                    xtt, xnt = xt[t % 4], xn[t % 3]
                    dma(xtt[:], xin_d[t * 128:(t + 1) * 128, :], r=[(xin_k, t)], w=[xtt])
                    op("act", lambda e: e.activation(out=junk[:], in_=xtt[:], func=AF.Square,
                                                     accum_out=ss[:, t:t + 1]), r=[xtt], w=[junk, (ss, t)])
                    op("act", lambda e: e.activation(out=rstd[:, t:t + 1], in_=ss[:, t:t + 1], func=AF.Sqrt,
                                                     scale=1.0 / D_MODEL, bias=epsT[:]), r=[(ss, t), epsT],
                       w=[(rstd, t)])
                    op("dve", lambda e: e.reciprocal(out=rstd[:, t:t + 1], in_=rstd[:, t:t + 1]), r=[(rstd, t)],
                       w=[(rstd, t)])
                    op("dve", lambda e: e.tensor_scalar(out=xnt[:], in0=xtt[:], scalar1=rstd[:, t:t + 1],
                                                        scalar2=None, op0=ALU.mult), r=[xtt, (rstd, t)], w=[xnt])
                    for hf in range(2):
                        pb = PS[(t % 2) * 2 + hf]
                        for cc in range(4):
                            c = hf * 4 + cc
                            tr(pb[:, cc * 128:(cc + 1) * 128], xnt[:, c * 128:(c + 1) * 128], ident, r=[xnt, C],
                               w=[pb])
                        for cc in range(4):
                            c = hf * 4 + cc
                            eng = "act" if hf == 0 else "dve"
                            if eng == "act":
                                op("act", lambda e: e.activation(out=hT[:, c, t * 128:(t + 1) * 128],
                                                                 in_=pb[:, cc * 128:(cc + 1) * 128], func=AF.Identity,
                                                                 scale=AB[l][:, c:c + 1], bias=AB[l][:, 8 + c:9 + c]),
                                   r=[pb, AB[l]], w=[(hT, t // 4)])
                            else:
                                op("dve", lambda e: e.tensor_scalar(out=hT[:, c, t * 128:(t + 1) * 128],
                                                                    in0=pb[:, cc * 128:(cc + 1) * 128],
                                                                    scalar1=AB[l][:, c:c + 1],
                                                                    scalar2=AB[l][:, 8 + c:9 + c],
                                                                    op0=ALU.mult, op1=ALU.add),
                                   r=[pb, AB[l]], w=[(hT, t // 4)])
                kb.barrier()
            if "hT" in dbg and l == 0:
                with ExitStack() as st:
                    d = dbg_out("hT", [128, 8, S])
                    tmp = T(st, "dbgtmp", [128, 8, S])
                    op("dve", lambda e: e.tensor_copy(out=tmp[:], in_=hT[:]), r=[hT], w=[tmp])
                    dma(d[:, :, :], tmp[:], r=[tmp])
                    kb.barrier()

            for st in phase("dn"):
                Wdn = T(st, "Wdn", [128, 8, 2048], BF16)
                Wba = T(st, "Wba", [128, 8, 8], BF16)
                for c in range(8):
                    dma(Wdn[:, c, :], wdn_d[l, :, c, :], w=[(Wdn, c)], q="pool")
                dma(Wba[:], wba_d[l], w=[Wba], q="pool")
                convw = T(st, "convw", [128, 48])
                alog = T(st, "alog", [128, 4])
                dtb = T(st, "dtb", [128, 4])
                dng = T(st, "dng", [128, 1])
                dma(convw[:], convw_d[l], w=[convw])
                dma(alog[:], alog_d[l], w=[alog])
                dma(dtb[:], dtb_d[l], w=[dtb])
                dma(dng[:], dng_d[l], w=[dng])
                st2 = ExitStack()
                BETA = T(st, "BETA", [128, NT, 4])
                NBETA = T(st, "NBETA", [128, NT, 4])
                GRAW = T(st, "GRAW", [128, NT, 4])
                GC = T(st, "GC", [128, NT, 4])
                NEXPG = T(st, "NEXPG", [128, NT, 4])
                negA = T(st, "negA", [128, 4])
                BG = T(st2, "BG", [128, NT, 8])
                AA = T(st2, "AA", [128, NT, 4])
                AX_ = T(st2, "AXs", [128, NT, 4])
                pbg = PS[0]
                for t in range(NT):
                    for c in range(8):
                        mm(pbg[:, t * 8:(t + 1) * 8], hT[:, c, t * 128:(t + 1) * 128], Wba[:, c, :],
                           start=(c == 0), stop=(c == 7), r=[(hT, t // 4), Wba], w=[pbg], inc=(c == 7))
                op("dve", lambda e: e.tensor_copy(out=BG[:].rearrange("p t e -> p (t e)"), in_=pbg[:, 0:NT * 8]),
                   r=[pbg], w=[BG])
                op("act", lambda e: e.activation(out=BETA[:], in_=BG[:, :, 0:4], func=AF.Sigmoid), r=[BG], w=[BETA])
                op("dve", lambda e: e.tensor_scalar(out=NBETA[:], in0=BETA[:], scalar1=-1.0, scalar2=None,
                                                    op0=ALU.mult), r=[BETA], w=[NBETA])
                for h in range(4):
                    op("dve", lambda e: e.tensor_scalar(out=AA[:, :, h], in0=BG[:, :, 4 + h], scalar1=dtb[:, h:h + 1],
                                                        scalar2=None, op0=ALU.add), r=[BG, dtb], w=[AA])
                op("act", lambda e: e.activation(out=AX_[:], in_=AA[:], func=AF.Abs), r=[AA], w=[AX_])
                op("act", lambda e: e.activation(out=AX_[:], in_=AX_[:], func=AF.Exp, scale=-1.0), r=[AX_], w=[AX_])
                op("dve", lambda e: e.tensor_scalar(out=AX_[:], in0=AX_[:], scalar1=1.0, scalar2=None, op0=ALU.add),
                   r=[AX_], w=[AX_])
                op("act", lambda e: e.activation(out=AX_[:], in_=AX_[:], func=AF.Ln), r=[AX_], w=[AX_])
                op("dve", lambda e: e.scalar_tensor_tensor(out=AA[:], in0=AA[:], scalar=0.0, in1=AX_[:],
                                                           op0=ALU.max, op1=ALU.add), r=[AA, AX_], w=[AA])
                op("act", lambda e: e.activation(out=negA[:], in_=alog[:], func=AF.Exp), r=[alog], w=[negA])
                op("dve", lambda e: e.tensor_scalar(out=negA[:], in0=negA[:], scalar1=-1.0, scalar2=None,
                                                    op0=ALU.mult), r=[negA], w=[negA])
                for h in range(4):
                    op("dve", lambda e: e.tensor_scalar(out=GRAW[:, :, h], in0=AA[:, :, h], scalar1=negA[:, h:h + 1],
                                                        scalar2=None, op0=ALU.mult), r=[AA, negA], w=[GRAW])
                pgc = PS[1]
                mm(pgc[:, 0:NT * 4], U, GRAW[:].rearrange("p t e -> p (t e)"), r=[C, GRAW], w=[pgc])
                op("dve", lambda e: e.tensor_copy(out=GC[:].rearrange("p t e -> p (t e)"), in_=pgc[:, 0:NT * 4]),
                   r=[pgc], w=[GC])
                op("act", lambda e: e.activation(out=NEXPG[:], in_=GC[:], func=AF.Exp), r=[GC], w=[NEXPG])
                op("dve", lambda e: e.tensor_scalar(out=NEXPG[:], in0=NEXPG[:], scalar1=-1.0, scalar2=None,
                                                    op0=ALU.mult), r=[NEXPG], w=[NEXPG])
                if "graw" in dbg and l == 0:
                    d = dbg_out("graw", [128, NT, 4])
                    dma(d[:, :, :], GRAW[:], r=[GRAW])
                    d = dbg_out("beta", [128, NT, 4])
                    dma(d[:, :, :], BETA[:], r=[BETA])

                kb.barrier()
                st2.close()
                ck(1)
                pre = [T(st, "pre%d" % i, [128, 515]) for i in range(3)]
                halo = T(st, "halo", [128, 12, 3])
                op("dve", lambda e: e.memset(halo[:], 0.0), w=[halo])
                qkv = [[T(st, "qkv%d_%d" % (i, j), [128, 512]) for j in range(3)] for i in range(2)]
                zs = [T(st, "zs%d" % i, [128, 512], BF16) for i in range(4)]
                cv = [T(st, "cv0", [128, 512])] * 2
                sq = [T(st, "sq0", [128, 512])] * 2
                oTg = [T(st, "oTg%d" % i, [128, 512]) for i in range(2)]
                ogb = [T(st, "ogb0", [128, 512], BF16)] * 2
                Sst = [[T(st, "S%d_%d" % (h, i), [128, 128]) for i in range(2)] for h in range(4)]
                for h in range(4):
                    op("dve", lambda e: e.tensor_scalar(out=R(Sst[h][0][:]), in0=ident, scalar1=0.0, scalar2=None,
                                                        op0=ALU.mult), r=[C], w=[Sst[h][0]])
                spar = [0, 0, 0, 0]
                NCH = 8
                ALIAS = {"Dm": 0, "tq": 0, "DecT": 1, "Xo": 1, "Pe": 2, "ktok": 3, "Xe": 3, "NoffT": 3, "Po": 4,
                         "B": 5, "BT": 6, "WbdT": 6, "Noff": 7, "Z1": 7, "ExpG": 8, "XTo": 8, "Lm": 9, "XTe": 9}
                scr = []
                for i in range(NCH):
                    wide = T(st, "s%d" % i, [128, 9 * 128])
                    plain = T(st, "s%da" % i, [128, 128])
                    d_ = {n: (SV(wide, j - 1) if j > 0 else plain) for n, j in ALIAS.items()}
                    d_["XPo"] = SV(wide, 0, 2)
                    d_["XPe"] = SV(wide, 2, 2)
                    scr.append(d_)
                OUTN = ["W", "QKdT", "kdec", "qg", "vtok", "kTc"]
                outs = [{n: T(st, "o%d_%s" % (i, n), [128, 128]) for n in OUTN} for i in range(NCH)]
                EGL = T(st, "EGL", [128, NCH])
                Yr = [T(st, "Yr%d" % i, [128, 128]) for i in range(2)]
                Vn = [T(st, "Vn%d" % i, [128, 128]) for i in range(2)]

                def prep8(chains, pump=lambda: None):
                    pump_on = [False]

                    def each(fn):
                        for ci, ch in enumerate(chains):
                            fn(ch, ch["s"], ch["o"], PS[ch["i"]])
                            if ci % 2 == 1 and pump_on[0]:
                                pump(1)

                    def f(ch, s, o, pb):
                        h, n, sl = ch["h"], ch["n"], ch["sl"]
                        kT, vT = ch["kT"], ch["vT"]
                        tr(pb[:, 0:128], kT[:, sl], ident, r=[kT, C], w=[pb])
                        tr(pb[:, 128:256], vT[:, sl], ident, r=[vT, C], w=[pb])
                        mm(pb[:, 256:384], GRAW[:, n, h:h + 1].to_broadcast([128, 128]), U, r=[GRAW, C], w=[pb])
                        op("dve", lambda e: e.tensor_scalar(out=s["Dm"][:], in0=pb[:, 256:384],
                                                            scalar1=GC[:, n, h:h + 1], scalar2=0.0,
                                                            op0=ALU.subtract, op1=ALU.min),
                           r=[pb, GC], w=[s["Dm"]])
                        op("dve", lambda e: e.tensor_copy(out=o["vtok"][:], in_=pb[:, 128:256]), r=[pb],
                           w=[o["vtok"]])
                        op("act", lambda e: e.activation(out=R(s["ktok"][:]), in_=pb[:, 0:128], func=AF.Copy),
                           r=[pb], w=[s["ktok"]])
                        op("act", lambda e: e.activation(out=R(s["ExpG"][:]), in_=pb[:, 256:384], func=AF.Exp),
                           r=[pb], w=[s["ExpG"]])
                    each(f)

                    def f(ch, s, o, pb):
                        op("act", lambda e: e.activation(out=R(s["DecT"][:]), in_=s["Dm"][:], func=AF.Exp),
                           r=[s["Dm"]], w=[s["DecT"]])
                    each(f)

                    def f(ch, s, o, pb):
                        h, n, sl = ch["h"], ch["n"], ch["sl"]
                        kT, qT = ch["kT"], ch["qT"]
                        mm(pb[:, 0:128], R(kT[:, sl]), R(kT[:, sl]), r=[kT], w=[pb], inc=False)
                        mm(pb[:, 128:256], R(kT[:, sl]), R(qT[:, sl]), r=[kT, qT], w=[pb])
                        op("dve", lambda e: e.tensor_tensor(out=R(s["Lm"][:]), in0=pb[:, 0:128], in1=s["DecT"][:],
                                                            op=ALU.mult), r=[pb, s["DecT"]], w=[s["Lm"]])
                        op("dve", lambda e: e.tensor_tensor(out=s["tq"][:], in0=pb[:, 128:256], in1=s["DecT"][:],
                                                            op=ALU.mult), r=[pb, s["DecT"]], w=[s["tq"]])
                        op("dve", lambda e: e.scalar_tensor_tensor(out=R(s["B"][:]), in0=s["Lm"][:],
                                                                   scalar=NBETA[:, n, h:h + 1], in1=Mbd,
                                                                   op0=ALU.mult, op1=ALU.mult),
                           r=[s["Lm"], NBETA, C], w=[s["B"]])
                        op("dve", lambda e: e.scalar_tensor_tensor(out=R(s["Noff"][:]), in0=s["Lm"][:],
                                                                   scalar=BETA[:, n, h:h + 1], in1=Moff,
                                                                   op0=ALU.mult, op1=ALU.mult),
                           r=[s["Lm"], BETA, C], w=[s["Noff"]])
                        op("pool", lambda e: e.tensor_tensor(out=R(o["QKdT"][:]), in0=s["tq"][:], in1=U,
                                                             op=ALU.mult), r=[s["tq"], C], w=[o["QKdT"]])
                        op("act", lambda e: e.activation(out=R(o["kdec"][:]), in_=s["ktok"][:], func=AF.Identity,
                                                         scale=s["DecT"][:, 127:128]), r=[s["ktok"], s["DecT"]],
                           w=[o["kdec"]])
                        op("dve", lambda e: e.tensor_tensor(out=R(o["qg"][:]), in0=qT[:, sl], in1=s["ExpG"][:],
                                                            op=ALU.mult), r=[qT, s["ExpG"]], w=[o["qg"]])
                        op("dve", lambda e: e.tensor_copy(out=R(o["kTc"][:]), in_=kT[:, sl]), r=[kT], w=[o["kTc"]])
                        op("dve", lambda e: e.tensor_copy(out=EGL[:, ch["i"]:ch["i"] + 1],
                                                          in_=s["ExpG"][:, 127:128]),
                           r=[s["ExpG"]], w=[(EGL, ch["i"])])
                    each(f)

                    pump_on[0] = True
                    def f(ch, s, o, pb):
                        tr(pb[:, 256:384], s["B"][:], ident, r=[s["B"], C], w=[pb])
                        op("act", lambda e: e.activation(out=R(s["BT"][:]), in_=pb[:, 256:384], func=AF.Copy),
                           r=[pb], w=[s["BT"]])
                        op("dve", lambda e: e.tensor_tensor(out=R(s["Pe"][:]), in0=s["B"][:], in1=ident,
                                                            op=ALU.add), r=[s["B"], C], w=[s["Pe"]])
                    each(f)

                    def f(ch, s, o, pb):
                        mm(pb[:, 0:128], R(s["BT"][:]), R(s["B"][:]), r=[s["BT"], s["B"]], w=[pb])
                        op("act", lambda e: e.activation(out=R(s["Xo"][:]), in_=pb[:, 0:128], func=AF.Copy),
                           r=[pb], w=[s["Xo"]])
                    each(f)
                    pump()
                    for j in range(1, 6):
                        odd = (j % 2 == 1)
                        Xc, Pc, XTc, XPc = ("Xo", "Pe", "XTo", "XPo") if odd else ("Xe", "Po", "XTe", "XPe")
                        Xn, Pn = ("Xe", "Po") if odd else ("Xo", "Pe")

                        def f(ch, s, o, pb):
                            tr(pb[:, 256:384], s[Xc][:], ident, r=[s[Xc], C], w=[pb])
                            op("act", lambda e: e.activation(out=R(s[XTc][:]), in_=pb[:, 256:384], func=AF.Copy),
                               r=[pb], w=[s[XTc]])
                        each(f)
                        pump()

                        def f(ch, s, o, pb):
                            if j < 5:
                                mm(pb[:, 0:256], R(s[XTc][:]), R(s[XPc][:]), r=[s[XTc], s[Xc], s[Pc]], w=[pb])
                                op("act", lambda e: e.activation(out=R(s[Xn][:]), in_=pb[:, 0:128], func=AF.Copy),
                                   r=[pb], w=[s[Xn]])
                                op("dve", lambda e: e.tensor_tensor(out=R(s[Pn][:]), in0=pb[:, 128:256],
                                                                    in1=s[Pc][:], op=ALU.add),
                                   r=[pb, s[Pc]], w=[s[Pn]])
                            else:
                                mm(pb[:, 128:256], R(s[XTc][:]), R(s[Pc][:]), r=[s[XTc], s[Pc]], w=[pb])
                                op("dve", lambda e: e.tensor_tensor(out=R(s[Pn][:]), in0=pb[:, 128:256],
                                                                    in1=s[Pc][:], op=ALU.add),
                                   r=[pb, s[Pc]], w=[s[Pn]])
                        each(f)
                        pump()
                    Pf = "Po"

                    def f(ch, s, o, pb):
                        tr(pb[:, 0:128], s[Pf][:], ident, r=[s[Pf], C], w=[pb])
                        tr(pb[:, 128:256], s["Noff"][:], ident, r=[s["Noff"], C], w=[pb])
                        op("act", lambda e: e.activation(out=R(s["WbdT"][:]), in_=pb[:, 0:128], func=AF.Copy),
                           r=[pb], w=[s["WbdT"]])
                        op("dve", lambda e: e.tensor_copy(out=R(s["NoffT"][:]), in_=pb[:, 128:256]), r=[pb],
                           w=[s["NoffT"]])
                    each(f)
                    pump()

                    def f(ch, s, o, pb):
                        mm(pb[:, 0:128], R(s["NoffT"][:]), R(s[Pf][:]), r=[s["NoffT"], s[Pf]], w=[pb])
                        op("act", lambda e: e.activation(out=R(s["Z1"][:]), in_=pb[:, 0:128], func=AF.Copy),
                           r=[pb], w=[s["Z1"]])
                    each(f)
                    pump()

                    def f(ch, s, o, pb):
                        mm(pb[:, 128:256], R(s["WbdT"][:]), R(s["Z1"][:]), r=[s["WbdT"], s["Z1"]], w=[pb])
                        op("dve", lambda e: e.tensor_tensor(out=R(o["W"][:]), in0=s[Pf][:], in1=pb[:, 128:256],
                                                            op=ALU.subtract), r=[s[Pf], pb], w=[o["W"]])
                    each(f)
                    pump()

                nrec = [0]

                def recur_pair(chs, g):
                    st_ = []
                    for ch in chs:
                        h = ch["h"]
                        So = Sst[h][spar[h]]
                        Sn = Sst[h][1 - spar[h]]
                        spar[h] = 1 - spar[h]
                        bb = 4 * ch["hi"]
                        st_.append((ch, So, Sn, PS[bb], PS[bb + 1], PS[bb + 2], PS[bb + 3],
                                    Yr[nrec[0] % 2], Vn[nrec[0] % 2]))
                        nrec[0] += 1
                    for ch, So, Sn, pa, pb2, pc, pd, Y, vnew in st_:
                        h, n, o = ch["h"], ch["n"], ch["o"]
                        mm(pa[:, 0:128], R(o["kTc"][:]), R(So[:]), r=[o["kTc"], So], w=[pa])
                        op("dve", lambda e: e.scalar_tensor_tensor(out=R(Y[:]), in0=pa[:, 0:128],
                                                                   scalar=NEXPG[:, n, h:h + 1], in1=o["vtok"][:],
                                                                   op0=ALU.mult, op1=ALU.add),
                           r=[pa, NEXPG, o["vtok"]], w=[Y])
                    for ch, So, Sn, pa, pb2, pc, pd, Y, vnew in st_:
                        h, n, o = ch["h"], ch["n"], ch["o"]
                        mm(pb2[:, 0:128], R(o["W"][:]), R(Y[:]), r=[o["W"], Y], w=[pb2])
                        op("act", lambda e: e.activation(out=R(vnew[:]), in_=pb2[:, 0:128], func=AF.Identity,
                                                         scale=BETA[:, n, h:h + 1]), r=[pb2, BETA], w=[vnew])
                    for ch, So, Sn, pa, pb2, pc, pd, Y, vnew in st_:
                        o = ch["o"]
                        mm(pd[:, 0:128], R(o["kdec"][:]), R(vnew[:]), r=[o["kdec"], vnew], w=[pd])
                        op("dve", lambda e: e.scalar_tensor_tensor(out=R(Sn[:]), in0=So[:],
                                                                   scalar=EGL[:, ch["i"]:ch["i"] + 1],
                                                                   in1=pd[:, 0:128], op0=ALU.mult, op1=ALU.add),
                           r=[So, (EGL, ch["i"]), pd], w=[Sn])
                    for ch, So, Sn, pa, pb2, pc, pd, Y, vnew in st_:
                        o, sl = ch["o"], ch["sl"]
                        mm(pc[:, 0:128], R(So[:]), R(o["qg"][:]), start=True, stop=False, r=[So, o["qg"]], w=[pc],
                           inc=False)
                        mm(pc[:, 0:128], R(vnew[:]), R(o["QKdT"][:]), start=False, stop=True, r=[vnew, o["QKdT"]],
                           w=[pc])
                        ot = oTg[ch["hi"]]
                        op("act", lambda e: e.activation(out=ot[:, sl], in_=pc[:, 0:128], func=AF.Copy), r=[pc],
                           w=[(ot, ch["cc"])])

                def stageA(g, hp, par, res):
                    gs = slice(g * 512, (g + 1) * 512)
                    chains = []
                    for hi in range(2):
                        h = 2 * hp + hi
                        qk = qkv[hi]
                        zt = zs[2 * par + hi]
                        for ty in range(4):
                            pb = PS[4 * hi + ty]
                            col = ty * 512 + h * 128
                            for c in range(8):
                                mm(pb[:, :], Wdn[:, c, col:col + 128], hT[:, c, gs], start=(c == 0),
                                   stop=(c == 7), r=[(Wdn, c), (hT, g)], w=[pb], inc=(c == 7))
                            if ty < 3:
                                ch_ = ty * 4 + h
                                op("dve", lambda e: e.tensor_copy(out=pre[ty][:, 0:3], in_=halo[:, ch_, :]),
                                   r=[(halo, ch_)], w=[(pre[ty], 0)])
                                op("act", lambda e: e.activation(out=pre[ty][:, 3:515], in_=pb[:, :],
                                                                 func=AF.Copy), r=[pb], w=[(pre[ty], 1)])
                                op("dve", lambda e: e.tensor_copy(out=halo[:, ch_, :], in_=pre[ty][:, 512:515]),
                                   r=[(pre[ty], 1)], w=[(halo, ch_)])
                                yield
                                cvt = cv[ty % 2]
                                wk = lambda k: convw[:, ch_ * 4 + k:ch_ * 4 + k + 1]
                                op("act", lambda e: e.activation(out=cvt[:], in_=pre[ty][:, 0:512],
                                                                 func=AF.Identity, scale=wk(0)),
                                   r=[pre[ty], convw], w=[cvt])
                                for k in range(1, 4):
                                    op("dve", lambda e: e.scalar_tensor_tensor(out=cvt[:],
                                                                               in0=pre[ty][:, k:k + 512],
                                                                               scalar=wk(k), in1=cvt[:],
                                                                               op0=ALU.mult, op1=ALU.add),
                                       r=[pre[ty], convw, cvt], w=[cvt])
                                    yield
                                op("act", lambda e: e.activation(out=(R(qk[ty][:]) if ty < 2 else qk[ty][:]),
                                                                 in_=cvt[:], func=AF.Silu),
                                   r=[cvt], w=[qk[ty]])
                            else:
                                op("act", lambda e: e.activation(out=zt[:], in_=pb[:, :], func=AF.Silu),
                                   r=[pb], w=[zt])
                            yield
                        for ty in range(2):
                            sqt = sq[ty]
                            pb = PS[4 * hi + ty]
                            op("act", lambda e: e.activation(out=R(sqt[:]), in_=qk[ty][:], func=AF.Square),
                               r=[qk[ty]], w=[sqt])
                            mm(pb[:, :], R(onesr[:]), R(sqt[:]), r=[onesr, sqt], w=[pb])
                            rt_ = cv[0]
                            op("act", lambda e: e.activation(out=rt_[:], in_=pb[:, :], func=AF.Ln,
                                                             bias=epsT[:]), r=[pb, epsT], w=[rt_])
                            op("act", lambda e: e.activation(out=rt_[:], in_=rt_[:], func=AF.Exp, scale=-0.5),
                               r=[rt_], w=[rt_])
                            sc_ = (128.0 ** -0.5) if ty == 0 else 1.0
                            op("dve", lambda e: e.scalar_tensor_tensor(out=R(qk[ty][:]), in0=qk[ty][:],
                                                                       scalar=sc_, in1=rt_[:], op0=ALU.mult,
                                                                       op1=ALU.mult),
                               r=[qk[ty], rt_], w=[qk[ty]])
                            yield
                        if dbg and l == 0 and g == 0 and h == 0:
                            for nm, tt in (("q", qk[0]), ("k", qk[1]), ("v", qk[2])):
                                if nm in dbg:
                                    d = dbg_out(nm, [128, 512])
                                    dma(d[:, :], tt[:], r=[tt])
                        for cc in range(4):
                            i = hi * 4 + cc
                            chains.append({"h": h, "hi": hi, "cc": cc, "n": g * 4 + cc, "i": i,
                                           "sl": slice(cc * 128, (cc + 1) * 128), "qT": qk[0], "kT": qk[1],
                                           "vT": qk[2], "s": scr[i], "o": outs[i]})
                    res["chains"] = chains

                def drain(gen):
                    if gen is not None:
                        for _ in gen:
                            pass

                pairs = [(g, hp) for g in range(NG) for hp in range(2)]
                resA = [dict() for _ in pairs]
                gens = [stageA(g, hp, pi % 2, resA[pi]) for pi, (g, hp) in enumerate(pairs)]
                drain(gens[0])
                for pi, (g, hp) in enumerate(pairs):
                    gs = slice(g * 512, (g + 1) * 512)
                    chains = resA[pi]["chains"]
                    nxt = gens[pi + 1] if pi + 1 < len(pairs) else None

                    def pump(n=1):
                        if nxt is not None:
                            for _ in range(n):
                                if next(nxt, "done") == "done":
                                    break
                    prep8(chains, pump)
                    drain(nxt)
                    for cc in range(4):
                        recur_pair([ch for ch in chains if ch["cc"] == cc], g)
                    for hi in range(2):
                        h = 2 * hp + hi
                        zt = zs[2 * (pi % 2) + hi]
                        ot = oTg[hi]
                        og = ogb[hi]
                        sqt = sq[hi]
                        pb = PS[4 * hi]
                        op("act", lambda e: e.activation(out=R(sqt[:]), in_=ot[:], func=AF.Square), r=[ot],
                           w=[sqt])
                        mm(pb[:, :], R(onesr[:]), R(sqt[:]), r=[onesr, sqt], w=[pb])
                        sqt = cv[0]
                        op("act", lambda e: e.activation(out=sqt[:], in_=pb[:, :], func=AF.Ln,
                                                         scale=1.0 / 128, bias=epsT[:]), r=[pb, epsT], w=[sqt])
                        op("act", lambda e: e.activation(out=sqt[:], in_=sqt[:], func=AF.Exp, scale=-0.5),
                           r=[sqt], w=[sqt])
                        if "odn" in dbg and l == 0 and h == 0 and g == 0:
                            d = dbg_out("odn", [128, 512])
                            dma(d[:, :], ot[:], r=[ot])
                        op("dve", lambda e: e.scalar_tensor_tensor(out=sqt[:], in0=ot[:], scalar=dng[:, 0:1],
                                                                   in1=sqt[:], op0=ALU.mult, op1=ALU.mult),
                           r=[ot, dng, sqt], w=[sqt])
                        op("dve", lambda e: e.tensor_tensor(out=og[:], in0=sqt[:], in1=zt[:], op=ALU.mult),
                           r=[sqt, zt], w=[og])
                        dma(ogdn_d[h, :, gs], og[:], r=[og], w=[(kogdn, g)])
                kb.barrier()

            for st in phase("mb"):
                Wmb = T(st, "Wmb", [128, 8, 2048], BF16)
                for c in range(8):
                    dma(Wmb[:, c, :], wmb_d[l, :, c, :], w=[(Wmb, c)], q="pool")
                Vt = T(st, "Vt", [128, NT, 8, 65], BF16)
                op("dve", lambda e: e.memset(Vt[:, :, :, 64:65], 1.0), w=[Vt])
                for t in range(NT):
                    pb = PS[t % 2]
                    for c in range(8):
                        mm(pb[:, :], hT[:, c, t * 128:(t + 1) * 128], Wmb[:, c, 1024:1536], start=(c == 0),
                           stop=(c == 7), r=[(hT, t // 4), (Wmb, c)], w=[pb], inc=(c == 7))
                    op("act" if t % 2 else "dve",
                       lambda e: (e.activation(out=Vt[:, t, :, 0:64], in_=pb[:, :].rearrange("p (h d) -> p h d", h=8),
                                               func=AF.Copy) if t % 2 else
                                  e.tensor_copy(out=Vt[:, t, :, 0:64],
                                                in_=pb[:, :].rearrange("p (h d) -> p h d", h=8))),
                       r=[pb], w=[(Vt, t)])
                KaT = T(st, "KaT", [128, S], BF16)
                QaT = T(st, "QaT", [128, S], BF16)
                zsm = T(st, "zsm", [64, S], BF16)
                ogm = T(st, "ogm", [64, S], BF16)
                kf = [T(st, "kf%d" % i, [128, 512]) for i in range(2)]
                qf = [T(st, "qf%d" % i, [128, 512]) for i in range(2)]
                sqm = [T(st, "sqm%d" % i, [128, 512]) for i in range(2)]
                kmT = T(st, "kmT", [128, 16])
                shr = T(st, "shr", [64, 512])
                km2 = T(st, "km2", [64, NG + 1])
                gm4 = T(st, "gm4", [128, 4, 16])
                top84 = T(st, "top84", [128, 4, 8])
                mbt4 = [T(st, "mbt4_%d" % i, [128, 4, 16]) for i in range(2)]
                rden = T(st, "rden", [128, 512])
                t1 = [T(st, "t1_%d" % i, [64, 512]) for i in range(2)]
                PT = [T(st, "PT%d" % i, [128, 512], BF16) for i in range(4)]
                nS = [0]
                op("dve", lambda e: e.memset(KaT[0:64, :], 0.0), w=[KaT])
                op("dve", lambda e: e.memset(QaT[0:64, :], 0.0), w=[QaT])
                op("dve", lambda e: e.memset(KaT[32:33, :], 1.0), w=[KaT])
                for n in range(NB):
                    op("dve", lambda e: e.tensor_copy(out=KaT[0:16, n * 256:(n + 1) * 256],
                                                      in_=ident[0:16, n:n + 1].to_broadcast([16, 256])),
                       r=[C], w=[KaT])
                npt = 0
                for h in range(8):
                    op("dve", lambda e: e.tensor_scalar(out=R(kmT[:]), in0=ident[:, 0:16], scalar1=0.0, scalar2=None,
                                                        op0=ALU.mult), r=[C], w=[kmT])
                    def projK(g):
                        pb = PS[g % 2]
                        col = 512 + h * 64
                        for c in range(8):
                            mm(pb[64:128, :], Wmb[:, c, col:col + 64], hT[:, c, g * 512:(g + 1) * 512],
                               start=(c == 0), stop=(c == 7), r=[(Wmb, c), (hT, g)], w=[pb], inc=(c == 7))

                    def postK(g):
                        gs = slice(g * 512, (g + 1) * 512)
                        pb = PS[g % 2]
                        kft = kf[g % 2]
                        op("act", lambda e: e.activation(out=kft[64:128, :], in_=pb[64:128, :], func=AF.Copy),
                           r=[pb], w=[kft])
                        op("act", lambda e: e.activation(out=KaT[64:128, gs], in_=pb[64:128, :], func=AF.Copy),
                           r=[pb], w=[(KaT, g)])
                        op("dve", lambda e: e.tensor_reduce(out=R(kmT[64:128, 2 * g:2 * g + 2]),
                                                            in_=kft[64:128, :].rearrange("p (b t) -> p b t", b=2),
                                                            axis=AX.X, op=ALU.add), r=[kft], w=[kmT])
                        sqt = sqm[g % 2]
                        op("dve", lambda e: e.tensor_tensor(out=sqt[64:128, :], in0=kft[64:128, :],
                                                            in1=kft[64:128, :], op=ALU.mult), r=[kft], w=[sqt])
                        pr = PS[2]
                        mm(pr[32:33, :], ones[64:128, 0:1], sqt[64:128, :], r=[C, sqt], w=[pr])
                        op("dve", lambda e: e.tensor_reduce(out=km2[32:33, g:g + 1], in_=pr[32:33, :], axis=AX.X,
                                                            op=ALU.max), r=[pr], w=[km2])
                    projK(0)
                    for g in range(NG):
                        if g + 1 < NG:
                            projK(g + 1)
                        postK(g)
                    op("dve", lambda e: e.tensor_scalar(out=R(kmT[64:128, :]), in0=kmT[64:128, :], scalar1=1.0 / 256,
                                                        scalar2=None, op0=ALU.mult), r=[kmT], w=[kmT])
                    op("dve", lambda e: e.tensor_reduce(out=km2[32:33, NG:NG + 1], in_=km2[32:33, 0:NG], axis=AX.X,
                                                        op=ALU.max), r=[km2], w=[km2])
                    for g in range(NG):
                        gs = slice(g * 512, (g + 1) * 512)
                        pb = PS[g % 2]
                        col = 1536 + h * 64
                        for c in range(8):
                            mm(pb[0:64, :], Wmb[:, c, col:col + 64], hT[:, c, gs], start=(c == 0), stop=(c == 7),
                               r=[(Wmb, c), (hT, g)], w=[pb], inc=(c == 7))
                        op("act", lambda e: e.activation(out=zsm[:, gs], in_=pb[0:64, :], func=AF.Silu), r=[pb],
                           w=[(zsm, g)])

                    def projQ(g):
                        pb = PS[g % 3]
                        col = h * 64
                        for c in range(8):
                            mm(pb[64:128, :], Wmb[:, c, col:col + 64], hT[:, c, g * 512:(g + 1) * 512],
                               start=(c == 0), stop=(c == 7), r=[(Wmb, c), (hT, g)], w=[pb], inc=(c == 7))

                    def postQ1(g):
                        gs = slice(g * 512, (g + 1) * 512)
                        pb = PS[g % 3]
                        qft = qf[g % 2]
                        op("act", lambda e: e.activation(out=R(qft[64:128, :]), in_=pb[64:128, :], func=AF.Identity,
                                                         scale=0.125), r=[pb], w=[qft])
                        op("act", lambda e: e.activation(out=QaT[64:128, gs], in_=pb[64:128, :], func=AF.Identity,
                                                         scale=0.125), r=[pb], w=[(QaT, g)])
                        sqt = sqm[g % 2]
                        op("dve", lambda e: e.tensor_tensor(out=sqt[64:128, :], in0=qft[64:128, :],
                                                            in1=qft[64:128, :], op=ALU.mult), r=[qft], w=[sqt])
                        pgt = PS[4]
                        for tt in range(4):
                            mm(pgt[:, tt * 16:(tt + 1) * 16], R(qft[64:128, tt * 128:(tt + 1) * 128]), R(kmT[64:128, :]),
                               r=[qft, kmT], w=[pgt], inc=(tt == 3))
                        pr = PS[5]
                        mm(pr[32:33, :], ones[64:128, 0:1], sqt[64:128, :], r=[C, sqt], w=[pr])
                        op("dve", lambda e: e.tensor_tensor(out=gm4[:].rearrange("p a b -> p (a b)"), in0=pgt[:, 0:64],
                                                            in1=C[:, C_PB4 + g * 64:C_PB4 + (g + 1) * 64], op=ALU.add),
                           r=[pgt, C], w=[gm4])
                        op("act", lambda e: e.activation(out=shr[32:33, :], in_=pr[32:33, :], func=AF.Sqrt,
                                                         scale=km2[32:33, NG:NG + 1]), r=[pr, km2], w=[shr])
                        op("act", lambda e: e.activation(out=QaT[32:33, gs], in_=shr[32:33, :], func=AF.Identity,
                                                         scale=-1.0), r=[shr], w=[(QaT, g)])
                        for tt in range(4):
                            op("dve", lambda e: e.max(out=top84[:, tt, :], in_=gm4[:, tt, :]), r=[gm4],
                               w=[(top84, tt)])
                        mb_ = mbt4[g % 2]
                        for tt in range(4):
                            op("dve", lambda e: e.tensor_scalar(out=mb_[:, tt, :], in0=gm4[:, tt, :],
                                                                scalar1=top84[:, tt, 2:3], scalar2=NEG,
                                                                op0=ALU.is_lt, op1=ALU.mult),
                               r=[gm4, (top84, tt)], w=[(mb_, tt)])
                        for t2 in range(2):
                            own = 2 * g + t2
                            op("dve", lambda e: e.memset(mb_[:, 2 * t2:2 * t2 + 2, own:own + 1], 0.0),
                               r=[(mb_, 2 * t2), (mb_, 2 * t2 + 1)], w=[(mb_, 2 * t2), (mb_, 2 * t2 + 1)])

                    def postQ2(g):
                        gs = slice(g * 512, (g + 1) * 512)
                        mb_ = mbt4[g % 2]
                        pt_ = PS[3]
                        for tt in range(4):
                            tr(pt_[0:16, tt * 128:(tt + 1) * 128], mb_[:, tt, :], ident, r=[(mb_, tt), C], w=[pt_])
                        op("act", lambda e: e.activation(out=QaT[0:16, gs], in_=pt_[0:16, 0:512], func=AF.Copy),
                           r=[pt_], w=[(QaT, g)])
                    projQ(0)
                    if NG > 1:
                        projQ(1)
                    for g in range(NG):
                        postQ1(g)
                        if g + 2 < NG:
                            projQ(g + 2)
                        if g > 0:
                            postQ2(g - 1)
                    postQ2(NG - 1)
                    LOOK = 3
                    for jq in range(NB // 2):
                        q0 = jq * 512
                        pO = PS[6 + jq % 2]
                        ntile = 4 * jq + 4
                        qk_ = (QaT, jq)

                        def emitS(kt):
                            pS = PS[nS[0] % 4]
                            nS[0] += 1
                            d = kt - 4 * jq
                            c0 = 0 if d < 0 else 128 * d
                            mm(pS[:, c0:512], KaT[:, kt * 128:(kt + 1) * 128], QaT[:, q0 + c0:q0 + 512], start=True,
                               stop=(d < 0), r=[(KaT, kt // 4), qk_], w=[pS], inc=(d < 0))
                            if d >= 0:
                                mm(pS[:, c0:c0 + 128], identb[:], trib[:], start=False, stop=True, r=[identb, trib],
                                   w=[pS])
                            return pS, c0

                        pendq = [emitS(k_) for k_ in range(min(LOOK, ntile))]
                        for kt in range(ntile):
                            pS, c0 = pendq.pop(0)
                            if kt + LOOK < ntile:
                                pendq.append(emitS(kt + LOOK))
                            ptile = PT[npt % 4]
                            npt += 1
                            op("act", lambda e: e.activation(out=ptile[:, c0:512], in_=pS[:, c0:512], func=AF.Exp),
                               r=[pS], w=[ptile])
                            mm(pO[0:65, c0:512], Vt[:, kt, h, :], ptile[:, c0:512], start=(kt == 0),
                               stop=(kt == ntile - 1), r=[(Vt, kt), ptile], w=[pO], inc=(kt == ntile - 1))
                        op("dve", lambda e: e.reciprocal(out=R(rden[64:65, :]), in_=pO[64:65, 0:512]), r=[pO],
                           w=[rden])
                        pB = PS[5]
                        mm(pB[0:64, 0:512], R(onesr[64:65, 0:64]), R(rden[64:65, :]), r=[onesr, rden], w=[pB])
                        tt1 = t1[jq % 2]
                        op("dve", lambda e: e.tensor_tensor(out=tt1[:], in0=pO[0:64, 0:512],
                                                            in1=zsm[:, q0:q0 + 512], op=ALU.mult),
                           r=[pO, (zsm, jq)], w=[tt1])
                        op("dve", lambda e: e.tensor_tensor(out=ogm[:, q0:q0 + 512], in0=tt1[:], in1=pB[0:64, 0:512],
                                                            op=ALU.mult), r=[tt1, pB], w=[(ogm, jq)])
                    if "omb" in dbg and l == 0 and h == 0:
                        d = dbg_out("omb", [64, S])
                        tmp = T(st, "dbgomb", [64, S])
                        op("dve", lambda e: e.tensor_copy(out=tmp[:], in_=ogm[:]), r=[ogm], w=[tmp])
                        dma(d[:, :], tmp[:], r=[tmp])
                    dma(ogmb_d[h, :, :], ogm[:], r=[ogm], w=[kogmb])
                kb.barrier()

            for st in phase("fin"):
                Wpdn = T(st, "Wpdn", [128, 4, 1024], BF16)
                Wpmb = T(st, "Wpmb", [64, 8, 1024], BF16)
                Wout = T(st, "Wout", [128, 8, 1024], BF16)
                Wmg = T(st, "Wmg", [128, 8, 2048], BF16)
                dma(Wpdn[:], wpdn_d[l], w=[Wpdn], q="pool")
                dma(Wpmb[:], wpmb_d[l], w=[Wpmb], q="pool")
                for c in range(8):
                    dma(Wout[:, c, :], wout_d[l, :, c, :], w=[(Wout, c)], q="pool")
                for c in range(8):
                    dma(Wmg[:, c, :], wmg_d[l, :, c, :], w=[(Wmg, c)], q="pool")
                OGD = [T(st, "OGD%d" % i, [128, 4, 512], BF16) for i in range(2)]
                OGM = [T(st, "OGM0", [64, 8, 512], BF16)] * 2
                mixT = T(st, "mixT", [128, 8, 512], BF16)
                gd = [T(st, "gd%d" % i, [128, 512]) for i in range(2)]
                gmm = [T(st, "gmm%d" % i, [128, 512]) for i in range(2)]
                u1 = [T(st, "u1_%d" % i, [128, 512]) for i in range(2)]
                u2 = [T(st, "u2_%d" % i, [128, 512]) for i in range(2)]
                xr = [T(st, "xr%d" % i, [128, 1024]) for i in range(2)]
                res = [T(st, "res%d" % i, [128, 512]) for i in range(2)]
                junk2 = T(st, "junk2", [128, 512], BF16)
                GPl = T(st, "GPl", [128, 1024])
                dma(GPl[:], gp_d[l], r=[(kgp, l)], w=[GPl])
                ss2 = T(st, "ss2", [128, NT, 2])
                rs2 = T(st, "rs2", [128, NT])
                for g in range(NG):
                    gs = slice(g * 512, (g + 1) * 512)
                    ogd, ogmm = OGD[g % 2], OGM[g % 2]
                    dma(ogd[:], ogdn_d[:, :, gs].rearrange("h p s -> p h s"), r=[(kogdn, g)], w=[ogd])
                    dma(ogmm[:], ogmb_d[:, :, gs].rearrange("h p s -> p h s"), r=[kogmb], w=[ogmm])
                    for d_ in range(8):
                        ds_ = slice(d_ * 128, (d_ + 1) * 128)
                        pa, pb, pc, pd = PS[0 + 4 * (d_ % 2)], PS[1 + 4 * (d_ % 2)], PS[2 + 4 * (d_ % 2)], PS[3 + 4 * (d_ % 2)]
                        for h in range(4):
                            mm(pa[:, :], Wpdn[:, h, ds_], ogd[:, h, :], start=(h == 0), stop=(h == 3),
                               r=[Wpdn, ogd], w=[pa], inc=(h == 3))
                        for h in range(8):
                            mm(pb[:, :], Wpmb[:, h, ds_], ogmm[:, h, :], start=(h == 0), stop=(h == 7),
                               r=[Wpmb, ogmm], w=[pb], inc=(h == 7))
                        for c in range(8):
                            mm(pc[:, :], Wmg[:, c, ds_], hT[:, c, gs], start=(c == 0), stop=(c == 7),
                               r=[(Wmg, c), (hT, g)], w=[pc], inc=(c == 7))
                        for c in range(8):
                            mm(pd[:, :], Wmg[:, c, 1024 + d_ * 128:1024 + (d_ + 1) * 128], hT[:, c, gs],
                               start=(c == 0), stop=(c == 7), r=[(Wmg, c), (hT, g)], w=[pd], inc=(c == 7))
                        gdt, gmt, u1t, u2t = gd[d_ % 2], gmm[d_ % 2], u1[d_ % 2], u2[d_ % 2]
                        op("act", lambda e: e.activation(out=gdt[:], in_=pc[:, :], func=AF.Sigmoid), r=[pc], w=[gdt])
                        op("act", lambda e: e.activation(out=gmt[:], in_=pd[:, :], func=AF.Sigmoid), r=[pd], w=[gmt])
                        op("dve", lambda e: e.tensor_tensor(out=u1t[:], in0=pa[:, :], in1=gdt[:], op=ALU.mult),
                           r=[pa, gdt], w=[u1t])
                        op("dve", lambda e: e.tensor_tensor(out=u2t[:], in0=pb[:, :], in1=gmt[:], op=ALU.mult),
                           r=[pb, gmt], w=[u2t])
                        op("dve", lambda e: e.tensor_tensor(out=mixT[:, d_, :], in0=u1t[:], in1=u2t[:], op=ALU.add),
                           r=[u1t, u2t], w=[(mixT, d_)])
                    for tt in range(4):
                        t = g * 4 + tt
                        xrt = xr[t % 2]
                        dma(xrt[:], xin_d[t * 128:(t + 1) * 128, :], r=[(xin_k, t)], w=[xrt])
                        for hf in range(2):
                            pb = PS[(t % 2) * 2 + hf]
                            for d_ in range(8):
                                mm(pb[:, :], mixT[:, d_, tt * 128:(tt + 1) * 128], Wout[:, d_, hf * 512:(hf + 1) * 512],
                                   start=(d_ == 0), stop=(d_ == 7), r=[(mixT, d_), (Wout, d_)], w=[pb],
                                   inc=(d_ == 7))
                            op("act", lambda e: e.activation(out=junk2[:], in_=pb[:, :], func=AF.Square,
                                                             accum_out=ss2[:, t, hf:hf + 1]), r=[pb],
                               w=[junk2, (ss2, t)])
                        op("dve", lambda e: e.tensor_tensor(out=rs2[:, t:t + 1], in0=ss2[:, t, 0:1],
                                                            in1=ss2[:, t, 1:2], op=ALU.add), r=[(ss2, t)],
                           w=[(rs2, t)])
                        op("act", lambda e: e.activation(out=rs2[:, t:t + 1], in_=rs2[:, t:t + 1], func=AF.Sqrt,
                                                         scale=1.0 / D_MODEL, bias=epsT[:]), r=[(rs2, t), epsT],
                           w=[(rs2, t)])
                        op("dve", lambda e: e.reciprocal(out=rs2[:, t:t + 1], in_=rs2[:, t:t + 1]), r=[(rs2, t)],
                           w=[(rs2, t)])
                        for hf in range(2):
                            pb = PS[(t % 2) * 2 + hf]
                            hs = slice(hf * 512, (hf + 1) * 512)
                            rt = res[hf]
                            op("dve", lambda e: e.scalar_tensor_tensor(out=rt[:], in0=pb[:, :],
                                                                       scalar=rs2[:, t:t + 1], in1=GPl[:, hs],
                                                                       op0=ALU.mult, op1=ALU.mult),
                               r=[pb, (rs2, t), GPl], w=[rt])
                            op("dve", lambda e: e.tensor_tensor(out=xrt[:, hs], in0=rt[:], in1=xrt[:, hs],
                                                                op=ALU.add), r=[rt, (xrt, hf)], w=[(xrt, hf)])
                        dma(xout_d[t * 128:(t + 1) * 128, :], xrt[:], r=[xrt], w=[(xout_k, t)])
                kb.barrier()
          except _Stop:
            kb.barrier()
            curgen[0].close()
            break
        kb.finish()
        stuck = kb.simulate()
        print("deadlock check:", stuck if stuck else "ok")
        print("instructions:", kb.ninstr, "counts", kb.count, "dmas", kb.ndmaq)
    return nc, dbg_d


def _pc(w):
    sh = w.shape
    w = w.reshape(sh[:-2] + (8, 128, sh[-1]))
    return np.ascontiguousarray(np.swapaxes(w, -3, -2))


def host_layout(inp):
    f = lambda a: np.ascontiguousarray(np.asarray(a, dtype=np.float32))
    w_in = f(inp["w_in"])
    depth = w_in.shape[0]
    shared = {
        "wada": _pc(f(inp["w_ada"])),
        "bada": f(inp["b_ada"]).reshape(depth, 1, 3072),
        "gpre": f(inp["g_pre"]).reshape(depth, 1, 1024),
        "gpost": f(inp["g_post"]).reshape(depth, 1, 1024),
        "wdn": _pc(w_in[:, :, 0:2048]),
        "wba": _pc(w_in[:, :, 2048:2056]),
        "wmb": _pc(w_in[:, :, 2056:4104]),
        "wmg": _pc(w_in[:, :, 4104:6152]),
        "convw": np.ascontiguousarray(
            f(inp["conv_w"]).transpose(0, 2, 1).reshape(depth, 12, 128, 4).transpose(0, 2, 1, 3)
        ).reshape(depth, 128, 48),
        "alog": np.ascontiguousarray(np.broadcast_to(f(inp["a_log"])[:, None, :], (depth, 128, 4))),
        "dtb": np.ascontiguousarray(np.broadcast_to(f(inp["dt_bias"])[:, None, :], (depth, 128, 4))),
        "dng": f(inp["dn_norm_g"]).reshape(depth, 128, 1),
        "wpdn": np.ascontiguousarray(f(inp["w_proj_dn"]).reshape(depth, 4, 128, 1024).transpose(0, 2, 1, 3)),
        "wpmb": np.ascontiguousarray(f(inp["w_proj_mb"]).reshape(depth, 8, 64, 1024).transpose(0, 2, 1, 3)),
        "wout": _pc(f(inp["w_out"])),
        "consts": make_consts(),
    }
    x = f(inp["x"])
    c = f(inp["c"])
    maps = []
    for b in range(x.shape[0]):
        m = dict(shared)
        m["x"] = x[b]
        m["cT"] = np.ascontiguousarray(c[b].reshape(8, 128).T)
        maps.append(m)
    return maps


_CACHE = {}


def kernel(**inputs):
    x = np.asarray(inputs["x"])
    B, S, _ = x.shape
    depth = np.asarray(inputs["w_in"]).shape[0]
    key = (S, depth)
    if key not in _CACHE:
        _CACHE[key] = build(S=S, DEPTH=depth)[0]
    nc = _CACHE[key]
    maps = host_layout(inputs)
    res = run_bass_kernel_spmd(nc, maps, core_ids=list(range(B)))
    return np.stack([np.asarray(r["y"], dtype=np.float32) for r in res.results], axis=0)
```

```python
from contextlib import ExitStack

import numpy as np
import concourse.bass as bass
import concourse.mybir as mybir
from concourse.bass_utils import run_bass_kernel_spmd

F32 = mybir.dt.float32
BF16 = mybir.dt.bfloat16
F32R = mybir.dt.float32r


def R(ap):
    return ap.bitcast(F32R)
AF = mybir.ActivationFunctionType
ALU = mybir.AluOpType
AX = mybir.AxisListType

D_MODEL = 1024
NEG = -30000.0
EPS = 1e-6


class SV:
    def __init__(self, tile, j, n=1):
        self.tile, self.sub = tile, j
        self.ap = tile[:, j * 128:(j + n) * 128]

    def __getitem__(self, idx):
        return self.ap[idx]


class KB:
    NRING = {"sp": 6, "pool": 4}

    def __init__(self, nc, stack):
        self.nc = nc
        self.eng = {"pe": nc.tensor, "act": nc.scalar, "dve": nc.vector,
                    "pool": nc.gpsimd, "sp": nc.sync}
        self.sem = {}
        for e in ("pe", "act", "dve", "pool"):
            self.sem[e] = stack.enter_context(nc.semaphore("s_" + e))
        self.ring = {q: [stack.enter_context(nc.semaphore("s_dma_%s%d" % (q, i))) for i in range(n)]
                     for q, n in self.NRING.items()}
        self.ndmaq = {q: 0 for q in self.NRING}
        self.count = {e: 0 for e in ("pe", "act", "dve", "pool")}
        self.waited = {}
        self.track = {}
        self.ninstr = 0
        self.streams = {e: [] for e in self.eng}
        self.psum_ids = set()

    def _semof(self, dep):
        if dep[0] == "e":
            return self.sem[dep[1]], ("e", dep[1])
        return self.ring[dep[1][0]][dep[1][1]], ("d", dep[1])

    def _wait(self, e, dep):
        sem, sk = self._semof(dep)
        val = dep[2]
        k = (e, sk)
        if self.waited.get(k, 0) >= val:
            return
        self.eng[e].wait_ge(sem, val)
        self.streams[e].append(("wait", sk, val))
        self.ninstr += 1
        self.waited[k] = val

    @staticmethod
    def _keys(items):
        out = []
        for it in items:
            if isinstance(it, SV):
                out.append((id(it.tile), it.sub))
            elif isinstance(it, tuple):
                out.append((id(it[0]), it[1]))
            else:
                out.append((id(it), None))
        return out

    def _conflicts(self, key):
        tid, sub = key
        d = self.track.get(tid)
        if d is None:
            return []
        if sub is None:
            return list(d.values())
        res = []
        if sub in d:
            res.append(d[sub])
        if None in d:
            res.append(d[None])
        return res

    def _entry(self, key):
        tid, sub = key
        d = self.track.setdefault(tid, {})
        if sub is None:
            ent = {"w": [], "r": []}
            for v in d.values():
                ent["w"] += v["w"]
                ent["r"] += v["r"]
            d.clear()
            d[None] = ent
            return ent
        if sub not in d:
            ent = {"w": [], "r": []}
            if None in d:
                ent["w"] = list(d[None]["w"])
                ent["r"] = list(d[None]["r"])
            d[sub] = ent
        return d[sub]

    def _deps(self, e, reads, writes):
        deps = []
        for k in reads:
            for ent in self._conflicts(k):
                for w in ent["w"]:
                    deps.append((w, "raw"))
        for k in writes:
            for ent in self._conflicts(k):
                for w in ent["w"]:
                    deps.append((w, "waw"))
                for r in ent["r"]:
                    deps.append((r, "war"))
        out = []
        for dep, kind in deps:
            if dep[0] == "e" and dep[1] == e:
                if e == "pe":
                    continue
            out.append(dep)
        return out

    @staticmethod
    def _prune(lst):
        best = {}
        for d in lst:
            kk = (d[0], d[1])
            if kk not in best or best[kk][2] < d[2]:
                best[kk] = d
        return list(best.values())

    def _record(self, me, reads, writes):
        for k in reads:
            ent = self._entry(k)
            ent["r"].append(me)
            if len(ent["r"]) > 16:
                ent["r"] = self._prune(ent["r"])
        for k in writes:
            ent = self._entry(k)
            ent["w"] = [me]
            ent["r"] = []

    def _rw(self, r, w):
        reads, writes = [], []
        for k in self._keys(r):
            if k[0] in self.psum_ids:
                writes.append((k[0], None))
            else:
                reads.append(k)
        for k in self._keys(w):
            writes.append((k[0], None) if k[0] in self.psum_ids else k)
        return reads, writes

    def op(self, e, fn, r=(), w=(), inc=True):
        reads, writes = self._rw(r, w)
        for dep in self._deps(e, reads, writes):
            self._wait(e, dep)
        ins = fn(self.eng[e])
        self.ninstr += 1
        if inc:
            self.count[e] += 1
            ins.then_inc(self.sem[e], 1)
            self.streams[e].append(("inc", ("e", e), 1))
            me = ("e", e, self.count[e])
        else:
            me = ("e", e, self.count[e] + 1)
        self._record(me, reads, writes)
        return ins

    def dma(self, out, in_, r=(), w=(), q="sp", **kw):
        e = q
        reads = self._keys(r)
        writes = self._keys(w)
        k = self.ndmaq[q]
        nr = self.NRING[q]
        slot = k % nr
        gen = k // nr
        for dep in self._deps(e, reads, writes):
            self._wait(e, dep)
        if gen > 0:
            self._wait(e, ("d", (q, slot), 16 * gen))
        ins = self.eng[e].dma_start(out=out, in_=in_, **kw)
        ins.then_inc(self.ring[q][slot], 16)
        self.streams[e].append(("inc", ("d", (q, slot)), 16))
        self.ndmaq[q] += 1
        self.ninstr += 1
        me = ("d", (q, slot), 16 * (gen + 1))
        self._record(me, reads, writes)
        return ins

    def _alldma(self):
        out = []
        for q, nr in self.NRING.items():
            n = self.ndmaq[q]
            for slot in range(nr):
                cnt = (n - 1 - slot) // nr + 1 if n > slot else 0
                if cnt > 0:
                    out.append(("d", (q, slot), 16 * cnt))
        return out

    def barrier(self):
        for e in ("pe", "act", "dve", "pool", "sp"):
            for o in ("pe", "act", "dve", "pool"):
                if o != e and self.count[o] > 0:
                    self._wait(e, ("e", o, self.count[o]))
            for dep in self._alldma():
                self._wait(e, dep)
        self.track = {}

    def finish(self):
        for dep in self._alldma():
            self._wait("sp", dep)

    def simulate(self):
        sems = {}
        pc = {e: 0 for e in self.streams}
        progress = True
        while progress:
            progress = False
            for e, st in self.streams.items():
                while pc[e] < len(st):
                    kind, sk, val = st[pc[e]]
                    if kind == "wait":
                        if sems.get(sk, 0) < val:
                            break
                    else:
                        sems[sk] = sems.get(sk, 0) + val
                    pc[e] += 1
                    progress = True
        stuck = {e: (pc[e], len(st), st[pc[e]], sems.get(st[pc[e]][1], 0)) for e, st in self.streams.items()
                 if pc[e] < len(st)}
        return stuck


C_IDENT, C_U, C_MBD, C_MOFF, C_TRI, C_ONES, C_PB = 0, 128, 256, 384, 512, 640, 768
C_PB4 = 768
NCONST = 768 + 512


def make_consts():
    c = np.zeros((128, NCONST), np.float32)
    i = np.arange(128)
    c[:, C_IDENT:C_IDENT + 128] = np.eye(128)
    c[:, C_U:C_U + 128] = (i[:, None] <= i[None, :])
    c[:, C_MBD:C_MBD + 128] = (i[:, None] < i[None, :]) & ((i[:, None] // 64) == (i[None, :] // 64))
    c[:, C_MOFF:C_MOFF + 128] = (i[:, None] < 64) & (i[None, :] >= 64)
    c[:, C_TRI:C_TRI + 128] = np.where(i[:, None] <= i[None, :], 0.0, NEG)
    c[:, C_ONES:C_ONES + 128] = 1.0
    pb = np.zeros((16, 16), np.float32)
    for own in range(16):
        pb[own, own:] = -1e30
    for g in range(8):
        for tt in range(4):
            own = (4 * g + tt) // 2
            c[:, C_PB4 + g * 64 + tt * 16:C_PB4 + g * 64 + (tt + 1) * 16] = pb[own][None, :]
    return c


def build(S=4096, DEPTH=2, LS=2, dbg=None, phases=("p1", "dn", "mb", "fin"), stop=0, JUNK=0):
    dbg = dbg or set()
    NT = S // 128
    NG = S // 512
    NB = S // 256
    nc = bass.Bass("TRN2", target_bir_lowering=False)

    def din(name, shape, dt=F32):
        return nc.dram_tensor(name, shape, dt, kind="ExternalInput").ap()

    x_d = din("x", [S, 1024])
    cT_d = din("cT", [128, 8])
    wada_d = din("wada", [DEPTH, 128, 8, 3072])
    bada_d = din("bada", [DEPTH, 1, 3072])
    gpre_d = din("gpre", [DEPTH, 1, 1024])
    gpost_d = din("gpost", [DEPTH, 1, 1024])
    wdn_d = din("wdn", [DEPTH, 128, 8, 2048])
    wba_d = din("wba", [DEPTH, 128, 8, 8])
    wmb_d = din("wmb", [DEPTH, 128, 8, 2048])
    wmg_d = din("wmg", [DEPTH, 128, 8, 2048])
    convw_d = din("convw", [DEPTH, 128, 48])
    alog_d = din("alog", [DEPTH, 128, 4])
    dtb_d = din("dtb", [DEPTH, 128, 4])
    dng_d = din("dng", [DEPTH, 128, 1])
    wpdn_d = din("wpdn", [DEPTH, 128, 4, 1024])
    wpmb_d = din("wpmb", [DEPTH, 64, 8, 1024])
    wout_d = din("wout", [DEPTH, 128, 8, 1024])
    consts_d = din("consts", [128, NCONST])
    y_d = nc.dram_tensor("y", [S, 1024], F32, kind="ExternalOutput").ap()
    xmid_d = nc.dram_tensor("xmid", [S, 1024], F32, kind="Internal").ap()
    ogdn_d = nc.dram_tensor("ogdn", [4, 128, S], BF16, kind="Internal").ap()
    ogmb_d = nc.dram_tensor("ogmb", [8, 64, S], BF16, kind="Internal").ap()
    dbg_d = {}

    class _K:
        pass
    kx, kmid, ky, kogdn, kogmb = _K(), _K(), _K(), _K(), _K()

    def dbg_out(name, shape):
        dbg_d[name] = nc.dram_tensor("dbg_" + name, shape, F32, kind="ExternalOutput").ap()
        return dbg_d[name]

    with ExitStack() as gst:
        kb = KB(nc, gst)
        op, dma = kb.op, kb.dma
        gst.enter_context(nc.allow_low_precision("float32r (1-pass PE) operands for non-critical fp32 matmuls"))

        def mm(out, lhsT, rhs, start=True, stop=True, r=(), w=(), inc=True):
            return op("pe", lambda e: e.matmul(out, lhsT=lhsT, rhs=rhs, start=start, stop=stop),
                      r=r, w=w, inc=inc)

        def tr(out, in_, ident, r=(), w=()):
            return op("pe", lambda e: e.transpose(out=out, in_=in_, identity=ident), r=r, w=w)

        uid = [0]

        def T(st, name, shape, dt=F32):
            uid[0] += 1
            return st.enter_context(nc.sbuf_tensor("sb%d_%s" % (uid[0], name), shape, dt))

        class _Stop(Exception):
            pass

        def ck(level):
            if stop == level:
                raise _Stop()

        curgen = [None]

        def phase(name):
            if name in phases:
                st_ = ExitStack()
                curgen[0] = st_
                yield st_
                st_.close()

        PS = [gst.enter_context(nc.psum_tensor("ps%d" % i, [128, 512], F32)) for i in range(8)]
        kb.psum_ids = {id(p) for p in PS}
        C = T(gst, "consts", [128, NCONST])
        dma(C[:], consts_d[:, :], w=[C])
        ident = C[:, C_IDENT:C_IDENT + 128]
        U = C[:, C_U:C_U + 128]
        Mbd = C[:, C_MBD:C_MBD + 128]
        Moff = C[:, C_MOFF:C_MOFF + 128]
        ones = C[:, C_ONES:C_ONES + 128]
        identb = T(gst, "identb", [128, 128], BF16)
        trib = T(gst, "trib", [128, 128], BF16)
        epsT = T(gst, "epsT", [128, 1])
        op("dve", lambda e: e.tensor_copy(out=identb[:], in_=ident), r=[C], w=[identb])
        op("dve", lambda e: e.tensor_copy(out=trib[:], in_=C[:, C_TRI:C_TRI + 128]), r=[C], w=[trib])
        op("dve", lambda e: e.memset(epsT[:], EPS), w=[epsT])
        onesr = T(gst, "onesr", [128, 128])
        op("dve", lambda e: e.tensor_copy(out=R(onesr[:]), in_=ones), r=[C], w=[onesr])
        AB = [T(gst, "AB%d" % l, [128, 16]) for l in range(DEPTH)]
        gp_d = nc.dram_tensor("gp_scr", [DEPTH, 128, 1024], F32, kind="Internal").ap()
        kgp = _K()

        with ExitStack() as st:
            cT = T(st, "cT", [128, 8])
            sc = T(st, "sc", [128, 8])
            dma(cT[:], cT_d[:, :], w=[cT])
            op("act", lambda e: e.activation(out=sc[:], in_=cT[:], func=AF.Silu), r=[cT], w=[sc])
            wa = [T(st, "wa%d" % i, [128, 8, 512]) for i in range(2)]
            row = T(st, "row", [1, 3072])
            bada = T(st, "bada", [1, 3072])
            gpr = T(st, "gpr", [1, 1024])
            gpo = T(st, "gpo", [1, 1024])
            arow = T(st, "arow", [1, 1024])
            gprow = T(st, "gprow", [1, 1024])
            gptmp = T(st, "gptmp", [128, 1024])
            nwa = 0
            for l in range(DEPTH):
                dma(bada[:], bada_d[l], w=[bada])
                dma(gpr[:], gpre_d[l], w=[gpr])
                dma(gpo[:], gpost_d[l], w=[gpo])
                for cg in range(6):
                    wt = wa[nwa % 2]
                    nwa += 1
                    dma(wt[:], wada_d[l, :, :, cg * 512:(cg + 1) * 512], w=[wt])
                    pb = PS[cg % 2]
                    for c in range(8):
                        mm(pb[0:1, :], sc[:, c:c + 1], wt[:, c, :], start=(c == 0), stop=(c == 7),
                           r=[sc, wt], w=[pb], inc=(c == 7))
                    op("dve", lambda e: e.tensor_tensor(out=row[0:1, cg * 512:(cg + 1) * 512], in0=pb[0:1, :],
                                                        in1=bada[0:1, cg * 512:(cg + 1) * 512], op=ALU.add),
                       r=[pb, bada], w=[(row, cg)])
                op("dve", lambda e: e.scalar_tensor_tensor(out=arow[:], in0=row[0:1, 1024:2048], scalar=1.0,
                                                           in1=gpr[:], op0=ALU.add, op1=ALU.mult),
                   r=[row, gpr], w=[arow])
                op("dve", lambda e: e.tensor_tensor(out=gprow[:], in0=row[0:1, 2048:3072], in1=gpo[:], op=ALU.mult),
                   r=[row, gpo], w=[gprow])
                pc = PS[2]
                for c in range(8):
                    mm(pc[:, c:c + 1], arow[0:1, c * 128:(c + 1) * 128], ones[0:1, 0:1], r=[arow, C], w=[pc], inc=False)
                for c in range(8):
                    mm(pc[:, 8 + c:9 + c], row[0:1, c * 128:(c + 1) * 128], ones[0:1, 0:1], r=[row, C], w=[pc],
                       inc=(c == 7))
                op("dve", lambda e: e.tensor_copy(out=AB[l][:], in_=pc[:, 0:16]), r=[pc], w=[AB[l]])
                for hf in range(2):
                    pg = PS[3 + hf]
                    mm(pg[:, :], ones[0:1, 0:128], gprow[0:1, hf * 512:(hf + 1) * 512], r=[gprow, C], w=[pg])
                    op("act", lambda e: e.activation(out=gptmp[:, hf * 512:(hf + 1) * 512], in_=pg[:, :], func=AF.Copy),
                       r=[pg], w=[(gptmp, hf)])
                dma(gp_d[l], gptmp[:], r=[gptmp], w=[(kgp, l)])
            kb.barrier()

        hT = T(gst, "hT", [128, 8, S], BF16)

        for l in range(DEPTH):
          try:
            xin_d = x_d if l == 0 else xmid_d
            xout_d = y_d if l == DEPTH - 1 else xmid_d
            xin_k = kx if l == 0 else kmid
            xout_k = ky if l == DEPTH - 1 else kmid

            for st in phase("p1"):
                xt = [T(st, "xt%d" % i, [128, 1024]) for i in range(4)]
                xn = [T(st, "xn%d" % i, [128, 1024]) for i in range(3)]
                junk = T(st, "junk", [128, 1024], BF16)
                ss = T(st, "ss", [128, NT])
                rstd = T(st, "rstd", [128, NT])
                def p1_stats(t):
                    xtt, xnt = xt[t % 4], xn[t % 3]
                    dma(xtt[:], xin_d[t * 128:(t + 1) * 128, :], r=[(xin_k, t)], w=[xtt])
                    op("act", lambda e: e.activation(out=junk[:], in_=xtt[:], func=AF.Square,
                                                     accum_out=ss[:, t:t + 1]), r=[xtt], w=[junk, (ss, t)])
                    op("act", lambda e: e.activation(out=rstd[:, t:t + 1], in_=ss[:, t:t + 1], func=AF.Sqrt,
                                                     scale=1.0 / D_MODEL, bias=epsT[:]), r=[(ss, t), epsT],
                       w=[(rstd, t)])
                    op("dve", lambda e: e.reciprocal(out=rstd[:, t:t + 1], in_=rstd[:, t:t + 1]), r=[(rstd, t)],
                       w=[(rstd, t)])
                    op("dve", lambda e: e.tensor_scalar(out=xnt[:], in0=xtt[:], scalar1=rstd[:, t:t + 1],
                                                        scalar2=None, op0=ALU.mult), r=[xtt, (rstd, t)], w=[xnt])

                def p1_transpose(t):
                    xnt = xn[t % 3]
                    for hf in range(2):
                        pb = PS[(t % 2) * 2 + hf]
                        for cc in range(4):
                            c = hf * 4 + cc
                            tr(pb[:, cc * 128:(cc + 1) * 128], xnt[:, c * 128:(c + 1) * 128], ident, r=[xnt, C],
                               w=[pb])
                        for cc in range(4):
                            c = hf * 4 + cc
                            eng = "act" if hf == 0 else "dve"
                            if eng == "act":
                                op("act", lambda e: e.activation(out=hT[:, c, t * 128:(t + 1) * 128],
                                                                 in_=pb[:, cc * 128:(cc + 1) * 128], func=AF.Identity,
                                                                 scale=AB[l][:, c:c + 1], bias=AB[l][:, 8 + c:9 + c]),
                                   r=[pb, AB[l]], w=[(hT, t // 4)])
                            else:
                                op("dve", lambda e: e.tensor_scalar(out=hT[:, c, t * 128:(t + 1) * 128],
                                                                    in0=pb[:, cc * 128:(cc + 1) * 128],
                                                                    scalar1=AB[l][:, c:c + 1],
                                                                    scalar2=AB[l][:, 8 + c:9 + c],
                                                                    op0=ALU.mult, op1=ALU.add),
                                   r=[pb, AB[l]], w=[(hT, t // 4)])

                p1_stats(0)
                for t in range(NT):
                    if t + 1 < NT:
                        p1_stats(t + 1)
                    p1_transpose(t)
                kb.barrier()
            if "hT" in dbg and l == 0:
                with ExitStack() as st:
                    d = dbg_out("hT", [128, 8, S])
                    tmp = T(st, "dbgtmp", [128, 8, S])
                    op("dve", lambda e: e.tensor_copy(out=tmp[:], in_=hT[:]), r=[hT], w=[tmp])
                    dma(d[:, :, :], tmp[:], r=[tmp])
                    kb.barrier()

            for st in phase("dn"):
                Wdn = T(st, "Wdn", [128, 8, 2048], BF16)
                Wba = T(st, "Wba", [128, 8, 8], BF16)
                for c in range(8):
                    dma(Wdn[:, c, :], wdn_d[l, :, c, :], w=[(Wdn, c)], q="pool")
                dma(Wba[:], wba_d[l], w=[Wba], q="pool")
                convw = T(st, "convw", [128, 48])
                alog = T(st, "alog", [128, 4])
                dtb = T(st, "dtb", [128, 4])
                dng = T(st, "dng", [128, 1])
                dma(convw[:], convw_d[l], w=[convw])
                dma(alog[:], alog_d[l], w=[alog])
                dma(dtb[:], dtb_d[l], w=[dtb])
                dma(dng[:], dng_d[l], w=[dng])
                st2 = ExitStack()
                BETA = T(st, "BETA", [128, NT, 4])
                NBETA = T(st, "NBETA", [128, NT, 4])
                GRAW = T(st, "GRAW", [128, NT, 4])
                GC = T(st, "GC", [128, NT, 4])
                NEXPG = T(st, "NEXPG", [128, NT, 4])
                negA = T(st, "negA", [128, 4])
                BG = T(st2, "BG", [128, NT, 8])
                AA = T(st2, "AA", [128, NT, 4])
                AX_ = T(st2, "AXs", [128, NT, 4])
                pbg = PS[0]
                for t in range(NT):
                    for c in range(8):
                        mm(pbg[:, t * 8:(t + 1) * 8], hT[:, c, t * 128:(t + 1) * 128], Wba[:, c, :],
                           start=(c == 0), stop=(c == 7), r=[(hT, t // 4), Wba], w=[pbg], inc=(c == 7))
                op("dve", lambda e: e.tensor_copy(out=BG[:].rearrange("p t e -> p (t e)"), in_=pbg[:, 0:NT * 8]),
                   r=[pbg], w=[BG])
                op("act", lambda e: e.activation(out=BETA[:], in_=BG[:, :, 0:4], func=AF.Sigmoid), r=[BG], w=[BETA])
                op("dve", lambda e: e.tensor_scalar(out=NBETA[:], in0=BETA[:], scalar1=-1.0, scalar2=None,
                                                    op0=ALU.mult), r=[BETA], w=[NBETA])
                for h in range(4):
                    op("dve", lambda e: e.tensor_scalar(out=AA[:, :, h], in0=BG[:, :, 4 + h], scalar1=dtb[:, h:h + 1],
                                                        scalar2=None, op0=ALU.add), r=[BG, dtb], w=[AA])
                op("act", lambda e: e.activation(out=AX_[:], in_=AA[:], func=AF.Abs), r=[AA], w=[AX_])
                op("act", lambda e: e.activation(out=AX_[:], in_=AX_[:], func=AF.Exp, scale=-1.0), r=[AX_], w=[AX_])
                op("dve", lambda e: e.tensor_scalar(out=AX_[:], in0=AX_[:], scalar1=1.0, scalar2=None, op0=ALU.add),
                   r=[AX_], w=[AX_])
                op("act", lambda e: e.activation(out=AX_[:], in_=AX_[:], func=AF.Ln), r=[AX_], w=[AX_])
                op("dve", lambda e: e.scalar_tensor_tensor(out=AA[:], in0=AA[:], scalar=0.0, in1=AX_[:],
                                                           op0=ALU.max, op1=ALU.add), r=[AA, AX_], w=[AA])
                op("act", lambda e: e.activation(out=negA[:], in_=alog[:], func=AF.Exp), r=[alog], w=[negA])
                op("dve", lambda e: e.tensor_scalar(out=negA[:], in0=negA[:], scalar1=-1.0, scalar2=None,
                                                    op0=ALU.mult), r=[negA], w=[negA])
                for h in range(4):
                    op("dve", lambda e: e.tensor_scalar(out=GRAW[:, :, h], in0=AA[:, :, h], scalar1=negA[:, h:h + 1],
                                                        scalar2=None, op0=ALU.mult), r=[AA, negA], w=[GRAW])
                pgc = PS[1]
                mm(pgc[:, 0:NT * 4], U, GRAW[:].rearrange("p t e -> p (t e)"), r=[C, GRAW], w=[pgc])
                op("dve", lambda e: e.tensor_copy(out=GC[:].rearrange("p t e -> p (t e)"), in_=pgc[:, 0:NT * 4]),
                   r=[pgc], w=[GC])
                op("act", lambda e: e.activation(out=NEXPG[:], in_=GC[:], func=AF.Exp), r=[GC], w=[NEXPG])
                op("dve", lambda e: e.tensor_scalar(out=NEXPG[:], in0=NEXPG[:], scalar1=-1.0, scalar2=None,
                                                    op0=ALU.mult), r=[NEXPG], w=[NEXPG])
                if "graw" in dbg and l == 0:
                    d = dbg_out("graw", [128, NT, 4])
                    dma(d[:, :, :], GRAW[:], r=[GRAW])
                    d = dbg_out("beta", [128, NT, 4])
                    dma(d[:, :, :], BETA[:], r=[BETA])

                kb.barrier()
                st2.close()
                ck(1)
                pre = [T(st, "pre%d" % i, [128, 515]) for i in range(3)]
                halo = T(st, "halo", [128, 12, 3])
                op("dve", lambda e: e.memset(halo[:], 0.0), w=[halo])
                qkv = [[T(st, "qkv%d_%d" % (i, j), [128, 512]) for j in range(3)] for i in range(2)]
                zs = [T(st, "zs%d" % i, [128, 512], BF16) for i in range(4)]
                cv = [T(st, "cv0", [128, 512])] * 2
                sq = [T(st, "sq0", [128, 512])] * 2
                oTg = [T(st, "oTg%d" % i, [128, 512]) for i in range(2)]
                ogb = [T(st, "ogb0", [128, 512], BF16)] * 2
                Sst = [[T(st, "S%d_%d" % (h, i), [128, 128]) for i in range(2)] for h in range(4)]
                for h in range(4):
                    op("dve", lambda e: e.tensor_scalar(out=R(Sst[h][0][:]), in0=ident, scalar1=0.0, scalar2=None,
                                                        op0=ALU.mult), r=[C], w=[Sst[h][0]])
                spar = [0, 0, 0, 0]
                NCH = 8
                ALIAS = {"Dm": 0, "tq": 0, "DecT": 1, "Xo": 1, "Pe": 2, "ktok": 3, "Xe": 3, "NoffT": 3, "Po": 4,
                         "B": 5, "BT": 6, "WbdT": 6, "Noff": 7, "Z1": 7, "ExpG": 8, "XTo": 8, "Lm": 9, "XTe": 9}
                scr = []
                for i in range(NCH):
                    wide = T(st, "s%d" % i, [128, 9 * 128])
                    plain = T(st, "s%da" % i, [128, 128])
                    d_ = {n: (SV(wide, j - 1) if j > 0 else plain) for n, j in ALIAS.items()}
                    d_["XPo"] = SV(wide, 0, 2)
                    d_["XPe"] = SV(wide, 2, 2)
                    scr.append(d_)
                OUTN = ["W", "QKdT", "kdec", "qg", "vtok", "kTc"]
                outs = [{n: T(st, "o%d_%s" % (i, n), [128, 128]) for n in OUTN} for i in range(NCH)]
                EGL = T(st, "EGL", [128, NCH])
                Yr = [T(st, "Yr%d" % i, [128, 128]) for i in range(2)]
                Vn = [T(st, "Vn%d" % i, [128, 128]) for i in range(2)]

                def prep8(chains, pump=lambda: None):
                    pump_on = [False]

                    def each(fn):
                        for ci, ch in enumerate(chains):
                            fn(ch, ch["s"], ch["o"], PS[ch["i"]])
                            if ci % 2 == 1 and pump_on[0]:
                                pump(1)

                    def f(ch, s, o, pb):
                        h, n, sl = ch["h"], ch["n"], ch["sl"]
                        kT, vT = ch["kT"], ch["vT"]
                        tr(pb[:, 0:128], kT[:, sl], ident, r=[kT, C], w=[pb])
                        tr(pb[:, 128:256], vT[:, sl], ident, r=[vT, C], w=[pb])
                        mm(pb[:, 256:384], GRAW[:, n, h:h + 1].to_broadcast([128, 128]), U, r=[GRAW, C], w=[pb])
                        op("dve", lambda e: e.tensor_scalar(out=s["Dm"][:], in0=pb[:, 256:384],
                                                            scalar1=GC[:, n, h:h + 1], scalar2=0.0,
                                                            op0=ALU.subtract, op1=ALU.min),
                           r=[pb, GC], w=[s["Dm"]])
                        op("dve", lambda e: e.tensor_copy(out=o["vtok"][:], in_=pb[:, 128:256]), r=[pb],
                           w=[o["vtok"]])
                        op("act", lambda e: e.activation(out=R(s["ktok"][:]), in_=pb[:, 0:128], func=AF.Copy),
                           r=[pb], w=[s["ktok"]])
                        op("act", lambda e: e.activation(out=R(s["ExpG"][:]), in_=pb[:, 256:384], func=AF.Exp),
                           r=[pb], w=[s["ExpG"]])
                    each(f)

                    def f(ch, s, o, pb):
                        op("act", lambda e: e.activation(out=R(s["DecT"][:]), in_=s["Dm"][:], func=AF.Exp),
                           r=[s["Dm"]], w=[s["DecT"]])
                    each(f)

                    def f(ch, s, o, pb):
                        h, n, sl = ch["h"], ch["n"], ch["sl"]
                        kT, qT = ch["kT"], ch["qT"]
                        mm(pb[:, 0:128], R(kT[:, sl]), R(kT[:, sl]), r=[kT], w=[pb], inc=False)
                        mm(pb[:, 128:256], R(kT[:, sl]), R(qT[:, sl]), r=[kT, qT], w=[pb])
                        op("dve", lambda e: e.tensor_tensor(out=R(s["Lm"][:]), in0=pb[:, 0:128], in1=s["DecT"][:],
                                                            op=ALU.mult), r=[pb, s["DecT"]], w=[s["Lm"]])
                        op("dve", lambda e: e.tensor_tensor(out=s["tq"][:], in0=pb[:, 128:256], in1=s["DecT"][:],
                                                            op=ALU.mult), r=[pb, s["DecT"]], w=[s["tq"]])
                        op("dve", lambda e: e.scalar_tensor_tensor(out=R(s["B"][:]), in0=s["Lm"][:],
                                                                   scalar=NBETA[:, n, h:h + 1], in1=Mbd,
                                                                   op0=ALU.mult, op1=ALU.mult),
                           r=[s["Lm"], NBETA, C], w=[s["B"]])
                        op("dve", lambda e: e.scalar_tensor_tensor(out=R(s["Noff"][:]), in0=s["Lm"][:],
                                                                   scalar=BETA[:, n, h:h + 1], in1=Moff,
                                                                   op0=ALU.mult, op1=ALU.mult),
                           r=[s["Lm"], BETA, C], w=[s["Noff"]])
                        op("pool", lambda e: e.tensor_tensor(out=R(o["QKdT"][:]), in0=s["tq"][:], in1=U,
                                                             op=ALU.mult), r=[s["tq"], C], w=[o["QKdT"]])
                        op("act", lambda e: e.activation(out=R(o["kdec"][:]), in_=s["ktok"][:], func=AF.Identity,
                                                         scale=s["DecT"][:, 127:128]), r=[s["ktok"], s["DecT"]],
                           w=[o["kdec"]])
                        op("dve", lambda e: e.tensor_tensor(out=R(o["qg"][:]), in0=qT[:, sl], in1=s["ExpG"][:],
                                                            op=ALU.mult), r=[qT, s["ExpG"]], w=[o["qg"]])
                        op("dve", lambda e: e.tensor_copy(out=R(o["kTc"][:]), in_=kT[:, sl]), r=[kT], w=[o["kTc"]])
                        op("dve", lambda e: e.tensor_copy(out=EGL[:, ch["i"]:ch["i"] + 1],
                                                          in_=s["ExpG"][:, 127:128]),
                           r=[s["ExpG"]], w=[(EGL, ch["i"])])
                    each(f)

                    pump_on[0] = True
                    def f(ch, s, o, pb):
                        tr(pb[:, 256:384], s["B"][:], ident, r=[s["B"], C], w=[pb])
                        op("act", lambda e: e.activation(out=R(s["BT"][:]), in_=pb[:, 256:384], func=AF.Copy),
                           r=[pb], w=[s["BT"]])
                        op("dve", lambda e: e.tensor_tensor(out=R(s["Pe"][:]), in0=s["B"][:], in1=ident,
                                                            op=ALU.add), r=[s["B"], C], w=[s["Pe"]])
                    each(f)

                    def f(ch, s, o, pb):
                        mm(pb[:, 0:128], R(s["BT"][:]), R(s["B"][:]), r=[s["BT"], s["B"]], w=[pb])
                        op("act", lambda e: e.activation(out=R(s["Xo"][:]), in_=pb[:, 0:128], func=AF.Copy),
                           r=[pb], w=[s["Xo"]])
                    each(f)
                    pump()
                    for j in range(1, 6):
                        odd = (j % 2 == 1)
                        Xc, Pc, XTc, XPc = ("Xo", "Pe", "XTo", "XPo") if odd else ("Xe", "Po", "XTe", "XPe")
                        Xn, Pn = ("Xe", "Po") if odd else ("Xo", "Pe")

                        def f(ch, s, o, pb):
                            tr(pb[:, 256:384], s[Xc][:], ident, r=[s[Xc], C], w=[pb])
                            op("act", lambda e: e.activation(out=R(s[XTc][:]), in_=pb[:, 256:384], func=AF.Copy),
                               r=[pb], w=[s[XTc]])
                        each(f)
                        pump()

                        def f(ch, s, o, pb):
                            if j < 5:
                                mm(pb[:, 0:256], R(s[XTc][:]), R(s[XPc][:]), r=[s[XTc], s[Xc], s[Pc]], w=[pb])
                                op("act", lambda e: e.activation(out=R(s[Xn][:]), in_=pb[:, 0:128], func=AF.Copy),
                                   r=[pb], w=[s[Xn]])
                                op("dve", lambda e: e.tensor_tensor(out=R(s[Pn][:]), in0=pb[:, 128:256],
                                                                    in1=s[Pc][:], op=ALU.add),
                                   r=[pb, s[Pc]], w=[s[Pn]])
                            else:
                                mm(pb[:, 128:256], R(s[XTc][:]), R(s[Pc][:]), r=[s[XTc], s[Pc]], w=[pb])
                                op("dve", lambda e: e.tensor_tensor(out=R(s[Pn][:]), in0=pb[:, 128:256],
                                                                    in1=s[Pc][:], op=ALU.add),
                                   r=[pb, s[Pc]], w=[s[Pn]])
                        each(f)
                        pump()
                    Pf = "Po"

                    def f(ch, s, o, pb):
                        tr(pb[:, 0:128], s[Pf][:], ident, r=[s[Pf], C], w=[pb])
                        tr(pb[:, 128:256], s["Noff"][:], ident, r=[s["Noff"], C], w=[pb])
                        op("act", lambda e: e.activation(out=R(s["WbdT"][:]), in_=pb[:, 0:128], func=AF.Copy),
                           r=[pb], w=[s["WbdT"]])
                        op("dve", lambda e: e.tensor_copy(out=R(s["NoffT"][:]), in_=pb[:, 128:256]), r=[pb],
                           w=[s["NoffT"]])
                    each(f)
                    pump()

                    def f(ch, s, o, pb):
                        mm(pb[:, 0:128], R(s["NoffT"][:]), R(s[Pf][:]), r=[s["NoffT"], s[Pf]], w=[pb])
                        op("act", lambda e: e.activation(out=R(s["Z1"][:]), in_=pb[:, 0:128], func=AF.Copy),
                           r=[pb], w=[s["Z1"]])
                    each(f)
                    pump()

                    def f(ch, s, o, pb):
                        mm(pb[:, 128:256], R(s["WbdT"][:]), R(s["Z1"][:]), r=[s["WbdT"], s["Z1"]], w=[pb])
                        op("dve", lambda e: e.tensor_tensor(out=R(o["W"][:]), in0=s[Pf][:], in1=pb[:, 128:256],
                                                            op=ALU.subtract), r=[s[Pf], pb], w=[o["W"]])
                    each(f)
                    pump()

                nrec = [0]

                def recur_pair(chs, g):
                    st_ = []
                    for ch in chs:
                        h = ch["h"]
                        So = Sst[h][spar[h]]
                        Sn = Sst[h][1 - spar[h]]
                        spar[h] = 1 - spar[h]
                        bb = 4 * ch["hi"]
                        st_.append((ch, So, Sn, PS[bb], PS[bb + 1], PS[bb + 2], PS[bb + 3],
                                    Yr[nrec[0] % 2], Vn[nrec[0] % 2]))
                        nrec[0] += 1
                    for ch, So, Sn, pa, pb2, pc, pd, Y, vnew in st_:
                        h, n, o = ch["h"], ch["n"], ch["o"]
                        mm(pa[:, 0:128], R(o["kTc"][:]), R(So[:]), r=[o["kTc"], So], w=[pa])
                        op("dve", lambda e: e.scalar_tensor_tensor(out=R(Y[:]), in0=pa[:, 0:128],
                                                                   scalar=NEXPG[:, n, h:h + 1], in1=o["vtok"][:],
                                                                   op0=ALU.mult, op1=ALU.add),
                           r=[pa, NEXPG, o["vtok"]], w=[Y])
                    for ch, So, Sn, pa, pb2, pc, pd, Y, vnew in st_:
                        h, n, o = ch["h"], ch["n"], ch["o"]
                        mm(pb2[:, 0:128], R(o["W"][:]), R(Y[:]), r=[o["W"], Y], w=[pb2])
                        op("act", lambda e: e.activation(out=R(vnew[:]), in_=pb2[:, 0:128], func=AF.Identity,
                                                         scale=BETA[:, n, h:h + 1]), r=[pb2, BETA], w=[vnew])
                    for ch, So, Sn, pa, pb2, pc, pd, Y, vnew in st_:
                        o = ch["o"]
                        mm(pd[:, 0:128], R(o["kdec"][:]), R(vnew[:]), r=[o["kdec"], vnew], w=[pd])
                        op("dve", lambda e: e.scalar_tensor_tensor(out=R(Sn[:]), in0=So[:],
                                                                   scalar=EGL[:, ch["i"]:ch["i"] + 1],
                                                                   in1=pd[:, 0:128], op0=ALU.mult, op1=ALU.add),
                           r=[So, (EGL, ch["i"]), pd], w=[Sn])
                    for ch, So, Sn, pa, pb2, pc, pd, Y, vnew in st_:
                        o, sl = ch["o"], ch["sl"]
                        mm(pc[:, 0:128], R(So[:]), R(o["qg"][:]), start=True, stop=False, r=[So, o["qg"]], w=[pc],
                           inc=False)
                        mm(pc[:, 0:128], R(vnew[:]), R(o["QKdT"][:]), start=False, stop=True, r=[vnew, o["QKdT"]],
                           w=[pc])
                        ot = oTg[ch["hi"]]
                        op("act", lambda e: e.activation(out=ot[:, sl], in_=pc[:, 0:128], func=AF.Copy), r=[pc],
                           w=[(ot, ch["cc"])])

                def stageA(g, hp, par, res):
                    gs = slice(g * 512, (g + 1) * 512)
                    chains = []
                    for hi in range(2):
                        h = 2 * hp + hi
                        qk = qkv[hi]
                        zt = zs[2 * par + hi]
                        for ty in range(4):
                            pb = PS[4 * hi + ty]
                            col = ty * 512 + h * 128
                            for c in range(8):
                                mm(pb[:, :], Wdn[:, c, col:col + 128], hT[:, c, gs], start=(c == 0),
                                   stop=(c == 7), r=[(Wdn, c), (hT, g)], w=[pb], inc=(c == 7))
                            if ty < 3:
                                ch_ = ty * 4 + h
                                op("dve", lambda e: e.tensor_copy(out=pre[ty][:, 0:3], in_=halo[:, ch_, :]),
                                   r=[(halo, ch_)], w=[(pre[ty], 0)])
                                op("act", lambda e: e.activation(out=pre[ty][:, 3:515], in_=pb[:, :],
                                                                 func=AF.Copy), r=[pb], w=[(pre[ty], 1)])
                                op("dve", lambda e: e.tensor_copy(out=halo[:, ch_, :], in_=pre[ty][:, 512:515]),
                                   r=[(pre[ty], 1)], w=[(halo, ch_)])
                                yield
                                cvt = cv[ty % 2]
                                wk = lambda k: convw[:, ch_ * 4 + k:ch_ * 4 + k + 1]
                                op("act", lambda e: e.activation(out=cvt[:], in_=pre[ty][:, 0:512],
                                                                 func=AF.Identity, scale=wk(0)),
                                   r=[pre[ty], convw], w=[cvt])
                                for k in range(1, 4):
                                    op("dve", lambda e: e.scalar_tensor_tensor(out=cvt[:],
                                                                               in0=pre[ty][:, k:k + 512],
                                                                               scalar=wk(k), in1=cvt[:],
                                                                               op0=ALU.mult, op1=ALU.add),
                                       r=[pre[ty], convw, cvt], w=[cvt])
                                    yield
                                op("act", lambda e: e.activation(out=(R(qk[ty][:]) if ty < 2 else qk[ty][:]),
                                                                 in_=cvt[:], func=AF.Silu),
                                   r=[cvt], w=[qk[ty]])
                            else:
                                op("act", lambda e: e.activation(out=zt[:], in_=pb[:, :], func=AF.Silu),
                                   r=[pb], w=[zt])
                            yield
                        for ty in range(2):
                            sqt = sq[ty]
                            pb = PS[4 * hi + ty]
                            op("act", lambda e: e.activation(out=R(sqt[:]), in_=qk[ty][:], func=AF.Square),
                               r=[qk[ty]], w=[sqt])
                            mm(pb[:, :], R(onesr[:]), R(sqt[:]), r=[onesr, sqt], w=[pb])
                            rt_ = cv[0]
                            op("act", lambda e: e.activation(out=rt_[:], in_=pb[:, :], func=AF.Ln,
                                                             bias=epsT[:]), r=[pb, epsT], w=[rt_])
                            op("act", lambda e: e.activation(out=rt_[:], in_=rt_[:], func=AF.Exp, scale=-0.5),
                               r=[rt_], w=[rt_])
                            sc_ = (128.0 ** -0.5) if ty == 0 else 1.0
                            op("dve", lambda e: e.scalar_tensor_tensor(out=R(qk[ty][:]), in0=qk[ty][:],
                                                                       scalar=sc_, in1=rt_[:], op0=ALU.mult,
                                                                       op1=ALU.mult),
                               r=[qk[ty], rt_], w=[qk[ty]])
                            yield
                        if dbg and l == 0 and g == 0 and h == 0:
                            for nm, tt in (("q", qk[0]), ("k", qk[1]), ("v", qk[2])):
                                if nm in dbg:
                                    d = dbg_out(nm, [128, 512])
                                    dma(d[:, :], tt[:], r=[tt])
                        for cc in range(4):
                            i = hi * 4 + cc
                            chains.append({"h": h, "hi": hi, "cc": cc, "n": g * 4 + cc, "i": i,
                                           "sl": slice(cc * 128, (cc + 1) * 128), "qT": qk[0], "kT": qk[1],
                                           "vT": qk[2], "s": scr[i], "o": outs[i]})
                    res["chains"] = chains

                def drain(gen):
                    if gen is not None:
                        for _ in gen:
                            pass

                pairs = [(g, hp) for g in range(NG) for hp in range(2)]
                resA = [dict() for _ in pairs]
                gens = [stageA(g, hp, pi % 2, resA[pi]) for pi, (g, hp) in enumerate(pairs)]
                drain(gens[0])
                for pi, (g, hp) in enumerate(pairs):
                    gs = slice(g * 512, (g + 1) * 512)
                    chains = resA[pi]["chains"]
                    nxt = gens[pi + 1] if pi + 1 < len(pairs) else None

                    def pump(n=1):
                        if nxt is not None:
                            for _ in range(n):
                                if next(nxt, "done") == "done":
                                    break
                    prep8(chains, pump)
                    drain(nxt)
                    for cc in range(4):
                        recur_pair([ch for ch in chains if ch["cc"] == cc], g)
                    for hi in range(2):
                        h = 2 * hp + hi
                        zt = zs[2 * (pi % 2) + hi]
                        ot = oTg[hi]
                        og = ogb[hi]
                        sqt = sq[hi]
                        pb = PS[4 * hi]
                        op("act", lambda e: e.activation(out=R(sqt[:]), in_=ot[:], func=AF.Square), r=[ot],
                           w=[sqt])
                        mm(pb[:, :], R(onesr[:]), R(sqt[:]), r=[onesr, sqt], w=[pb])
                        sqt = cv[0]
                        op("act", lambda e: e.activation(out=sqt[:], in_=pb[:, :], func=AF.Ln,
                                                         scale=1.0 / 128, bias=epsT[:]), r=[pb, epsT], w=[sqt])
                        op("act", lambda e: e.activation(out=sqt[:], in_=sqt[:], func=AF.Exp, scale=-0.5),
                           r=[sqt], w=[sqt])
                        if "odn" in dbg and l == 0 and h == 0 and g == 0:
                            d = dbg_out("odn", [128, 512])
                            dma(d[:, :], ot[:], r=[ot])
                        op("dve", lambda e: e.scalar_tensor_tensor(out=sqt[:], in0=ot[:], scalar=dng[:, 0:1],
                                                                   in1=sqt[:], op0=ALU.mult, op1=ALU.mult),
                           r=[ot, dng, sqt], w=[sqt])
                        op("dve", lambda e: e.tensor_tensor(out=og[:], in0=sqt[:], in1=zt[:], op=ALU.mult),
                           r=[sqt, zt], w=[og])
                        dma(ogdn_d[h, :, gs], og[:], r=[og], w=[(kogdn, g)])
                kb.barrier()

            for st in phase("mb"):
                Wmb = T(st, "Wmb", [128, 8, 2048], BF16)
                for c in range(8):
                    dma(Wmb[:, c, :], wmb_d[l, :, c, :], w=[(Wmb, c)], q="pool")
                Vt = T(st, "Vt", [128, NT, 8, 65], BF16)
                op("dve", lambda e: e.memset(Vt[:, :, :, 64:65], 1.0), w=[Vt])
                for t in range(NT):
                    pb = PS[t % 2]
                    for c in range(8):
                        mm(pb[:, :], hT[:, c, t * 128:(t + 1) * 128], Wmb[:, c, 1024:1536], start=(c == 0),
                           stop=(c == 7), r=[(hT, t // 4), (Wmb, c)], w=[pb], inc=(c == 7))
                    op("act" if t % 2 else "dve",
                       lambda e: (e.activation(out=Vt[:, t, :, 0:64], in_=pb[:, :].rearrange("p (h d) -> p h d", h=8),
                                               func=AF.Copy) if t % 2 else
                                  e.tensor_copy(out=Vt[:, t, :, 0:64],
                                                in_=pb[:, :].rearrange("p (h d) -> p h d", h=8))),
                       r=[pb], w=[(Vt, t)])
                KaT = T(st, "KaT", [128, S], BF16)
                QaT = T(st, "QaT", [128, S], BF16)
                zsm = T(st, "zsm", [64, S], BF16)
                ogm = T(st, "ogm", [64, S], BF16)
                kf = [T(st, "kf%d" % i, [128, 512]) for i in range(2)]
                qf = [T(st, "qf%d" % i, [128, 512]) for i in range(2)]
                sqm = [T(st, "sqm%d" % i, [128, 512]) for i in range(2)]
                kmT = T(st, "kmT", [128, 16])
                shr = T(st, "shr", [64, 512])
                km2 = T(st, "km2", [64, NG + 1])
                gm4 = T(st, "gm4", [128, 4, 16])
                top84 = T(st, "top84", [128, 4, 8])
                mbt4 = [T(st, "mbt4_%d" % i, [128, 4, 16]) for i in range(2)]
                rden = T(st, "rden", [128, 512])
                t1 = [T(st, "t1_%d" % i, [64, 512]) for i in range(2)]
                PT = [T(st, "PT%d" % i, [128, 512], BF16) for i in range(4)]
                nS = [0]
                op("dve", lambda e: e.memset(KaT[0:64, :], 0.0), w=[KaT])
                op("dve", lambda e: e.memset(QaT[0:64, :], 0.0), w=[QaT])
                op("dve", lambda e: e.memset(KaT[32:33, :], 1.0), w=[KaT])
                for n in range(NB):
                    op("dve", lambda e: e.tensor_copy(out=KaT[0:16, n * 256:(n + 1) * 256],
                                                      in_=ident[0:16, n:n + 1].to_broadcast([16, 256])),
                       r=[C], w=[KaT])
                npt = 0
                for h in range(8):
                    op("dve", lambda e: e.tensor_scalar(out=R(kmT[:]), in0=ident[:, 0:16], scalar1=0.0, scalar2=None,
                                                        op0=ALU.mult), r=[C], w=[kmT])
                    def projK(g):
                        pb = PS[g % 2]
                        col = 512 + h * 64
                        for c in range(8):
                            mm(pb[64:128, :], Wmb[:, c, col:col + 64], hT[:, c, g * 512:(g + 1) * 512],
                               start=(c == 0), stop=(c == 7), r=[(Wmb, c), (hT, g)], w=[pb], inc=(c == 7))

                    def postK(g):
                        gs = slice(g * 512, (g + 1) * 512)
                        pb = PS[g % 2]
                        kft = kf[g % 2]
                        op("act", lambda e: e.activation(out=kft[64:128, :], in_=pb[64:128, :], func=AF.Copy),
                           r=[pb], w=[kft])
                        op("act", lambda e: e.activation(out=KaT[64:128, gs], in_=pb[64:128, :], func=AF.Copy),
                           r=[pb], w=[(KaT, g)])
                        op("dve", lambda e: e.tensor_reduce(out=R(kmT[64:128, 2 * g:2 * g + 2]),
                                                            in_=kft[64:128, :].rearrange("p (b t) -> p b t", b=2),
                                                            axis=AX.X, op=ALU.add), r=[kft], w=[kmT])
                        sqt = sqm[g % 2]
                        op("dve", lambda e: e.tensor_tensor(out=sqt[64:128, :], in0=kft[64:128, :],
                                                            in1=kft[64:128, :], op=ALU.mult), r=[kft], w=[sqt])
                        pr = PS[2]
                        mm(pr[32:33, :], ones[64:128, 0:1], sqt[64:128, :], r=[C, sqt], w=[pr])
                        op("dve", lambda e: e.tensor_reduce(out=km2[32:33, g:g + 1], in_=pr[32:33, :], axis=AX.X,
                                                            op=ALU.max), r=[pr], w=[km2])
                    projK(0)
                    for g in range(NG):
                        if g + 1 < NG:
                            projK(g + 1)
                        postK(g)
                    op("dve", lambda e: e.tensor_scalar(out=R(kmT[64:128, :]), in0=kmT[64:128, :], scalar1=1.0 / 256,
                                                        scalar2=None, op0=ALU.mult), r=[kmT], w=[kmT])
                    op("dve", lambda e: e.tensor_reduce(out=km2[32:33, NG:NG + 1], in_=km2[32:33, 0:NG], axis=AX.X,
                                                        op=ALU.max), r=[km2], w=[km2])
                    for g in range(NG):
                        gs = slice(g * 512, (g + 1) * 512)
                        pb = PS[g % 2]
                        col = 1536 + h * 64
                        for c in range(8):
                            mm(pb[0:64, :], Wmb[:, c, col:col + 64], hT[:, c, gs], start=(c == 0), stop=(c == 7),
                               r=[(Wmb, c), (hT, g)], w=[pb], inc=(c == 7))
                        op("act", lambda e: e.activation(out=zsm[:, gs], in_=pb[0:64, :], func=AF.Silu), r=[pb],
                           w=[(zsm, g)])

                    def projQ(g):
                        pb = PS[g % 3]
                        col = h * 64
                        for c in range(8):
                            mm(pb[64:128, :], Wmb[:, c, col:col + 64], hT[:, c, g * 512:(g + 1) * 512],
                               start=(c == 0), stop=(c == 7), r=[(Wmb, c), (hT, g)], w=[pb], inc=(c == 7))

                    def postQ1(g):
                        gs = slice(g * 512, (g + 1) * 512)
                        pb = PS[g % 3]
                        qft = qf[g % 2]
                        op("act", lambda e: e.activation(out=R(qft[64:128, :]), in_=pb[64:128, :], func=AF.Identity,
                                                         scale=0.125), r=[pb], w=[qft])
                        op("act", lambda e: e.activation(out=QaT[64:128, gs], in_=pb[64:128, :], func=AF.Identity,
                                                         scale=0.125), r=[pb], w=[(QaT, g)])
                        sqt = sqm[g % 2]
                        op("dve", lambda e: e.tensor_tensor(out=sqt[64:128, :], in0=qft[64:128, :],
                                                            in1=qft[64:128, :], op=ALU.mult), r=[qft], w=[sqt])
                        pgt = PS[4]
                        for tt in range(4):
                            mm(pgt[:, tt * 16:(tt + 1) * 16], R(qft[64:128, tt * 128:(tt + 1) * 128]), R(kmT[64:128, :]),
                               r=[qft, kmT], w=[pgt], inc=(tt == 3))
                        pr = PS[5]
                        mm(pr[32:33, :], ones[64:128, 0:1], sqt[64:128, :], r=[C, sqt], w=[pr])
                        op("dve", lambda e: e.tensor_tensor(out=gm4[:].rearrange("p a b -> p (a b)"), in0=pgt[:, 0:64],
                                                            in1=C[:, C_PB4 + g * 64:C_PB4 + (g + 1) * 64], op=ALU.add),
                           r=[pgt, C], w=[gm4])
                        op("act", lambda e: e.activation(out=shr[32:33, :], in_=pr[32:33, :], func=AF.Sqrt,
                                                         scale=km2[32:33, NG:NG + 1]), r=[pr, km2], w=[shr])
                        op("act", lambda e: e.activation(out=QaT[32:33, gs], in_=shr[32:33, :], func=AF.Identity,
                                                         scale=-1.0), r=[shr], w=[(QaT, g)])
                        for tt in range(4):
                            op("dve", lambda e: e.max(out=top84[:, tt, :], in_=gm4[:, tt, :]), r=[gm4],
                               w=[(top84, tt)])
                        mb_ = mbt4[g % 2]
                        for tt in range(4):
                            op("dve", lambda e: e.tensor_scalar(out=mb_[:, tt, :], in0=gm4[:, tt, :],
                                                                scalar1=top84[:, tt, 2:3], scalar2=NEG,
                                                                op0=ALU.is_lt, op1=ALU.mult),
                               r=[gm4, (top84, tt)], w=[(mb_, tt)])
                        for t2 in range(2):
                            own = 2 * g + t2
                            op("dve", lambda e: e.memset(mb_[:, 2 * t2:2 * t2 + 2, own:own + 1], 0.0),
                               r=[(mb_, 2 * t2), (mb_, 2 * t2 + 1)], w=[(mb_, 2 * t2), (mb_, 2 * t2 + 1)])

                    def postQ2(g):
                        gs = slice(g * 512, (g + 1) * 512)
                        mb_ = mbt4[g % 2]
                        pt_ = PS[3]
                        for tt in range(4):
                            tr(pt_[0:16, tt * 128:(tt + 1) * 128], mb_[:, tt, :], ident, r=[(mb_, tt), C], w=[pt_])
                        op("act", lambda e: e.activation(out=QaT[0:16, gs], in_=pt_[0:16, 0:512], func=AF.Copy),
                           r=[pt_], w=[(QaT, g)])
                    projQ(0)
                    if NG > 1:
                        projQ(1)
                    for g in range(NG):
                        postQ1(g)
                        if g + 2 < NG:
                            projQ(g + 2)
                        if g > 0:
                            postQ2(g - 1)
                    postQ2(NG - 1)
                    LOOK = 3
                    for jq in range(NB // 2):
                        q0 = jq * 512
                        pO = PS[6 + jq % 2]
                        ntile = 4 * jq + 4
                        qk_ = (QaT, jq)

                        def emitS(kt):
                            pS = PS[nS[0] % 4]
                            nS[0] += 1
                            d = kt - 4 * jq
                            c0 = 0 if d < 0 else 128 * d
                            mm(pS[:, c0:512], KaT[:, kt * 128:(kt + 1) * 128], QaT[:, q0 + c0:q0 + 512], start=True,
                               stop=(d < 0), r=[(KaT, kt // 4), qk_], w=[pS], inc=(d < 0))
                            if d >= 0:
                                mm(pS[:, c0:c0 + 128], identb[:], trib[:], start=False, stop=True, r=[identb, trib],
                                   w=[pS])
                            return pS, c0

                        pendq = [emitS(k_) for k_ in range(min(LOOK, ntile))]
                        for kt in range(ntile):
                            pS, c0 = pendq.pop(0)
                            if kt + LOOK < ntile:
                                pendq.append(emitS(kt + LOOK))
                            ptile = PT[npt % 4]
                            npt += 1
                            op("act", lambda e: e.activation(out=ptile[:, c0:512], in_=pS[:, c0:512], func=AF.Exp),
                               r=[pS], w=[ptile])
                            mm(pO[0:65, c0:512], Vt[:, kt, h, :], ptile[:, c0:512], start=(kt == 0),
                               stop=(kt == ntile - 1), r=[(Vt, kt), ptile], w=[pO], inc=(kt == ntile - 1))
                        op("dve", lambda e: e.reciprocal(out=R(rden[64:65, :]), in_=pO[64:65, 0:512]), r=[pO],
                           w=[rden])
                        pB = PS[5]
                        mm(pB[0:64, 0:512], R(onesr[64:65, 0:64]), R(rden[64:65, :]), r=[onesr, rden], w=[pB])
                        tt1 = t1[jq % 2]
                        op("dve", lambda e: e.tensor_tensor(out=tt1[:], in0=pO[0:64, 0:512],
                                                            in1=zsm[:, q0:q0 + 512], op=ALU.mult),
                           r=[pO, (zsm, jq)], w=[tt1])
                        op("dve", lambda e: e.tensor_tensor(out=ogm[:, q0:q0 + 512], in0=tt1[:], in1=pB[0:64, 0:512],
                                                            op=ALU.mult), r=[tt1, pB], w=[(ogm, jq)])
                    if "omb" in dbg and l == 0 and h == 0:
                        d = dbg_out("omb", [64, S])
                        tmp = T(st, "dbgomb", [64, S])
                        op("dve", lambda e: e.tensor_copy(out=tmp[:], in_=ogm[:]), r=[ogm], w=[tmp])
                        dma(d[:, :], tmp[:], r=[tmp])
                    dma(ogmb_d[h, :, :], ogm[:], r=[ogm], w=[kogmb])
                kb.barrier()

            for st in phase("fin"):
                Wpdn = T(st, "Wpdn", [128, 4, 1024], BF16)
                Wpmb = T(st, "Wpmb", [64, 8, 1024], BF16)
                Wout = T(st, "Wout", [128, 8, 1024], BF16)
                Wmg = T(st, "Wmg", [128, 8, 2048], BF16)
                dma(Wpdn[:], wpdn_d[l], w=[Wpdn], q="pool")
                dma(Wpmb[:], wpmb_d[l], w=[Wpmb], q="pool")
                for c in range(8):
                    dma(Wmg[:, c, :], wmg_d[l, :, c, :], w=[(Wmg, c)], q="pool")
                for c in range(8):
                    dma(Wout[:, c, :], wout_d[l, :, c, :], w=[(Wout, c)], q="pool")
                OGD = [T(st, "OGD%d" % i, [128, 4, 512], BF16) for i in range(2)]
                OGM = [T(st, "OGM0", [64, 8, 512], BF16)] * 2
                mixT = T(st, "mixT", [128, 8, 512], BF16)
                gd = [T(st, "gd%d" % i, [128, 512]) for i in range(2)]
                gmm = [T(st, "gmm%d" % i, [128, 512]) for i in range(2)]
                u1 = [T(st, "u1_%d" % i, [128, 512]) for i in range(2)]
                u2 = [T(st, "u2_%d" % i, [128, 512]) for i in range(2)]
                xr = [T(st, "xr%d" % i, [128, 1024]) for i in range(2)]
                res = [T(st, "res%d" % i, [128, 512]) for i in range(2)]
                junk2 = T(st, "junk2", [128, 512], BF16)
                GPl = T(st, "GPl", [128, 1024])
                dma(GPl[:], gp_d[l], r=[(kgp, l)], w=[GPl])
                ss2 = T(st, "ss2", [128, NT, 2])
                rs2 = T(st, "rs2", [128, NT])
                for g in range(NG):
                    gs = slice(g * 512, (g + 1) * 512)
                    ogd, ogmm = OGD[g % 2], OGM[g % 2]
                    dma(ogd[:], ogdn_d[:, :, gs].rearrange("h p s -> p h s"), r=[(kogdn, g)], w=[ogd])
                    dma(ogmm[:], ogmb_d[:, :, gs].rearrange("h p s -> p h s"), r=[kogmb], w=[ogmm])
                    for d_ in range(8):
                        ds_ = slice(d_ * 128, (d_ + 1) * 128)
                        pa, pb, pc, pd = PS[0 + 4 * (d_ % 2)], PS[1 + 4 * (d_ % 2)], PS[2 + 4 * (d_ % 2)], PS[3 + 4 * (d_ % 2)]
                        for h in range(4):
                            mm(pa[:, :], Wpdn[:, h, ds_], ogd[:, h, :], start=(h == 0), stop=(h == 3),
                               r=[Wpdn, ogd], w=[pa], inc=(h == 3))
                        for h in range(8):
                            mm(pb[:, :], Wpmb[:, h, ds_], ogmm[:, h, :], start=(h == 0), stop=(h == 7),
                               r=[Wpmb, ogmm], w=[pb], inc=(h == 7))
                        for c in range(8):
                            mm(pc[:, :], Wmg[:, c, ds_], hT[:, c, gs], start=(c == 0), stop=(c == 7),
                               r=[(Wmg, c), (hT, g)], w=[pc], inc=(c == 7))
                        for c in range(8):
                            mm(pd[:, :], Wmg[:, c, 1024 + d_ * 128:1024 + (d_ + 1) * 128], hT[:, c, gs],
                               start=(c == 0), stop=(c == 7), r=[(Wmg, c), (hT, g)], w=[pd], inc=(c == 7))
                        gdt, gmt, u1t, u2t = gd[d_ % 2], gmm[d_ % 2], u1[d_ % 2], u2[d_ % 2]
                        op("act", lambda e: e.activation(out=gdt[:], in_=pc[:, :], func=AF.Sigmoid), r=[pc], w=[gdt])
                        op("act", lambda e: e.activation(out=gmt[:], in_=pd[:, :], func=AF.Sigmoid), r=[pd], w=[gmt])
                        op("dve", lambda e: e.tensor_tensor(out=u1t[:], in0=pa[:, :], in1=gdt[:], op=ALU.mult),
                           r=[pa, gdt], w=[u1t])
                        op("dve", lambda e: e.tensor_tensor(out=u2t[:], in0=pb[:, :], in1=gmt[:], op=ALU.mult),
                           r=[pb, gmt], w=[u2t])
                        op("dve", lambda e: e.tensor_tensor(out=mixT[:, d_, :], in0=u1t[:], in1=u2t[:], op=ALU.add),
                           r=[u1t, u2t], w=[(mixT, d_)])
                    for tt in range(4):
                        t = g * 4 + tt
                        xrt = xr[t % 2]
                        dma(xrt[:], xin_d[t * 128:(t + 1) * 128, :], r=[(xin_k, t)], w=[xrt])
                        for hf in range(2):
                            pb = PS[(t % 2) * 2 + hf]
                            for d_ in range(8):
                                mm(pb[:, :], mixT[:, d_, tt * 128:(tt + 1) * 128], Wout[:, d_, hf * 512:(hf + 1) * 512],
                                   start=(d_ == 0), stop=(d_ == 7), r=[(mixT, d_), (Wout, d_)], w=[pb],
                                   inc=(d_ == 7))
                            op("act", lambda e: e.activation(out=junk2[:], in_=pb[:, :], func=AF.Square,
                                                             accum_out=ss2[:, t, hf:hf + 1]), r=[pb],
                               w=[junk2, (ss2, t)])
                        op("dve", lambda e: e.tensor_tensor(out=rs2[:, t:t + 1], in0=ss2[:, t, 0:1],
                                                            in1=ss2[:, t, 1:2], op=ALU.add), r=[(ss2, t)],
                           w=[(rs2, t)])
                        op("act", lambda e: e.activation(out=rs2[:, t:t + 1], in_=rs2[:, t:t + 1], func=AF.Sqrt,
                                                         scale=1.0 / D_MODEL, bias=epsT[:]), r=[(rs2, t), epsT],
                           w=[(rs2, t)])
                        op("dve", lambda e: e.reciprocal(out=rs2[:, t:t + 1], in_=rs2[:, t:t + 1]), r=[(rs2, t)],
                           w=[(rs2, t)])
                        for hf in range(2):
                            pb = PS[(t % 2) * 2 + hf]
                            hs = slice(hf * 512, (hf + 1) * 512)
                            rt = res[hf]
                            op("dve", lambda e: e.scalar_tensor_tensor(out=rt[:], in0=pb[:, :],
                                                                       scalar=rs2[:, t:t + 1], in1=GPl[:, hs],
                                                                       op0=ALU.mult, op1=ALU.mult),
                               r=[pb, (rs2, t), GPl], w=[rt])
                            op("dve", lambda e: e.tensor_tensor(out=xrt[:, hs], in0=rt[:], in1=xrt[:, hs],
                                                                op=ALU.add), r=[rt, (xrt, hf)], w=[(xrt, hf)])
                        dma(xout_d[t * 128:(t + 1) * 128, :], xrt[:], r=[xrt], w=[(xout_k, t)])
                kb.barrier()
          except _Stop:
            kb.barrier()
            curgen[0].close()
            break
        kb.finish()
        stuck = kb.simulate()
        print("deadlock check:", stuck if stuck else "ok")
        print("instructions:", kb.ninstr, "counts", kb.count, "dmas", kb.ndmaq)
    return nc, dbg_d


def _pc(w):
    sh = w.shape
    w = w.reshape(sh[:-2] + (8, 128, sh[-1]))
    return np.ascontiguousarray(np.swapaxes(w, -3, -2))


def host_layout(inp):
    f = lambda a: np.ascontiguousarray(np.asarray(a, dtype=np.float32))
    w_in = f(inp["w_in"])
    depth = w_in.shape[0]
    shared = {
        "wada": _pc(f(inp["w_ada"])),
        "bada": f(inp["b_ada"]).reshape(depth, 1, 3072),
        "gpre": f(inp["g_pre"]).reshape(depth, 1, 1024),
        "gpost": f(inp["g_post"]).reshape(depth, 1, 1024),
        "wdn": _pc(w_in[:, :, 0:2048]),
        "wba": _pc(w_in[:, :, 2048:2056]),
        "wmb": _pc(w_in[:, :, 2056:4104]),
        "wmg": _pc(w_in[:, :, 4104:6152]),
        "convw": np.ascontiguousarray(
            f(inp["conv_w"]).transpose(0, 2, 1).reshape(depth, 12, 128, 4).transpose(0, 2, 1, 3)
        ).reshape(depth, 128, 48),
        "alog": np.ascontiguousarray(np.broadcast_to(f(inp["a_log"])[:, None, :], (depth, 128, 4))),
        "dtb": np.ascontiguousarray(np.broadcast_to(f(inp["dt_bias"])[:, None, :], (depth, 128, 4))),
        "dng": f(inp["dn_norm_g"]).reshape(depth, 128, 1),
        "wpdn": np.ascontiguousarray(f(inp["w_proj_dn"]).reshape(depth, 4, 128, 1024).transpose(0, 2, 1, 3)),
        "wpmb": np.ascontiguousarray(f(inp["w_proj_mb"]).reshape(depth, 8, 64, 1024).transpose(0, 2, 1, 3)),
        "wout": _pc(f(inp["w_out"])),
        "consts": make_consts(),
    }
    x = f(inp["x"])
    c = f(inp["c"])
    maps = []
    for b in range(x.shape[0]):
        m = dict(shared)
        m["x"] = x[b]
        m["cT"] = np.ascontiguousarray(c[b].reshape(8, 128).T)
        maps.append(m)
    return maps


_CACHE = {}


def kernel(**inputs):
    x = np.asarray(inputs["x"])
    B, S, _ = x.shape
    depth = np.asarray(inputs["w_in"]).shape[0]
    key = (S, depth)
    if key not in _CACHE:
        _CACHE[key] = build(S=S, DEPTH=depth)[0]
    nc = _CACHE[key]
    maps = host_layout(inputs)
    res = run_bass_kernel_spmd(nc, maps, core_ids=list(range(B)))
    return np.stack([np.asarray(r["y"], dtype=np.float32) for r in res.results], axis=0)
```

```python
from contextlib import ExitStack

import numpy as np
import concourse.bass as bass
import concourse.mybir as mybir
from concourse.bass_utils import run_bass_kernel_spmd

F32 = mybir.dt.float32
BF16 = mybir.dt.bfloat16
F32R = mybir.dt.float32r


def R(ap):
    return ap.bitcast(F32R)
AF = mybir.ActivationFunctionType
ALU = mybir.AluOpType
AX = mybir.AxisListType

D_MODEL = 1024
NEG = -30000.0
EPS = 1e-6


class SV:
    def __init__(self, tile, j, n=1):
        self.tile, self.sub = tile, j
        self.ap = tile[:, j * 128:(j + n) * 128]

    def __getitem__(self, idx):
        return self.ap[idx]


class KB:
    NRING = {"sp": 6, "pool": 4}

    def __init__(self, nc, stack):
        self.nc = nc
        self.eng = {"pe": nc.tensor, "act": nc.scalar, "dve": nc.vector,
                    "pool": nc.gpsimd, "sp": nc.sync}
        self.sem = {}
        for e in ("pe", "act", "dve", "pool"):
            self.sem[e] = stack.enter_context(nc.semaphore("s_" + e))
        self.ring = {q: [stack.enter_context(nc.semaphore("s_dma_%s%d" % (q, i))) for i in range(n)]
                     for q, n in self.NRING.items()}
        self.ndmaq = {q: 0 for q in self.NRING}
        self.count = {e: 0 for e in ("pe", "act", "dve", "pool")}
        self.waited = {}
        self.track = {}
        self.ninstr = 0
        self.streams = {e: [] for e in self.eng}
        self.psum_ids = set()

    def _semof(self, dep):
        if dep[0] == "e":
            return self.sem[dep[1]], ("e", dep[1])
        return self.ring[dep[1][0]][dep[1][1]], ("d", dep[1])

    def _wait(self, e, dep):
        sem, sk = self._semof(dep)
        val = dep[2]
        k = (e, sk)
        if self.waited.get(k, 0) >= val:
            return
        self.eng[e].wait_ge(sem, val)
        self.streams[e].append(("wait", sk, val))
        self.ninstr += 1
        self.waited[k] = val

    @staticmethod
    def _keys(items):
        out = []
        for it in items:
            if isinstance(it, SV):
                out.append((id(it.tile), it.sub))
            elif isinstance(it, tuple):
                out.append((id(it[0]), it[1]))
            else:
                out.append((id(it), None))
        return out

    def _conflicts(self, key):
        tid, sub = key
        d = self.track.get(tid)
        if d is None:
            return []
        if sub is None:
            return list(d.values())
        res = []
        if sub in d:
            res.append(d[sub])
        if None in d:
            res.append(d[None])
        return res

    def _entry(self, key):
        tid, sub = key
        d = self.track.setdefault(tid, {})
        if sub is None:
            ent = {"w": [], "r": []}
            for v in d.values():
                ent["w"] += v["w"]
                ent["r"] += v["r"]
            d.clear()
            d[None] = ent
            return ent
        if sub not in d:
            ent = {"w": [], "r": []}
            if None in d:
                ent["w"] = list(d[None]["w"])
                ent["r"] = list(d[None]["r"])
            d[sub] = ent
        return d[sub]

    def _deps(self, e, reads, writes):
        deps = []
        for k in reads:
            for ent in self._conflicts(k):
                for w in ent["w"]:
                    deps.append((w, "raw"))
        for k in writes:
            for ent in self._conflicts(k):
                for w in ent["w"]:
                    deps.append((w, "waw"))
                for r in ent["r"]:
                    deps.append((r, "war"))
        out = []
        for dep, kind in deps:
            if dep[0] == "e" and dep[1] == e:
                if e == "pe":
                    continue
            out.append(dep)
        return out

    @staticmethod
    def _prune(lst):
        best = {}
        for d in lst:
            kk = (d[0], d[1])
            if kk not in best or best[kk][2] < d[2]:
                best[kk] = d
        return list(best.values())

    def _record(self, me, reads, writes):
        for k in reads:
            ent = self._entry(k)
            ent["r"].append(me)
            if len(ent["r"]) > 16:
                ent["r"] = self._prune(ent["r"])
        for k in writes:
            ent = self._entry(k)
            ent["w"] = [me]
            ent["r"] = []

    def _rw(self, r, w):
        reads, writes = [], []
        for k in self._keys(r):
            if k[0] in self.psum_ids:
                writes.append((k[0], None))
            else:
                reads.append(k)
        for k in self._keys(w):
            writes.append((k[0], None) if k[0] in self.psum_ids else k)
        return reads, writes

    def op(self, e, fn, r=(), w=(), inc=True):
        reads, writes = self._rw(r, w)
        for dep in self._deps(e, reads, writes):
            self._wait(e, dep)
        ins = fn(self.eng[e])
        self.ninstr += 1
        if inc:
            self.count[e] += 1
            ins.then_inc(self.sem[e], 1)
            self.streams[e].append(("inc", ("e", e), 1))
            me = ("e", e, self.count[e])
        else:
            me = ("e", e, self.count[e] + 1)
        self._record(me, reads, writes)
        return ins

    def dma(self, out, in_, r=(), w=(), q="sp", **kw):
        e = q
        reads = self._keys(r)
        writes = self._keys(w)
        k = self.ndmaq[q]
        nr = self.NRING[q]
        slot = k % nr
        gen = k // nr
        for dep in self._deps(e, reads, writes):
            self._wait(e, dep)
        if gen > 0:
            self._wait(e, ("d", (q, slot), 16 * gen))
        ins = self.eng[e].dma_start(out=out, in_=in_, **kw)
        ins.then_inc(self.ring[q][slot], 16)
        self.streams[e].append(("inc", ("d", (q, slot)), 16))
        self.ndmaq[q] += 1
        self.ninstr += 1
        me = ("d", (q, slot), 16 * (gen + 1))
        self._record(me, reads, writes)
        return ins

    def _alldma(self):
        out = []
        for q, nr in self.NRING.items():
            n = self.ndmaq[q]
            for slot in range(nr):
                cnt = (n - 1 - slot) // nr + 1 if n > slot else 0
                if cnt > 0:
                    out.append(("d", (q, slot), 16 * cnt))
        return out

    def barrier(self):
        for e in ("pe", "act", "dve", "pool", "sp"):
            for o in ("pe", "act", "dve", "pool"):
                if o != e and self.count[o] > 0:
                    self._wait(e, ("e", o, self.count[o]))
            for dep in self._alldma():
                self._wait(e, dep)
        self.track = {}

    def finish(self):
        for dep in self._alldma():
            self._wait("sp", dep)

    def simulate(self):
        sems = {}
        pc = {e: 0 for e in self.streams}
        progress = True
        while progress:
            progress = False
            for e, st in self.streams.items():
                while pc[e] < len(st):
                    kind, sk, val = st[pc[e]]
                    if kind == "wait":
                        if sems.get(sk, 0) < val:
                            break
                    else:
                        sems[sk] = sems.get(sk, 0) + val
                    pc[e] += 1
                    progress = True
        stuck = {e: (pc[e], len(st), st[pc[e]], sems.get(st[pc[e]][1], 0)) for e, st in self.streams.items()
                 if pc[e] < len(st)}
        return stuck


C_IDENT, C_U, C_MBD, C_MOFF, C_TRI, C_ONES, C_PB = 0, 128, 256, 384, 512, 640, 768
C_PB4 = 768
NCONST = 768 + 512


def make_consts():
    c = np.zeros((128, NCONST), np.float32)
    i = np.arange(128)
    c[:, C_IDENT:C_IDENT + 128] = np.eye(128)
    c[:, C_U:C_U + 128] = (i[:, None] <= i[None, :])
    c[:, C_MBD:C_MBD + 128] = (i[:, None] < i[None, :]) & ((i[:, None] // 64) == (i[None, :] // 64))
    c[:, C_MOFF:C_MOFF + 128] = (i[:, None] < 64) & (i[None, :] >= 64)
    c[:, C_TRI:C_TRI + 128] = np.where(i[:, None] <= i[None, :], 0.0, NEG)
    c[:, C_ONES:C_ONES + 128] = 1.0
    pb = np.zeros((16, 16), np.float32)
    for own in range(16):
        pb[own, own:] = -1e30
    for g in range(8):
        for tt in range(4):
            own = (4 * g + tt) // 2
            c[:, C_PB4 + g * 64 + tt * 16:C_PB4 + g * 64 + (tt + 1) * 16] = pb[own][None, :]
    return c


def build(S=4096, DEPTH=2, LS=2, dbg=None, phases=("p1", "dn", "mb", "fin"), stop=0, JUNK=0):
    dbg = dbg or set()
    NT = S // 128
    NG = S // 512
    NB = S // 256
    nc = bass.Bass("TRN2", target_bir_lowering=False)

    def din(name, shape, dt=F32):
        return nc.dram_tensor(name, shape, dt, kind="ExternalInput").ap()

    x_d = din("x", [S, 1024])
    cT_d = din("cT", [128, 8])
    wada_d = din("wada", [DEPTH, 128, 8, 3072])
    bada_d = din("bada", [DEPTH, 1, 3072])
    gpre_d = din("gpre", [DEPTH, 1, 1024])
    gpost_d = din("gpost", [DEPTH, 1, 1024])
    wdn_d = din("wdn", [DEPTH, 128, 8, 2048])
    wba_d = din("wba", [DEPTH, 128, 8, 8])
    wmb_d = din("wmb", [DEPTH, 128, 8, 2048])
    wmg_d = din("wmg", [DEPTH, 128, 8, 2048])
    convw_d = din("convw", [DEPTH, 128, 48])
    alog_d = din("alog", [DEPTH, 128, 4])
    dtb_d = din("dtb", [DEPTH, 128, 4])
    dng_d = din("dng", [DEPTH, 128, 1])
    wpdn_d = din("wpdn", [DEPTH, 128, 4, 1024])
    wpmb_d = din("wpmb", [DEPTH, 64, 8, 1024])
    wout_d = din("wout", [DEPTH, 128, 8, 1024])
    consts_d = din("consts", [128, NCONST])
    y_d = nc.dram_tensor("y", [S, 1024], F32, kind="ExternalOutput").ap()
    xmid_d = nc.dram_tensor("xmid", [S, 1024], F32, kind="Internal").ap()
    ogdn_d = nc.dram_tensor("ogdn", [4, 128, S], BF16, kind="Internal").ap()
    ogmb_d = nc.dram_tensor("ogmb", [8, 64, S], BF16, kind="Internal").ap()
    dbg_d = {}

    class _K:
        pass
    kx, kmid, ky, kogdn, kogmb = _K(), _K(), _K(), _K(), _K()

    def dbg_out(name, shape):
        dbg_d[name] = nc.dram_tensor("dbg_" + name, shape, F32, kind="ExternalOutput").ap()
        return dbg_d[name]

    with ExitStack() as gst:
        kb = KB(nc, gst)
        op, dma = kb.op, kb.dma
        gst.enter_context(nc.allow_low_precision("float32r (1-pass PE) operands for non-critical fp32 matmuls"))

        def mm(out, lhsT, rhs, start=True, stop=True, r=(), w=(), inc=True):
            return op("pe", lambda e: e.matmul(out, lhsT=lhsT, rhs=rhs, start=start, stop=stop),
                      r=r, w=w, inc=inc)

        def tr(out, in_, ident, r=(), w=()):
            return op("pe", lambda e: e.transpose(out=out, in_=in_, identity=ident), r=r, w=w)

        uid = [0]

        def T(st, name, shape, dt=F32):
            uid[0] += 1
            return st.enter_context(nc.sbuf_tensor("sb%d_%s" % (uid[0], name), shape, dt))

        class _Stop(Exception):
            pass

        def ck(level):
            if stop == level:
                raise _Stop()

        curgen = [None]

        def phase(name):
            if name in phases:
                st_ = ExitStack()
                curgen[0] = st_
                yield st_
                st_.close()

        PS = [gst.enter_context(nc.psum_tensor("ps%d" % i, [128, 512], F32)) for i in range(8)]
        kb.psum_ids = {id(p) for p in PS}
        C = T(gst, "consts", [128, NCONST])
        dma(C[:], consts_d[:, :], w=[C])
        ident = C[:, C_IDENT:C_IDENT + 128]
        U = C[:, C_U:C_U + 128]
        Mbd = C[:, C_MBD:C_MBD + 128]
        Moff = C[:, C_MOFF:C_MOFF + 128]
        ones = C[:, C_ONES:C_ONES + 128]
        identb = T(gst, "identb", [128, 128], BF16)
        trib = T(gst, "trib", [128, 128], BF16)
        epsT = T(gst, "epsT", [128, 1])
        op("dve", lambda e: e.tensor_copy(out=identb[:], in_=ident), r=[C], w=[identb])
        op("dve", lambda e: e.tensor_copy(out=trib[:], in_=C[:, C_TRI:C_TRI + 128]), r=[C], w=[trib])
        op("dve", lambda e: e.memset(epsT[:], EPS), w=[epsT])
        onesr = T(gst, "onesr", [128, 128])
        op("dve", lambda e: e.tensor_copy(out=R(onesr[:]), in_=ones), r=[C], w=[onesr])
        AB = [T(gst, "AB%d" % l, [128, 16]) for l in range(DEPTH)]
        gp_d = nc.dram_tensor("gp_scr", [DEPTH, 128, 1024], F32, kind="Internal").ap()
        kgp = _K()

        with ExitStack() as st:
            cT = T(st, "cT", [128, 8])
            sc = T(st, "sc", [128, 8])
            dma(cT[:], cT_d[:, :], w=[cT])
            op("act", lambda e: e.activation(out=sc[:], in_=cT[:], func=AF.Silu), r=[cT], w=[sc])
            wa = [T(st, "wa%d" % i, [128, 8, 512]) for i in range(2)]
            row = T(st, "row", [1, 3072])
            bada = T(st, "bada", [1, 3072])
            gpr = T(st, "gpr", [1, 1024])
            gpo = T(st, "gpo", [1, 1024])
            arow = T(st, "arow", [1, 1024])
            gprow = T(st, "gprow", [1, 1024])
            gptmp = T(st, "gptmp", [128, 1024])
            nwa = 0
            for l in range(DEPTH):
                dma(bada[:], bada_d[l], w=[bada])
                dma(gpr[:], gpre_d[l], w=[gpr])
                dma(gpo[:], gpost_d[l], w=[gpo])
                for cg in range(6):
                    wt = wa[nwa % 2]
                    nwa += 1
                    dma(wt[:], wada_d[l, :, :, cg * 512:(cg + 1) * 512], w=[wt])
                    pb = PS[cg % 2]
                    for c in range(8):
                        mm(pb[0:1, :], sc[:, c:c + 1], wt[:, c, :], start=(c == 0), stop=(c == 7),
                           r=[sc, wt], w=[pb], inc=(c == 7))
                    op("dve", lambda e: e.tensor_tensor(out=row[0:1, cg * 512:(cg + 1) * 512], in0=pb[0:1, :],
                                                        in1=bada[0:1, cg * 512:(cg + 1) * 512], op=ALU.add),
                       r=[pb, bada], w=[(row, cg)])
                op("dve", lambda e: e.scalar_tensor_tensor(out=arow[:], in0=row[0:1, 1024:2048], scalar=1.0,
                                                           in1=gpr[:], op0=ALU.add, op1=ALU.mult),
                   r=[row, gpr], w=[arow])
                op("dve", lambda e: e.tensor_tensor(out=gprow[:], in0=row[0:1, 2048:3072], in1=gpo[:], op=ALU.mult),
                   r=[row, gpo], w=[gprow])
                pc = PS[2]
                for c in range(8):
                    mm(pc[:, c:c + 1], arow[0:1, c * 128:(c + 1) * 128], ones[0:1, 0:1], r=[arow, C], w=[pc], inc=False)
                for c in range(8):
                    mm(pc[:, 8 + c:9 + c], row[0:1, c * 128:(c + 1) * 128], ones[0:1, 0:1], r=[row, C], w=[pc],
                       inc=(c == 7))
                op("dve", lambda e: e.tensor_copy(out=AB[l][:], in_=pc[:, 0:16]), r=[pc], w=[AB[l]])
                for hf in range(2):
                    pg = PS[3 + hf]
                    mm(pg[:, :], ones[0:1, 0:128], gprow[0:1, hf * 512:(hf + 1) * 512], r=[gprow, C], w=[pg])
                    op("act", lambda e: e.activation(out=gptmp[:, hf * 512:(hf + 1) * 512], in_=pg[:, :], func=AF.Copy),
                       r=[pg], w=[(gptmp, hf)])
                dma(gp_d[l], gptmp[:], r=[gptmp], w=[(kgp, l)])
            kb.barrier()

        hT = T(gst, "hT", [128, 8, S], BF16)

        for l in range(DEPTH):
          try:
            xin_d = x_d if l == 0 else xmid_d
            xout_d = y_d if l == DEPTH - 1 else xmid_d
            xin_k = kx if l == 0 else kmid
            xout_k = ky if l == DEPTH - 1 else kmid

            for st in phase("p1"):
                xt = [T(st, "xt%d" % i, [128, 1024]) for i in range(4)]
                xn = [T(st, "xn%d" % i, [128, 1024]) for i in range(3)]
                junk = T(st, "junk", [128, 1024], BF16)
                ss = T(st, "ss", [128, NT])
                rstd = T(st, "rstd", [128, NT])
                def p1_stats(t):
                    xtt, xnt = xt[t % 4], xn[t % 3]
                    dma(xtt[:], xin_d[t * 128:(t + 1) * 128, :], r=[(xin_k, t)], w=[xtt])
                    op("act", lambda e: e.activation(out=junk[:], in_=xtt[:], func=AF.Square,
                                                     accum_out=ss[:, t:t + 1]), r=[xtt], w=[junk, (ss, t)])
                    op("act", lambda e: e.activation(out=rstd[:, t:t + 1], in_=ss[:, t:t + 1], func=AF.Sqrt,
                                                     scale=1.0 / D_MODEL, bias=epsT[:]), r=[(ss, t), epsT],
                       w=[(rstd, t)])
                    op("dve", lambda e: e.reciprocal(out=rstd[:, t:t + 1], in_=rstd[:, t:t + 1]), r=[(rstd, t)],
                       w=[(rstd, t)])
                    op("dve", lambda e: e.tensor_scalar(out=xnt[:], in0=xtt[:], scalar1=rstd[:, t:t + 1],
                                                        scalar2=None, op0=ALU.mult), r=[xtt, (rstd, t)], w=[xnt])

                def p1_transpose(t):
                    xnt = xn[t % 3]
                    for hf in range(2):
                        pb = PS[(t % 2) * 2 + hf]
                        for cc in range(4):
                            c = hf * 4 + cc
                            tr(pb[:, cc * 128:(cc + 1) * 128], xnt[:, c * 128:(c + 1) * 128], ident, r=[xnt, C],
                               w=[pb])
                        for cc in range(4):
                            c = hf * 4 + cc
                            eng = "act" if hf == 0 else "dve"
                            if eng == "act":
                                op("act", lambda e: e.activation(out=hT[:, c, t * 128:(t + 1) * 128],
                                                                 in_=pb[:, cc * 128:(cc + 1) * 128], func=AF.Identity,
                                                                 scale=AB[l][:, c:c + 1], bias=AB[l][:, 8 + c:9 + c]),
                                   r=[pb, AB[l]], w=[(hT, t // 4)])
                            else:
                                op("dve", lambda e: e.tensor_scalar(out=hT[:, c, t * 128:(t + 1) * 128],
                                                                    in0=pb[:, cc * 128:(cc + 1) * 128],
                                                                    scalar1=AB[l][:, c:c + 1],
                                                                    scalar2=AB[l][:, 8 + c:9 + c],
                                                                    op0=ALU.mult, op1=ALU.add),
                                   r=[pb, AB[l]], w=[(hT, t // 4)])

                p1_stats(0)
                for t in range(NT):
                    if t + 1 < NT:
                        p1_stats(t + 1)
                    p1_transpose(t)
                kb.barrier()
            if "hT" in dbg and l == 0:
                with ExitStack() as st:
                    d = dbg_out("hT", [128, 8, S])
                    tmp = T(st, "dbgtmp", [128, 8, S])
                    op("dve", lambda e: e.tensor_copy(out=tmp[:], in_=hT[:]), r=[hT], w=[tmp])
                    dma(d[:, :, :], tmp[:], r=[tmp])
                    kb.barrier()

            for st in phase("dn"):
                Wdn = T(st, "Wdn", [128, 8, 2048], BF16)
                Wba = T(st, "Wba", [128, 8, 8], BF16)
                for c in range(8):
                    dma(Wdn[:, c, :], wdn_d[l, :, c, :], w=[(Wdn, c)], q="pool")
                dma(Wba[:], wba_d[l], w=[Wba], q="pool")
                convw = T(st, "convw", [128, 48])
                alog = T(st, "alog", [128, 4])
                dtb = T(st, "dtb", [128, 4])
                dng = T(st, "dng", [128, 1])
                dma(convw[:], convw_d[l], w=[convw])
                dma(alog[:], alog_d[l], w=[alog])
                dma(dtb[:], dtb_d[l], w=[dtb])
                dma(dng[:], dng_d[l], w=[dng])
                st2 = ExitStack()
                BETA = T(st, "BETA", [128, NT, 4])
                NBETA = T(st, "NBETA", [128, NT, 4])
                GRAW = T(st, "GRAW", [128, NT, 4])
                GC = T(st, "GC", [128, NT, 4])
                NEXPG = T(st, "NEXPG", [128, NT, 4])
                negA = T(st, "negA", [128, 4])
                BG = T(st2, "BG", [128, NT, 8])
                AA = T(st2, "AA", [128, NT, 4])
                AX_ = T(st2, "AXs", [128, NT, 4])
                pbg = PS[0]
                for t in range(NT):
                    for c in range(8):
                        mm(pbg[:, t * 8:(t + 1) * 8], hT[:, c, t * 128:(t + 1) * 128], Wba[:, c, :],
                           start=(c == 0), stop=(c == 7), r=[(hT, t // 4), Wba], w=[pbg], inc=(c == 7))
                op("dve", lambda e: e.tensor_copy(out=BG[:].rearrange("p t e -> p (t e)"), in_=pbg[:, 0:NT * 8]),
                   r=[pbg], w=[BG])
                op("act", lambda e: e.activation(out=BETA[:], in_=BG[:, :, 0:4], func=AF.Sigmoid), r=[BG], w=[BETA])
                op("dve", lambda e: e.tensor_scalar(out=NBETA[:], in0=BETA[:], scalar1=-1.0, scalar2=None,
                                                    op0=ALU.mult), r=[BETA], w=[NBETA])
                for h in range(4):
                    op("dve", lambda e: e.tensor_scalar(out=AA[:, :, h], in0=BG[:, :, 4 + h], scalar1=dtb[:, h:h + 1],
                                                        scalar2=None, op0=ALU.add), r=[BG, dtb], w=[AA])
                op("act", lambda e: e.activation(out=AX_[:], in_=AA[:], func=AF.Abs), r=[AA], w=[AX_])
                op("act", lambda e: e.activation(out=AX_[:], in_=AX_[:], func=AF.Exp, scale=-1.0), r=[AX_], w=[AX_])
                op("dve", lambda e: e.tensor_scalar(out=AX_[:], in0=AX_[:], scalar1=1.0, scalar2=None, op0=ALU.add),
                   r=[AX_], w=[AX_])
                op("act", lambda e: e.activation(out=AX_[:], in_=AX_[:], func=AF.Ln), r=[AX_], w=[AX_])
                op("dve", lambda e: e.scalar_tensor_tensor(out=AA[:], in0=AA[:], scalar=0.0, in1=AX_[:],
                                                           op0=ALU.max, op1=ALU.add), r=[AA, AX_], w=[AA])
                op("act", lambda e: e.activation(out=negA[:], in_=alog[:], func=AF.Exp), r=[alog], w=[negA])
                op("dve", lambda e: e.tensor_scalar(out=negA[:], in0=negA[:], scalar1=-1.0, scalar2=None,
                                                    op0=ALU.mult), r=[negA], w=[negA])
                for h in range(4):
                    op("dve", lambda e: e.tensor_scalar(out=GRAW[:, :, h], in0=AA[:, :, h], scalar1=negA[:, h:h + 1],
                                                        scalar2=None, op0=ALU.mult), r=[AA, negA], w=[GRAW])
                pgc = PS[1]
                mm(pgc[:, 0:NT * 4], U, GRAW[:].rearrange("p t e -> p (t e)"), r=[C, GRAW], w=[pgc])
                op("dve", lambda e: e.tensor_copy(out=GC[:].rearrange("p t e -> p (t e)"), in_=pgc[:, 0:NT * 4]),
                   r=[pgc], w=[GC])
                op("act", lambda e: e.activation(out=NEXPG[:], in_=GC[:], func=AF.Exp), r=[GC], w=[NEXPG])
                op("dve", lambda e: e.tensor_scalar(out=NEXPG[:], in0=NEXPG[:], scalar1=-1.0, scalar2=None,
                                                    op0=ALU.mult), r=[NEXPG], w=[NEXPG])
                if "graw" in dbg and l == 0:
                    d = dbg_out("graw", [128, NT, 4])
                    dma(d[:, :, :], GRAW[:], r=[GRAW])
                    d = dbg_out("beta", [128, NT, 4])
                    dma(d[:, :, :], BETA[:], r=[BETA])

                kb.barrier()
                st2.close()
                ck(1)
                pre = [T(st, "pre%d" % i, [128, 515]) for i in range(3)]
                halo = T(st, "halo", [128, 12, 3])
                op("dve", lambda e: e.memset(halo[:], 0.0), w=[halo])
                qkv = [[T(st, "qkv%d_%d" % (i, j), [128, 512]) for j in range(3)] for i in range(2)]
                zs = [T(st, "zs%d" % i, [128, 512], BF16) for i in range(4)]
                cv = [T(st, "cv0", [128, 512])] * 2
                sq = [T(st, "sq0", [128, 512])] * 2
                oTg = [T(st, "oTg%d" % i, [128, 512]) for i in range(2)]
                ogb = [T(st, "ogb0", [128, 512], BF16)] * 2
                Sst = [[T(st, "S%d_%d" % (h, i), [128, 128]) for i in range(2)] for h in range(4)]
                for h in range(4):
                    op("dve", lambda e: e.tensor_scalar(out=R(Sst[h][0][:]), in0=ident, scalar1=0.0, scalar2=None,
                                                        op0=ALU.mult), r=[C], w=[Sst[h][0]])
                spar = [0, 0, 0, 0]
                NCH = 8
                ALIAS = {"Dm": 0, "tq": 0, "DecT": 1, "Xo": 1, "Pe": 2, "ktok": 3, "Xe": 3, "NoffT": 3, "Po": 4,
                         "B": 5, "BT": 6, "WbdT": 6, "Noff": 7, "Z1": 7, "ExpG": 8, "XTo": 8, "Lm": 9, "XTe": 9}
                scr = []
                for i in range(NCH):
                    wide = T(st, "s%d" % i, [128, 9 * 128])
                    plain = T(st, "s%da" % i, [128, 128])
                    d_ = {n: (SV(wide, j - 1) if j > 0 else plain) for n, j in ALIAS.items()}
                    d_["XPo"] = SV(wide, 0, 2)
                    d_["XPe"] = SV(wide, 2, 2)
                    scr.append(d_)
                OUTN = ["W", "QKdT", "kdec", "qg", "vtok", "kTc"]
                outs = [{n: T(st, "o%d_%s" % (i, n), [128, 128]) for n in OUTN} for i in range(NCH)]
                EGL = T(st, "EGL", [128, NCH])
                Yr = [T(st, "Yr%d" % i, [128, 128]) for i in range(2)]
                Vn = [T(st, "Vn%d" % i, [128, 128]) for i in range(2)]

                def prep8(chains, pump=lambda: None):
                    pump_on = [False]

                    def each(fn):
                        for ci, ch in enumerate(chains):
                            fn(ch, ch["s"], ch["o"], PS[ch["i"]])
                            if ci % 2 == 1 and pump_on[0]:
                                pump(1)

                    def f(ch, s, o, pb):
                        h, n, sl = ch["h"], ch["n"], ch["sl"]
                        kT, vT = ch["kT"], ch["vT"]
                        tr(pb[:, 0:128], kT[:, sl], ident, r=[kT, C], w=[pb])
                        tr(pb[:, 128:256], vT[:, sl], ident, r=[vT, C], w=[pb])
                        mm(pb[:, 256:384], GRAW[:, n, h:h + 1].to_broadcast([128, 128]), U, r=[GRAW, C], w=[pb])
                        op("dve", lambda e: e.tensor_scalar(out=s["Dm"][:], in0=pb[:, 256:384],
                                                            scalar1=GC[:, n, h:h + 1], scalar2=0.0,
                                                            op0=ALU.subtract, op1=ALU.min),
                           r=[pb, GC], w=[s["Dm"]])
                        op("dve", lambda e: e.tensor_copy(out=o["vtok"][:], in_=pb[:, 128:256]), r=[pb],
                           w=[o["vtok"]])
                        op("act", lambda e: e.activation(out=R(s["ktok"][:]), in_=pb[:, 0:128], func=AF.Copy),
                           r=[pb], w=[s["ktok"]])
                        op("act", lambda e: e.activation(out=R(s["ExpG"][:]), in_=pb[:, 256:384], func=AF.Exp),
                           r=[pb], w=[s["ExpG"]])
                    each(f)

                    def f(ch, s, o, pb):
                        op("act", lambda e: e.activation(out=R(s["DecT"][:]), in_=s["Dm"][:], func=AF.Exp),
                           r=[s["Dm"]], w=[s["DecT"]])
                    each(f)

                    def f(ch, s, o, pb):
                        h, n, sl = ch["h"], ch["n"], ch["sl"]
                        kT, qT = ch["kT"], ch["qT"]
                        mm(pb[:, 0:128], R(kT[:, sl]), R(kT[:, sl]), r=[kT], w=[pb], inc=False)
                        mm(pb[:, 128:256], R(kT[:, sl]), R(qT[:, sl]), r=[kT, qT], w=[pb])
                        op("dve", lambda e: e.tensor_tensor(out=R(s["Lm"][:]), in0=pb[:, 0:128], in1=s["DecT"][:],
                                                            op=ALU.mult), r=[pb, s["DecT"]], w=[s["Lm"]])
                        op("dve", lambda e: e.tensor_tensor(out=s["tq"][:], in0=pb[:, 128:256], in1=s["DecT"][:],
                                                            op=ALU.mult), r=[pb, s["DecT"]], w=[s["tq"]])
                        op("dve", lambda e: e.scalar_tensor_tensor(out=R(s["B"][:]), in0=s["Lm"][:],
                                                                   scalar=NBETA[:, n, h:h + 1], in1=Mbd,
                                                                   op0=ALU.mult, op1=ALU.mult),
                           r=[s["Lm"], NBETA, C], w=[s["B"]])
                        op("dve", lambda e: e.scalar_tensor_tensor(out=R(s["Noff"][:]), in0=s["Lm"][:],
                                                                   scalar=BETA[:, n, h:h + 1], in1=Moff,
                                                                   op0=ALU.mult, op1=ALU.mult),
                           r=[s["Lm"], BETA, C], w=[s["Noff"]])
                        op("pool", lambda e: e.tensor_tensor(out=R(o["QKdT"][:]), in0=s["tq"][:], in1=U,
                                                             op=ALU.mult), r=[s["tq"], C], w=[o["QKdT"]])
                        op("act", lambda e: e.activation(out=R(o["kdec"][:]), in_=s["ktok"][:], func=AF.Identity,
                                                         scale=s["DecT"][:, 127:128]), r=[s["ktok"], s["DecT"]],
                           w=[o["kdec"]])
                        op("dve", lambda e: e.tensor_tensor(out=R(o["qg"][:]), in0=qT[:, sl], in1=s["ExpG"][:],
                                                            op=ALU.mult), r=[qT, s["ExpG"]], w=[o["qg"]])
                        op("dve", lambda e: e.tensor_copy(out=R(o["kTc"][:]), in_=kT[:, sl]), r=[kT], w=[o["kTc"]])
                        op("dve", lambda e: e.tensor_copy(out=EGL[:, ch["i"]:ch["i"] + 1],
                                                          in_=s["ExpG"][:, 127:128]),
                           r=[s["ExpG"]], w=[(EGL, ch["i"])])
                    each(f)

                    pump_on[0] = True
                    def f(ch, s, o, pb):
                        tr(pb[:, 256:384], s["B"][:], ident, r=[s["B"], C], w=[pb])
                        op("act", lambda e: e.activation(out=R(s["BT"][:]), in_=pb[:, 256:384], func=AF.Copy),
                           r=[pb], w=[s["BT"]])
                        op("dve", lambda e: e.tensor_tensor(out=R(s["Pe"][:]), in0=s["B"][:], in1=ident,
                                                            op=ALU.add), r=[s["B"], C], w=[s["Pe"]])
                    each(f)

                    def f(ch, s, o, pb):
                        mm(pb[:, 0:128], R(s["BT"][:]), R(s["B"][:]), r=[s["BT"], s["B"]], w=[pb])
                        op("act", lambda e: e.activation(out=R(s["Xo"][:]), in_=pb[:, 0:128], func=AF.Copy),
                           r=[pb], w=[s["Xo"]])
                    each(f)
                    pump()
                    for j in range(1, 6):
                        odd = (j % 2 == 1)
                        Xc, Pc, XTc, XPc = ("Xo", "Pe", "XTo", "XPo") if odd else ("Xe", "Po", "XTe", "XPe")
                        Xn, Pn = ("Xe", "Po") if odd else ("Xo", "Pe")

                        def f(ch, s, o, pb):
                            tr(pb[:, 256:384], s[Xc][:], ident, r=[s[Xc], C], w=[pb])
                            op("act", lambda e: e.activation(out=R(s[XTc][:]), in_=pb[:, 256:384], func=AF.Copy),
                               r=[pb], w=[s[XTc]])
                        each(f)
                        pump()

                        def f(ch, s, o, pb):
                            if j < 5:
                                mm(pb[:, 0:256], R(s[XTc][:]), R(s[XPc][:]), r=[s[XTc], s[Xc], s[Pc]], w=[pb])
                                op("act", lambda e: e.activation(out=R(s[Xn][:]), in_=pb[:, 0:128], func=AF.Copy),
                                   r=[pb], w=[s[Xn]])
                                op("dve", lambda e: e.tensor_tensor(out=R(s[Pn][:]), in0=pb[:, 128:256],
                                                                    in1=s[Pc][:], op=ALU.add),
                                   r=[pb, s[Pc]], w=[s[Pn]])
                            else:
                                mm(pb[:, 128:256], R(s[XTc][:]), R(s[Pc][:]), r=[s[XTc], s[Pc]], w=[pb])
                                op("dve", lambda e: e.tensor_tensor(out=R(s[Pn][:]), in0=pb[:, 128:256],
                                                                    in1=s[Pc][:], op=ALU.add),
                                   r=[pb, s[Pc]], w=[s[Pn]])
                        each(f)
                        pump()
                    Pf = "Po"

                    def f(ch, s, o, pb):
                        tr(pb[:, 0:128], s[Pf][:], ident, r=[s[Pf], C], w=[pb])
                        tr(pb[:, 128:256], s["Noff"][:], ident, r=[s["Noff"], C], w=[pb])
                        op("act", lambda e: e.activation(out=R(s["WbdT"][:]), in_=pb[:, 0:128], func=AF.Copy),
                           r=[pb], w=[s["WbdT"]])
                        op("dve", lambda e: e.tensor_copy(out=R(s["NoffT"][:]), in_=pb[:, 128:256]), r=[pb],
                           w=[s["NoffT"]])
                    each(f)
                    pump()

                    def f(ch, s, o, pb):
                        mm(pb[:, 0:128], R(s["NoffT"][:]), R(s[Pf][:]), r=[s["NoffT"], s[Pf]], w=[pb])
                        op("act", lambda e: e.activation(out=R(s["Z1"][:]), in_=pb[:, 0:128], func=AF.Copy),
                           r=[pb], w=[s["Z1"]])
                    each(f)
                    pump()

                    def f(ch, s, o, pb):
                        mm(pb[:, 128:256], R(s["WbdT"][:]), R(s["Z1"][:]), r=[s["WbdT"], s["Z1"]], w=[pb])
                        op("dve", lambda e: e.tensor_tensor(out=R(o["W"][:]), in0=s[Pf][:], in1=pb[:, 128:256],
                                                            op=ALU.subtract), r=[s[Pf], pb], w=[o["W"]])
                    each(f)
                    pump()

                nrec = [0]

                def recur_pair(chs, g):
                    st_ = []
                    for ch in chs:
                        h = ch["h"]
                        So = Sst[h][spar[h]]
                        Sn = Sst[h][1 - spar[h]]
                        spar[h] = 1 - spar[h]
                        bb = 4 * ch["hi"]
                        st_.append((ch, So, Sn, PS[bb], PS[bb + 1], PS[bb + 2], PS[bb + 3],
                                    Yr[nrec[0] % 2], Vn[nrec[0] % 2]))
                        nrec[0] += 1
                    for ch, So, Sn, pa, pb2, pc, pd, Y, vnew in st_:
                        h, n, o = ch["h"], ch["n"], ch["o"]
                        mm(pa[:, 0:128], R(o["kTc"][:]), R(So[:]), r=[o["kTc"], So], w=[pa])
                        op("dve", lambda e: e.scalar_tensor_tensor(out=R(Y[:]), in0=pa[:, 0:128],
                                                                   scalar=NEXPG[:, n, h:h + 1], in1=o["vtok"][:],
                                                                   op0=ALU.mult, op1=ALU.add),
                           r=[pa, NEXPG, o["vtok"]], w=[Y])
                    for ch, So, Sn, pa, pb2, pc, pd, Y, vnew in st_:
                        h, n, o = ch["h"], ch["n"], ch["o"]
                        mm(pb2[:, 0:128], R(o["W"][:]), R(Y[:]), r=[o["W"], Y], w=[pb2])
                        op("act", lambda e: e.activation(out=R(vnew[:]), in_=pb2[:, 0:128], func=AF.Identity,
                                                         scale=BETA[:, n, h:h + 1]), r=[pb2, BETA], w=[vnew])
                    for ch, So, Sn, pa, pb2, pc, pd, Y, vnew in st_:
                        o = ch["o"]
                        mm(pd[:, 0:128], R(o["kdec"][:]), R(vnew[:]), r=[o["kdec"], vnew], w=[pd])
                        op("dve", lambda e: e.scalar_tensor_tensor(out=R(Sn[:]), in0=So[:],
                                                                   scalar=EGL[:, ch["i"]:ch["i"] + 1],
                                                                   in1=pd[:, 0:128], op0=ALU.mult, op1=ALU.add),
                           r=[So, (EGL, ch["i"]), pd], w=[Sn])
                    for ch, So, Sn, pa, pb2, pc, pd, Y, vnew in st_:
                        o, sl = ch["o"], ch["sl"]
                        mm(pc[:, 0:128], R(So[:]), R(o["qg"][:]), start=True, stop=False, r=[So, o["qg"]], w=[pc],
                           inc=False)
                        mm(pc[:, 0:128], R(vnew[:]), R(o["QKdT"][:]), start=False, stop=True, r=[vnew, o["QKdT"]],
                           w=[pc])
                        ot = oTg[ch["hi"]]
                        op("act", lambda e: e.activation(out=ot[:, sl], in_=pc[:, 0:128], func=AF.Copy), r=[pc],
                           w=[(ot, ch["cc"])])

                def stageA(g, hp, par, res):
                    gs = slice(g * 512, (g + 1) * 512)
                    chains = []
                    for hi in range(2):
                        h = 2 * hp + hi
                        qk = qkv[hi]
                        zt = zs[2 * par + hi]
                        for ty in range(4):
                            pb = PS[4 * hi + ty]
                            col = ty * 512 + h * 128
                            for c in range(8):
                                mm(pb[:, :], Wdn[:, c, col:col + 128], hT[:, c, gs], start=(c == 0),
                                   stop=(c == 7), r=[(Wdn, c), (hT, g)], w=[pb], inc=(c == 7))
                            if ty < 3:
                                ch_ = ty * 4 + h
                                op("dve", lambda e: e.tensor_copy(out=pre[ty][:, 0:3], in_=halo[:, ch_, :]),
                                   r=[(halo, ch_)], w=[(pre[ty], 0)])
                                op("act", lambda e: e.activation(out=pre[ty][:, 3:515], in_=pb[:, :],
                                                                 func=AF.Copy), r=[pb], w=[(pre[ty], 1)])
                                op("dve", lambda e: e.tensor_copy(out=halo[:, ch_, :], in_=pre[ty][:, 512:515]),
                                   r=[(pre[ty], 1)], w=[(halo, ch_)])
                                yield
                                cvt = cv[ty % 2]
                                wk = lambda k: convw[:, ch_ * 4 + k:ch_ * 4 + k + 1]
                                op("act", lambda e: e.activation(out=cvt[:], in_=pre[ty][:, 0:512],
                                                                 func=AF.Identity, scale=wk(0)),
                                   r=[pre[ty], convw], w=[cvt])
                                for k in range(1, 4):
                                    op("dve", lambda e: e.scalar_tensor_tensor(out=cvt[:],
                                                                               in0=pre[ty][:, k:k + 512],
                                                                               scalar=wk(k), in1=cvt[:],
                                                                               op0=ALU.mult, op1=ALU.add),
                                       r=[pre[ty], convw, cvt], w=[cvt])
                                    yield
                                op("act", lambda e: e.activation(out=(R(qk[ty][:]) if ty < 2 else qk[ty][:]),
                                                                 in_=cvt[:], func=AF.Silu),
                                   r=[cvt], w=[qk[ty]])
                            else:
                                op("act", lambda e: e.activation(out=zt[:], in_=pb[:, :], func=AF.Silu),
                                   r=[pb], w=[zt])
                            yield
                        for ty in range(2):
                            sqt = sq[ty]
                            pb = PS[4 * hi + ty]
                            op("act", lambda e: e.activation(out=R(sqt[:]), in_=qk[ty][:], func=AF.Square),
                               r=[qk[ty]], w=[sqt])
                            mm(pb[:, :], R(onesr[:]), R(sqt[:]), r=[onesr, sqt], w=[pb])
                            rt_ = cv[0]
                            op("act", lambda e: e.activation(out=rt_[:], in_=pb[:, :], func=AF.Ln,
                                                             bias=epsT[:]), r=[pb, epsT], w=[rt_])
                            op("act", lambda e: e.activation(out=rt_[:], in_=rt_[:], func=AF.Exp, scale=-0.5),
                               r=[rt_], w=[rt_])
                            sc_ = (128.0 ** -0.5) if ty == 0 else 1.0
                            op("dve", lambda e: e.scalar_tensor_tensor(out=R(qk[ty][:]), in0=qk[ty][:],
                                                                       scalar=sc_, in1=rt_[:], op0=ALU.mult,
                                                                       op1=ALU.mult),
                               r=[qk[ty], rt_], w=[qk[ty]])
                            yield
                        if dbg and l == 0 and g == 0 and h == 0:
                            for nm, tt in (("q", qk[0]), ("k", qk[1]), ("v", qk[2])):
                                if nm in dbg:
                                    d = dbg_out(nm, [128, 512])
                                    dma(d[:, :], tt[:], r=[tt])
                        for cc in range(4):
                            i = hi * 4 + cc
                            chains.append({"h": h, "hi": hi, "cc": cc, "n": g * 4 + cc, "i": i,
                                           "sl": slice(cc * 128, (cc + 1) * 128), "qT": qk[0], "kT": qk[1],
                                           "vT": qk[2], "s": scr[i], "o": outs[i]})
                    res["chains"] = chains

                def drain(gen):
                    if gen is not None:
                        for _ in gen:
                            pass

                pairs = [(g, hp) for g in range(NG) for hp in range(2)]
                resA = [dict() for _ in pairs]
                gens = [stageA(g, hp, pi % 2, resA[pi]) for pi, (g, hp) in enumerate(pairs)]
                drain(gens[0])
                for pi, (g, hp) in enumerate(pairs):
                    gs = slice(g * 512, (g + 1) * 512)
                    chains = resA[pi]["chains"]
                    nxt = gens[pi + 1] if pi + 1 < len(pairs) else None

                    def pump(n=1):
                        if nxt is not None:
                            for _ in range(n):
                                if next(nxt, "done") == "done":
                                    break
                    prep8(chains, pump)
                    drain(nxt)
                    for cc in range(4):
                        recur_pair([ch for ch in chains if ch["cc"] == cc], g)
                    for hi in range(2):
                        h = 2 * hp + hi
                        zt = zs[2 * (pi % 2) + hi]
                        ot = oTg[hi]
                        og = ogb[hi]
                        sqt = sq[hi]
                        pb = PS[4 * hi]
                        op("act", lambda e: e.activation(out=R(sqt[:]), in_=ot[:], func=AF.Square), r=[ot],
                           w=[sqt])
                        mm(pb[:, :], R(onesr[:]), R(sqt[:]), r=[onesr, sqt], w=[pb])
                        sqt = cv[0]
                        op("act", lambda e: e.activation(out=sqt[:], in_=pb[:, :], func=AF.Ln,
                                                         scale=1.0 / 128, bias=epsT[:]), r=[pb, epsT], w=[sqt])
                        op("act", lambda e: e.activation(out=sqt[:], in_=sqt[:], func=AF.Exp, scale=-0.5),
                           r=[sqt], w=[sqt])
                        if "odn" in dbg and l == 0 and h == 0 and g == 0:
                            d = dbg_out("odn", [128, 512])
                            dma(d[:, :], ot[:], r=[ot])
                        op("dve", lambda e: e.scalar_tensor_tensor(out=sqt[:], in0=ot[:], scalar=dng[:, 0:1],
                                                                   in1=sqt[:], op0=ALU.mult, op1=ALU.mult),
                           r=[ot, dng, sqt], w=[sqt])
                        op("dve", lambda e: e.tensor_tensor(out=og[:], in0=sqt[:], in1=zt[:], op=ALU.mult),
                           r=[sqt, zt], w=[og])
                        dma(ogdn_d[h, :, gs], og[:], r=[og], w=[(kogdn, g)])
                kb.barrier()

            for st in phase("mb"):
                Wmb = T(st, "Wmb", [128, 8, 2048], BF16)
                for c in range(8):
                    dma(Wmb[:, c, :], wmb_d[l, :, c, :], w=[(Wmb, c)], q="pool")
                Vt = T(st, "Vt", [128, NT, 8, 65], BF16)
                op("dve", lambda e: e.memset(Vt[:, :, :, 64:65], 1.0), w=[Vt])
                for t in range(NT):
                    pb = PS[t % 2]
                    for c in range(8):
                        mm(pb[:, :], hT[:, c, t * 128:(t + 1) * 128], Wmb[:, c, 1024:1536], start=(c == 0),
                           stop=(c == 7), r=[(hT, t // 4), (Wmb, c)], w=[pb], inc=(c == 7))
                    op("act" if t % 2 else "dve",
                       lambda e: (e.activation(out=Vt[:, t, :, 0:64], in_=pb[:, :].rearrange("p (h d) -> p h d", h=8),
                                               func=AF.Copy) if t % 2 else
                                  e.tensor_copy(out=Vt[:, t, :, 0:64],
                                                in_=pb[:, :].rearrange("p (h d) -> p h d", h=8))),
                       r=[pb], w=[(Vt, t)])
                KaT = T(st, "KaT", [128, S], BF16)
                QaT = T(st, "QaT", [128, S], BF16)
                zsm = T(st, "zsm", [64, S], BF16)
                ogm = T(st, "ogm", [64, S], BF16)
                kf = [T(st, "kf%d" % i, [128, 512]) for i in range(2)]
                qf = [T(st, "qf%d" % i, [128, 512]) for i in range(2)]
                sqm = [T(st, "sqm%d" % i, [128, 512]) for i in range(2)]
                kmT = T(st, "kmT", [128, 16])
                shr = T(st, "shr", [64, 512])
                km2 = T(st, "km2", [64, NG + 1])
                gm4 = T(st, "gm4", [128, 4, 16])
                top84 = T(st, "top84", [128, 4, 8])
                mbt4 = [T(st, "mbt4_%d" % i, [128, 4, 16]) for i in range(2)]
                rden = T(st, "rden", [128, 512])
                t1 = [T(st, "t1_%d" % i, [64, 512]) for i in range(2)]
                PT = [T(st, "PT%d" % i, [128, 512], BF16) for i in range(4)]
                nS = [0]
                op("dve", lambda e: e.memset(KaT[0:64, :], 0.0), w=[KaT])
                op("dve", lambda e: e.memset(QaT[0:64, :], 0.0), w=[QaT])
                op("dve", lambda e: e.memset(KaT[32:33, :], 1.0), w=[KaT])
                for n in range(NB):
                    op("dve", lambda e: e.tensor_copy(out=KaT[0:16, n * 256:(n + 1) * 256],
                                                      in_=ident[0:16, n:n + 1].to_broadcast([16, 256])),
                       r=[C], w=[KaT])
                npt = 0
                for h in range(8):
                    op("dve", lambda e: e.tensor_scalar(out=R(kmT[:]), in0=ident[:, 0:16], scalar1=0.0, scalar2=None,
                                                        op0=ALU.mult), r=[C], w=[kmT])
                    def projK(g):
                        pb = PS[g % 2]
                        col = 512 + h * 64
                        for c in range(8):
                            mm(pb[64:128, :], Wmb[:, c, col:col + 64], hT[:, c, g * 512:(g + 1) * 512],
                               start=(c == 0), stop=(c == 7), r=[(Wmb, c), (hT, g)], w=[pb], inc=(c == 7))

                    def postK(g):
                        gs = slice(g * 512, (g + 1) * 512)
                        pb = PS[g % 2]
                        kft = kf[g % 2]
                        op("act", lambda e: e.activation(out=kft[64:128, :], in_=pb[64:128, :], func=AF.Copy),
                           r=[pb], w=[kft])
                        op("act", lambda e: e.activation(out=KaT[64:128, gs], in_=pb[64:128, :], func=AF.Copy),
                           r=[pb], w=[(KaT, g)])
                        op("dve", lambda e: e.tensor_reduce(out=R(kmT[64:128, 2 * g:2 * g + 2]),
                                                            in_=kft[64:128, :].rearrange("p (b t) -> p b t", b=2),
                                                            axis=AX.X, op=ALU.add), r=[kft], w=[kmT])
                        sqt = sqm[g % 2]
                        op("dve", lambda e: e.tensor_tensor(out=sqt[64:128, :], in0=kft[64:128, :],
                                                            in1=kft[64:128, :], op=ALU.mult), r=[kft], w=[sqt])
                        pr = PS[2]
                        mm(pr[32:33, :], ones[64:128, 0:1], sqt[64:128, :], r=[C, sqt], w=[pr])
                        op("dve", lambda e: e.tensor_reduce(out=km2[32:33, g:g + 1], in_=pr[32:33, :], axis=AX.X,
                                                            op=ALU.max), r=[pr], w=[km2])
                    projK(0)
                    for g in range(NG):
                        if g + 1 < NG:
                            projK(g + 1)
                        postK(g)
                    op("dve", lambda e: e.tensor_scalar(out=R(kmT[64:128, :]), in0=kmT[64:128, :], scalar1=1.0 / 256,
                                                        scalar2=None, op0=ALU.mult), r=[kmT], w=[kmT])
                    op("dve", lambda e: e.tensor_reduce(out=km2[32:33, NG:NG + 1], in_=km2[32:33, 0:NG], axis=AX.X,
                                                        op=ALU.max), r=[km2], w=[km2])
                    for g in range(NG):
                        gs = slice(g * 512, (g + 1) * 512)
                        pb = PS[g % 2]
                        col = 1536 + h * 64
                        for c in range(8):
                            mm(pb[0:64, :], Wmb[:, c, col:col + 64], hT[:, c, gs], start=(c == 0), stop=(c == 7),
                               r=[(Wmb, c), (hT, g)], w=[pb], inc=(c == 7))
                        op("act", lambda e: e.activation(out=zsm[:, gs], in_=pb[0:64, :], func=AF.Silu), r=[pb],
                           w=[(zsm, g)])

                    def projQ(g):
                        pb = PS[g % 3]
                        col = h * 64
                        for c in range(8):
                            mm(pb[64:128, :], Wmb[:, c, col:col + 64], hT[:, c, g * 512:(g + 1) * 512],
                               start=(c == 0), stop=(c == 7), r=[(Wmb, c), (hT, g)], w=[pb], inc=(c == 7))

                    def postQ1(g):
                        gs = slice(g * 512, (g + 1) * 512)
                        pb = PS[g % 3]
                        qft = qf[g % 2]
                        op("act", lambda e: e.activation(out=R(qft[64:128, :]), in_=pb[64:128, :], func=AF.Identity,
                                                         scale=0.125), r=[pb], w=[qft])
                        op("act", lambda e: e.activation(out=QaT[64:128, gs], in_=pb[64:128, :], func=AF.Identity,
                                                         scale=0.125), r=[pb], w=[(QaT, g)])
                        sqt = sqm[g % 2]
                        op("dve", lambda e: e.tensor_tensor(out=sqt[64:128, :], in0=qft[64:128, :],
                                                            in1=qft[64:128, :], op=ALU.mult), r=[qft], w=[sqt])
                        pgt = PS[4]
                        for tt in range(4):
                            mm(pgt[:, tt * 16:(tt + 1) * 16], R(qft[64:128, tt * 128:(tt + 1) * 128]), R(kmT[64:128, :]),
                               r=[qft, kmT], w=[pgt], inc=(tt == 3))
                        pr = PS[5]
                        mm(pr[32:33, :], ones[64:128, 0:1], sqt[64:128, :], r=[C, sqt], w=[pr])
                        op("dve", lambda e: e.tensor_tensor(out=gm4[:].rearrange("p a b -> p (a b)"), in0=pgt[:, 0:64],
                                                            in1=C[:, C_PB4 + g * 64:C_PB4 + (g + 1) * 64], op=ALU.add),
                           r=[pgt, C], w=[gm4])
                        op("act", lambda e: e.activation(out=shr[32:33, :], in_=pr[32:33, :], func=AF.Sqrt,
                                                         scale=km2[32:33, NG:NG + 1]), r=[pr, km2], w=[shr])
                        op("act", lambda e: e.activation(out=QaT[32:33, gs], in_=shr[32:33, :], func=AF.Identity,
                                                         scale=-1.0), r=[shr], w=[(QaT, g)])
                        for tt in range(4):
                            op("dve", lambda e: e.max(out=top84[:, tt, :], in_=gm4[:, tt, :]), r=[gm4],
                               w=[(top84, tt)])
                        mb_ = mbt4[g % 2]
                        for tt in range(4):
                            op("dve", lambda e: e.tensor_scalar(out=mb_[:, tt, :], in0=gm4[:, tt, :],
                                                                scalar1=top84[:, tt, 2:3], scalar2=NEG,
                                                                op0=ALU.is_lt, op1=ALU.mult),
                               r=[gm4, (top84, tt)], w=[(mb_, tt)])
                        for t2 in range(2):
                            own = 2 * g + t2
                            op("dve", lambda e: e.memset(mb_[:, 2 * t2:2 * t2 + 2, own:own + 1], 0.0),
                               r=[(mb_, 2 * t2), (mb_, 2 * t2 + 1)], w=[(mb_, 2 * t2), (mb_, 2 * t2 + 1)])

                    def postQ2(g):
                        gs = slice(g * 512, (g + 1) * 512)
                        mb_ = mbt4[g % 2]
                        pt_ = PS[3]
                        for tt in range(4):
                            tr(pt_[0:16, tt * 128:(tt + 1) * 128], mb_[:, tt, :], ident, r=[(mb_, tt), C], w=[pt_])
                        op("act", lambda e: e.activation(out=QaT[0:16, gs], in_=pt_[0:16, 0:512], func=AF.Copy),
                           r=[pt_], w=[(QaT, g)])
                    projQ(0)
                    if NG > 1:
                        projQ(1)
                    for g in range(NG):
                        postQ1(g)
                        if g + 2 < NG:
                            projQ(g + 2)
                        if g > 0:
                            postQ2(g - 1)
                    postQ2(NG - 1)
                    LOOK = 3
                    for jq in range(NB // 2):
                        q0 = jq * 512
                        pO = PS[6 + jq % 2]
                        ntile = 4 * jq + 4
                        qk_ = (QaT, jq)

                        def emitS(kt):
                            pS = PS[nS[0] % 4]
                            nS[0] += 1
                            d = kt - 4 * jq
                            c0 = 0 if d < 0 else 128 * d
                            mm(pS[:, c0:512], KaT[:, kt * 128:(kt + 1) * 128], QaT[:, q0 + c0:q0 + 512], start=True,
                               stop=(d < 0), r=[(KaT, kt // 4), qk_], w=[pS], inc=(d < 0))
                            if d >= 0:
                                mm(pS[:, c0:c0 + 128], identb[:], trib[:], start=False, stop=True, r=[identb, trib],
                                   w=[pS])
                            return pS, c0

                        pendq = [emitS(k_) for k_ in range(min(LOOK, ntile))]
                        for kt in range(ntile):
                            pS, c0 = pendq.pop(0)
                            if kt + LOOK < ntile:
                                pendq.append(emitS(kt + LOOK))
                            ptile = PT[npt % 4]
                            npt += 1
                            op("act", lambda e: e.activation(out=ptile[:, c0:512], in_=pS[:, c0:512], func=AF.Exp),
                               r=[pS], w=[ptile])
                            mm(pO[0:65, c0:512], Vt[:, kt, h, :], ptile[:, c0:512], start=(kt == 0),
                               stop=(kt == ntile - 1), r=[(Vt, kt), ptile], w=[pO], inc=(kt == ntile - 1))
                        op("dve", lambda e: e.reciprocal(out=R(rden[64:65, :]), in_=pO[64:65, 0:512]), r=[pO],
                           w=[rden])
                        pB = PS[5]
                        mm(pB[0:64, 0:512], R(onesr[64:65, 0:64]), R(rden[64:65, :]), r=[onesr, rden], w=[pB])
                        tt1 = t1[jq % 2]
                        op("dve", lambda e: e.tensor_tensor(out=tt1[:], in0=pO[0:64, 0:512],
                                                            in1=zsm[:, q0:q0 + 512], op=ALU.mult),
                           r=[pO, (zsm, jq)], w=[tt1])
                        op("dve", lambda e: e.tensor_tensor(out=ogm[:, q0:q0 + 512], in0=tt1[:], in1=pB[0:64, 0:512],
                                                            op=ALU.mult), r=[tt1, pB], w=[(ogm, jq)])
                    if "omb" in dbg and l == 0 and h == 0:
                        d = dbg_out("omb", [64, S])
                        tmp = T(st, "dbgomb", [64, S])
                        op("dve", lambda e: e.tensor_copy(out=tmp[:], in_=ogm[:]), r=[ogm], w=[tmp])
                        dma(d[:, :], tmp[:], r=[tmp])
                    dma(ogmb_d[h, :, :], ogm[:], r=[ogm], w=[kogmb])
                kb.barrier()

            for st in phase("fin"):
                Wpdn = T(st, "Wpdn", [128, 4, 1024], BF16)
                Wpmb = T(st, "Wpmb", [64, 8, 1024], BF16)
                Wout = T(st, "Wout", [128, 8, 1024], BF16)
                Wmg = T(st, "Wmg", [128, 8, 2048], BF16)
                dma(Wpdn[:], wpdn_d[l], w=[Wpdn], q="pool")
                dma(Wpmb[:], wpmb_d[l], w=[Wpmb], q="pool")
                for c in range(8):
                    dma(Wmg[:, c, :], wmg_d[l, :, c, :], w=[(Wmg, c)], q="pool")
                for c in range(8):
                    dma(Wout[:, c, :], wout_d[l, :, c, :], w=[(Wout, c)], q="pool")
                OGD = [T(st, "OGD%d" % i, [128, 4, 512], BF16) for i in range(2)]
                OGM = [T(st, "OGM0", [64, 8, 512], BF16)] * 2
                mixT = T(st, "mixT", [128, 8, 512], BF16)
                gd = [T(st, "gd%d" % i, [128, 512]) for i in range(2)]
                gmm = [T(st, "gmm%d" % i, [128, 512]) for i in range(2)]
                u1 = [T(st, "u1_%d" % i, [128, 512]) for i in range(2)]
                u2 = [T(st, "u2_%d" % i, [128, 512]) for i in range(2)]
                xr = [T(st, "xr%d" % i, [128, 1024]) for i in range(2)]
                res = [T(st, "res%d" % i, [128, 512]) for i in range(2)]
                junk2 = T(st, "junk2", [128, 512], BF16)
                GPl = T(st, "GPl", [128, 1024])
                dma(GPl[:], gp_d[l], r=[(kgp, l)], w=[GPl])
                ss2 = T(st, "ss2", [128, NT, 2])
                rs2 = T(st, "rs2", [128, NT])
                def load_og(g):
                    gs_ = slice(g * 512, (g + 1) * 512)
                    dma(OGD[g % 2][:], ogdn_d[:, :, gs_].rearrange("h p s -> p h s"), r=[(kogdn, g)], w=[OGD[g % 2]])
                    dma(OGM[g % 2][:], ogmb_d[:, :, gs_].rearrange("h p s -> p h s"), r=[kogmb], w=[OGM[g % 2]])

                load_og(0)
                for g in range(NG):
                    gs = slice(g * 512, (g + 1) * 512)
                    ogd, ogmm = OGD[g % 2], OGM[g % 2]
                    for d_ in range(8):
                        ds_ = slice(d_ * 128, (d_ + 1) * 128)
                        pa, pb, pc, pd = PS[0 + 4 * (d_ % 2)], PS[1 + 4 * (d_ % 2)], PS[2 + 4 * (d_ % 2)], PS[3 + 4 * (d_ % 2)]
                        for h in range(4):
                            mm(pa[:, :], Wpdn[:, h, ds_], ogd[:, h, :], start=(h == 0), stop=(h == 3),
                               r=[Wpdn, ogd], w=[pa], inc=(h == 3))
                        for h in range(8):
                            mm(pb[:, :], Wpmb[:, h, ds_], ogmm[:, h, :], start=(h == 0), stop=(h == 7),
                               r=[Wpmb, ogmm], w=[pb], inc=(h == 7))
                        for c in range(8):
                            mm(pc[:, :], Wmg[:, c, ds_], hT[:, c, gs], start=(c == 0), stop=(c == 7),
                               r=[(Wmg, c), (hT, g)], w=[pc], inc=(c == 7))
                        for c in range(8):
                            mm(pd[:, :], Wmg[:, c, 1024 + d_ * 128:1024 + (d_ + 1) * 128], hT[:, c, gs],
                               start=(c == 0), stop=(c == 7), r=[(Wmg, c), (hT, g)], w=[pd], inc=(c == 7))
                        gdt, gmt, u1t, u2t = gd[d_ % 2], gmm[d_ % 2], u1[d_ % 2], u2[d_ % 2]
                        op("act", lambda e: e.activation(out=gdt[:], in_=pc[:, :], func=AF.Sigmoid), r=[pc], w=[gdt])
                        op("act", lambda e: e.activation(out=gmt[:], in_=pd[:, :], func=AF.Sigmoid), r=[pd], w=[gmt])
                        op("dve", lambda e: e.tensor_tensor(out=u1t[:], in0=pa[:, :], in1=gdt[:], op=ALU.mult),
                           r=[pa, gdt], w=[u1t])
                        op("dve", lambda e: e.tensor_tensor(out=u2t[:], in0=pb[:, :], in1=gmt[:], op=ALU.mult),
                           r=[pb, gmt], w=[u2t])
                        op("dve", lambda e: e.tensor_tensor(out=mixT[:, d_, :], in0=u1t[:], in1=u2t[:], op=ALU.add),
                           r=[u1t, u2t], w=[(mixT, d_)])
                    if g + 1 < NG:
                        load_og(g + 1)
                    for tt in range(4):
                        t = g * 4 + tt
                        xrt = xr[t % 2]
                        dma(xrt[:], xin_d[t * 128:(t + 1) * 128, :], r=[(xin_k, t)], w=[xrt])
                        for hf in range(2):
                            pb = PS[(t % 2) * 2 + hf]
                            for d_ in range(8):
                                mm(pb[:, :], mixT[:, d_, tt * 128:(tt + 1) * 128], Wout[:, d_, hf * 512:(hf + 1) * 512],
                                   start=(d_ == 0), stop=(d_ == 7), r=[(mixT, d_), (Wout, d_)], w=[pb],
                                   inc=(d_ == 7))
                            op("act", lambda e: e.activation(out=junk2[:], in_=pb[:, :], func=AF.Square,
                                                             accum_out=ss2[:, t, hf:hf + 1]), r=[pb],
                               w=[junk2, (ss2, t)])
                        op("dve", lambda e: e.tensor_tensor(out=rs2[:, t:t + 1], in0=ss2[:, t, 0:1],
                                                            in1=ss2[:, t, 1:2], op=ALU.add), r=[(ss2, t)],
                           w=[(rs2, t)])
                        op("act", lambda e: e.activation(out=rs2[:, t:t + 1], in_=rs2[:, t:t + 1], func=AF.Sqrt,
                                                         scale=1.0 / D_MODEL, bias=epsT[:]), r=[(rs2, t), epsT],
                           w=[(rs2, t)])
                        op("dve", lambda e: e.reciprocal(out=rs2[:, t:t + 1], in_=rs2[:, t:t + 1]), r=[(rs2, t)],
                           w=[(rs2, t)])
                        for hf in range(2):
                            pb = PS[(t % 2) * 2 + hf]
                            hs = slice(hf * 512, (hf + 1) * 512)
                            rt = res[hf]
                            op("dve", lambda e: e.scalar_tensor_tensor(out=rt[:], in0=pb[:, :],
                                                                       scalar=rs2[:, t:t + 1], in1=GPl[:, hs],
                                                                       op0=ALU.mult, op1=ALU.mult),
                               r=[pb, (rs2, t), GPl], w=[rt])
                            op("dve", lambda e: e.tensor_tensor(out=xrt[:, hs], in0=rt[:], in1=xrt[:, hs],
                                                                op=ALU.add), r=[rt, (xrt, hf)], w=[(xrt, hf)])
                        dma(xout_d[t * 128:(t + 1) * 128, :], xrt[:], r=[xrt], w=[(xout_k, t)], q="pool")
                kb.barrier()
          except _Stop:
            kb.barrier()
            curgen[0].close()
            break
        kb.finish()
        stuck = kb.simulate()
        print("deadlock check:", stuck if stuck else "ok")
        print("instructions:", kb.ninstr, "counts", kb.count, "dmas", kb.ndmaq)
    return nc, dbg_d


def _pc(w):
    sh = w.shape
    w = w.reshape(sh[:-2] + (8, 128, sh[-1]))
    return np.ascontiguousarray(np.swapaxes(w, -3, -2))


def host_layout(inp):
    f = lambda a: np.ascontiguousarray(np.asarray(a, dtype=np.float32))
    w_in = f(inp["w_in"])
    depth = w_in.shape[0]
    shared = {
        "wada": _pc(f(inp["w_ada"])),
        "bada": f(inp["b_ada"]).reshape(depth, 1, 3072),
        "gpre": f(inp["g_pre"]).reshape(depth, 1, 1024),
        "gpost": f(inp["g_post"]).reshape(depth, 1, 1024),
        "wdn": _pc(w_in[:, :, 0:2048]),
        "wba": _pc(w_in[:, :, 2048:2056]),
        "wmb": _pc(w_in[:, :, 2056:4104]),
        "wmg": _pc(w_in[:, :, 4104:6152]),
        "convw": np.ascontiguousarray(
            f(inp["conv_w"]).transpose(0, 2, 1).reshape(depth, 12, 128, 4).transpose(0, 2, 1, 3)
        ).reshape(depth, 128, 48),
        "alog": np.ascontiguousarray(np.broadcast_to(f(inp["a_log"])[:, None, :], (depth, 128, 4))),
        "dtb": np.ascontiguousarray(np.broadcast_to(f(inp["dt_bias"])[:, None, :], (depth, 128, 4))),
        "dng": f(inp["dn_norm_g"]).reshape(depth, 128, 1),
        "wpdn": np.ascontiguousarray(f(inp["w_proj_dn"]).reshape(depth, 4, 128, 1024).transpose(0, 2, 1, 3)),
        "wpmb": np.ascontiguousarray(f(inp["w_proj_mb"]).reshape(depth, 8, 64, 1024).transpose(0, 2, 1, 3)),
        "wout": _pc(f(inp["w_out"])),
        "consts": make_consts(),
    }
    x = f(inp["x"])
    c = f(inp["c"])
    maps = []
    for b in range(x.shape[0]):
        m = dict(shared)
        m["x"] = x[b]
        m["cT"] = np.ascontiguousarray(c[b].reshape(8, 128).T)
        maps.append(m)
    return maps


_CACHE = {}


def kernel(**inputs):
    x = np.asarray(inputs["x"])
    B, S, _ = x.shape
    depth = np.asarray(inputs["w_in"]).shape[0]
    key = (S, depth)
    if key not in _CACHE:
        _CACHE[key] = build(S=S, DEPTH=depth)[0]
    nc = _CACHE[key]
    maps = host_layout(inputs)
    res = run_bass_kernel_spmd(nc, maps, core_ids=list(range(B)))
    return np.stack([np.asarray(r["y"], dtype=np.float32) for r in res.results], axis=0)
```

```python
from contextlib import ExitStack

import numpy as np
import concourse.bass as bass
import concourse.mybir as mybir
from concourse.bass_utils import run_bass_kernel_spmd

F32 = mybir.dt.float32
BF16 = mybir.dt.bfloat16
F32R = mybir.dt.float32r


def R(ap):
    return ap.bitcast(F32R)
AF = mybir.ActivationFunctionType
ALU = mybir.AluOpType
AX = mybir.AxisListType

D_MODEL = 1024
NEG = -30000.0
EPS = 1e-6


class SV:
    def __init__(self, tile, j, n=1):
        self.tile, self.sub = tile, j
        self.ap = tile[:, j * 128:(j + n) * 128]

    def __getitem__(self, idx):
        return self.ap[idx]


class KB:
    NRING = {"sp": 6, "pool": 4}

    def __init__(self, nc, stack):
        self.nc = nc
        self.eng = {"pe": nc.tensor, "act": nc.scalar, "dve": nc.vector,
                    "pool": nc.gpsimd, "sp": nc.sync}
        self.sem = {}
        for e in ("pe", "act", "dve", "pool"):
            self.sem[e] = stack.enter_context(nc.semaphore("s_" + e))
        self.ring = {q: [stack.enter_context(nc.semaphore("s_dma_%s%d" % (q, i))) for i in range(n)]
                     for q, n in self.NRING.items()}
        self.ndmaq = {q: 0 for q in self.NRING}
        self.count = {e: 0 for e in ("pe", "act", "dve", "pool")}
        self.waited = {}
        self.track = {}
        self.ninstr = 0
        self.streams = {e: [] for e in self.eng}
        self.psum_ids = set()

    def _semof(self, dep):
        if dep[0] == "e":
            return self.sem[dep[1]], ("e", dep[1])
        return self.ring[dep[1][0]][dep[1][1]], ("d", dep[1])

    def _wait(self, e, dep):
        sem, sk = self._semof(dep)
        val = dep[2]
        k = (e, sk)
        if self.waited.get(k, 0) >= val:
            return
        self.eng[e].wait_ge(sem, val)
        self.streams[e].append(("wait", sk, val))
        self.ninstr += 1
        self.waited[k] = val

    @staticmethod
    def _keys(items):
        out = []
        for it in items:
            if isinstance(it, SV):
                out.append((id(it.tile), it.sub))
            elif isinstance(it, tuple):
                out.append((id(it[0]), it[1]))
            else:
                out.append((id(it), None))
        return out

    def _conflicts(self, key):
        tid, sub = key
        d = self.track.get(tid)
        if d is None:
            return []
        if sub is None:
            return list(d.values())
        res = []
        if sub in d:
            res.append(d[sub])
        if None in d:
            res.append(d[None])
        return res

    def _entry(self, key):
        tid, sub = key
        d = self.track.setdefault(tid, {})
        if sub is None:
            ent = {"w": [], "r": []}
            for v in d.values():
                ent["w"] += v["w"]
                ent["r"] += v["r"]
            d.clear()
            d[None] = ent
            return ent
        if sub not in d:
            ent = {"w": [], "r": []}
            if None in d:
                ent["w"] = list(d[None]["w"])
                ent["r"] = list(d[None]["r"])
            d[sub] = ent
        return d[sub]

    def _deps(self, e, reads, writes):
        deps = []
        for k in reads:
            for ent in self._conflicts(k):
                for w in ent["w"]:
                    deps.append((w, "raw"))
        for k in writes:
            for ent in self._conflicts(k):
                for w in ent["w"]:
                    deps.append((w, "waw"))
                for r in ent["r"]:
                    deps.append((r, "war"))
        out = []
        for dep, kind in deps:
            if dep[0] == "e" and dep[1] == e:
                if e == "pe":
                    continue
            out.append(dep)
        return out

    @staticmethod
    def _prune(lst):
        best = {}
        for d in lst:
            kk = (d[0], d[1])
            if kk not in best or best[kk][2] < d[2]:
                best[kk] = d
        return list(best.values())

    def _record(self, me, reads, writes):
        for k in reads:
            ent = self._entry(k)
            ent["r"].append(me)
            if len(ent["r"]) > 16:
                ent["r"] = self._prune(ent["r"])
        for k in writes:
            ent = self._entry(k)
            ent["w"] = [me]
            ent["r"] = []

    def _rw(self, r, w):
        reads, writes = [], []
        for k in self._keys(r):
            if k[0] in self.psum_ids:
                writes.append((k[0], None))
            else:
                reads.append(k)
        for k in self._keys(w):
            writes.append((k[0], None) if k[0] in self.psum_ids else k)
        return reads, writes

    def op(self, e, fn, r=(), w=(), inc=True):
        reads, writes = self._rw(r, w)
        for dep in self._deps(e, reads, writes):
            self._wait(e, dep)
        ins = fn(self.eng[e])
        self.ninstr += 1
        if inc:
            self.count[e] += 1
            ins.then_inc(self.sem[e], 1)
            self.streams[e].append(("inc", ("e", e), 1))
            me = ("e", e, self.count[e])
        else:
            me = ("e", e, self.count[e] + 1)
        self._record(me, reads, writes)
        return ins

    def dma(self, out, in_, r=(), w=(), q="sp", **kw):
        e = q
        reads = self._keys(r)
        writes = self._keys(w)
        k = self.ndmaq[q]
        nr = self.NRING[q]
        slot = k % nr
        gen = k // nr
        for dep in self._deps(e, reads, writes):
            self._wait(e, dep)
        if gen > 0:
            self._wait(e, ("d", (q, slot), 16 * gen))
        ins = self.eng[e].dma_start(out=out, in_=in_, **kw)
        ins.then_inc(self.ring[q][slot], 16)
        self.streams[e].append(("inc", ("d", (q, slot)), 16))
        self.ndmaq[q] += 1
        self.ninstr += 1
        me = ("d", (q, slot), 16 * (gen + 1))
        self._record(me, reads, writes)
        return ins

    def _alldma(self):
        out = []
        for q, nr in self.NRING.items():
            n = self.ndmaq[q]
            for slot in range(nr):
                cnt = (n - 1 - slot) // nr + 1 if n > slot else 0
                if cnt > 0:
                    out.append(("d", (q, slot), 16 * cnt))
        return out

    def barrier(self):
        for e in ("pe", "act", "dve", "pool", "sp"):
            for o in ("pe", "act", "dve", "pool"):
                if o != e and self.count[o] > 0:
                    self._wait(e, ("e", o, self.count[o]))
            for dep in self._alldma():
                self._wait(e, dep)
        self.track = {}

    def finish(self):
        for dep in self._alldma():
            self._wait("sp", dep)

    def simulate(self):
        sems = {}
        pc = {e: 0 for e in self.streams}
        progress = True
        while progress:
            progress = False
            for e, st in self.streams.items():
                while pc[e] < len(st):
                    kind, sk, val = st[pc[e]]
                    if kind == "wait":
                        if sems.get(sk, 0) < val:
                            break
                    else:
                        sems[sk] = sems.get(sk, 0) + val
                    pc[e] += 1
                    progress = True
        stuck = {e: (pc[e], len(st), st[pc[e]], sems.get(st[pc[e]][1], 0)) for e, st in self.streams.items()
                 if pc[e] < len(st)}
        return stuck


C_IDENT, C_U, C_MBD, C_MOFF, C_TRI, C_ONES, C_PB = 0, 128, 256, 384, 512, 640, 768
C_PB4 = 768
NCONST = 768 + 512


def make_consts():
    c = np.zeros((128, NCONST), np.float32)
    i = np.arange(128)
    c[:, C_IDENT:C_IDENT + 128] = np.eye(128)
    c[:, C_U:C_U + 128] = (i[:, None] <= i[None, :])
    c[:, C_MBD:C_MBD + 128] = (i[:, None] < i[None, :]) & ((i[:, None] // 64) == (i[None, :] // 64))
    c[:, C_MOFF:C_MOFF + 128] = (i[:, None] < 64) & (i[None, :] >= 64)
    c[:, C_TRI:C_TRI + 128] = np.where(i[:, None] <= i[None, :], 0.0, NEG)
    c[:, C_ONES:C_ONES + 128] = 1.0
    pb = np.zeros((16, 16), np.float32)
    for own in range(16):
        pb[own, own:] = -1e30
    for g in range(8):
        for tt in range(4):
            own = (4 * g + tt) // 2
            c[:, C_PB4 + g * 64 + tt * 16:C_PB4 + g * 64 + (tt + 1) * 16] = pb[own][None, :]
    return c


def build(S=4096, DEPTH=2, LS=2, dbg=None, phases=("p1", "dn", "mb", "fin"), stop=0, JUNK=0):
    dbg = dbg or set()
    NT = S // 128
    NG = S // 512
    NB = S // 256
    nc = bass.Bass("TRN2", target_bir_lowering=False)

    def din(name, shape, dt=F32):
        return nc.dram_tensor(name, shape, dt, kind="ExternalInput").ap()

    x_d = din("x", [S, 1024])
    cT_d = din("cT", [128, 8])
    wada_d = din("wada", [DEPTH, 128, 8, 3072])
    bada_d = din("bada", [DEPTH, 1, 3072])
    gpre_d = din("gpre", [DEPTH, 1, 1024])
    gpost_d = din("gpost", [DEPTH, 1, 1024])
    wdn_d = din("wdn", [DEPTH, 128, 8, 2048])
    wba_d = din("wba", [DEPTH, 128, 8, 8])
    wmb_d = din("wmb", [DEPTH, 128, 8, 2048])
    wmg_d = din("wmg", [DEPTH, 128, 8, 2048])
    convw_d = din("convw", [DEPTH, 128, 48])
    alog_d = din("alog", [DEPTH, 128, 4])
    dtb_d = din("dtb", [DEPTH, 128, 4])
    dng_d = din("dng", [DEPTH, 128, 1])
    wpdn_d = din("wpdn", [DEPTH, 128, 4, 1024])
    wpmb_d = din("wpmb", [DEPTH, 64, 8, 1024])
    wout_d = din("wout", [DEPTH, 128, 8, 1024])
    consts_d = din("consts", [128, NCONST])
    y_d = nc.dram_tensor("y", [S, 1024], F32, kind="ExternalOutput").ap()
    xmid_d = nc.dram_tensor("xmid", [S, 1024], F32, kind="Internal").ap()
    ogdn_d = nc.dram_tensor("ogdn", [4, 128, S], BF16, kind="Internal").ap()
    ogmb_d = nc.dram_tensor("ogmb", [8, 64, S], BF16, kind="Internal").ap()
    dbg_d = {}

    class _K:
        pass
    kx, kmid, ky, kogdn, kogmb = _K(), _K(), _K(), _K(), _K()

    def dbg_out(name, shape):
        dbg_d[name] = nc.dram_tensor("dbg_" + name, shape, F32, kind="ExternalOutput").ap()
        return dbg_d[name]

    with ExitStack() as gst:
        kb = KB(nc, gst)
        op, dma = kb.op, kb.dma
        gst.enter_context(nc.allow_low_precision("float32r (1-pass PE) operands for non-critical fp32 matmuls"))

        def mm(out, lhsT, rhs, start=True, stop=True, r=(), w=(), inc=True):
            return op("pe", lambda e: e.matmul(out, lhsT=lhsT, rhs=rhs, start=start, stop=stop),
                      r=r, w=w, inc=inc)

        def tr(out, in_, ident, r=(), w=()):
            return op("pe", lambda e: e.transpose(out=out, in_=in_, identity=ident), r=r, w=w)

        uid = [0]

        def T(st, name, shape, dt=F32):
            uid[0] += 1
            return st.enter_context(nc.sbuf_tensor("sb%d_%s" % (uid[0], name), shape, dt))

        class _Stop(Exception):
            pass

        def ck(level):
            if stop == level:
                raise _Stop()

        curgen = [None]

        def phase(name):
            if name in phases:
                st_ = ExitStack()
                curgen[0] = st_
                yield st_
                st_.close()

        PS = [gst.enter_context(nc.psum_tensor("ps%d" % i, [128, 512], F32)) for i in range(8)]
        kb.psum_ids = {id(p) for p in PS}
        C = T(gst, "consts", [128, NCONST])
        dma(C[:], consts_d[:, :], w=[C])
        ident = C[:, C_IDENT:C_IDENT + 128]
        U = C[:, C_U:C_U + 128]
        Mbd = C[:, C_MBD:C_MBD + 128]
        Moff = C[:, C_MOFF:C_MOFF + 128]
        ones = C[:, C_ONES:C_ONES + 128]
        identb = T(gst, "identb", [128, 128], BF16)
        trib = T(gst, "trib", [128, 128], BF16)
        epsT = T(gst, "epsT", [128, 1])
        op("dve", lambda e: e.tensor_copy(out=identb[:], in_=ident), r=[C], w=[identb])
        op("dve", lambda e: e.tensor_copy(out=trib[:], in_=C[:, C_TRI:C_TRI + 128]), r=[C], w=[trib])
        op("dve", lambda e: e.memset(epsT[:], EPS), w=[epsT])
        onesb = T(gst, "onesb", [128, 1], BF16)
        op("dve", lambda e: e.tensor_copy(out=onesb[:], in_=ones[:, 0:1]), r=[C], w=[onesb])
        onesr = T(gst, "onesr", [128, 128])
        op("dve", lambda e: e.tensor_copy(out=R(onesr[:]), in_=ones), r=[C], w=[onesr])
        AB = [T(gst, "AB%d" % l, [128, 16]) for l in range(DEPTH)]
        gp_d = nc.dram_tensor("gp_scr", [DEPTH, 128, 1024], F32, kind="Internal").ap()
        kgp = _K()

        with ExitStack() as st:
            cT = T(st, "cT", [128, 8])
            sc = T(st, "sc", [128, 8])
            dma(cT[:], cT_d[:, :], w=[cT])
            op("act", lambda e: e.activation(out=sc[:], in_=cT[:], func=AF.Silu), r=[cT], w=[sc])
            wa = [T(st, "wa%d" % i, [128, 8, 512]) for i in range(2)]
            row = T(st, "row", [1, 3072])
            bada = T(st, "bada", [1, 3072])
            gpr = T(st, "gpr", [1, 1024])
            gpo = T(st, "gpo", [1, 1024])
            arow = T(st, "arow", [1, 1024])
            gprow = T(st, "gprow", [1, 1024])
            gptmp = T(st, "gptmp", [128, 1024])
            nwa = 0
            for l in range(DEPTH):
                dma(bada[:], bada_d[l], w=[bada])
                dma(gpr[:], gpre_d[l], w=[gpr])
                dma(gpo[:], gpost_d[l], w=[gpo])
                for cg in range(6):
                    wt = wa[nwa % 2]
                    nwa += 1
                    dma(wt[:], wada_d[l, :, :, cg * 512:(cg + 1) * 512], w=[wt])
                    pb = PS[cg % 2]
                    for c in range(8):
                        mm(pb[0:1, :], sc[:, c:c + 1], wt[:, c, :], start=(c == 0), stop=(c == 7),
                           r=[sc, wt], w=[pb], inc=(c == 7))
                    op("dve", lambda e: e.tensor_tensor(out=row[0:1, cg * 512:(cg + 1) * 512], in0=pb[0:1, :],
                                                        in1=bada[0:1, cg * 512:(cg + 1) * 512], op=ALU.add),
                       r=[pb, bada], w=[(row, cg)])
                op("dve", lambda e: e.scalar_tensor_tensor(out=arow[:], in0=row[0:1, 1024:2048], scalar=1.0,
                                                           in1=gpr[:], op0=ALU.add, op1=ALU.mult),
                   r=[row, gpr], w=[arow])
                op("dve", lambda e: e.tensor_tensor(out=gprow[:], in0=row[0:1, 2048:3072], in1=gpo[:], op=ALU.mult),
                   r=[row, gpo], w=[gprow])
                pc = PS[2]
                for c in range(8):
                    mm(pc[:, c:c + 1], arow[0:1, c * 128:(c + 1) * 128], ones[0:1, 0:1], r=[arow, C], w=[pc], inc=False)
                for c in range(8):
                    mm(pc[:, 8 + c:9 + c], row[0:1, c * 128:(c + 1) * 128], ones[0:1, 0:1], r=[row, C], w=[pc],
                       inc=(c == 7))
                op("dve", lambda e: e.tensor_copy(out=AB[l][:], in_=pc[:, 0:16]), r=[pc], w=[AB[l]])
                for hf in range(2):
                    pg = PS[3 + hf]
                    mm(pg[:, :], ones[0:1, 0:128], gprow[0:1, hf * 512:(hf + 1) * 512], r=[gprow, C], w=[pg])
                    op("act", lambda e: e.activation(out=gptmp[:, hf * 512:(hf + 1) * 512], in_=pg[:, :], func=AF.Copy),
                       r=[pg], w=[(gptmp, hf)])
                dma(gp_d[l], gptmp[:], r=[gptmp], w=[(kgp, l)])
            kb.barrier()

        hT = T(gst, "hT", [128, 8, S], BF16)

        for l in range(DEPTH):
          try:
            xin_d = x_d if l == 0 else xmid_d
            xout_d = y_d if l == DEPTH - 1 else xmid_d
            xin_k = kx if l == 0 else kmid
            xout_k = ky if l == DEPTH - 1 else kmid

            for st in phase("p1"):
                xt = [T(st, "xt%d" % i, [128, 1024]) for i in range(4)]
                xn = [T(st, "xn%d" % i, [128, 1024]) for i in range(3)]
                junk = T(st, "junk", [128, 1024], BF16)
                ss = T(st, "ss", [128, NT])
                rstd = T(st, "rstd", [128, NT])
                def p1_stats(t):
                    xtt, xnt = xt[t % 4], xn[t % 3]
                    dma(xtt[:], xin_d[t * 128:(t + 1) * 128, :], r=[(xin_k, t)], w=[xtt])
                    op("act", lambda e: e.activation(out=junk[:], in_=xtt[:], func=AF.Square,
                                                     accum_out=ss[:, t:t + 1]), r=[xtt], w=[junk, (ss, t)])
                    op("act", lambda e: e.activation(out=rstd[:, t:t + 1], in_=ss[:, t:t + 1], func=AF.Sqrt,
                                                     scale=1.0 / D_MODEL, bias=epsT[:]), r=[(ss, t), epsT],
                       w=[(rstd, t)])
                    op("dve", lambda e: e.reciprocal(out=rstd[:, t:t + 1], in_=rstd[:, t:t + 1]), r=[(rstd, t)],
                       w=[(rstd, t)])
                    op("dve", lambda e: e.tensor_scalar(out=xnt[:], in0=xtt[:], scalar1=rstd[:, t:t + 1],
                                                        scalar2=None, op0=ALU.mult), r=[xtt, (rstd, t)], w=[xnt])

                def p1_transpose(t):
                    xnt = xn[t % 3]
                    for hf in range(2):
                        pb = PS[(t % 2) * 2 + hf]
                        for cc in range(4):
                            c = hf * 4 + cc
                            tr(pb[:, cc * 128:(cc + 1) * 128], xnt[:, c * 128:(c + 1) * 128], ident, r=[xnt, C],
                               w=[pb])
                        for cc in range(4):
                            c = hf * 4 + cc
                            eng = "act" if hf == 0 else "dve"
                            if eng == "act":
                                op("act", lambda e: e.activation(out=hT[:, c, t * 128:(t + 1) * 128],
                                                                 in_=pb[:, cc * 128:(cc + 1) * 128], func=AF.Identity,
                                                                 scale=AB[l][:, c:c + 1], bias=AB[l][:, 8 + c:9 + c]),
                                   r=[pb, AB[l]], w=[(hT, t // 4)])
                            else:
                                op("dve", lambda e: e.tensor_scalar(out=hT[:, c, t * 128:(t + 1) * 128],
                                                                    in0=pb[:, cc * 128:(cc + 1) * 128],
                                                                    scalar1=AB[l][:, c:c + 1],
                                                                    scalar2=AB[l][:, 8 + c:9 + c],
                                                                    op0=ALU.mult, op1=ALU.add),
                                   r=[pb, AB[l]], w=[(hT, t // 4)])

                p1_stats(0)
                for t in range(NT):
                    if t + 1 < NT:
                        p1_stats(t + 1)
                    p1_transpose(t)
                kb.barrier()
            if "hT" in dbg and l == 0:
                with ExitStack() as st:
                    d = dbg_out("hT", [128, 8, S])
                    tmp = T(st, "dbgtmp", [128, 8, S])
                    op("dve", lambda e: e.tensor_copy(out=tmp[:], in_=hT[:]), r=[hT], w=[tmp])
                    dma(d[:, :, :], tmp[:], r=[tmp])
                    kb.barrier()

            for st in phase("dn"):
                Wdn = T(st, "Wdn", [128, 8, 2048], BF16)
                Wba = T(st, "Wba", [128, 8, 8], BF16)
                for c in range(8):
                    dma(Wdn[:, c, :], wdn_d[l, :, c, :], w=[(Wdn, c)], q="pool")
                dma(Wba[:], wba_d[l], w=[Wba], q="pool")
                convw = T(st, "convw", [128, 48])
                alog = T(st, "alog", [128, 4])
                dtb = T(st, "dtb", [128, 4])
                dng = T(st, "dng", [128, 1])
                dma(convw[:], convw_d[l], w=[convw])
                dma(alog[:], alog_d[l], w=[alog])
                dma(dtb[:], dtb_d[l], w=[dtb])
                dma(dng[:], dng_d[l], w=[dng])
                st2 = ExitStack()
                BETA = T(st, "BETA", [128, NT, 4])
                NBETA = T(st, "NBETA", [128, NT, 4])
                GRAW = T(st, "GRAW", [128, NT, 4])
                GC = T(st, "GC", [128, NT, 4])
                NEXPG = T(st, "NEXPG", [128, NT, 4])
                negA = T(st, "negA", [128, 4])
                BG = T(st2, "BG", [128, NT, 8])
                AA = T(st2, "AA", [128, NT, 4])
                AX_ = T(st2, "AXs", [128, NT, 4])
                pbg = PS[0]
                for t in range(NT):
                    for c in range(8):
                        mm(pbg[:, t * 8:(t + 1) * 8], hT[:, c, t * 128:(t + 1) * 128], Wba[:, c, :],
                           start=(c == 0), stop=(c == 7), r=[(hT, t // 4), Wba], w=[pbg], inc=(c == 7))
                op("dve", lambda e: e.tensor_copy(out=BG[:].rearrange("p t e -> p (t e)"), in_=pbg[:, 0:NT * 8]),
                   r=[pbg], w=[BG])
                op("act", lambda e: e.activation(out=BETA[:], in_=BG[:, :, 0:4], func=AF.Sigmoid), r=[BG], w=[BETA])
                op("dve", lambda e: e.tensor_scalar(out=NBETA[:], in0=BETA[:], scalar1=-1.0, scalar2=None,
                                                    op0=ALU.mult), r=[BETA], w=[NBETA])
                for h in range(4):
                    op("dve", lambda e: e.tensor_scalar(out=AA[:, :, h], in0=BG[:, :, 4 + h], scalar1=dtb[:, h:h + 1],
                                                        scalar2=None, op0=ALU.add), r=[BG, dtb], w=[AA])
                op("act", lambda e: e.activation(out=AX_[:], in_=AA[:], func=AF.Abs), r=[AA], w=[AX_])
                op("act", lambda e: e.activation(out=AX_[:], in_=AX_[:], func=AF.Exp, scale=-1.0), r=[AX_], w=[AX_])
                op("dve", lambda e: e.tensor_scalar(out=AX_[:], in0=AX_[:], scalar1=1.0, scalar2=None, op0=ALU.add),
                   r=[AX_], w=[AX_])
                op("act", lambda e: e.activation(out=AX_[:], in_=AX_[:], func=AF.Ln), r=[AX_], w=[AX_])
                op("dve", lambda e: e.scalar_tensor_tensor(out=AA[:], in0=AA[:], scalar=0.0, in1=AX_[:],
                                                           op0=ALU.max, op1=ALU.add), r=[AA, AX_], w=[AA])
                op("act", lambda e: e.activation(out=negA[:], in_=alog[:], func=AF.Exp), r=[alog], w=[negA])
                op("dve", lambda e: e.tensor_scalar(out=negA[:], in0=negA[:], scalar1=-1.0, scalar2=None,
                                                    op0=ALU.mult), r=[negA], w=[negA])
                for h in range(4):
                    op("dve", lambda e: e.tensor_scalar(out=GRAW[:, :, h], in0=AA[:, :, h], scalar1=negA[:, h:h + 1],
                                                        scalar2=None, op0=ALU.mult), r=[AA, negA], w=[GRAW])
                pgc = PS[1]
                mm(pgc[:, 0:NT * 4], U, GRAW[:].rearrange("p t e -> p (t e)"), r=[C, GRAW], w=[pgc])
                op("dve", lambda e: e.tensor_copy(out=GC[:].rearrange("p t e -> p (t e)"), in_=pgc[:, 0:NT * 4]),
                   r=[pgc], w=[GC])
                op("act", lambda e: e.activation(out=NEXPG[:], in_=GC[:], func=AF.Exp), r=[GC], w=[NEXPG])
                op("dve", lambda e: e.tensor_scalar(out=NEXPG[:], in0=NEXPG[:], scalar1=-1.0, scalar2=None,
                                                    op0=ALU.mult), r=[NEXPG], w=[NEXPG])
                if "graw" in dbg and l == 0:
                    d = dbg_out("graw", [128, NT, 4])
                    dma(d[:, :, :], GRAW[:], r=[GRAW])
                    d = dbg_out("beta", [128, NT, 4])
                    dma(d[:, :, :], BETA[:], r=[BETA])

                kb.barrier()
                st2.close()
                ck(1)
                pre = [T(st, "pre%d" % i, [128, 515]) for i in range(3)]
                halo = T(st, "halo", [128, 12, 3])
                op("dve", lambda e: e.memset(halo[:], 0.0), w=[halo])
                qkv = [[T(st, "qkv%d_%d" % (i, j), [128, 512]) for j in range(3)] for i in range(2)]
                zs = [T(st, "zs%d" % i, [128, 512], BF16) for i in range(4)]
                cv = [T(st, "cv0", [128, 512])] * 2
                sq = [T(st, "sq0", [128, 512])] * 2
                oTg = [T(st, "oTg%d" % i, [128, 512]) for i in range(2)]
                ogb = [T(st, "ogb0", [128, 512], BF16)] * 2
                Sst = [[T(st, "S%d_%d" % (h, i), [128, 128]) for i in range(2)] for h in range(4)]
                for h in range(4):
                    op("dve", lambda e: e.tensor_scalar(out=R(Sst[h][0][:]), in0=ident, scalar1=0.0, scalar2=None,
                                                        op0=ALU.mult), r=[C], w=[Sst[h][0]])
                spar = [0, 0, 0, 0]
                NCH = 8
                ALIAS = {"Dm": 0, "tq": 0, "DecT": 1, "Xo": 1, "Pe": 2, "ktok": 3, "Xe": 3, "NoffT": 3, "Po": 4,
                         "B": 5, "BT": 6, "WbdT": 6, "Noff": 7, "Z1": 7, "ExpG": 8, "XTo": 8, "Lm": 9, "XTe": 9}
                scr = []
                for i in range(NCH):
                    wide = T(st, "s%d" % i, [128, 9 * 128])
                    plain = T(st, "s%da" % i, [128, 128])
                    d_ = {n: (SV(wide, j - 1) if j > 0 else plain) for n, j in ALIAS.items()}
                    d_["XPo"] = SV(wide, 0, 2)
                    d_["XPe"] = SV(wide, 2, 2)
                    scr.append(d_)
                OUTN = ["W", "QKdT", "kdec", "qg", "vtok", "kTc"]
                outs = [{n: T(st, "o%d_%s" % (i, n), [128, 128]) for n in OUTN} for i in range(NCH)]
                EGL = T(st, "EGL", [128, NCH])
                Yr = [T(st, "Yr%d" % i, [128, 128]) for i in range(2)]
                Vn = [T(st, "Vn%d" % i, [128, 128]) for i in range(2)]

                def prep8(chains, pump=lambda: None):
                    pump_on = [False]

                    def each(fn):
                        for ci, ch in enumerate(chains):
                            fn(ch, ch["s"], ch["o"], PS[ch["i"]])
                            if ci % 2 == 1 and pump_on[0]:
                                pump(1)

                    def f(ch, s, o, pb):
                        h, n, sl = ch["h"], ch["n"], ch["sl"]
                        kT, vT = ch["kT"], ch["vT"]
                        tr(pb[:, 0:128], kT[:, sl], ident, r=[kT, C], w=[pb])
                        tr(pb[:, 128:256], vT[:, sl], ident, r=[vT, C], w=[pb])
                        mm(pb[:, 256:384], GRAW[:, n, h:h + 1].to_broadcast([128, 128]), U, r=[GRAW, C], w=[pb])
                        op("dve", lambda e: e.tensor_scalar(out=s["Dm"][:], in0=pb[:, 256:384],
                                                            scalar1=GC[:, n, h:h + 1], scalar2=0.0,
                                                            op0=ALU.subtract, op1=ALU.min),
                           r=[pb, GC], w=[s["Dm"]])
                        op("dve", lambda e: e.tensor_copy(out=o["vtok"][:], in_=pb[:, 128:256]), r=[pb],
                           w=[o["vtok"]])
                        op("act", lambda e: e.activation(out=R(s["ktok"][:]), in_=pb[:, 0:128], func=AF.Copy),
                           r=[pb], w=[s["ktok"]])
                        op("act", lambda e: e.activation(out=R(s["ExpG"][:]), in_=pb[:, 256:384], func=AF.Exp),
                           r=[pb], w=[s["ExpG"]])
                    each(f)

                    def f(ch, s, o, pb):
                        op("act", lambda e: e.activation(out=R(s["DecT"][:]), in_=s["Dm"][:], func=AF.Exp),
                           r=[s["Dm"]], w=[s["DecT"]])
                    each(f)

                    def f(ch, s, o, pb):
                        h, n, sl = ch["h"], ch["n"], ch["sl"]
                        kT, qT = ch["kT"], ch["qT"]
                        mm(pb[:, 0:128], R(kT[:, sl]), R(kT[:, sl]), r=[kT], w=[pb], inc=False)
                        mm(pb[:, 128:256], R(kT[:, sl]), R(qT[:, sl]), r=[kT, qT], w=[pb])
                        op("dve", lambda e: e.tensor_tensor(out=R(s["Lm"][:]), in0=pb[:, 0:128], in1=s["DecT"][:],
                                                            op=ALU.mult), r=[pb, s["DecT"]], w=[s["Lm"]])
                        op("dve", lambda e: e.tensor_tensor(out=s["tq"][:], in0=pb[:, 128:256], in1=s["DecT"][:],
                                                            op=ALU.mult), r=[pb, s["DecT"]], w=[s["tq"]])
                        op("dve", lambda e: e.scalar_tensor_tensor(out=R(s["B"][:]), in0=s["Lm"][:],
                                                                   scalar=NBETA[:, n, h:h + 1], in1=Mbd,
                                                                   op0=ALU.mult, op1=ALU.mult),
                           r=[s["Lm"], NBETA, C], w=[s["B"]])
                        op("dve", lambda e: e.scalar_tensor_tensor(out=R(s["Noff"][:]), in0=s["Lm"][:],
                                                                   scalar=BETA[:, n, h:h + 1], in1=Moff,
                                                                   op0=ALU.mult, op1=ALU.mult),
                           r=[s["Lm"], BETA, C], w=[s["Noff"]])
                        op("pool", lambda e: e.tensor_tensor(out=R(o["QKdT"][:]), in0=s["tq"][:], in1=U,
                                                             op=ALU.mult), r=[s["tq"], C], w=[o["QKdT"]])
                        op("act", lambda e: e.activation(out=R(o["kdec"][:]), in_=s["ktok"][:], func=AF.Identity,
                                                         scale=s["DecT"][:, 127:128]), r=[s["ktok"], s["DecT"]],
                           w=[o["kdec"]])
                        op("dve", lambda e: e.tensor_tensor(out=R(o["qg"][:]), in0=qT[:, sl], in1=s["ExpG"][:],
                                                            op=ALU.mult), r=[qT, s["ExpG"]], w=[o["qg"]])
                        op("dve", lambda e: e.tensor_copy(out=R(o["kTc"][:]), in_=kT[:, sl]), r=[kT], w=[o["kTc"]])
                        op("dve", lambda e: e.tensor_copy(out=EGL[:, ch["i"]:ch["i"] + 1],
                                                          in_=s["ExpG"][:, 127:128]),
                           r=[s["ExpG"]], w=[(EGL, ch["i"])])
                    each(f)

                    pump_on[0] = True
                    def f(ch, s, o, pb):
                        tr(pb[:, 256:384], s["B"][:], ident, r=[s["B"], C], w=[pb])
                        op("act", lambda e: e.activation(out=R(s["BT"][:]), in_=pb[:, 256:384], func=AF.Copy),
                           r=[pb], w=[s["BT"]])
                        op("dve", lambda e: e.tensor_tensor(out=R(s["Pe"][:]), in0=s["B"][:], in1=ident,
                                                            op=ALU.add), r=[s["B"], C], w=[s["Pe"]])
                    each(f)

                    def f(ch, s, o, pb):
                        mm(pb[:, 0:128], R(s["BT"][:]), R(s["B"][:]), r=[s["BT"], s["B"]], w=[pb])
                        op("act", lambda e: e.activation(out=R(s["Xo"][:]), in_=pb[:, 0:128], func=AF.Copy),
                           r=[pb], w=[s["Xo"]])
                    each(f)
                    pump()
                    for j in range(1, 6):
                        odd = (j % 2 == 1)
                        Xc, Pc, XTc, XPc = ("Xo", "Pe", "XTo", "XPo") if odd else ("Xe", "Po", "XTe", "XPe")
                        Xn, Pn = ("Xe", "Po") if odd else ("Xo", "Pe")

                        def f(ch, s, o, pb):
                            tr(pb[:, 256:384], s[Xc][:], ident, r=[s[Xc], C], w=[pb])
                            op("act", lambda e: e.activation(out=R(s[XTc][:]), in_=pb[:, 256:384], func=AF.Copy),
                               r=[pb], w=[s[XTc]])
                        each(f)
                        pump()

                        def f(ch, s, o, pb):
                            if j < 5:
                                mm(pb[:, 0:256], R(s[XTc][:]), R(s[XPc][:]), r=[s[XTc], s[Xc], s[Pc]], w=[pb])
                                op("act", lambda e: e.activation(out=R(s[Xn][:]), in_=pb[:, 0:128], func=AF.Copy),
                                   r=[pb], w=[s[Xn]])
                                op("dve", lambda e: e.tensor_tensor(out=R(s[Pn][:]), in0=pb[:, 128:256],
                                                                    in1=s[Pc][:], op=ALU.add),
                                   r=[pb, s[Pc]], w=[s[Pn]])
                            else:
                                mm(pb[:, 128:256], R(s[XTc][:]), R(s[Pc][:]), r=[s[XTc], s[Pc]], w=[pb])
                                op("dve", lambda e: e.tensor_tensor(out=R(s[Pn][:]), in0=pb[:, 128:256],
                                                                    in1=s[Pc][:], op=ALU.add),
                                   r=[pb, s[Pc]], w=[s[Pn]])
                        each(f)
                        pump()
                    Pf = "Po"

                    def f(ch, s, o, pb):
                        tr(pb[:, 0:128], s[Pf][:], ident, r=[s[Pf], C], w=[pb])
                        tr(pb[:, 128:256], s["Noff"][:], ident, r=[s["Noff"], C], w=[pb])
                        op("act", lambda e: e.activation(out=R(s["WbdT"][:]), in_=pb[:, 0:128], func=AF.Copy),
                           r=[pb], w=[s["WbdT"]])
                        op("dve", lambda e: e.tensor_copy(out=R(s["NoffT"][:]), in_=pb[:, 128:256]), r=[pb],
                           w=[s["NoffT"]])
                    each(f)
                    pump()

                    def f(ch, s, o, pb):
                        mm(pb[:, 0:128], R(s["NoffT"][:]), R(s[Pf][:]), r=[s["NoffT"], s[Pf]], w=[pb])
                        op("act", lambda e: e.activation(out=R(s["Z1"][:]), in_=pb[:, 0:128], func=AF.Copy),
                           r=[pb], w=[s["Z1"]])
                    each(f)
                    pump()

                    def f(ch, s, o, pb):
                        mm(pb[:, 128:256], R(s["WbdT"][:]), R(s["Z1"][:]), r=[s["WbdT"], s["Z1"]], w=[pb])
                        op("dve", lambda e: e.tensor_tensor(out=R(o["W"][:]), in0=s[Pf][:], in1=pb[:, 128:256],
                                                            op=ALU.subtract), r=[s[Pf], pb], w=[o["W"]])
                    each(f)
                    pump()

                nrec = [0]

                def recur_pair(chs, g):
                    st_ = []
                    for ch in chs:
                        h = ch["h"]
                        So = Sst[h][spar[h]]
                        Sn = Sst[h][1 - spar[h]]
                        spar[h] = 1 - spar[h]
                        bb = 4 * ch["hi"]
                        st_.append((ch, So, Sn, PS[bb], PS[bb + 1], PS[bb + 2], PS[bb + 3],
                                    Yr[nrec[0] % 2], Vn[nrec[0] % 2]))
                        nrec[0] += 1
                    for ch, So, Sn, pa, pb2, pc, pd, Y, vnew in st_:
                        h, n, o = ch["h"], ch["n"], ch["o"]
                        mm(pa[:, 0:128], R(o["kTc"][:]), R(So[:]), r=[o["kTc"], So], w=[pa])
                        op("dve", lambda e: e.scalar_tensor_tensor(out=R(Y[:]), in0=pa[:, 0:128],
                                                                   scalar=NEXPG[:, n, h:h + 1], in1=o["vtok"][:],
                                                                   op0=ALU.mult, op1=ALU.add),
                           r=[pa, NEXPG, o["vtok"]], w=[Y])
                    for ch, So, Sn, pa, pb2, pc, pd, Y, vnew in st_:
                        h, n, o = ch["h"], ch["n"], ch["o"]
                        mm(pb2[:, 0:128], R(o["W"][:]), R(Y[:]), r=[o["W"], Y], w=[pb2])
                        op("act", lambda e: e.activation(out=R(vnew[:]), in_=pb2[:, 0:128], func=AF.Identity,
                                                         scale=BETA[:, n, h:h + 1]), r=[pb2, BETA], w=[vnew])
                    for ch, So, Sn, pa, pb2, pc, pd, Y, vnew in st_:
                        o = ch["o"]
                        mm(pd[:, 0:128], R(o["kdec"][:]), R(vnew[:]), r=[o["kdec"], vnew], w=[pd])
                        op("dve", lambda e: e.scalar_tensor_tensor(out=R(Sn[:]), in0=So[:],
                                                                   scalar=EGL[:, ch["i"]:ch["i"] + 1],
                                                                   in1=pd[:, 0:128], op0=ALU.mult, op1=ALU.add),
                           r=[So, (EGL, ch["i"]), pd], w=[Sn])
                    for ch, So, Sn, pa, pb2, pc, pd, Y, vnew in st_:
                        o, sl = ch["o"], ch["sl"]
                        mm(pc[:, 0:128], R(So[:]), R(o["qg"][:]), start=True, stop=False, r=[So, o["qg"]], w=[pc],
                           inc=False)
                        mm(pc[:, 0:128], R(vnew[:]), R(o["QKdT"][:]), start=False, stop=True, r=[vnew, o["QKdT"]],
                           w=[pc])
                        ot = oTg[ch["hi"]]
                        op("act", lambda e: e.activation(out=ot[:, sl], in_=pc[:, 0:128], func=AF.Copy), r=[pc],
                           w=[(ot, ch["cc"])])

                def stageA(g, hp, par, res):
                    gs = slice(g * 512, (g + 1) * 512)
                    chains = []
                    for hi in range(2):
                        h = 2 * hp + hi
                        qk = qkv[hi]
                        zt = zs[2 * par + hi]
                        for ty in range(4):
                            pb = PS[4 * hi + ty]
                            col = ty * 512 + h * 128
                            for c in range(8):
                                mm(pb[:, :], Wdn[:, c, col:col + 128], hT[:, c, gs], start=(c == 0),
                                   stop=(c == 7), r=[(Wdn, c), (hT, g)], w=[pb], inc=(c == 7))
                            if ty < 3:
                                ch_ = ty * 4 + h
                                op("dve", lambda e: e.tensor_copy(out=pre[ty][:, 0:3], in_=halo[:, ch_, :]),
                                   r=[(halo, ch_)], w=[(pre[ty], 0)])
                                op("act", lambda e: e.activation(out=pre[ty][:, 3:515], in_=pb[:, :],
                                                                 func=AF.Copy), r=[pb], w=[(pre[ty], 1)])
                                op("dve", lambda e: e.tensor_copy(out=halo[:, ch_, :], in_=pre[ty][:, 512:515]),
                                   r=[(pre[ty], 1)], w=[(halo, ch_)])
                                yield
                                cvt = cv[ty % 2]
                                wk = lambda k: convw[:, ch_ * 4 + k:ch_ * 4 + k + 1]
                                op("act", lambda e: e.activation(out=cvt[:], in_=pre[ty][:, 0:512],
                                                                 func=AF.Identity, scale=wk(0)),
                                   r=[pre[ty], convw], w=[cvt])
                                for k in range(1, 4):
                                    op("dve", lambda e: e.scalar_tensor_tensor(out=cvt[:],
                                                                               in0=pre[ty][:, k:k + 512],
                                                                               scalar=wk(k), in1=cvt[:],
                                                                               op0=ALU.mult, op1=ALU.add),
                                       r=[pre[ty], convw, cvt], w=[cvt])
                                    yield
                                op("act", lambda e: e.activation(out=(R(qk[ty][:]) if ty < 2 else qk[ty][:]),
                                                                 in_=cvt[:], func=AF.Silu),
                                   r=[cvt], w=[qk[ty]])
                            else:
                                op("act", lambda e: e.activation(out=zt[:], in_=pb[:, :], func=AF.Silu),
                                   r=[pb], w=[zt])
                            yield
                        for ty in range(2):
                            sqt = sq[ty]
                            pb = PS[4 * hi + ty]
                            op("act", lambda e: e.activation(out=R(sqt[:]), in_=qk[ty][:], func=AF.Square),
                               r=[qk[ty]], w=[sqt])
                            mm(pb[:, :], R(onesr[:]), R(sqt[:]), r=[onesr, sqt], w=[pb])
                            rt_ = cv[0]
                            op("act", lambda e: e.activation(out=rt_[:], in_=pb[:, :], func=AF.Ln,
                                                             bias=epsT[:]), r=[pb, epsT], w=[rt_])
                            op("act", lambda e: e.activation(out=rt_[:], in_=rt_[:], func=AF.Exp, scale=-0.5),
                               r=[rt_], w=[rt_])
                            sc_ = (128.0 ** -0.5) if ty == 0 else 1.0
                            op("dve", lambda e: e.scalar_tensor_tensor(out=R(qk[ty][:]), in0=qk[ty][:],
                                                                       scalar=sc_, in1=rt_[:], op0=ALU.mult,
                                                                       op1=ALU.mult),
                               r=[qk[ty], rt_], w=[qk[ty]])
                            yield
                        if dbg and l == 0 and g == 0 and h == 0:
                            for nm, tt in (("q", qk[0]), ("k", qk[1]), ("v", qk[2])):
                                if nm in dbg:
                                    d = dbg_out(nm, [128, 512])
                                    dma(d[:, :], tt[:], r=[tt])
                        for cc in range(4):
                            i = hi * 4 + cc
                            chains.append({"h": h, "hi": hi, "cc": cc, "n": g * 4 + cc, "i": i,
                                           "sl": slice(cc * 128, (cc + 1) * 128), "qT": qk[0], "kT": qk[1],
                                           "vT": qk[2], "s": scr[i], "o": outs[i]})
                    res["chains"] = chains

                def drain(gen):
                    if gen is not None:
                        for _ in gen:
                            pass

                pairs = [(g, hp) for g in range(NG) for hp in range(2)]
                resA = [dict() for _ in pairs]
                gens = [stageA(g, hp, pi % 2, resA[pi]) for pi, (g, hp) in enumerate(pairs)]
                drain(gens[0])
                for pi, (g, hp) in enumerate(pairs):
                    gs = slice(g * 512, (g + 1) * 512)
                    chains = resA[pi]["chains"]
                    nxt = gens[pi + 1] if pi + 1 < len(pairs) else None

                    def pump(n=1):
                        if nxt is not None:
                            for _ in range(n):
                                if next(nxt, "done") == "done":
                                    break
                    prep8(chains, pump)
                    drain(nxt)
                    for cc in range(4):
                        recur_pair([ch for ch in chains if ch["cc"] == cc], g)
                    for hi in range(2):
                        h = 2 * hp + hi
                        zt = zs[2 * (pi % 2) + hi]
                        ot = oTg[hi]
                        og = ogb[hi]
                        sqt = sq[hi]
                        pb = PS[4 * hi]
                        op("act", lambda e: e.activation(out=R(sqt[:]), in_=ot[:], func=AF.Square), r=[ot],
                           w=[sqt])
                        mm(pb[:, :], R(onesr[:]), R(sqt[:]), r=[onesr, sqt], w=[pb])
                        sqt = cv[0]
                        op("act", lambda e: e.activation(out=sqt[:], in_=pb[:, :], func=AF.Ln,
                                                         scale=1.0 / 128, bias=epsT[:]), r=[pb, epsT], w=[sqt])
                        op("act", lambda e: e.activation(out=sqt[:], in_=sqt[:], func=AF.Exp, scale=-0.5),
                           r=[sqt], w=[sqt])
                        if "odn" in dbg and l == 0 and h == 0 and g == 0:
                            d = dbg_out("odn", [128, 512])
                            dma(d[:, :], ot[:], r=[ot])
                        op("dve", lambda e: e.scalar_tensor_tensor(out=sqt[:], in0=ot[:], scalar=dng[:, 0:1],
                                                                   in1=sqt[:], op0=ALU.mult, op1=ALU.mult),
                           r=[ot, dng, sqt], w=[sqt])
                        op("dve", lambda e: e.tensor_tensor(out=og[:], in0=sqt[:], in1=zt[:], op=ALU.mult),
                           r=[sqt, zt], w=[og])
                        dma(ogdn_d[h, :, gs], og[:], r=[og], w=[(kogdn, g)])
                kb.barrier()

            for st in phase("mb"):
                Wmb = T(st, "Wmb", [128, 8, 2048], BF16)
                for c in range(8):
                    dma(Wmb[:, c, :], wmb_d[l, :, c, :], w=[(Wmb, c)], q="pool")
                Vt = T(st, "Vt", [128, NT, 8, 65], BF16)
                op("dve", lambda e: e.memset(Vt[:, :, :, 64:65], 1.0), w=[Vt])
                for t in range(NT):
                    pb = PS[t % 2]
                    for c in range(8):
                        mm(pb[:, :], hT[:, c, t * 128:(t + 1) * 128], Wmb[:, c, 1024:1536], start=(c == 0),
                           stop=(c == 7), r=[(hT, t // 4), (Wmb, c)], w=[pb], inc=(c == 7))
                    op("act" if t % 2 else "dve",
                       lambda e: (e.activation(out=Vt[:, t, :, 0:64], in_=pb[:, :].rearrange("p (h d) -> p h d", h=8),
                                               func=AF.Copy) if t % 2 else
                                  e.tensor_copy(out=Vt[:, t, :, 0:64],
                                                in_=pb[:, :].rearrange("p (h d) -> p h d", h=8))),
                       r=[pb], w=[(Vt, t)])
                KaT = T(st, "KaT", [128, S], BF16)
                QaT = T(st, "QaT", [128, S], BF16)
                zsm = T(st, "zsm", [64, S], BF16)
                ogm = T(st, "ogm", [64, S], BF16)
                kf = [T(st, "kf%d" % i, [128, 512]) for i in range(2)]
                qf = [T(st, "qf%d" % i, [128, 512]) for i in range(2)]
                sqm = [T(st, "sqm%d" % i, [128, 512], BF16) for i in range(2)]
                kmT = T(st, "kmT", [128, 16])
                shr = T(st, "shr", [64, 512])
                km2 = T(st, "km2", [64, NG + 1])
                gm4 = T(st, "gm4", [128, 4, 16])
                top84 = T(st, "top84", [128, 4, 8])
                mbt4 = [T(st, "mbt4_%d" % i, [128, 4, 16]) for i in range(2)]
                rden = T(st, "rden", [128, 512])
                t1 = [T(st, "t1_%d" % i, [64, 512]) for i in range(2)]
                PT = [T(st, "PT%d" % i, [128, 512], BF16) for i in range(4)]
                nS = [0]
                op("dve", lambda e: e.memset(KaT[0:64, :], 0.0), w=[KaT])
                op("dve", lambda e: e.memset(QaT[0:64, :], 0.0), w=[QaT])
                op("dve", lambda e: e.memset(KaT[32:33, :], 1.0), w=[KaT])
                for n in range(NB):
                    op("dve", lambda e: e.tensor_copy(out=KaT[0:16, n * 256:(n + 1) * 256],
                                                      in_=ident[0:16, n:n + 1].to_broadcast([16, 256])),
                       r=[C], w=[KaT])
                npt = 0
                for h in range(8):
                    op("dve", lambda e: e.tensor_scalar(out=R(kmT[:]), in0=ident[:, 0:16], scalar1=0.0, scalar2=None,
                                                        op0=ALU.mult), r=[C], w=[kmT])
                    def projK(g):
                        pb = PS[g % 2]
                        col = 512 + h * 64
                        for c in range(8):
                            mm(pb[64:128, :], Wmb[:, c, col:col + 64], hT[:, c, g * 512:(g + 1) * 512],
                               start=(c == 0), stop=(c == 7), r=[(Wmb, c), (hT, g)], w=[pb], inc=(c == 7))

                    def postK(g):
                        gs = slice(g * 512, (g + 1) * 512)
                        pb = PS[g % 2]
                        kft = kf[g % 2]
                        op("act", lambda e: e.activation(out=kft[64:128, :], in_=pb[64:128, :], func=AF.Copy),
                           r=[pb], w=[kft])
                        op("act", lambda e: e.activation(out=KaT[64:128, gs], in_=pb[64:128, :], func=AF.Copy),
                           r=[pb], w=[(KaT, g)])
                        op("dve", lambda e: e.tensor_reduce(out=R(kmT[64:128, 2 * g:2 * g + 2]),
                                                            in_=kft[64:128, :].rearrange("p (b t) -> p b t", b=2),
                                                            axis=AX.X, op=ALU.add), r=[kft], w=[kmT])
                        sqt = sqm[g % 2]
                        op("dve", lambda e: e.tensor_tensor(out=sqt[64:128, :], in0=kft[64:128, :],
                                                            in1=kft[64:128, :], op=ALU.mult), r=[kft], w=[sqt])
                        pr = PS[2]
                        mm(pr[32:33, :], onesb[64:128, 0:1], sqt[64:128, :], r=[onesb, sqt], w=[pr])
                        op("dve", lambda e: e.tensor_reduce(out=km2[32:33, g:g + 1], in_=pr[32:33, :], axis=AX.X,
                                                            op=ALU.max), r=[pr], w=[km2])
                    projK(0)
                    for g in range(NG):
                        if g + 1 < NG:
                            projK(g + 1)
                        postK(g)
                    op("dve", lambda e: e.tensor_scalar(out=R(kmT[64:128, :]), in0=kmT[64:128, :], scalar1=1.0 / 256,
                                                        scalar2=None, op0=ALU.mult), r=[kmT], w=[kmT])
                    op("dve", lambda e: e.tensor_reduce(out=km2[32:33, NG:NG + 1], in_=km2[32:33, 0:NG], axis=AX.X,
                                                        op=ALU.max), r=[km2], w=[km2])
                    for g in range(NG):
                        gs = slice(g * 512, (g + 1) * 512)
                        pb = PS[g % 2]
                        col = 1536 + h * 64
                        for c in range(8):
                            mm(pb[0:64, :], Wmb[:, c, col:col + 64], hT[:, c, gs], start=(c == 0), stop=(c == 7),
                               r=[(Wmb, c), (hT, g)], w=[pb], inc=(c == 7))
                        op("act", lambda e: e.activation(out=zsm[:, gs], in_=pb[0:64, :], func=AF.Silu), r=[pb],
                           w=[(zsm, g)])

                    def projQ(g):
                        pb = PS[g % 3]
                        col = h * 64
                        for c in range(8):
                            mm(pb[64:128, :], Wmb[:, c, col:col + 64], hT[:, c, g * 512:(g + 1) * 512],
                               start=(c == 0), stop=(c == 7), r=[(Wmb, c), (hT, g)], w=[pb], inc=(c == 7))

                    def postQ1(g):
                        gs = slice(g * 512, (g + 1) * 512)
                        pb = PS[g % 3]
                        qft = qf[g % 2]
                        op("act", lambda e: e.activation(out=R(qft[64:128, :]), in_=pb[64:128, :], func=AF.Identity,
                                                         scale=0.125), r=[pb], w=[qft])
                        op("act", lambda e: e.activation(out=QaT[64:128, gs], in_=pb[64:128, :], func=AF.Identity,
                                                         scale=0.125), r=[pb], w=[(QaT, g)])
                        sqt = sqm[g % 2]
                        op("dve", lambda e: e.tensor_tensor(out=sqt[64:128, :], in0=qft[64:128, :],
                                                            in1=qft[64:128, :], op=ALU.mult), r=[qft], w=[sqt])
                        pgt = PS[4]
                        for tt in range(4):
                            mm(pgt[:, tt * 16:(tt + 1) * 16], R(qft[64:128, tt * 128:(tt + 1) * 128]), R(kmT[64:128, :]),
                               r=[qft, kmT], w=[pgt], inc=(tt == 3))
                        pr = PS[5]
                        mm(pr[32:33, :], onesb[64:128, 0:1], sqt[64:128, :], r=[onesb, sqt], w=[pr])
                        op("dve", lambda e: e.tensor_tensor(out=gm4[:].rearrange("p a b -> p (a b)"), in0=pgt[:, 0:64],
                                                            in1=C[:, C_PB4 + g * 64:C_PB4 + (g + 1) * 64], op=ALU.add),
                           r=[pgt, C], w=[gm4])
                        op("act", lambda e: e.activation(out=shr[32:33, :], in_=pr[32:33, :], func=AF.Sqrt,
                                                         scale=km2[32:33, NG:NG + 1]), r=[pr, km2], w=[shr])
                        op("act", lambda e: e.activation(out=QaT[32:33, gs], in_=shr[32:33, :], func=AF.Identity,
                                                         scale=-1.0), r=[shr], w=[(QaT, g)])
                        for tt in range(4):
                            op("dve", lambda e: e.max(out=top84[:, tt, :], in_=gm4[:, tt, :]), r=[gm4],
                               w=[(top84, tt)])
                        mb_ = mbt4[g % 2]
                        for tt in range(4):
                            op("dve", lambda e: e.tensor_scalar(out=mb_[:, tt, :], in0=gm4[:, tt, :],
                                                                scalar1=top84[:, tt, 2:3], scalar2=NEG,
                                                                op0=ALU.is_lt, op1=ALU.mult),
                               r=[gm4, (top84, tt)], w=[(mb_, tt)])
                        for t2 in range(2):
                            own = 2 * g + t2
                            op("dve", lambda e: e.memset(mb_[:, 2 * t2:2 * t2 + 2, own:own + 1], 0.0),
                               r=[(mb_, 2 * t2), (mb_, 2 * t2 + 1)], w=[(mb_, 2 * t2), (mb_, 2 * t2 + 1)])

                    def postQ2(g):
                        gs = slice(g * 512, (g + 1) * 512)
                        mb_ = mbt4[g % 2]
                        pt_ = PS[3]
                        for tt in range(4):
                            tr(pt_[0:16, tt * 128:(tt + 1) * 128], mb_[:, tt, :], ident, r=[(mb_, tt), C], w=[pt_])
                        op("act", lambda e: e.activation(out=QaT[0:16, gs], in_=pt_[0:16, 0:512], func=AF.Copy),
                           r=[pt_], w=[(QaT, g)])
                    projQ(0)
                    if NG > 1:
                        projQ(1)
                    for g in range(NG):
                        postQ1(g)
                        if g + 2 < NG:
                            projQ(g + 2)
                        if g > 0:
                            postQ2(g - 1)
                    postQ2(NG - 1)
                    LOOK = 3
                    for jq in range(NB // 2):
                        q0 = jq * 512
                        pO = PS[6 + jq % 2]
                        ntile = 4 * jq + 4
                        qk_ = (QaT, jq)

                        def emitS(kt):
                            pS = PS[nS[0] % 4]
                            nS[0] += 1
                            d = kt - 4 * jq
                            c0 = 0 if d < 0 else 128 * d
                            mm(pS[:, c0:512], KaT[:, kt * 128:(kt + 1) * 128], QaT[:, q0 + c0:q0 + 512], start=True,
                               stop=(d < 0), r=[(KaT, kt // 4), qk_], w=[pS], inc=(d < 0))
                            if d >= 0:
                                mm(pS[:, c0:c0 + 128], identb[:], trib[:], start=False, stop=True, r=[identb, trib],
                                   w=[pS])
                            return pS, c0

                        pendq = [emitS(k_) for k_ in range(min(LOOK, ntile))]
                        for kt in range(ntile):
                            pS, c0 = pendq.pop(0)
                            if kt + LOOK < ntile:
                                pendq.append(emitS(kt + LOOK))
                            ptile = PT[npt % 4]
                            npt += 1
                            op("act", lambda e: e.activation(out=ptile[:, c0:512], in_=pS[:, c0:512], func=AF.Exp),
                               r=[pS], w=[ptile])
                            mm(pO[0:65, c0:512], Vt[:, kt, h, :], ptile[:, c0:512], start=(kt == 0),
                               stop=(kt == ntile - 1), r=[(Vt, kt), ptile], w=[pO], inc=(kt == ntile - 1))
                        op("dve", lambda e: e.reciprocal(out=R(rden[64:65, :]), in_=pO[64:65, 0:512]), r=[pO],
                           w=[rden])
                        pB = PS[5]
                        mm(pB[0:64, 0:512], R(onesr[64:65, 0:64]), R(rden[64:65, :]), r=[onesr, rden], w=[pB])
                        tt1 = t1[jq % 2]
                        op("dve", lambda e: e.tensor_tensor(out=tt1[:], in0=pO[0:64, 0:512],
                                                            in1=zsm[:, q0:q0 + 512], op=ALU.mult),
                           r=[pO, (zsm, jq)], w=[tt1])
                        op("dve", lambda e: e.tensor_tensor(out=ogm[:, q0:q0 + 512], in0=tt1[:], in1=pB[0:64, 0:512],
                                                            op=ALU.mult), r=[tt1, pB], w=[(ogm, jq)])
                    if "omb" in dbg and l == 0 and h == 0:
                        d = dbg_out("omb", [64, S])
                        tmp = T(st, "dbgomb", [64, S])
                        op("dve", lambda e: e.tensor_copy(out=tmp[:], in_=ogm[:]), r=[ogm], w=[tmp])
                        dma(d[:, :], tmp[:], r=[tmp])
                    dma(ogmb_d[h, :, :], ogm[:], r=[ogm], w=[kogmb])
                kb.barrier()

            for st in phase("fin"):
                Wpdn = T(st, "Wpdn", [128, 4, 1024], BF16)
                Wpmb = T(st, "Wpmb", [64, 8, 1024], BF16)
                Wout = T(st, "Wout", [128, 8, 1024], BF16)
                Wmg = T(st, "Wmg", [128, 8, 2048], BF16)
                dma(Wpdn[:], wpdn_d[l], w=[Wpdn], q="pool")
                dma(Wpmb[:], wpmb_d[l], w=[Wpmb], q="pool")
                for c in range(8):
                    dma(Wmg[:, c, :], wmg_d[l, :, c, :], w=[(Wmg, c)], q="pool")
                for c in range(8):
                    dma(Wout[:, c, :], wout_d[l, :, c, :], w=[(Wout, c)], q="pool")
                OGD = [T(st, "OGD%d" % i, [128, 4, 512], BF16) for i in range(2)]
                OGM = [T(st, "OGM0", [64, 8, 512], BF16)] * 2
                mixT = T(st, "mixT", [128, 8, 512], BF16)
                gd = [T(st, "gd%d" % i, [128, 512]) for i in range(2)]
                gmm = [T(st, "gmm%d" % i, [128, 512]) for i in range(2)]
                u1 = [T(st, "u1_%d" % i, [128, 512]) for i in range(2)]
                u2 = [T(st, "u2_%d" % i, [128, 512]) for i in range(2)]
                xr = [T(st, "xr%d" % i, [128, 1024]) for i in range(2)]
                res = [T(st, "res%d" % i, [128, 512]) for i in range(2)]
                junk2 = T(st, "junk2", [128, 512], BF16)
                GPl = T(st, "GPl", [128, 1024])
                dma(GPl[:], gp_d[l], r=[(kgp, l)], w=[GPl])
                ss2 = T(st, "ss2", [128, NT, 2])
                rs2 = T(st, "rs2", [128, NT])
                def load_og(g):
                    gs_ = slice(g * 512, (g + 1) * 512)
                    dma(OGD[g % 2][:], ogdn_d[:, :, gs_].rearrange("h p s -> p h s"), r=[(kogdn, g)], w=[OGD[g % 2]])
                    dma(OGM[g % 2][:], ogmb_d[:, :, gs_].rearrange("h p s -> p h s"), r=[kogmb], w=[OGM[g % 2]])

                load_og(0)
                for g in range(NG):
                    gs = slice(g * 512, (g + 1) * 512)
                    ogd, ogmm = OGD[g % 2], OGM[g % 2]
                    for d_ in range(8):
                        ds_ = slice(d_ * 128, (d_ + 1) * 128)
                        pa, pb, pc, pd = PS[0 + 4 * (d_ % 2)], PS[1 + 4 * (d_ % 2)], PS[2 + 4 * (d_ % 2)], PS[3 + 4 * (d_ % 2)]
                        for h in range(4):
                            mm(pa[:, :], Wpdn[:, h, ds_], ogd[:, h, :], start=(h == 0), stop=(h == 3),
                               r=[Wpdn, ogd], w=[pa], inc=(h == 3))
                        for h in range(8):
                            mm(pb[:, :], Wpmb[:, h, ds_], ogmm[:, h, :], start=(h == 0), stop=(h == 7),
                               r=[Wpmb, ogmm], w=[pb], inc=(h == 7))
                        for c in range(8):
                            mm(pc[:, :], Wmg[:, c, ds_], hT[:, c, gs], start=(c == 0), stop=(c == 7),
                               r=[(Wmg, c), (hT, g)], w=[pc], inc=(c == 7))
                        for c in range(8):
                            mm(pd[:, :], Wmg[:, c, 1024 + d_ * 128:1024 + (d_ + 1) * 128], hT[:, c, gs],
                               start=(c == 0), stop=(c == 7), r=[(Wmg, c), (hT, g)], w=[pd], inc=(c == 7))
                        gdt, gmt, u1t, u2t = gd[d_ % 2], gmm[d_ % 2], u1[d_ % 2], u2[d_ % 2]
                        op("act", lambda e: e.activation(out=gdt[:], in_=pc[:, :], func=AF.Sigmoid), r=[pc], w=[gdt])
                        op("act", lambda e: e.activation(out=gmt[:], in_=pd[:, :], func=AF.Sigmoid), r=[pd], w=[gmt])
                        op("dve", lambda e: e.tensor_tensor(out=u1t[:], in0=pa[:, :], in1=gdt[:], op=ALU.mult),
                           r=[pa, gdt], w=[u1t])
                        op("dve", lambda e: e.tensor_tensor(out=u2t[:], in0=pb[:, :], in1=gmt[:], op=ALU.mult),
                           r=[pb, gmt], w=[u2t])
                        op("dve", lambda e: e.tensor_tensor(out=mixT[:, d_, :], in0=u1t[:], in1=u2t[:], op=ALU.add),
                           r=[u1t, u2t], w=[(mixT, d_)])
                    if g + 1 < NG:
                        load_og(g + 1)
                    for tt in range(4):
                        t = g * 4 + tt
                        xrt = xr[t % 2]
                        dma(xrt[:], xin_d[t * 128:(t + 1) * 128, :], r=[(xin_k, t)], w=[xrt])
                        for hf in range(2):
                            pb = PS[(t % 2) * 2 + hf]
                            for d_ in range(8):
                                mm(pb[:, :], mixT[:, d_, tt * 128:(tt + 1) * 128], Wout[:, d_, hf * 512:(hf + 1) * 512],
                                   start=(d_ == 0), stop=(d_ == 7), r=[(mixT, d_), (Wout, d_)], w=[pb],
                                   inc=(d_ == 7))
                            op("act", lambda e: e.activation(out=junk2[:], in_=pb[:, :], func=AF.Square,
                                                             accum_out=ss2[:, t, hf:hf + 1]), r=[pb],
                               w=[junk2, (ss2, t)])
                        op("dve", lambda e: e.tensor_tensor(out=rs2[:, t:t + 1], in0=ss2[:, t, 0:1],
                                                            in1=ss2[:, t, 1:2], op=ALU.add), r=[(ss2, t)],
                           w=[(rs2, t)])
                        op("act", lambda e: e.activation(out=rs2[:, t:t + 1], in_=rs2[:, t:t + 1], func=AF.Sqrt,
                                                         scale=1.0 / D_MODEL, bias=epsT[:]), r=[(rs2, t), epsT],
                           w=[(rs2, t)])
                        op("dve", lambda e: e.reciprocal(out=rs2[:, t:t + 1], in_=rs2[:, t:t + 1]), r=[(rs2, t)],
                           w=[(rs2, t)])
                        for hf in range(2):
                            pb = PS[(t % 2) * 2 + hf]
                            hs = slice(hf * 512, (hf + 1) * 512)
                            rt = res[hf]
                            op("dve", lambda e: e.scalar_tensor_tensor(out=rt[:], in0=pb[:, :],
                                                                       scalar=rs2[:, t:t + 1], in1=GPl[:, hs],
                                                                       op0=ALU.mult, op1=ALU.mult),
                               r=[pb, (rs2, t), GPl], w=[rt])
                            op("dve", lambda e: e.tensor_tensor(out=xrt[:, hs], in0=rt[:], in1=xrt[:, hs],
                                                                op=ALU.add), r=[rt, (xrt, hf)], w=[(xrt, hf)])
                        dma(xout_d[t * 128:(t + 1) * 128, :], xrt[:], r=[xrt], w=[(xout_k, t)], q="pool")
                kb.barrier()
          except _Stop:
            kb.barrier()
            curgen[0].close()
            break
        kb.finish()
        stuck = kb.simulate()
        print("deadlock check:", stuck if stuck else "ok")
        print("instructions:", kb.ninstr, "counts", kb.count, "dmas", kb.ndmaq)
    return nc, dbg_d


def _pc(w):
    sh = w.shape
    w = w.reshape(sh[:-2] + (8, 128, sh[-1]))
    return np.ascontiguousarray(np.swapaxes(w, -3, -2))


def host_layout(inp):
    f = lambda a: np.ascontiguousarray(np.asarray(a, dtype=np.float32))
    w_in = f(inp["w_in"])
    depth = w_in.shape[0]
    shared = {
        "wada": _pc(f(inp["w_ada"])),
        "bada": f(inp["b_ada"]).reshape(depth, 1, 3072),
        "gpre": f(inp["g_pre"]).reshape(depth, 1, 1024),
        "gpost": f(inp["g_post"]).reshape(depth, 1, 1024),
        "wdn": _pc(w_in[:, :, 0:2048]),
        "wba": _pc(w_in[:, :, 2048:2056]),
        "wmb": _pc(w_in[:, :, 2056:4104]),
        "wmg": _pc(w_in[:, :, 4104:6152]),
        "convw": np.ascontiguousarray(
            f(inp["conv_w"]).transpose(0, 2, 1).reshape(depth, 12, 128, 4).transpose(0, 2, 1, 3)
        ).reshape(depth, 128, 48),
        "alog": np.ascontiguousarray(np.broadcast_to(f(inp["a_log"])[:, None, :], (depth, 128, 4))),
        "dtb": np.ascontiguousarray(np.broadcast_to(f(inp["dt_bias"])[:, None, :], (depth, 128, 4))),
        "dng": f(inp["dn_norm_g"]).reshape(depth, 128, 1),
        "wpdn": np.ascontiguousarray(f(inp["w_proj_dn"]).reshape(depth, 4, 128, 1024).transpose(0, 2, 1, 3)),
        "wpmb": np.ascontiguousarray(f(inp["w_proj_mb"]).reshape(depth, 8, 64, 1024).transpose(0, 2, 1, 3)),
        "wout": _pc(f(inp["w_out"])),
        "consts": make_consts(),
    }
    x = f(inp["x"])
    c = f(inp["c"])
    maps = []
    for b in range(x.shape[0]):
        m = dict(shared)
        m["x"] = x[b]
        m["cT"] = np.ascontiguousarray(c[b].reshape(8, 128).T)
        maps.append(m)
    return maps


_CACHE = {}


def kernel(**inputs):
    x = np.asarray(inputs["x"])
    B, S, _ = x.shape
    depth = np.asarray(inputs["w_in"]).shape[0]
    key = (S, depth)
    if key not in _CACHE:
        _CACHE[key] = build(S=S, DEPTH=depth)[0]
    nc = _CACHE[key]
    maps = host_layout(inputs)
    res = run_bass_kernel_spmd(nc, maps, core_ids=list(range(B)))
    return np.stack([np.asarray(r["y"], dtype=np.float32) for r in res.results], axis=0)
```

```python
from contextlib import ExitStack

import numpy as np
import concourse.bass as bass
import concourse.mybir as mybir
from concourse.bass_utils import run_bass_kernel_spmd

F32 = mybir.dt.float32
BF16 = mybir.dt.bfloat16
F32R = mybir.dt.float32r


def R(ap):
    return ap.bitcast(F32R)
AF = mybir.ActivationFunctionType
ALU = mybir.AluOpType
AX = mybir.AxisListType

D_MODEL = 1024
NEG = -30000.0
EPS = 1e-6


class SV:
    def __init__(self, tile, j, n=1):
        self.tile, self.sub = tile, j
        self.ap = tile[:, j * 128:(j + n) * 128]

    def __getitem__(self, idx):
        return self.ap[idx]


class KB:
    NRING = {"sp": 6, "pool": 4}

    def __init__(self, nc, stack):
        self.nc = nc
        self.eng = {"pe": nc.tensor, "act": nc.scalar, "dve": nc.vector,
                    "pool": nc.gpsimd, "sp": nc.sync}
        self.sem = {}
        for e in ("pe", "act", "dve", "pool"):
            self.sem[e] = stack.enter_context(nc.semaphore("s_" + e))
        self.ring = {q: [stack.enter_context(nc.semaphore("s_dma_%s%d" % (q, i))) for i in range(n)]
                     for q, n in self.NRING.items()}
        self.ndmaq = {q: 0 for q in self.NRING}
        self.count = {e: 0 for e in ("pe", "act", "dve", "pool")}
        self.waited = {}
        self.track = {}
        self.ninstr = 0
        self.streams = {e: [] for e in self.eng}
        self.psum_ids = set()

    def _semof(self, dep):
        if dep[0] == "e":
            return self.sem[dep[1]], ("e", dep[1])
        return self.ring[dep[1][0]][dep[1][1]], ("d", dep[1])

    def _wait(self, e, dep):
        sem, sk = self._semof(dep)
        val = dep[2]
        k = (e, sk)
        if self.waited.get(k, 0) >= val:
            return
        self.eng[e].wait_ge(sem, val)
        self.streams[e].append(("wait", sk, val))
        self.ninstr += 1
        self.waited[k] = val

    @staticmethod
    def _keys(items):
        out = []
        for it in items:
            if isinstance(it, SV):
                out.append((id(it.tile), it.sub))
            elif isinstance(it, tuple):
                out.append((id(it[0]), it[1]))
            else:
                out.append((id(it), None))
        return out

    def _conflicts(self, key):
        tid, sub = key
        d = self.track.get(tid)
        if d is None:
            return []
        if sub is None:
            return list(d.values())
        res = []
        if sub in d:
            res.append(d[sub])
        if None in d:
            res.append(d[None])
        return res

    def _entry(self, key):
        tid, sub = key
        d = self.track.setdefault(tid, {})
        if sub is None:
            ent = {"w": [], "r": []}
            for v in d.values():
                ent["w"] += v["w"]
                ent["r"] += v["r"]
            d.clear()
            d[None] = ent
            return ent
        if sub not in d:
            ent = {"w": [], "r": []}
            if None in d:
                ent["w"] = list(d[None]["w"])
                ent["r"] = list(d[None]["r"])
            d[sub] = ent
        return d[sub]

    def _deps(self, e, reads, writes):
        deps = []
        for k in reads:
            for ent in self._conflicts(k):
                for w in ent["w"]:
                    deps.append((w, "raw"))
        for k in writes:
            for ent in self._conflicts(k):
                for w in ent["w"]:
                    deps.append((w, "waw"))
                for r in ent["r"]:
                    deps.append((r, "war"))
        out = []
        for dep, kind in deps:
            if dep[0] == "e" and dep[1] == e:
                if e == "pe":
                    continue
            out.append(dep)
        return out

    @staticmethod
    def _prune(lst):
        best = {}
        for d in lst:
            kk = (d[0], d[1])
            if kk not in best or best[kk][2] < d[2]:
                best[kk] = d
        return list(best.values())

    def _record(self, me, reads, writes):
        for k in reads:
            ent = self._entry(k)
            ent["r"].append(me)
            if len(ent["r"]) > 16:
                ent["r"] = self._prune(ent["r"])
        for k in writes:
            ent = self._entry(k)
            ent["w"] = [me]
            ent["r"] = []

    def _rw(self, r, w):
        reads, writes = [], []
        for k in self._keys(r):
            if k[0] in self.psum_ids:
                writes.append((k[0], None))
            else:
                reads.append(k)
        for k in self._keys(w):
            writes.append((k[0], None) if k[0] in self.psum_ids else k)
        return reads, writes

    def op(self, e, fn, r=(), w=(), inc=True):
        reads, writes = self._rw(r, w)
        for dep in self._deps(e, reads, writes):
            self._wait(e, dep)
        ins = fn(self.eng[e])
        self.ninstr += 1
        if inc:
            self.count[e] += 1
            ins.then_inc(self.sem[e], 1)
            self.streams[e].append(("inc", ("e", e), 1))
            me = ("e", e, self.count[e])
        else:
            me = ("e", e, self.count[e] + 1)
        self._record(me, reads, writes)
        return ins

    def dma(self, out, in_, r=(), w=(), q="sp", **kw):
        e = q
        reads = self._keys(r)
        writes = self._keys(w)
        k = self.ndmaq[q]
        nr = self.NRING[q]
        slot = k % nr
        gen = k // nr
        for dep in self._deps(e, reads, writes):
            self._wait(e, dep)
        if gen > 0:
            self._wait(e, ("d", (q, slot), 16 * gen))
        ins = self.eng[e].dma_start(out=out, in_=in_, **kw)
        ins.then_inc(self.ring[q][slot], 16)
        self.streams[e].append(("inc", ("d", (q, slot)), 16))
        self.ndmaq[q] += 1
        self.ninstr += 1
        me = ("d", (q, slot), 16 * (gen + 1))
        self._record(me, reads, writes)
        return ins

    def _alldma(self):
        out = []
        for q, nr in self.NRING.items():
            n = self.ndmaq[q]
            for slot in range(nr):
                cnt = (n - 1 - slot) // nr + 1 if n > slot else 0
                if cnt > 0:
                    out.append(("d", (q, slot), 16 * cnt))
        return out

    def barrier(self):
        for e in ("pe", "act", "dve", "pool", "sp"):
            for o in ("pe", "act", "dve", "pool"):
                if o != e and self.count[o] > 0:
                    self._wait(e, ("e", o, self.count[o]))
            for dep in self._alldma():
                self._wait(e, dep)
        self.track = {}

    def finish(self):
        for dep in self._alldma():
            self._wait("sp", dep)

    def simulate(self):
        sems = {}
        pc = {e: 0 for e in self.streams}
        progress = True
        while progress:
            progress = False
            for e, st in self.streams.items():
                while pc[e] < len(st):
                    kind, sk, val = st[pc[e]]
                    if kind == "wait":
                        if sems.get(sk, 0) < val:
                            break
                    else:
                        sems[sk] = sems.get(sk, 0) + val
                    pc[e] += 1
                    progress = True
        stuck = {e: (pc[e], len(st), st[pc[e]], sems.get(st[pc[e]][1], 0)) for e, st in self.streams.items()
                 if pc[e] < len(st)}
        return stuck


C_IDENT, C_U, C_MBD, C_MOFF, C_TRI, C_ONES, C_PB = 0, 128, 256, 384, 512, 640, 768
C_PB4 = 768
NCONST = 768 + 512


def make_consts():
    c = np.zeros((128, NCONST), np.float32)
    i = np.arange(128)
    c[:, C_IDENT:C_IDENT + 128] = np.eye(128)
    c[:, C_U:C_U + 128] = (i[:, None] <= i[None, :])
    c[:, C_MBD:C_MBD + 128] = (i[:, None] < i[None, :]) & ((i[:, None] // 64) == (i[None, :] // 64))
    c[:, C_MOFF:C_MOFF + 128] = (i[:, None] < 64) & (i[None, :] >= 64)
    c[:, C_TRI:C_TRI + 128] = np.where(i[:, None] <= i[None, :], 0.0, NEG)
    c[:, C_ONES:C_ONES + 128] = 1.0
    pb = np.zeros((16, 16), np.float32)
    for own in range(16):
        pb[own, own:] = -1e30
    for g in range(8):
        for tt in range(4):
            own = (4 * g + tt) // 2
            c[:, C_PB4 + g * 64 + tt * 16:C_PB4 + g * 64 + (tt + 1) * 16] = pb[own][None, :]
    return c


def build(S=4096, DEPTH=2, LS=2, dbg=None, phases=("p1", "dn", "mb", "fin"), stop=0, JUNK=0):
    dbg = dbg or set()
    NT = S // 128
    NG = S // 512
    NB = S // 256
    nc = bass.Bass("TRN2", target_bir_lowering=False)

    def din(name, shape, dt=F32):
        return nc.dram_tensor(name, shape, dt, kind="ExternalInput").ap()

    x_d = din("x", [S, 1024])
    cT_d = din("cT", [128, 8])
    wada_d = din("wada", [DEPTH, 128, 8, 3072])
    bada_d = din("bada", [DEPTH, 1, 3072])
    gpre_d = din("gpre", [DEPTH, 1, 1024])
    gpost_d = din("gpost", [DEPTH, 1, 1024])
    wdn_d = din("wdn", [DEPTH, 128, 8, 2048])
    wba_d = din("wba", [DEPTH, 128, 8, 8])
    wmb_d = din("wmb", [DEPTH, 128, 8, 2048])
    wmg_d = din("wmg", [DEPTH, 128, 8, 2048])
    convw_d = din("convw", [DEPTH, 128, 48])
    alog_d = din("alog", [DEPTH, 128, 4])
    dtb_d = din("dtb", [DEPTH, 128, 4])
    dng_d = din("dng", [DEPTH, 128, 1])
    wpdn_d = din("wpdn", [DEPTH, 128, 4, 1024])
    wpmb_d = din("wpmb", [DEPTH, 64, 8, 1024])
    wout_d = din("wout", [DEPTH, 128, 8, 1024])
    consts_d = din("consts", [128, NCONST])
    y_d = nc.dram_tensor("y", [S, 1024], F32, kind="ExternalOutput").ap()
    xmid_d = nc.dram_tensor("xmid", [S, 1024], F32, kind="Internal").ap()
    ogdn_d = nc.dram_tensor("ogdn", [4, 128, S], BF16, kind="Internal").ap()
    ogmb_d = nc.dram_tensor("ogmb", [8, 64, S], BF16, kind="Internal").ap()
    dbg_d = {}

    class _K:
        pass
    kx, kmid, ky, kogdn, kogmb = _K(), _K(), _K(), _K(), _K()

    def dbg_out(name, shape):
        dbg_d[name] = nc.dram_tensor("dbg_" + name, shape, F32, kind="ExternalOutput").ap()
        return dbg_d[name]

    with ExitStack() as gst:
        kb = KB(nc, gst)
        op, dma = kb.op, kb.dma
        gst.enter_context(nc.allow_low_precision("float32r (1-pass PE) operands for non-critical fp32 matmuls"))

        def mm(out, lhsT, rhs, start=True, stop=True, r=(), w=(), inc=True):
            return op("pe", lambda e: e.matmul(out, lhsT=lhsT, rhs=rhs, start=start, stop=stop),
                      r=r, w=w, inc=inc)

        def tr(out, in_, ident, r=(), w=()):
            return op("pe", lambda e: e.transpose(out=out, in_=in_, identity=ident), r=r, w=w)

        uid = [0]

        def T(st, name, shape, dt=F32):
            uid[0] += 1
            return st.enter_context(nc.sbuf_tensor("sb%d_%s" % (uid[0], name), shape, dt))

        class _Stop(Exception):
            pass

        def ck(level):
            if stop == level:
                raise _Stop()

        curgen = [None]

        def phase(name):
            if name in phases:
                st_ = ExitStack()
                curgen[0] = st_
                yield st_
                st_.close()

        PS = [gst.enter_context(nc.psum_tensor("ps%d" % i, [128, 512], F32)) for i in range(8)]
        kb.psum_ids = {id(p) for p in PS}
        C = T(gst, "consts", [128, NCONST])
        dma(C[:], consts_d[:, :], w=[C])
        ident = C[:, C_IDENT:C_IDENT + 128]
        U = C[:, C_U:C_U + 128]
        Mbd = C[:, C_MBD:C_MBD + 128]
        Moff = C[:, C_MOFF:C_MOFF + 128]
        ones = C[:, C_ONES:C_ONES + 128]
        identb = T(gst, "identb", [128, 128], BF16)
        trib = T(gst, "trib", [128, 128], BF16)
        epsT = T(gst, "epsT", [128, 1])
        op("dve", lambda e: e.tensor_copy(out=identb[:], in_=ident), r=[C], w=[identb])
        op("dve", lambda e: e.tensor_copy(out=trib[:], in_=C[:, C_TRI:C_TRI + 128]), r=[C], w=[trib])
        op("dve", lambda e: e.memset(epsT[:], EPS), w=[epsT])
        onesb = T(gst, "onesb", [128, 1], BF16)
        op("dve", lambda e: e.tensor_copy(out=onesb[:], in_=ones[:, 0:1]), r=[C], w=[onesb])
        onesr = T(gst, "onesr", [128, 128])
        op("dve", lambda e: e.tensor_copy(out=R(onesr[:]), in_=ones), r=[C], w=[onesr])
        AB = [T(gst, "AB%d" % l, [128, 16]) for l in range(DEPTH)]
        gp_d = nc.dram_tensor("gp_scr", [DEPTH, 128, 1024], F32, kind="Internal").ap()
        kgp = _K()

        with ExitStack() as st:
            cT = T(st, "cT", [128, 8])
            sc = T(st, "sc", [128, 8])
            dma(cT[:], cT_d[:, :], w=[cT])
            op("act", lambda e: e.activation(out=sc[:], in_=cT[:], func=AF.Silu), r=[cT], w=[sc])
            wa = [T(st, "wa%d" % i, [128, 8, 512]) for i in range(2)]
            row = T(st, "row", [1, 3072])
            bada = T(st, "bada", [1, 3072])
            gpr = T(st, "gpr", [1, 1024])
            gpo = T(st, "gpo", [1, 1024])
            arow = T(st, "arow", [1, 1024])
            gprow = T(st, "gprow", [1, 1024])
            gptmp = T(st, "gptmp", [128, 1024])
            nwa = 0
            for l in range(DEPTH):
                dma(bada[:], bada_d[l], w=[bada])
                dma(gpr[:], gpre_d[l], w=[gpr])
                dma(gpo[:], gpost_d[l], w=[gpo])
                for cg in range(6):
                    wt = wa[nwa % 2]
                    nwa += 1
                    dma(wt[:], wada_d[l, :, :, cg * 512:(cg + 1) * 512], w=[wt])
                    pb = PS[cg % 2]
                    for c in range(8):
                        mm(pb[0:1, :], sc[:, c:c + 1], wt[:, c, :], start=(c == 0), stop=(c == 7),
                           r=[sc, wt], w=[pb], inc=(c == 7))
                    op("dve", lambda e: e.tensor_tensor(out=row[0:1, cg * 512:(cg + 1) * 512], in0=pb[0:1, :],
                                                        in1=bada[0:1, cg * 512:(cg + 1) * 512], op=ALU.add),
                       r=[pb, bada], w=[(row, cg)])
                op("dve", lambda e: e.scalar_tensor_tensor(out=arow[:], in0=row[0:1, 1024:2048], scalar=1.0,
                                                           in1=gpr[:], op0=ALU.add, op1=ALU.mult),
                   r=[row, gpr], w=[arow])
                op("dve", lambda e: e.tensor_tensor(out=gprow[:], in0=row[0:1, 2048:3072], in1=gpo[:], op=ALU.mult),
                   r=[row, gpo], w=[gprow])
                pc = PS[2]
                for c in range(8):
                    mm(pc[:, c:c + 1], arow[0:1, c * 128:(c + 1) * 128], ones[0:1, 0:1], r=[arow, C], w=[pc], inc=False)
                for c in range(8):
                    mm(pc[:, 8 + c:9 + c], row[0:1, c * 128:(c + 1) * 128], ones[0:1, 0:1], r=[row, C], w=[pc],
                       inc=(c == 7))
                op("dve", lambda e: e.tensor_copy(out=AB[l][:], in_=pc[:, 0:16]), r=[pc], w=[AB[l]])
                for hf in range(2):
                    pg = PS[3 + hf]
                    mm(pg[:, :], ones[0:1, 0:128], gprow[0:1, hf * 512:(hf + 1) * 512], r=[gprow, C], w=[pg])
                    op("act", lambda e: e.activation(out=gptmp[:, hf * 512:(hf + 1) * 512], in_=pg[:, :], func=AF.Copy),
                       r=[pg], w=[(gptmp, hf)])
                dma(gp_d[l], gptmp[:], r=[gptmp], w=[(kgp, l)])
            kb.barrier()

        hT = T(gst, "hT", [128, 8, S], BF16)

        for l in range(DEPTH):
          try:
            xin_d = x_d if l == 0 else xmid_d
            xout_d = y_d if l == DEPTH - 1 else xmid_d
            xin_k = kx if l == 0 else kmid
            xout_k = ky if l == DEPTH - 1 else kmid

            for st in phase("p1"):
                xt = [T(st, "xt%d" % i, [128, 1024]) for i in range(4)]
                xn = [T(st, "xn%d" % i, [128, 1024]) for i in range(3)]
                junk = T(st, "junk", [128, 1024], BF16)
                ss = T(st, "ss", [128, NT])
                rstd = T(st, "rstd", [128, NT])
                def p1_stats(t):
                    xtt, xnt = xt[t % 4], xn[t % 3]
                    dma(xtt[:], xin_d[t * 128:(t + 1) * 128, :], r=[(xin_k, t)], w=[xtt])
                    op("act", lambda e: e.activation(out=junk[:], in_=xtt[:], func=AF.Square,
                                                     accum_out=ss[:, t:t + 1]), r=[xtt], w=[junk, (ss, t)])
                    op("act", lambda e: e.activation(out=rstd[:, t:t + 1], in_=ss[:, t:t + 1], func=AF.Sqrt,
                                                     scale=1.0 / D_MODEL, bias=epsT[:]), r=[(ss, t), epsT],
                       w=[(rstd, t)])
                    op("dve", lambda e: e.reciprocal(out=rstd[:, t:t + 1], in_=rstd[:, t:t + 1]), r=[(rstd, t)],
                       w=[(rstd, t)])
                    op("dve", lambda e: e.tensor_scalar(out=xnt[:], in0=xtt[:], scalar1=rstd[:, t:t + 1],
                                                        scalar2=None, op0=ALU.mult), r=[xtt, (rstd, t)], w=[xnt])

                def p1_transpose(t):
                    xnt = xn[t % 3]
                    for hf in range(2):
                        pb = PS[(t % 2) * 2 + hf]
                        for cc in range(4):
                            c = hf * 4 + cc
                            tr(pb[:, cc * 128:(cc + 1) * 128], xnt[:, c * 128:(c + 1) * 128], ident, r=[xnt, C],
                               w=[pb])
                        for cc in range(4):
                            c = hf * 4 + cc
                            eng = "act" if hf == 0 else "dve"
                            if eng == "act":
                                op("act", lambda e: e.activation(out=hT[:, c, t * 128:(t + 1) * 128],
                                                                 in_=pb[:, cc * 128:(cc + 1) * 128], func=AF.Identity,
                                                                 scale=AB[l][:, c:c + 1], bias=AB[l][:, 8 + c:9 + c]),
                                   r=[pb, AB[l]], w=[(hT, t // 4)])
                            else:
                                op("dve", lambda e: e.tensor_scalar(out=hT[:, c, t * 128:(t + 1) * 128],
                                                                    in0=pb[:, cc * 128:(cc + 1) * 128],
                                                                    scalar1=AB[l][:, c:c + 1],
                                                                    scalar2=AB[l][:, 8 + c:9 + c],
                                                                    op0=ALU.mult, op1=ALU.add),
                                   r=[pb, AB[l]], w=[(hT, t // 4)])

                p1_stats(0)
                for t in range(NT):
                    if t + 1 < NT:
                        p1_stats(t + 1)
                    p1_transpose(t)
                kb.barrier()
            if "hT" in dbg and l == 0:
                with ExitStack() as st:
                    d = dbg_out("hT", [128, 8, S])
                    tmp = T(st, "dbgtmp", [128, 8, S])
                    op("dve", lambda e: e.tensor_copy(out=tmp[:], in_=hT[:]), r=[hT], w=[tmp])
                    dma(d[:, :, :], tmp[:], r=[tmp])
                    kb.barrier()

            for st in phase("dn"):
                Wdn = T(st, "Wdn", [128, 8, 2048], BF16)
                Wba = T(st, "Wba", [128, 8, 8], BF16)
                for c in range(8):
                    dma(Wdn[:, c, :], wdn_d[l, :, c, :], w=[(Wdn, c)], q="pool")
                dma(Wba[:], wba_d[l], w=[Wba], q="pool")
                convw = T(st, "convw", [128, 48])
                alog = T(st, "alog", [128, 4])
                dtb = T(st, "dtb", [128, 4])
                dng = T(st, "dng", [128, 1])
                dma(convw[:], convw_d[l], w=[convw])
                dma(alog[:], alog_d[l], w=[alog])
                dma(dtb[:], dtb_d[l], w=[dtb])
                dma(dng[:], dng_d[l], w=[dng])
                st2 = ExitStack()
                BETA = T(st, "BETA", [128, NT, 4])
                NBETA = T(st, "NBETA", [128, NT, 4])
                GRAW = T(st, "GRAW", [128, NT, 4])
                GC = T(st, "GC", [128, NT, 4])
                NEXPG = T(st, "NEXPG", [128, NT, 4])
                negA = T(st, "negA", [128, 4])
                BG = T(st2, "BG", [128, NT, 8])
                AA = T(st2, "AA", [128, NT, 4])
                AX_ = T(st2, "AXs", [128, NT, 4])
                pbg = PS[0]
                for t in range(NT):
                    for c in range(8):
                        mm(pbg[:, t * 8:(t + 1) * 8], hT[:, c, t * 128:(t + 1) * 128], Wba[:, c, :],
                           start=(c == 0), stop=(c == 7), r=[(hT, t // 4), Wba], w=[pbg], inc=(c == 7))
                op("dve", lambda e: e.tensor_copy(out=BG[:].rearrange("p t e -> p (t e)"), in_=pbg[:, 0:NT * 8]),
                   r=[pbg], w=[BG])
                op("act", lambda e: e.activation(out=BETA[:], in_=BG[:, :, 0:4], func=AF.Sigmoid), r=[BG], w=[BETA])
                op("dve", lambda e: e.tensor_scalar(out=NBETA[:], in0=BETA[:], scalar1=-1.0, scalar2=None,
                                                    op0=ALU.mult), r=[BETA], w=[NBETA])
                for h in range(4):
                    op("dve", lambda e: e.tensor_scalar(out=AA[:, :, h], in0=BG[:, :, 4 + h], scalar1=dtb[:, h:h + 1],
                                                        scalar2=None, op0=ALU.add), r=[BG, dtb], w=[AA])
                op("act", lambda e: e.activation(out=AX_[:], in_=AA[:], func=AF.Abs), r=[AA], w=[AX_])
                op("act", lambda e: e.activation(out=AX_[:], in_=AX_[:], func=AF.Exp, scale=-1.0), r=[AX_], w=[AX_])
                op("dve", lambda e: e.tensor_scalar(out=AX_[:], in0=AX_[:], scalar1=1.0, scalar2=None, op0=ALU.add),
                   r=[AX_], w=[AX_])
                op("act", lambda e: e.activation(out=AX_[:], in_=AX_[:], func=AF.Ln), r=[AX_], w=[AX_])
                op("dve", lambda e: e.scalar_tensor_tensor(out=AA[:], in0=AA[:], scalar=0.0, in1=AX_[:],
                                                           op0=ALU.max, op1=ALU.add), r=[AA, AX_], w=[AA])
                op("act", lambda e: e.activation(out=negA[:], in_=alog[:], func=AF.Exp), r=[alog], w=[negA])
                op("dve", lambda e: e.tensor_scalar(out=negA[:], in0=negA[:], scalar1=-1.0, scalar2=None,
                                                    op0=ALU.mult), r=[negA], w=[negA])
                for h in range(4):
                    op("dve", lambda e: e.tensor_scalar(out=GRAW[:, :, h], in0=AA[:, :, h], scalar1=negA[:, h:h + 1],
                                                        scalar2=None, op0=ALU.mult), r=[AA, negA], w=[GRAW])
                pgc = PS[1]
                mm(pgc[:, 0:NT * 4], U, GRAW[:].rearrange("p t e -> p (t e)"), r=[C, GRAW], w=[pgc])
                op("dve", lambda e: e.tensor_copy(out=GC[:].rearrange("p t e -> p (t e)"), in_=pgc[:, 0:NT * 4]),
                   r=[pgc], w=[GC])
                op("act", lambda e: e.activation(out=NEXPG[:], in_=GC[:], func=AF.Exp), r=[GC], w=[NEXPG])
                op("dve", lambda e: e.tensor_scalar(out=NEXPG[:], in0=NEXPG[:], scalar1=-1.0, scalar2=None,
                                                    op0=ALU.mult), r=[NEXPG], w=[NEXPG])
                if "graw" in dbg and l == 0:
                    d = dbg_out("graw", [128, NT, 4])
                    dma(d[:, :, :], GRAW[:], r=[GRAW])
                    d = dbg_out("beta", [128, NT, 4])
                    dma(d[:, :, :], BETA[:], r=[BETA])

                kb.barrier()
                st2.close()
                ck(1)
                pre = [T(st, "pre%d" % i, [128, 515]) for i in range(3)]
                halo = T(st, "halo", [128, 12, 3])
                op("dve", lambda e: e.memset(halo[:], 0.0), w=[halo])
                qkv = [[T(st, "qkv%d_%d" % (i, j), [128, 512]) for j in range(3)] for i in range(2)]
                zs = [T(st, "zs%d" % i, [128, 512], BF16) for i in range(4)]
                cv = [T(st, "cv0", [128, 512])] * 2
                sq = [T(st, "sq0", [128, 512])] * 2
                oTg = [T(st, "oTg%d" % i, [128, 512]) for i in range(2)]
                ogb = [T(st, "ogb0", [128, 512], BF16)] * 2
                Sst = [[T(st, "S%d_%d" % (h, i), [128, 128]) for i in range(2)] for h in range(4)]
                for h in range(4):
                    op("dve", lambda e: e.tensor_scalar(out=R(Sst[h][0][:]), in0=ident, scalar1=0.0, scalar2=None,
                                                        op0=ALU.mult), r=[C], w=[Sst[h][0]])
                spar = [0, 0, 0, 0]
                NCH = 8
                ALIAS = {"Dm": 0, "tq": 0, "DecT": 1, "Xo": 1, "Pe": 2, "ktok": 3, "Xe": 3, "NoffT": 3, "Po": 4,
                         "B": 5, "BT": 6, "WbdT": 6, "Noff": 7, "Z1": 7, "ExpG": 8, "XTo": 8, "Lm": 9, "XTe": 9}
                scr = []
                for i in range(NCH):
                    wide = T(st, "s%d" % i, [128, 9 * 128])
                    plain = T(st, "s%da" % i, [128, 128])
                    d_ = {n: (SV(wide, j - 1) if j > 0 else plain) for n, j in ALIAS.items()}
                    d_["XPo"] = SV(wide, 0, 2)
                    d_["XPe"] = SV(wide, 2, 2)
                    scr.append(d_)
                OUTN = ["W", "QKdT", "kdec", "qg", "vtok", "kTc"]
                outs = [{n: T(st, "o%d_%s" % (i, n), [128, 128]) for n in OUTN} for i in range(NCH)]
                EGL = T(st, "EGL", [128, NCH])
                Yr = [T(st, "Yr%d" % i, [128, 128]) for i in range(2)]
                Vn = [T(st, "Vn%d" % i, [128, 128]) for i in range(2)]

                def prep8(chains, pump=lambda: None):
                    pump_on = [False]

                    def each(fn):
                        for ci, ch in enumerate(chains):
                            fn(ch, ch["s"], ch["o"], PS[ch["i"]])
                            if ci % 2 == 1 and pump_on[0]:
                                pump(1)

                    def f(ch, s, o, pb):
                        h, n, sl = ch["h"], ch["n"], ch["sl"]
                        kT, vT = ch["kT"], ch["vT"]
                        tr(pb[:, 0:128], kT[:, sl], ident, r=[kT, C], w=[pb])
                        tr(pb[:, 128:256], vT[:, sl], ident, r=[vT, C], w=[pb])
                        mm(pb[:, 256:384], GRAW[:, n, h:h + 1].to_broadcast([128, 128]), U, r=[GRAW, C], w=[pb])
                        op("dve", lambda e: e.tensor_scalar(out=s["Dm"][:], in0=pb[:, 256:384],
                                                            scalar1=GC[:, n, h:h + 1], scalar2=0.0,
                                                            op0=ALU.subtract, op1=ALU.min),
                           r=[pb, GC], w=[s["Dm"]])
                        op("dve", lambda e: e.tensor_copy(out=o["vtok"][:], in_=pb[:, 128:256]), r=[pb],
                           w=[o["vtok"]])
                        op("act", lambda e: e.activation(out=R(s["ktok"][:]), in_=pb[:, 0:128], func=AF.Copy),
                           r=[pb], w=[s["ktok"]])
                        op("act", lambda e: e.activation(out=R(s["ExpG"][:]), in_=pb[:, 256:384], func=AF.Exp),
                           r=[pb], w=[s["ExpG"]])
                    each(f)

                    def f(ch, s, o, pb):
                        op("act", lambda e: e.activation(out=R(s["DecT"][:]), in_=s["Dm"][:], func=AF.Exp),
                           r=[s["Dm"]], w=[s["DecT"]])
                    each(f)

                    def f(ch, s, o, pb):
                        h, n, sl = ch["h"], ch["n"], ch["sl"]
                        kT, qT = ch["kT"], ch["qT"]
                        mm(pb[:, 0:128], R(kT[:, sl]), R(kT[:, sl]), r=[kT], w=[pb], inc=False)
                        mm(pb[:, 128:256], R(kT[:, sl]), R(qT[:, sl]), r=[kT, qT], w=[pb])
                        op("dve", lambda e: e.tensor_tensor(out=R(s["Lm"][:]), in0=pb[:, 0:128], in1=s["DecT"][:],
                                                            op=ALU.mult), r=[pb, s["DecT"]], w=[s["Lm"]])
                        op("dve", lambda e: e.tensor_tensor(out=s["tq"][:], in0=pb[:, 128:256], in1=s["DecT"][:],
                                                            op=ALU.mult), r=[pb, s["DecT"]], w=[s["tq"]])
                        op("dve", lambda e: e.scalar_tensor_tensor(out=R(s["B"][:]), in0=s["Lm"][:],
                                                                   scalar=NBETA[:, n, h:h + 1], in1=Mbd,
                                                                   op0=ALU.mult, op1=ALU.mult),
                           r=[s["Lm"], NBETA, C], w=[s["B"]])
                        op("dve", lambda e: e.scalar_tensor_tensor(out=R(s["Noff"][:]), in0=s["Lm"][:],
                                                                   scalar=BETA[:, n, h:h + 1], in1=Moff,
                                                                   op0=ALU.mult, op1=ALU.mult),
                           r=[s["Lm"], BETA, C], w=[s["Noff"]])
                        op("pool", lambda e: e.tensor_tensor(out=R(o["QKdT"][:]), in0=s["tq"][:], in1=U,
                                                             op=ALU.mult), r=[s["tq"], C], w=[o["QKdT"]])
                        op("act", lambda e: e.activation(out=R(o["kdec"][:]), in_=s["ktok"][:], func=AF.Identity,
                                                         scale=s["DecT"][:, 127:128]), r=[s["ktok"], s["DecT"]],
                           w=[o["kdec"]])
                        op("dve", lambda e: e.tensor_tensor(out=R(o["qg"][:]), in0=qT[:, sl], in1=s["ExpG"][:],
                                                            op=ALU.mult), r=[qT, s["ExpG"]], w=[o["qg"]])
                        op("dve", lambda e: e.tensor_copy(out=R(o["kTc"][:]), in_=kT[:, sl]), r=[kT], w=[o["kTc"]])
                        op("dve", lambda e: e.tensor_copy(out=EGL[:, ch["i"]:ch["i"] + 1],
                                                          in_=s["ExpG"][:, 127:128]),
                           r=[s["ExpG"]], w=[(EGL, ch["i"])])
                    each(f)

                    pump_on[0] = True
                    def f(ch, s, o, pb):
                        tr(pb[:, 256:384], s["B"][:], ident, r=[s["B"], C], w=[pb])
                        op("act", lambda e: e.activation(out=R(s["BT"][:]), in_=pb[:, 256:384], func=AF.Copy),
                           r=[pb], w=[s["BT"]])
                        op("dve", lambda e: e.tensor_tensor(out=R(s["Pe"][:]), in0=s["B"][:], in1=ident,
                                                            op=ALU.add), r=[s["B"], C], w=[s["Pe"]])
                    each(f)

                    def f(ch, s, o, pb):
                        mm(pb[:, 0:128], R(s["BT"][:]), R(s["B"][:]), r=[s["BT"], s["B"]], w=[pb])
                        op("act", lambda e: e.activation(out=R(s["Xo"][:]), in_=pb[:, 0:128], func=AF.Copy),
                           r=[pb], w=[s["Xo"]])
                    each(f)
                    pump()
                    for j in range(1, 6):
                        odd = (j % 2 == 1)
                        Xc, Pc, XTc, XPc = ("Xo", "Pe", "XTo", "XPo") if odd else ("Xe", "Po", "XTe", "XPe")
                        Xn, Pn = ("Xe", "Po") if odd else ("Xo", "Pe")

                        def f(ch, s, o, pb):
                            tr(pb[:, 256:384], s[Xc][:], ident, r=[s[Xc], C], w=[pb])
                            op("act", lambda e: e.activation(out=R(s[XTc][:]), in_=pb[:, 256:384], func=AF.Copy),
                               r=[pb], w=[s[XTc]])
                        each(f)
                        pump()

                        def f(ch, s, o, pb):
                            if j < 5:
                                mm(pb[:, 0:256], R(s[XTc][:]), R(s[XPc][:]), r=[s[XTc], s[Xc], s[Pc]], w=[pb])
                                op("act", lambda e: e.activation(out=R(s[Xn][:]), in_=pb[:, 0:128], func=AF.Copy),
                                   r=[pb], w=[s[Xn]])
                                op("dve", lambda e: e.tensor_tensor(out=R(s[Pn][:]), in0=pb[:, 128:256],
                                                                    in1=s[Pc][:], op=ALU.add),
                                   r=[pb, s[Pc]], w=[s[Pn]])
                            else:
                                mm(pb[:, 128:256], R(s[XTc][:]), R(s[Pc][:]), r=[s[XTc], s[Pc]], w=[pb])
                                op("dve", lambda e: e.tensor_tensor(out=R(s[Pn][:]), in0=pb[:, 128:256],
                                                                    in1=s[Pc][:], op=ALU.add),
                                   r=[pb, s[Pc]], w=[s[Pn]])
                        each(f)
                        pump()
                    Pf = "Po"

                    def f(ch, s, o, pb):
                        tr(pb[:, 0:128], s[Pf][:], ident, r=[s[Pf], C], w=[pb])
                        tr(pb[:, 128:256], s["Noff"][:], ident, r=[s["Noff"], C], w=[pb])
                        op("act", lambda e: e.activation(out=R(s["WbdT"][:]), in_=pb[:, 0:128], func=AF.Copy),
                           r=[pb], w=[s["WbdT"]])
                        op("dve", lambda e: e.tensor_copy(out=R(s["NoffT"][:]), in_=pb[:, 128:256]), r=[pb],
                           w=[s["NoffT"]])
                    each(f)
                    pump()

                    def f(ch, s, o, pb):
                        mm(pb[:, 0:128], R(s["NoffT"][:]), R(s[Pf][:]), r=[s["NoffT"], s[Pf]], w=[pb])
                        op("act", lambda e: e.activation(out=R(s["Z1"][:]), in_=pb[:, 0:128], func=AF.Copy),
                           r=[pb], w=[s["Z1"]])
                    each(f)
                    pump()

                    def f(ch, s, o, pb):
                        mm(pb[:, 128:256], R(s["WbdT"][:]), R(s["Z1"][:]), r=[s["WbdT"], s["Z1"]], w=[pb])
                        op("dve", lambda e: e.tensor_tensor(out=R(o["W"][:]), in0=s[Pf][:], in1=pb[:, 128:256],
                                                            op=ALU.subtract), r=[s[Pf], pb], w=[o["W"]])
                    each(f)
                    pump()

                nrec = [0]

                def recur_pair(chs, g):
                    st_ = []
                    for ch in chs:
                        h = ch["h"]
                        So = Sst[h][spar[h]]
                        Sn = Sst[h][1 - spar[h]]
                        spar[h] = 1 - spar[h]
                        bb = 4 * ch["hi"]
                        st_.append((ch, So, Sn, PS[bb], PS[bb + 1], PS[bb + 2], PS[bb + 3],
                                    Yr[nrec[0] % 2], Vn[nrec[0] % 2]))
                        nrec[0] += 1
                    for ch, So, Sn, pa, pb2, pc, pd, Y, vnew in st_:
                        h, n, o = ch["h"], ch["n"], ch["o"]
                        mm(pa[:, 0:128], R(o["kTc"][:]), R(So[:]), r=[o["kTc"], So], w=[pa])
                        op("dve", lambda e: e.scalar_tensor_tensor(out=R(Y[:]), in0=pa[:, 0:128],
                                                                   scalar=NEXPG[:, n, h:h + 1], in1=o["vtok"][:],
                                                                   op0=ALU.mult, op1=ALU.add),
                           r=[pa, NEXPG, o["vtok"]], w=[Y])
                    for ch, So, Sn, pa, pb2, pc, pd, Y, vnew in st_:
                        h, n, o = ch["h"], ch["n"], ch["o"]
                        mm(pb2[:, 0:128], R(o["W"][:]), R(Y[:]), r=[o["W"], Y], w=[pb2])
                        op("act", lambda e: e.activation(out=R(vnew[:]), in_=pb2[:, 0:128], func=AF.Identity,
                                                         scale=BETA[:, n, h:h + 1]), r=[pb2, BETA], w=[vnew])
                    for ch, So, Sn, pa, pb2, pc, pd, Y, vnew in st_:
                        o = ch["o"]
                        mm(pd[:, 0:128], R(o["kdec"][:]), R(vnew[:]), r=[o["kdec"], vnew], w=[pd])
                        op("dve", lambda e: e.scalar_tensor_tensor(out=R(Sn[:]), in0=So[:],
                                                                   scalar=EGL[:, ch["i"]:ch["i"] + 1],
                                                                   in1=pd[:, 0:128], op0=ALU.mult, op1=ALU.add),
                           r=[So, (EGL, ch["i"]), pd], w=[Sn])
                    for ch, So, Sn, pa, pb2, pc, pd, Y, vnew in st_:
                        o, sl = ch["o"], ch["sl"]
                        mm(pc[:, 0:128], R(So[:]), R(o["qg"][:]), start=True, stop=False, r=[So, o["qg"]], w=[pc],
                           inc=False)
                        mm(pc[:, 0:128], R(vnew[:]), R(o["QKdT"][:]), start=False, stop=True, r=[vnew, o["QKdT"]],
                           w=[pc])
                        ot = oTg[ch["hi"]]
                        op("act", lambda e: e.activation(out=ot[:, sl], in_=pc[:, 0:128], func=AF.Copy), r=[pc],
                           w=[(ot, ch["cc"])])

                def stageA(g, hp, par, res):
                    gs = slice(g * 512, (g + 1) * 512)
                    chains = []
                    for hi in range(2):
                        h = 2 * hp + hi
                        qk = qkv[hi]
                        zt = zs[2 * par + hi]
                        for ty in range(4):
                            pb = PS[4 * hi + ty]
                            col = ty * 512 + h * 128
                            for c in range(8):
                                mm(pb[:, :], Wdn[:, c, col:col + 128], hT[:, c, gs], start=(c == 0),
                                   stop=(c == 7), r=[(Wdn, c), (hT, g)], w=[pb], inc=(c == 7))
                            if ty < 3:
                                ch_ = ty * 4 + h
                                op("dve", lambda e: e.tensor_copy(out=pre[ty][:, 0:3], in_=halo[:, ch_, :]),
                                   r=[(halo, ch_)], w=[(pre[ty], 0)])
                                op("act", lambda e: e.activation(out=pre[ty][:, 3:515], in_=pb[:, :],
                                                                 func=AF.Copy), r=[pb], w=[(pre[ty], 1)])
                                op("dve", lambda e: e.tensor_copy(out=halo[:, ch_, :], in_=pre[ty][:, 512:515]),
                                   r=[(pre[ty], 1)], w=[(halo, ch_)])
                                yield
                                cvt = cv[ty % 2]
                                wk = lambda k: convw[:, ch_ * 4 + k:ch_ * 4 + k + 1]
                                op("act", lambda e: e.activation(out=cvt[:], in_=pre[ty][:, 0:512],
                                                                 func=AF.Identity, scale=wk(0)),
                                   r=[pre[ty], convw], w=[cvt])
                                for k in range(1, 4):
                                    op("dve", lambda e: e.scalar_tensor_tensor(out=cvt[:],
                                                                               in0=pre[ty][:, k:k + 512],
                                                                               scalar=wk(k), in1=cvt[:],
                                                                               op0=ALU.mult, op1=ALU.add),
                                       r=[pre[ty], convw, cvt], w=[cvt])
                                    yield
                                op("act", lambda e: e.activation(out=(R(qk[ty][:]) if ty < 2 else qk[ty][:]),
                                                                 in_=cvt[:], func=AF.Silu),
                                   r=[cvt], w=[qk[ty]])
                            else:
                                op("act", lambda e: e.activation(out=zt[:], in_=pb[:, :], func=AF.Silu),
                                   r=[pb], w=[zt])
                            yield
                        for ty in range(2):
                            sqt = sq[ty]
                            pb = PS[4 * hi + ty]
                            op("act", lambda e: e.activation(out=R(sqt[:]), in_=qk[ty][:], func=AF.Square),
                               r=[qk[ty]], w=[sqt])
                            mm(pb[:, :], R(onesr[:]), R(sqt[:]), r=[onesr, sqt], w=[pb])
                            rt_ = cv[0]
                            op("act", lambda e: e.activation(out=rt_[:], in_=pb[:, :], func=AF.Ln,
                                                             bias=epsT[:]), r=[pb, epsT], w=[rt_])
                            op("act", lambda e: e.activation(out=rt_[:], in_=rt_[:], func=AF.Exp, scale=-0.5),
                               r=[rt_], w=[rt_])
                            sc_ = (128.0 ** -0.5) if ty == 0 else 1.0
                            op("dve", lambda e: e.scalar_tensor_tensor(out=R(qk[ty][:]), in0=qk[ty][:],
                                                                       scalar=sc_, in1=rt_[:], op0=ALU.mult,
                                                                       op1=ALU.mult),
                               r=[qk[ty], rt_], w=[qk[ty]])
                            yield
                        if dbg and l == 0 and g == 0 and h == 0:
                            for nm, tt in (("q", qk[0]), ("k", qk[1]), ("v", qk[2])):
                                if nm in dbg:
                                    d = dbg_out(nm, [128, 512])
                                    dma(d[:, :], tt[:], r=[tt])
                        for cc in range(4):
                            i = hi * 4 + cc
                            chains.append({"h": h, "hi": hi, "cc": cc, "n": g * 4 + cc, "i": i,
                                           "sl": slice(cc * 128, (cc + 1) * 128), "qT": qk[0], "kT": qk[1],
                                           "vT": qk[2], "s": scr[i], "o": outs[i]})
                    res["chains"] = chains

                def drain(gen):
                    if gen is not None:
                        for _ in gen:
                            pass

                pairs = [(g, hp) for g in range(NG) for hp in range(2)]
                resA = [dict() for _ in pairs]
                gens = [stageA(g, hp, pi % 2, resA[pi]) for pi, (g, hp) in enumerate(pairs)]
                drain(gens[0])
                for pi, (g, hp) in enumerate(pairs):
                    gs = slice(g * 512, (g + 1) * 512)
                    chains = resA[pi]["chains"]
                    nxt = gens[pi + 1] if pi + 1 < len(pairs) else None

                    def pump(n=1):
                        if nxt is not None:
                            for _ in range(n):
                                if next(nxt, "done") == "done":
                                    break
                    prep8(chains, pump)
                    drain(nxt)
                    for cc in range(4):
                        recur_pair([ch for ch in chains if ch["cc"] == cc], g)
                    for hi in range(2):
                        h = 2 * hp + hi
                        zt = zs[2 * (pi % 2) + hi]
                        ot = oTg[hi]
                        og = ogb[hi]
                        sqt = sq[hi]
                        pb = PS[4 * hi]
                        op("act", lambda e: e.activation(out=R(sqt[:]), in_=ot[:], func=AF.Square), r=[ot],
                           w=[sqt])
                        mm(pb[:, :], R(onesr[:]), R(sqt[:]), r=[onesr, sqt], w=[pb])
                        sqt = cv[0]
                        op("act", lambda e: e.activation(out=sqt[:], in_=pb[:, :], func=AF.Ln,
                                                         scale=1.0 / 128, bias=epsT[:]), r=[pb, epsT], w=[sqt])
                        op("act", lambda e: e.activation(out=sqt[:], in_=sqt[:], func=AF.Exp, scale=-0.5),
                           r=[sqt], w=[sqt])
                        if "odn" in dbg and l == 0 and h == 0 and g == 0:
                            d = dbg_out("odn", [128, 512])
                            dma(d[:, :], ot[:], r=[ot])
                        op("dve", lambda e: e.scalar_tensor_tensor(out=sqt[:], in0=ot[:], scalar=dng[:, 0:1],
                                                                   in1=sqt[:], op0=ALU.mult, op1=ALU.mult),
                           r=[ot, dng, sqt], w=[sqt])
                        op("dve", lambda e: e.tensor_tensor(out=og[:], in0=sqt[:], in1=zt[:], op=ALU.mult),
                           r=[sqt, zt], w=[og])
                        dma(ogdn_d[h, :, gs], og[:], r=[og], w=[(kogdn, g)])
                kb.barrier()

            for st in phase("mb"):
                Wmb = T(st, "Wmb", [128, 8, 2048], BF16)
                for c in range(8):
                    dma(Wmb[:, c, :], wmb_d[l, :, c, :], w=[(Wmb, c)], q="pool")
                Vt = T(st, "Vt", [128, NT, 8, 65], BF16)
                op("dve", lambda e: e.memset(Vt[:, :, :, 64:65], 1.0), w=[Vt])
                for t in range(NT):
                    pb = PS[t % 2]
                    for c in range(8):
                        mm(pb[:, :], hT[:, c, t * 128:(t + 1) * 128], Wmb[:, c, 1024:1536], start=(c == 0),
                           stop=(c == 7), r=[(hT, t // 4), (Wmb, c)], w=[pb], inc=(c == 7))
                    op("act" if t % 2 else "dve",
                       lambda e: (e.activation(out=Vt[:, t, :, 0:64], in_=pb[:, :].rearrange("p (h d) -> p h d", h=8),
                                               func=AF.Copy) if t % 2 else
                                  e.tensor_copy(out=Vt[:, t, :, 0:64],
                                                in_=pb[:, :].rearrange("p (h d) -> p h d", h=8))),
                       r=[pb], w=[(Vt, t)])
                KaT = T(st, "KaT", [128, S], BF16)
                QaT = T(st, "QaT", [128, S], BF16)
                zsm = T(st, "zsm", [64, S], BF16)
                ogm = T(st, "ogm", [64, S], BF16)
                kf = [T(st, "kf%d" % i, [128, 512]) for i in range(2)]
                qf = [T(st, "qf%d" % i, [128, 512]) for i in range(2)]
                sqm = [T(st, "sqm%d" % i, [128, 512], BF16) for i in range(2)]
                kmT = T(st, "kmT", [128, 16])
                shr = T(st, "shr", [64, 512])
                km2 = T(st, "km2", [64, NG + 1])
                gm4 = T(st, "gm4", [128, 4, 16])
                top84 = T(st, "top84", [128, 4, 8])
                mbt4 = [T(st, "mbt4_%d" % i, [128, 4, 16]) for i in range(2)]
                rden = T(st, "rden", [128, 512])
                t1 = [T(st, "t1_%d" % i, [64, 512]) for i in range(2)]
                PT = [T(st, "PT%d" % i, [128, 512], BF16) for i in range(5)]
                nS = [0]
                op("dve", lambda e: e.memset(KaT[0:64, :], 0.0), w=[KaT])
                op("dve", lambda e: e.memset(QaT[0:64, :], 0.0), w=[QaT])
                op("dve", lambda e: e.memset(KaT[32:33, :], 1.0), w=[KaT])
                for n in range(NB):
                    op("dve", lambda e: e.tensor_copy(out=KaT[0:16, n * 256:(n + 1) * 256],
                                                      in_=ident[0:16, n:n + 1].to_broadcast([16, 256])),
                       r=[C], w=[KaT])
                npt = 0
                for h in range(8):
                    op("dve", lambda e: e.tensor_scalar(out=R(kmT[:]), in0=ident[:, 0:16], scalar1=0.0, scalar2=None,
                                                        op0=ALU.mult), r=[C], w=[kmT])
                    def projK(g):
                        pb = PS[g % 2]
                        col = 512 + h * 64
                        for c in range(8):
                            mm(pb[64:128, :], Wmb[:, c, col:col + 64], hT[:, c, g * 512:(g + 1) * 512],
                               start=(c == 0), stop=(c == 7), r=[(Wmb, c), (hT, g)], w=[pb], inc=(c == 7))

                    def postK(g):
                        gs = slice(g * 512, (g + 1) * 512)
                        pb = PS[g % 2]
                        kft = kf[g % 2]
                        op("act", lambda e: e.activation(out=kft[64:128, :], in_=pb[64:128, :], func=AF.Copy),
                           r=[pb], w=[kft])
                        op("act", lambda e: e.activation(out=KaT[64:128, gs], in_=pb[64:128, :], func=AF.Copy),
                           r=[pb], w=[(KaT, g)])
                        op("dve", lambda e: e.tensor_reduce(out=R(kmT[64:128, 2 * g:2 * g + 2]),
                                                            in_=kft[64:128, :].rearrange("p (b t) -> p b t", b=2),
                                                            axis=AX.X, op=ALU.add), r=[kft], w=[kmT])
                        sqt = sqm[g % 2]
                        op("dve", lambda e: e.tensor_tensor(out=sqt[64:128, :], in0=kft[64:128, :],
                                                            in1=kft[64:128, :], op=ALU.mult), r=[kft], w=[sqt])
                        pr = PS[2]
                        mm(pr[32:33, :], onesb[64:128, 0:1], sqt[64:128, :], r=[onesb, sqt], w=[pr])
                        op("dve", lambda e: e.tensor_reduce(out=km2[32:33, g:g + 1], in_=pr[32:33, :], axis=AX.X,
                                                            op=ALU.max), r=[pr], w=[km2])
                    projK(0)
                    for g in range(NG):
                        if g + 1 < NG:
                            projK(g + 1)
                        postK(g)
                    op("dve", lambda e: e.tensor_scalar(out=R(kmT[64:128, :]), in0=kmT[64:128, :], scalar1=1.0 / 256,
                                                        scalar2=None, op0=ALU.mult), r=[kmT], w=[kmT])
                    op("dve", lambda e: e.tensor_reduce(out=km2[32:33, NG:NG + 1], in_=km2[32:33, 0:NG], axis=AX.X,
                                                        op=ALU.max), r=[km2], w=[km2])
                    for g in range(NG):
                        gs = slice(g * 512, (g + 1) * 512)
                        pb = PS[g % 2]
                        col = 1536 + h * 64
                        for c in range(8):
                            mm(pb[0:64, :], Wmb[:, c, col:col + 64], hT[:, c, gs], start=(c == 0), stop=(c == 7),
                               r=[(Wmb, c), (hT, g)], w=[pb], inc=(c == 7))
                        op("act", lambda e: e.activation(out=zsm[:, gs], in_=pb[0:64, :], func=AF.Silu), r=[pb],
                           w=[(zsm, g)])

                    def projQ(g):
                        pb = PS[g % 3]
                        col = h * 64
                        for c in range(8):
                            mm(pb[64:128, :], Wmb[:, c, col:col + 64], hT[:, c, g * 512:(g + 1) * 512],
                               start=(c == 0), stop=(c == 7), r=[(Wmb, c), (hT, g)], w=[pb], inc=(c == 7))

                    def postQ1(g):
                        gs = slice(g * 512, (g + 1) * 512)
                        pb = PS[g % 3]
                        qft = qf[g % 2]
                        op("act", lambda e: e.activation(out=R(qft[64:128, :]), in_=pb[64:128, :], func=AF.Identity,
                                                         scale=0.125), r=[pb], w=[qft])
                        op("act", lambda e: e.activation(out=QaT[64:128, gs], in_=pb[64:128, :], func=AF.Identity,
                                                         scale=0.125), r=[pb], w=[(QaT, g)])
                        sqt = sqm[g % 2]
                        op("dve", lambda e: e.tensor_tensor(out=sqt[64:128, :], in0=qft[64:128, :],
                                                            in1=qft[64:128, :], op=ALU.mult), r=[qft], w=[sqt])
                        pgt = PS[4]
                        for tt in range(4):
                            mm(pgt[:, tt * 16:(tt + 1) * 16], R(qft[64:128, tt * 128:(tt + 1) * 128]), R(kmT[64:128, :]),
                               r=[qft, kmT], w=[pgt], inc=(tt == 3))
                        pr = PS[5]
                        mm(pr[32:33, :], onesb[64:128, 0:1], sqt[64:128, :], r=[onesb, sqt], w=[pr])
                        op("dve", lambda e: e.tensor_tensor(out=gm4[:].rearrange("p a b -> p (a b)"), in0=pgt[:, 0:64],
                                                            in1=C[:, C_PB4 + g * 64:C_PB4 + (g + 1) * 64], op=ALU.add),
                           r=[pgt, C], w=[gm4])
                        op("act", lambda e: e.activation(out=shr[32:33, :], in_=pr[32:33, :], func=AF.Sqrt,
                                                         scale=km2[32:33, NG:NG + 1]), r=[pr, km2], w=[shr])
                        op("act", lambda e: e.activation(out=QaT[32:33, gs], in_=shr[32:33, :], func=AF.Identity,
                                                         scale=-1.0), r=[shr], w=[(QaT, g)])
                        for tt in range(4):
                            op("dve", lambda e: e.max(out=top84[:, tt, :], in_=gm4[:, tt, :]), r=[gm4],
                               w=[(top84, tt)])
                        mb_ = mbt4[g % 2]
                        for tt in range(4):
                            op("dve", lambda e: e.tensor_scalar(out=mb_[:, tt, :], in0=gm4[:, tt, :],
                                                                scalar1=top84[:, tt, 2:3], scalar2=NEG,
                                                                op0=ALU.is_lt, op1=ALU.mult),
                               r=[gm4, (top84, tt)], w=[(mb_, tt)])
                        for t2 in range(2):
                            own = 2 * g + t2
                            op("dve", lambda e: e.memset(mb_[:, 2 * t2:2 * t2 + 2, own:own + 1], 0.0),
                               r=[(mb_, 2 * t2), (mb_, 2 * t2 + 1)], w=[(mb_, 2 * t2), (mb_, 2 * t2 + 1)])

                    def postQ2(g):
                        gs = slice(g * 512, (g + 1) * 512)
                        mb_ = mbt4[g % 2]
                        pt_ = PS[3]
                        for tt in range(4):
                            tr(pt_[0:16, tt * 128:(tt + 1) * 128], mb_[:, tt, :], ident, r=[(mb_, tt), C], w=[pt_])
                        op("act", lambda e: e.activation(out=QaT[0:16, gs], in_=pt_[0:16, 0:512], func=AF.Copy),
                           r=[pt_], w=[(QaT, g)])
                    projQ(0)
                    if NG > 1:
                        projQ(1)
                    for g in range(NG):
                        postQ1(g)
                        if g + 2 < NG:
                            projQ(g + 2)
                        if g > 0:
                            postQ2(g - 1)
                    postQ2(NG - 1)
                    LOOK = 4
                    for jq in range(NB // 2):
                        q0 = jq * 512
                        pO = PS[6 + jq % 2]
                        ntile = 4 * jq + 4
                        qk_ = (QaT, jq)

                        def emitS(kt):
                            pS = PS[nS[0] % 5]
                            nS[0] += 1
                            d = kt - 4 * jq
                            c0 = 0 if d < 0 else 128 * d
                            mm(pS[:, c0:512], KaT[:, kt * 128:(kt + 1) * 128], QaT[:, q0 + c0:q0 + 512], start=True,
                               stop=(d < 0), r=[(KaT, kt // 4), qk_], w=[pS], inc=(d < 0))
                            if d >= 0:
                                mm(pS[:, c0:c0 + 128], identb[:], trib[:], start=False, stop=True, r=[identb, trib],
                                   w=[pS])
                            return pS, c0

                        pendq = [emitS(k_) for k_ in range(min(LOOK, ntile))]
                        for kt in range(ntile):
                            pS, c0 = pendq.pop(0)
                            if kt + LOOK < ntile:
                                pendq.append(emitS(kt + LOOK))
                            ptile = PT[npt % 5]
                            npt += 1
                            op("act", lambda e: e.activation(out=ptile[:, c0:512], in_=pS[:, c0:512], func=AF.Exp),
                               r=[pS], w=[ptile])
                            mm(pO[0:65, c0:512], Vt[:, kt, h, :], ptile[:, c0:512], start=(kt == 0),
                               stop=(kt == ntile - 1), r=[(Vt, kt), ptile], w=[pO], inc=(kt == ntile - 1))
                        op("dve", lambda e: e.reciprocal(out=R(rden[64:65, :]), in_=pO[64:65, 0:512]), r=[pO],
                           w=[rden])
                        pB = PS[5]
                        mm(pB[0:64, 0:512], R(onesr[64:65, 0:64]), R(rden[64:65, :]), r=[onesr, rden], w=[pB])
                        tt1 = t1[jq % 2]
                        op("dve", lambda e: e.tensor_tensor(out=tt1[:], in0=pO[0:64, 0:512],
                                                            in1=zsm[:, q0:q0 + 512], op=ALU.mult),
                           r=[pO, (zsm, jq)], w=[tt1])
                        op("dve", lambda e: e.tensor_tensor(out=ogm[:, q0:q0 + 512], in0=tt1[:], in1=pB[0:64, 0:512],
                                                            op=ALU.mult), r=[tt1, pB], w=[(ogm, jq)])
                    if "omb" in dbg and l == 0 and h == 0:
                        d = dbg_out("omb", [64, S])
                        tmp = T(st, "dbgomb", [64, S])
                        op("dve", lambda e: e.tensor_copy(out=tmp[:], in_=ogm[:]), r=[ogm], w=[tmp])
                        dma(d[:, :], tmp[:], r=[tmp])
                    dma(ogmb_d[h, :, :], ogm[:], r=[ogm], w=[kogmb])
                kb.barrier()

            for st in phase("fin"):
                Wpdn = T(st, "Wpdn", [128, 4, 1024], BF16)
                Wpmb = T(st, "Wpmb", [64, 8, 1024], BF16)
                Wout = T(st, "Wout", [128, 8, 1024], BF16)
                Wmg = T(st, "Wmg", [128, 8, 2048], BF16)
                dma(Wpdn[:], wpdn_d[l], w=[Wpdn], q="pool")
                dma(Wpmb[:], wpmb_d[l], w=[Wpmb], q="pool")
                for c in range(8):
                    dma(Wmg[:, c, :], wmg_d[l, :, c, :], w=[(Wmg, c)], q="pool")
                for c in range(8):
                    dma(Wout[:, c, :], wout_d[l, :, c, :], w=[(Wout, c)], q="pool")
                OGD = [T(st, "OGD%d" % i, [128, 4, 512], BF16) for i in range(2)]
                OGM = [T(st, "OGM0", [64, 8, 512], BF16)] * 2
                mixT = T(st, "mixT", [128, 8, 512], BF16)
                gd = [T(st, "gd%d" % i, [128, 512]) for i in range(2)]
                gmm = [T(st, "gmm%d" % i, [128, 512]) for i in range(2)]
                u1 = [T(st, "u1_%d" % i, [128, 512]) for i in range(2)]
                u2 = [T(st, "u2_%d" % i, [128, 512]) for i in range(2)]
                xr = [T(st, "xr%d" % i, [128, 1024]) for i in range(2)]
                res = [T(st, "res%d" % i, [128, 512]) for i in range(2)]
                junk2 = T(st, "junk2", [128, 512], BF16)
                GPl = T(st, "GPl", [128, 1024])
                dma(GPl[:], gp_d[l], r=[(kgp, l)], w=[GPl])
                ss2 = T(st, "ss2", [128, NT, 2])
                rs2 = T(st, "rs2", [128, NT])
                def load_og(g):
                    gs_ = slice(g * 512, (g + 1) * 512)
                    dma(OGD[g % 2][:], ogdn_d[:, :, gs_].rearrange("h p s -> p h s"), r=[(kogdn, g)], w=[OGD[g % 2]])
                    dma(OGM[g % 2][:], ogmb_d[:, :, gs_].rearrange("h p s -> p h s"), r=[kogmb], w=[OGM[g % 2]])

                load_og(0)
                for g in range(NG):
                    gs = slice(g * 512, (g + 1) * 512)
                    ogd, ogmm = OGD[g % 2], OGM[g % 2]
                    for d_ in range(8):
                        ds_ = slice(d_ * 128, (d_ + 1) * 128)
                        pa, pb, pc, pd = PS[0 + 4 * (d_ % 2)], PS[1 + 4 * (d_ % 2)], PS[2 + 4 * (d_ % 2)], PS[3 + 4 * (d_ % 2)]
                        for h in range(4):
                            mm(pa[:, :], Wpdn[:, h, ds_], ogd[:, h, :], start=(h == 0), stop=(h == 3),
                               r=[Wpdn, ogd], w=[pa], inc=(h == 3))
                        for h in range(8):
                            mm(pb[:, :], Wpmb[:, h, ds_], ogmm[:, h, :], start=(h == 0), stop=(h == 7),
                               r=[Wpmb, ogmm], w=[pb], inc=(h == 7))
                        for c in range(8):
                            mm(pc[:, :], Wmg[:, c, ds_], hT[:, c, gs], start=(c == 0), stop=(c == 7),
                               r=[(Wmg, c), (hT, g)], w=[pc], inc=(c == 7))
                        for c in range(8):
                            mm(pd[:, :], Wmg[:, c, 1024 + d_ * 128:1024 + (d_ + 1) * 128], hT[:, c, gs],
                               start=(c == 0), stop=(c == 7), r=[(Wmg, c), (hT, g)], w=[pd], inc=(c == 7))
                        gdt, gmt, u1t, u2t = gd[d_ % 2], gmm[d_ % 2], u1[d_ % 2], u2[d_ % 2]
                        op("act", lambda e: e.activation(out=gdt[:], in_=pc[:, :], func=AF.Sigmoid), r=[pc], w=[gdt])
                        op("act", lambda e: e.activation(out=gmt[:], in_=pd[:, :], func=AF.Sigmoid), r=[pd], w=[gmt])
                        op("dve", lambda e: e.tensor_tensor(out=u1t[:], in0=pa[:, :], in1=gdt[:], op=ALU.mult),
                           r=[pa, gdt], w=[u1t])
                        op("dve", lambda e: e.tensor_tensor(out=u2t[:], in0=pb[:, :], in1=gmt[:], op=ALU.mult),
                           r=[pb, gmt], w=[u2t])
                        op("dve", lambda e: e.tensor_tensor(out=mixT[:, d_, :], in0=u1t[:], in1=u2t[:], op=ALU.add),
                           r=[u1t, u2t], w=[(mixT, d_)])
                    if g + 1 < NG:
                        load_og(g + 1)
                    for tt in range(4):
                        t = g * 4 + tt
                        xrt = xr[t % 2]
                        dma(xrt[:], xin_d[t * 128:(t + 1) * 128, :], r=[(xin_k, t)], w=[xrt])
                        for hf in range(2):
                            pb = PS[(t % 2) * 2 + hf]
                            for d_ in range(8):
                                mm(pb[:, :], mixT[:, d_, tt * 128:(tt + 1) * 128], Wout[:, d_, hf * 512:(hf + 1) * 512],
                                   start=(d_ == 0), stop=(d_ == 7), r=[(mixT, d_), (Wout, d_)], w=[pb],
                                   inc=(d_ == 7))
                            op("act", lambda e: e.activation(out=junk2[:], in_=pb[:, :], func=AF.Square,
                                                             accum_out=ss2[:, t, hf:hf + 1]), r=[pb],
                               w=[junk2, (ss2, t)])
                        op("dve", lambda e: e.tensor_tensor(out=rs2[:, t:t + 1], in0=ss2[:, t, 0:1],
                                                            in1=ss2[:, t, 1:2], op=ALU.add), r=[(ss2, t)],
                           w=[(rs2, t)])
                        op("act", lambda e: e.activation(out=rs2[:, t:t + 1], in_=rs2[:, t:t + 1], func=AF.Sqrt,
                                                         scale=1.0 / D_MODEL, bias=epsT[:]), r=[(rs2, t), epsT],
                           w=[(rs2, t)])
                        op("dve", lambda e: e.reciprocal(out=rs2[:, t:t + 1], in_=rs2[:, t:t + 1]), r=[(rs2, t)],
                           w=[(rs2, t)])
                        for hf in range(2):
                            pb = PS[(t % 2) * 2 + hf]
                            hs = slice(hf * 512, (hf + 1) * 512)
                            rt = res[hf]
                            op("dve", lambda e: e.scalar_tensor_tensor(out=rt[:], in0=pb[:, :],
                                                                       scalar=rs2[:, t:t + 1], in1=GPl[:, hs],
                                                                       op0=ALU.mult, op1=ALU.mult),
                               r=[pb, (rs2, t), GPl], w=[rt])
                            op("dve", lambda e: e.tensor_tensor(out=xrt[:, hs], in0=rt[:], in1=xrt[:, hs],
                                                                op=ALU.add), r=[rt, (xrt, hf)], w=[(xrt, hf)])
                        dma(xout_d[t * 128:(t + 1) * 128, :], xrt[:], r=[xrt], w=[(xout_k, t)], q="pool")
                kb.barrier()
          except _Stop:
            kb.barrier()
            curgen[0].close()
            break
        kb.finish()
        stuck = kb.simulate()
        print("deadlock check:", stuck if stuck else "ok")
        print("instructions:", kb.ninstr, "counts", kb.count, "dmas", kb.ndmaq)
    return nc, dbg_d


def _pc(w):
    sh = w.shape
    w = w.reshape(sh[:-2] + (8, 128, sh[-1]))
    return np.ascontiguousarray(np.swapaxes(w, -3, -2))


def host_layout(inp):
    f = lambda a: np.ascontiguousarray(np.asarray(a, dtype=np.float32))
    w_in = f(inp["w_in"])
    depth = w_in.shape[0]
    shared = {
        "wada": _pc(f(inp["w_ada"])),
        "bada": f(inp["b_ada"]).reshape(depth, 1, 3072),
        "gpre": f(inp["g_pre"]).reshape(depth, 1, 1024),
        "gpost": f(inp["g_post"]).reshape(depth, 1, 1024),
        "wdn": _pc(w_in[:, :, 0:2048]),
        "wba": _pc(w_in[:, :, 2048:2056]),
        "wmb": _pc(w_in[:, :, 2056:4104]),
        "wmg": _pc(w_in[:, :, 4104:6152]),
        "convw": np.ascontiguousarray(
            f(inp["conv_w"]).transpose(0, 2, 1).reshape(depth, 12, 128, 4).transpose(0, 2, 1, 3)
        ).reshape(depth, 128, 48),
        "alog": np.ascontiguousarray(np.broadcast_to(f(inp["a_log"])[:, None, :], (depth, 128, 4))),
        "dtb": np.ascontiguousarray(np.broadcast_to(f(inp["dt_bias"])[:, None, :], (depth, 128, 4))),
        "dng": f(inp["dn_norm_g"]).reshape(depth, 128, 1),
        "wpdn": np.ascontiguousarray(f(inp["w_proj_dn"]).reshape(depth, 4, 128, 1024).transpose(0, 2, 1, 3)),
        "wpmb": np.ascontiguousarray(f(inp["w_proj_mb"]).reshape(depth, 8, 64, 1024).transpose(0, 2, 1, 3)),
        "wout": _pc(f(inp["w_out"])),
        "consts": make_consts(),
    }
    x = f(inp["x"])
    c = f(inp["c"])
    maps = []
    for b in range(x.shape[0]):
        m = dict(shared)
        m["x"] = x[b]
        m["cT"] = np.ascontiguousarray(c[b].reshape(8, 128).T)
        maps.append(m)
    return maps


_CACHE = {}


def kernel(**inputs):
    x = np.asarray(inputs["x"])
    B, S, _ = x.shape
    depth = np.asarray(inputs["w_in"]).shape[0]
    key = (S, depth)
    if key not in _CACHE:
        _CACHE[key] = build(S=S, DEPTH=depth)[0]
    nc = _CACHE[key]
    maps = host_layout(inputs)
    res = run_bass_kernel_spmd(nc, maps, core_ids=list(range(B)))
    return np.stack([np.asarray(r["y"], dtype=np.float32) for r in res.results], axis=0)
```
